# Optimizing a Trainium2 kernel written in Bass

```python
import math
import jax, jax.numpy as jnp
from jax import lax
import numpy as np

D_MODEL = 2048
BATCH = 1
SEQ = 8192
DEPTH = 4

GRID_W = 64
CTX_LEN = 256
HEAD_DIM = 128
ROPE_THETA = 10000.0
RMS_EPS = 1e-6
N_MOD = 6
MLP_HIDDEN = 4 * D_MODEL
SSM_INNER = D_MODEL
SSM_HEAD_DIM = 64
SSM_HEADS = SSM_INNER // SSM_HEAD_DIM
SSM_GROUPS = 8
SSM_STATE = 128
SSM_NORM_GROUPS = 8
CONV_K = 5
SSD_CHUNK = 128
CONV_DIM = SSM_INNER + 2 * SSM_GROUPS * SSM_STATE
NA_WIDTH = D_MODEL // 2
NA_HEADS = NA_WIDTH // HEAD_DIM
NA_ROWS = 8
NA_COLS = 16
AB_IN = SSM_INNER + CONV_DIM + 2 * SSM_HEADS + 3 * NA_WIDTH
AB_OUT = SSM_INNER + NA_WIDTH
ATTN_HEADS = D_MODEL // HEAD_DIM
KV_HEADS = ATTN_HEADS // 4
Q_BLOCK = 128
C_QKV = (ATTN_HEADS + 2 * KV_HEADS) * HEAD_DIM

kernel_name = 'hybrid_ssd_natten_gqa_diffusion_trunk'

F32 = jnp.float32


def _rms(u, w):
    uf = u.astype(F32)
    y = uf * lax.rsqrt(jnp.mean(uf * uf, axis=-1, keepdims=True) + RMS_EPS)
    return (y * w.astype(F32)).astype(u.dtype)


def _mlp(h, w1, w2):
    a = jax.nn.relu(h @ w1)
    return (a * a) @ w2


def _rope_tables(n, dtype):
    t = jnp.arange(n)
    half = HEAD_DIM // 2
    inv = 1.0 / (ROPE_THETA ** (jnp.arange(0, half, 2, dtype=F32) / half))
    ang = jnp.concatenate([(t // GRID_W).astype(F32)[:, None] * inv,
                           (t % GRID_W).astype(F32)[:, None] * inv], axis=-1)
    return jnp.cos(ang).astype(dtype), jnp.sin(ang).astype(dtype)


def _rope(u, cos, sin):
    pairs = u.reshape(u.shape[:-1] + (HEAD_DIM // 2, 2))
    u1, u2 = pairs[..., 0], pairs[..., 1]
    c = cos[None, :, None, :]
    s = sin[None, :, None, :]
    return jnp.stack([u1 * c - u2 * s, u1 * s + u2 * c], axis=-1).reshape(u.shape)


def _attend_blocks(q, k, v):
    b, lq, hq, d = q.shape
    hkv = k.shape[2]
    g = hq // hkv
    scale = d ** -0.5
    qb = q.reshape(b, lq // Q_BLOCK, Q_BLOCK, hkv, g, d).swapaxes(0, 1)

    def one_block(qi):
        s = jnp.einsum('bqkgd,bskd->bkgqs', qi, k).astype(F32) * scale
        p = jax.nn.softmax(s, axis=-1).astype(v.dtype)
        return jnp.einsum('bkgqs,bskd->bqkgd', p, v)

    out = lax.map(one_block, qb)
    return out.swapaxes(0, 1).reshape(b, lq, hq * d)


def _neighbourhood_attention(q, k, v, k_ctx, v_ctx, rpb, rows):
    b, n, h, d = q.shape
    kr = min(NA_ROWS, rows)
    scale = d ** -0.5
    r = jnp.arange(rows)
    row_idx = jnp.clip(r - kr // 2, 0, rows - kr)[:, None] + jnp.arange(kr)[None, :]
    col = jnp.arange(GRID_W)
    c0 = jnp.clip(col - NA_COLS // 2, 0, GRID_W - NA_COLS)
    col_mask = (col[None, :] >= c0[:, None]) & (col[None, :] < c0[:, None] + NA_COLS)
    qg = q.reshape(b, rows, GRID_W, h, d)
    kg = k.reshape(b, rows, GRID_W, h, d)[:, row_idx]
    vg = v.reshape(b, rows, GRID_W, h, d)[:, row_idx]
    dr = row_idx - r[:, None] + (NA_ROWS - 1)
    dc = jnp.clip(col[None, :] - col[:, None], 1 - NA_COLS, NA_COLS - 1) + (NA_COLS - 1)
    bias = jnp.take(rpb[:, dr], dc, axis=-1).transpose(0, 1, 3, 2, 4).astype(F32)
    s_win = jnp.einsum('brqhd,brikhd->bhrqik', qg, kg).astype(F32) * scale + bias[None]
    s_win = jnp.where(col_mask[:, None, :], s_win, -jnp.inf).reshape(b, h, rows, GRID_W, kr * GRID_W)
    s_ctx = jnp.einsum('brqhd,bmhd->bhrqm', qg, k_ctx).astype(F32) * scale
    p = jax.nn.softmax(jnp.concatenate([s_win, s_ctx], axis=-1), axis=-1).astype(v.dtype)
    p_win = p[..., :kr * GRID_W].reshape(b, h, rows, GRID_W, kr, GRID_W)
    p_ctx = p[..., kr * GRID_W:]
    out = jnp.einsum('bhrqik,brikhd->brqhd', p_win, vg) + jnp.einsum('bhrqm,bmhd->brqhd', p_ctx, v_ctx)
    return out.reshape(b, n, h * d)


def _ssd_chunked(xs, dt, a, bm, cm, state0):
    b, l, h, p = xs.shape
    g, n = bm.shape[-2:]
    k = h // g
    nc = l // SSD_CHUNK
    xd = (xs.astype(F32) * dt[..., None]).reshape(b, nc, SSD_CHUNK, g, k, p)
    la = jnp.cumsum((dt * a).reshape(b, nc, SSD_CHUNK, g, k), axis=2)
    bc = bm.reshape(b, nc, SSD_CHUNK, g, n)
    cc = cm.reshape(b, nc, SSD_CHUNK, g, n)
    lower = jnp.tril(jnp.ones((SSD_CHUNK, SSD_CHUNK), dtype=bool))
    seg = la[:, :, :, None] - la[:, :, None, :]
    decay = jnp.exp(jnp.where(lower[:, :, None, None], seg, -jnp.inf))
    cb = jnp.einsum('bclgn,bcsgn->bclsg', cc, bc).astype(F32)
    y_diag = jnp.einsum('bclsgk,bcsgkp->bclgkp', cb[..., None] * decay, xd)
    decay_end = jnp.exp(la[:, :, -1:] - la)
    chunk_states = jnp.einsum('bclgn,bclgk,bclgkp->bcgkpn', bc.astype(F32), decay_end, xd)
    chunk_decay = jnp.exp(la[:, :, -1])

    def step(state, inp):
        dec, st = inp
        return state * dec[..., None, None] + st, state

    final, prev = lax.scan(step, state0, (chunk_decay.swapaxes(0, 1), chunk_states.swapaxes(0, 1)))
    y_off = jnp.einsum('bclgn,bcgkpn,bclgk->bclgkp', cc.astype(F32), prev.swapaxes(0, 1), jnp.exp(la))
    return (y_diag + y_off).reshape(b, l, h, p).astype(xs.dtype), final


def _bi_ssd(xs, bm, cm, dt_raw, dt_bias, a_log, d_skip, init_f, init_b):
    inits = (init_f, init_b)
    ys, finals = [], []
    for d in range(2):
        rev = (lambda t: jnp.flip(t, axis=1)) if d == 1 else (lambda t: t)
        dt = jax.nn.softplus((dt_raw[:, :, d] + dt_bias[d]).astype(F32))
        a = -jnp.exp(a_log[d].astype(F32))
        y, fin = _ssd_chunked(rev(xs), rev(dt), a, rev(bm), rev(cm), inits[d])
        ys.append(rev(y) + d_skip[d][:, None].astype(xs.dtype) * xs)
        finals.append(fin)
    return ys[0] + ys[1], finals[0], finals[1]


def _dwconv(u, w, bias):
    out = lax.conv_general_dilated(u, w[:, None, :], window_strides=(1,),
                                   padding=[(CONV_K // 2, CONV_K // 2)],
                                   dimension_numbers=('NWC', 'WIO', 'NWC'),
                                   feature_group_count=u.shape[-1])
    return out + bias


def _ab_inputs(h, w_in, conv_w, conv_b):
    b, l, _ = h.shape
    proj = h @ w_in
    i0 = SSM_INNER
    i1 = i0 + CONV_DIM
    i2 = i1 + 2 * SSM_HEADS
    z = proj[..., :i0]
    xbc = jax.nn.silu(_dwconv(proj[..., i0:i1], conv_w, conv_b))
    dt_raw = proj[..., i1:i2].reshape(b, l, 2, SSM_HEADS)
    gn = SSM_GROUPS * SSM_STATE
    xs = xbc[..., :SSM_INNER].reshape(b, l, SSM_HEADS, SSM_HEAD_DIM)
    bm = xbc[..., SSM_INNER:SSM_INNER + gn].reshape(b, l, SSM_GROUPS, SSM_STATE)
    cm = xbc[..., SSM_INNER + gn:].reshape(b, l, SSM_GROUPS, SSM_STATE)
    q, k, v = jnp.split(proj[..., i2:], 3, axis=-1)
    shp = (b, l, NA_HEADS, HEAD_DIM)
    return z, xs, bm, cm, dt_raw, q.reshape(shp), k.reshape(shp), v.reshape(shp)


def _gated_norm(y, z, w):
    b, l = z.shape[:2]
    u = (y.reshape(b, l, SSM_INNER) * jax.nn.silu(z)).astype(F32).reshape(b, l, SSM_NORM_GROUPS, -1)
    u = u * lax.rsqrt(jnp.mean(u * u, axis=-1, keepdims=True) + RMS_EPS)
    return (u.reshape(b, l, SSM_INNER) * w.astype(F32)).astype(z.dtype)


def _mixer_ab(h_lat, h_ctx, rows, need_ctx, w_in, conv_w, conv_b, dt_bias, a_log, d_skip,
              norm_w, q_norm, k_norm, rpb, w_out):
    zc, xc, bc, cc, dtc, qc, kc, vc = _ab_inputs(h_ctx, w_in, conv_w, conv_b)
    zl, xl, bl, cl, dtl, ql, kl, vl = _ab_inputs(h_lat, w_in, conv_w, conv_b)
    b = h_lat.shape[0]
    zero = jnp.zeros((b, SSM_GROUPS, SSM_HEADS // SSM_GROUPS, SSM_HEAD_DIM, SSM_STATE), F32)
    yc, sf, sb = _bi_ssd(xc, bc, cc, dtc, dt_bias, a_log, d_skip, zero, zero)
    yl, _, _ = _bi_ssd(xl, bl, cl, dtl, dt_bias, a_log, d_skip, sf, sb)
    kc = _rms(kc, k_norm)
    al = _neighbourhood_attention(_rms(ql, q_norm), _rms(kl, k_norm), vl, kc, vc, rpb, rows)
    out_lat = jnp.concatenate([_gated_norm(yl, zl, norm_w), al], axis=-1) @ w_out
    out_ctx = None
    if need_ctx:
        ac = _attend_blocks(_rms(qc, q_norm), kc, vc)
        out_ctx = jnp.concatenate([_gated_norm(yc, zc, norm_w), ac], axis=-1) @ w_out
    return out_lat, out_ctx


def _gqa_inputs(h, w_qkv, q_norm, k_norm):
    b, l, _ = h.shape
    p = h @ w_qkv
    qd = ATTN_HEADS * HEAD_DIM
    kd = KV_HEADS * HEAD_DIM
    q = _rms(p[..., :qd].reshape(b, l, ATTN_HEADS, HEAD_DIM), q_norm)
    k = _rms(p[..., qd:qd + kd].reshape(b, l, KV_HEADS, HEAD_DIM), k_norm)
    v = p[..., qd + kd:].reshape(b, l, KV_HEADS, HEAD_DIM)
    return q, k, v


def _mixer_c(h_lat, h_ctx, cos, sin, need_ctx, w_qkv, q_norm, k_norm, w_out):
    ql, kl, vl = _gqa_inputs(h_lat, w_qkv, q_norm, k_norm)
    qc, kc, vc = _gqa_inputs(h_ctx, w_qkv, q_norm, k_norm)
    ql = _rope(ql, cos, sin)
    kl = _rope(kl, cos, sin)
    k_all = jnp.concatenate([kc, kl], axis=1)
    v_all = jnp.concatenate([vc, vl], axis=1)
    out_lat = _attend_blocks(ql, k_all, v_all) @ w_out
    out_ctx = _attend_blocks(qc, kc, vc) @ w_out if need_ctx else None
    return out_lat, out_ctx


def setup_inputs(seed: int = 0) -> dict:
    key = jax.random.key(seed)
    ks = iter(jax.random.split(key, 40))
    n_even = (DEPTH + 1) // 2
    n_odd = DEPTH // 2

    def nrm(shape, scale):
        return jax.random.normal(next(ks), shape, jnp.float32) * scale

    x = nrm((BATCH, SEQ, D_MODEL), 1.0)
    c = nrm((BATCH, D_MODEL), 1.0)
    ctx = nrm((BATCH, CTX_LEN, D_MODEL), 1.0)
    c_ctx = nrm((D_MODEL,), 1.0)
    w_mod = nrm((DEPTH, D_MODEL, N_MOD * D_MODEL), 0.5 * D_MODEL ** -0.5)
    b_mod = nrm((DEPTH, N_MOD * D_MODEL), 0.02)
    norm1_w = 1.0 + nrm((DEPTH, D_MODEL), 0.05)
    norm2_w = 1.0 + nrm((DEPTH, D_MODEL), 0.05)
    w_mlp_in = nrm((DEPTH, D_MODEL, MLP_HIDDEN), D_MODEL ** -0.5)
    w_mlp_out = nrm((DEPTH, MLP_HIDDEN, D_MODEL), MLP_HIDDEN ** -0.5)
    ab_w_in = nrm((n_even, D_MODEL, AB_IN), D_MODEL ** -0.5)
    ab_conv_w = nrm((n_even, CONV_K, CONV_DIM), CONV_K ** -0.5)
    ab_conv_b = nrm((n_even, CONV_DIM), 0.02)
    dt0 = jnp.exp(jax.random.uniform(next(ks), (n_even, 2, SSM_HEADS), jnp.float32,
                                     minval=math.log(1e-3), maxval=math.log(1e-1)))
    ab_dt_bias = dt0 + jnp.log(-jnp.expm1(-dt0))
    ab_a_log = jnp.log(jax.random.uniform(next(ks), (n_even, 2, SSM_HEADS), jnp.float32, minval=1.0, maxval=16.0))
    ab_d_skip = 1.0 + nrm((n_even, 2, SSM_HEADS), 0.1)
    ab_norm_w = 1.0 + nrm((n_even, SSM_INNER), 0.05)
    ab_q_norm = 1.0 + nrm((n_even, HEAD_DIM), 0.05)
    ab_k_norm = 1.0 + nrm((n_even, HEAD_DIM), 0.05)
    ab_rpb = nrm((n_even, NA_HEADS, 2 * NA_ROWS - 1, 2 * NA_COLS - 1), 0.1)
    ab_w_out = nrm((n_even, AB_OUT, D_MODEL), AB_OUT ** -0.5)
    c_w_qkv = nrm((n_odd, D_MODEL, C_QKV), D_MODEL ** -0.5)
    c_q_norm = 1.0 + nrm((n_odd, HEAD_DIM), 0.05)
    c_k_norm = 1.0 + nrm((n_odd, HEAD_DIM), 0.05)
    c_w_out = nrm((n_odd, ATTN_HEADS * HEAD_DIM, D_MODEL), (ATTN_HEADS * HEAD_DIM) ** -0.5)
    return {'x': x, 'c': c, 'ctx': ctx, 'c_ctx': c_ctx, 'w_mod': w_mod, 'b_mod': b_mod,
            'norm1_w': norm1_w, 'norm2_w': norm2_w, 'w_mlp_in': w_mlp_in, 'w_mlp_out': w_mlp_out,
            'ab_w_in': ab_w_in, 'ab_conv_w': ab_conv_w, 'ab_conv_b': ab_conv_b, 'ab_dt_bias': ab_dt_bias,
            'ab_a_log': ab_a_log, 'ab_d_skip': ab_d_skip, 'ab_norm_w': ab_norm_w, 'ab_q_norm': ab_q_norm,
            'ab_k_norm': ab_k_norm, 'ab_rpb': ab_rpb, 'ab_w_out': ab_w_out, 'c_w_qkv': c_w_qkv,
            'c_q_norm': c_q_norm, 'c_k_norm': c_k_norm, 'c_w_out': c_w_out}


def reference(x, c, ctx, c_ctx, w_mod, b_mod, norm1_w, norm2_w, w_mlp_in, w_mlp_out,
              ab_w_in, ab_conv_w, ab_conv_b, ab_dt_bias, ab_a_log, ab_d_skip, ab_norm_w,
              ab_q_norm, ab_k_norm, ab_rpb, ab_w_out, c_w_qkv, c_q_norm, c_k_norm, c_w_out):
    n_lat = x.shape[1]
    rows = n_lat // GRID_W
    cos, sin = _rope_tables(n_lat, x.dtype)
    s_lat = jax.nn.silu(c)
    s_ctx = jax.nn.silu(c_ctx)
    for layer in range(DEPTH):
        need_ctx = layer < DEPTH - 1
        mod_l = (s_lat @ w_mod[layer] + b_mod[layer])[:, None, :]
        sh1, sc1, g1, sh2, sc2, g2 = jnp.split(mod_l, N_MOD, axis=-1)
        mod_c = s_ctx @ w_mod[layer] + b_mod[layer]
        csh1, csc1, cg1, csh2, csc2, cg2 = jnp.split(mod_c, N_MOD, axis=-1)
        h_lat = _rms(x, norm1_w[layer]) * (1 + sc1) + sh1
        h_ctx = _rms(ctx, norm1_w[layer]) * (1 + csc1) + csh1
        i = layer // 2
        if layer % 2 == 0:
            m_lat, m_ctx = _mixer_ab(h_lat, h_ctx, rows, need_ctx, ab_w_in[i], ab_conv_w[i], ab_conv_b[i],
                                     ab_dt_bias[i], ab_a_log[i], ab_d_skip[i], ab_norm_w[i],
                                     ab_q_norm[i], ab_k_norm[i], ab_rpb[i], ab_w_out[i])
        else:
            m_lat, m_ctx = _mixer_c(h_lat, h_ctx, cos, sin, need_ctx, c_w_qkv[i], c_q_norm[i],
                                    c_k_norm[i], c_w_out[i])
        x = x + g1 * m_lat
        x = x + g2 * _mlp(_rms(x, norm2_w[layer]) * (1 + sc2) + sh2, w_mlp_in[layer], w_mlp_out[layer])
        if need_ctx:
            ctx = ctx + cg1 * m_ctx
            ctx = ctx + cg2 * _mlp(_rms(ctx, norm2_w[layer]) * (1 + csc2) + csh2, w_mlp_in[layer], w_mlp_out[layer])
    return x
```

```python
import numpy as np
from contextlib import ExitStack

import concourse.bass as bass
import concourse.mybir as mybir
from concourse.bass_utils import run_bass_kernel_spmd

F32 = mybir.dt.float32
BF16 = mybir.dt.bfloat16
AF = mybir.ActivationFunctionType
ALU = mybir.AluOpType
AX = mybir.AxisListType

NCORES = 8


class Buf:
    __slots__ = ("ap", "w", "wd", "r", "rd", "name")

    def __init__(self, ap, name=""):
        self.ap = ap
        self.w = {}
        self.wd = []
        self.r = {}
        self.rd = []
        self.name = name

    def __getitem__(self, idx):
        return self.ap[idx]


class Prog:
    ENGS = ("pe", "act", "dve", "pool", "sp")
    DMA_SLOTS = 8

    def __init__(self):
        self.nc = bass.Bass("TRN2", target_bir_lowering=False)
        self.stack = ExitStack()
        self.ops = []
        self.n_by_eng = {e: 0 for e in self.ENGS}
        self._cnt = 0

    def dram_in(self, name, shape, dtype=F32):
        return self.nc.dram_tensor(name, list(shape), dtype, kind="ExternalInput").ap()

    def dram_out(self, name, shape, dtype=F32):
        return self.nc.dram_tensor(name, list(shape), dtype, kind="ExternalOutput").ap()

    ARENA_WORDS = 52736

    def _ensure_arena(self):
        if getattr(self, "arena", None) is None:
            self.arena = self.stack.enter_context(
                self.nc.sbuf_tensor("arena", [128, self.ARENA_WORDS], F32))
            self.top = 0
            self.banks = [Buf(self.stack.enter_context(self.nc.psum_tensor(f"bank{i}", [128, 512], F32)),
                              f"bank{i}") for i in range(8)]
            self.bank_i = 0
            self.last_by_eng = {}
            self.dma_since_barrier = []
            self.barrier_op = None
            self.after_barrier = set()

    def sbuf(self, shape, dtype=F32, name=None):
        self._ensure_arena()
        shape = list(shape)
        esz = 4 if dtype == F32 else 2
        n = 1
        for d in shape[1:]:
            n *= d
        words = (n * esz + 3) // 4
        words = (words + 7) // 8 * 8
        assert self.top + words <= self.ARENA_WORDS, f"SBUF arena overflow {name} {shape} top={self.top}"
        ap = self.arena[0:shape[0], self.top:self.top + words]
        self.top += words
        if dtype != F32:
            ap = ap.bitcast(dtype)
        ap = ap[:, 0:n]
        if len(shape) == 3:
            ap = ap.rearrange("p (a b) -> p a b", a=shape[1])
        elif len(shape) == 4:
            ap = ap.rearrange("p (a b c) -> p a b c", a=shape[1], b=shape[2])
        return Buf(ap, name or "")

    def psum(self, shape=None, dtype=F32, name=None):
        self._ensure_arena()
        b = self.banks[self.bank_i % 8]
        self.bank_i += 1
        return b

    def bank(self, i):
        self._ensure_arena()
        return self.banks[i]

    def mark(self):
        self._ensure_arena()
        return self.top

    def release(self, m):
        self.barrier()
        self.top = m

    def barrier(self):
        self._ensure_arena()
        if not hasattr(self, "_bar_buf"):
            self._bar_buf = self.sbuf([128, 8], F32, "barbuf")
        bb = self._bar_buf
        idx = self.op("dve", "memset", writes=[bb], ap=bb[:], constant=0.0)
        deps = self.ops[idx]["deps"]
        for e, last in self.last_by_eng.items():
            if last != idx:
                deps.add(last)
        deps.update(self.dma_since_barrier)
        deps.discard(idx)
        self.dma_since_barrier = []
        self.barrier_op = idx
        self.after_barrier = set()

    def op(self, eng, meth, reads=(), writes=(), dma=False, accum=False, **kw):
        fn = (meth, kw)
        idx = len(self.ops)
        deps = set()
        for b in reads:
            deps.update(b.w.values())
            deps.update(b.wd)
        for b in writes:
            has_readers = bool(b.r) or bool(b.rd)
            if has_readers:
                for e2, r in b.r.items():
                    if dma or e2 != eng:
                        deps.add(r)
                deps.update(b.rd)
            for e2, w in b.w.items():
                if dma or e2 != eng:
                    deps.add(w)
            if not dma:
                deps.update(b.wd)
        self._ensure_arena()
        if self.barrier_op is not None and eng not in self.after_barrier:
            deps.add(self.barrier_op)
            self.after_barrier.add(eng)
        deps.discard(idx)
        self.ops.append(dict(eng=eng, fn=fn, deps=deps, dma=dma))
        if dma:
            self.dma_since_barrier.append(idx)
        self.last_by_eng[eng] = idx
        for b in reads:
            if dma:
                b.rd.append(idx)
            else:
                b.r[eng] = idx
        for b in writes:
            had_readers = bool(b.r) or bool(b.rd)
            if dma:
                if had_readers:
                    b.w = {}
                    b.wd = []
                b.wd.append(idx)
            else:
                if had_readers:
                    b.w = {}
                b.wd = []
                b.w[eng] = idx
            b.r = {}
            b.rd = []
        return idx

    def pe(self, meth, reads=(), writes=(), accum=False, **kw):
        return self.op("pe", meth, reads, writes, accum=accum, **kw)

    def act(self, meth, reads=(), writes=(), **kw):
        return self.op("act", meth, reads, writes, **kw)

    def dve(self, meth, reads=(), writes=(), **kw):
        return self.op("dve", meth, reads, writes, **kw)

    def pool(self, meth, reads=(), writes=(), **kw):
        return self.op("pool", meth, reads, writes, **kw)

    def dma(self, out, in_, reads=(), writes=(), eng="sp", **kw):
        return self.op(eng, "dma_start", reads, writes, dma=True, out=out, in_=in_, **kw)

    def finish(self, final_wait_ops=None):
        nc = self.nc
        ops = self.ops
        n = len(ops)
        needed = [False] * n
        for i, o in enumerate(ops):
            for d in o["deps"]:
                od = ops[d]
                needed[d] = True
        if final_wait_ops is None:
            final_wait_ops = [i for i, o in enumerate(ops) if o["dma"]][-64:]
        for d in final_wait_ops:
            needed[d] = True
        sems = {e: self.stack.enter_context(nc.semaphore(f"s_{e}")) for e in self.ENGS}
        dma_sems = {e: [self.stack.enter_context(nc.semaphore(f"d_{e}{k}"))
                        for k in range(self.DMA_SLOTS)] for e in self.ENGS}
        sig = [None] * n
        cnt = {e: 0 for e in self.ENGS}
        dcnt = {e: 0 for e in self.ENGS}
        dslot_val = {e: [0] * self.DMA_SLOTS for e in self.ENGS}
        prev_slot_sig = [None] * n
        for i, o in enumerate(ops):
            e = o["eng"]
            if o["dma"]:
                k = dcnt[e] % self.DMA_SLOTS
                dcnt[e] += 1
                if dslot_val[e][k] > 0:
                    prev_slot_sig[i] = (dma_sems[e][k], dslot_val[e][k])
                dslot_val[e][k] += 16
                sig[i] = (dma_sems[e][k], dslot_val[e][k], 16)
            elif needed[i]:
                cnt[e] += 1
                sig[i] = (sems[e], cnt[e], 1)
        per_eng = {e: [] for e in self.ENGS}
        seen = {e: {} for e in self.ENGS}
        for i, o in enumerate(ops):
            e = o["eng"]
            waits = []
            want = {}
            for d in o["deps"]:
                s = sig[d]
                if s is None:
                    continue
                key = id(s[0])
                if key not in want or want[key][1] < s[1]:
                    want[key] = (s[0], s[1])
            if prev_slot_sig[i] is not None:
                s = prev_slot_sig[i]
                key = id(s[0])
                if key not in want or want[key][1] < s[1]:
                    want[key] = s
            for key, (sm, val) in want.items():
                if seen[e].get(key, 0) >= val:
                    continue
                seen[e][key] = val
                waits.append((sm, val))
            per_eng[e].append((waits, o["fn"], sig[i]))
        finals = [sig[d] for d in final_wait_ops]

        def run(engobj, lst, tail=None):
            for waits, fn, s in lst:
                for sm, val in waits:
                    engobj.wait_ge(sm, val)
                ins = getattr(engobj, fn[0])(**fn[1])
                if s is not None:
                    ins.then_inc(s[0], s[2])
            if tail:
                done = {}
                for sm, val, _ in tail:
                    done[id(sm)] = (sm, max(val, done.get(id(sm), (None, 0))[1]))
                for sm, val in done.values():
                    engobj.wait_ge(sm, val)

        with nc.Block() as block:
            @block.tensor
            def _(t):
                run(t, per_eng["pe"])

            @block.scalar
            def _(a):
                run(a, per_eng["act"])

            @block.vector
            def _(v):
                run(v, per_eng["dve"])

            @block.gpsimd
            def _(g):
                run(g, per_eng["pool"])

            @block.sync
            def _(s):
                run(s, per_eng["sp"], tail=finals)
        self.stack.close()
        return nc


D = 2048
KD = D // 128
T_LAT = 1024
T_CTX = 256
T_ALL = T_LAT + T_CTX
TILES = [(0, 512, 0), (512, 512, 0), (1024, 256, 1)]
EPS = 1e-6
HID = 8192


class Rot:
    def __init__(self, bufs):
        self.bufs = bufs
        self.i = 0

    def next(self):
        b = self.bufs[self.i % len(self.bufs)]
        self.i += 1
        return b


def load_consts(P, ones_dram):
    ones_f = P.sbuf([128, 128], F32, "ones_f")
    ones_b = P.sbuf([128, 128], BF16, "ones_b")
    P.dma(ones_f[:], ones_dram, writes=[ones_f])
    P.dve("tensor_copy", reads=[ones_f], writes=[ones_b], out=ones_b[:], in_=ones_f[:])
    return ones_f, ones_b


def mod_vectors(P, mod_dram, nw_dram, slot_sc):
    mod_sb = P.sbuf([128, 6, KD, 2], F32, "mod_sb")
    nw = P.sbuf([128, KD], F32, "nw")
    P.dma(mod_sb[:], mod_dram, writes=[mod_sb])
    P.dma(nw[:], nw_dram, writes=[nw])
    A = []
    for w in range(2):
        a = P.sbuf([128, KD], F32, f"modA{w}")
        P.dve("scalar_tensor_tensor", reads=[mod_sb, nw], writes=[a],
              out=a[:], in0=mod_sb[:, slot_sc, :, w], scalar=1.0, in1=nw[:], op0=ALU.add, op1=ALU.mult)
        A.append(a)
    return mod_sb, A


def norm_mod_tile(P, x_ap_fn, xbuf, h_ap_fn, hbuf, n, w, A, mod_sb, slot_sh, ones_b, sq_ap, sqbuf, t1, tmp):
    for k in range(KD):
        P.act("activation", reads=[xbuf], writes=[sqbuf], out=sq_ap[:, k, :n], in_=x_ap_fn(k), func=AF.Square)
    ps = P.psum()
    for k in range(KD):
        P.pe("matmul", reads=[ones_b, sqbuf], writes=[ps], accum=(k > 0),
             out=ps[:, :n], lhsT=ones_b[:], rhs=sq_ap[:, k, :n], start=(k == 0), stop=(k == KD - 1))
    r = t1.next()
    P.dve("tensor_scalar", reads=[ps], writes=[r], out=r[:, :n], in0=ps[:, :n], scalar1=1.0 / D, scalar2=EPS,
          op0=ALU.mult, op1=ALU.add)
    P.act("activation", reads=[r], writes=[r], out=r[:, :n], in_=r[:, :n], func=AF.Sqrt)
    P.dve("reciprocal", reads=[r], writes=[r], out=r[:, :n], in_=r[:, :n])
    for k in range(KD):
        t = tmp.next()
        P.dve("scalar_tensor_tensor", reads=[xbuf, A[w], r], writes=[t],
              out=t[:, :n], in0=x_ap_fn(k), scalar=A[w][:, k:k + 1], in1=r[:, :n], op0=ALU.mult, op1=ALU.mult)
        P.act("activation", reads=[t, mod_sb], writes=[hbuf],
              out=h_ap_fn(k), in_=t[:, :n], func=AF.Identity, bias=mod_sb[:, slot_sh, k, w:w + 1], scale=1.0)


class WStream:
    def __init__(self, P, words, cast_eng="pool", name="w", nstage=1, nbf=2):
        self.P = P
        self.words = words
        self.stage = Rot([P.sbuf([128, words], F32, f"{name}_st{i}") for i in range(nstage)])
        self.wb = Rot([P.sbuf([128, words], BF16, f"{name}_bf{i}") for i in range(nbf)])
        self.cast_eng = cast_eng
        self.n = 0

    def load(self, dram_ap, a, b):
        P = self.P
        s = self.stage.next(); wb = self.wb.next()
        n = a * b
        assert n <= self.words
        sv = s[:, 0:n].rearrange("p (a b) -> p a b", a=a)
        wv = wb[:, 0:n].rearrange("p (a b) -> p a b", a=a)
        P.dma(sv, dram_ap, writes=[s])
        eng = self.cast_eng
        if eng == "alt":
            eng = ("pool", "dve")[self.n % 2]
        self.n += 1
        if eng == "act":
            P.act("activation", reads=[s], writes=[wb], out=wv, in_=sv, func=AF.Copy)
        else:
            P.op(eng, "tensor_copy", reads=[s], writes=[wb], out=wv, in_=sv)
        return wb, wv


def build_post(KM, NL=2):
    KC = KM // 128
    P = Prog()
    TP = NL * 512 + T_CTX
    tiles = [(j * 512, 512, 0) for j in range(NL)] + [(NL * 512, T_CTX, 1)]
    uT_d = P.dram_in("uT", [128, KC, TP], BF16)
    xT_d = P.dram_in("xT", [128, KD, TP])
    wo_d = P.dram_in("wo", [16, 128, KC, 128])
    w1_d = P.dram_in("w1", [64, 128, KD, 128])
    w2_d = P.dram_in("w2", [32, 128, 32, 128])
    mod_d = P.dram_in("mod", [128, 6, KD, 2])
    nw_d = P.dram_in("nw", [128, KD])
    ones_d = P.dram_in("ones", [128, 128])
    out_d = P.dram_out("xo", [128, KD, TP])

    ones_f, ones_b = load_consts(P, ones_d)
    mod_sb, A = mod_vectors(P, mod_d, nw_d, slot_sc=4)
    xs = P.sbuf([128, KD, 512], F32, "xs")
    us = P.sbuf([128, KC, 512], BF16, "us")
    hs = P.sbuf([128, KD, 512], BF16, "hs")
    aT = P.sbuf([128, 64, 512], BF16, "aT")
    t1 = Rot([P.sbuf([128, 512], F32, f"t1_{i}") for i in range(2)])
    tmp = Rot([P.sbuf([128, 512], F32, f"tmp{i}") for i in range(3)])
    osb = Rot([P.sbuf([128, 512], F32, f"osb{i}") for i in range(3)])
    ws = WStream(P, 4096, name="ws")

    for (st, n, w) in tiles:
        P.dma(xs[:, :, :n], xT_d[:, :, st:st + n], writes=[xs])
        P.dma(us[:, :, :n], uT_d[:, :, st:st + n], writes=[us])
        for ob in range(16):
            wb, wv = ws.load(wo_d[ob], KC, 128)
            ps = P.psum()
            for k in range(KC):
                P.pe("matmul", reads=[wb, us], writes=[ps], accum=(k > 0),
                     out=ps[:, :n], lhsT=wv[:, k, :], rhs=us[:, k, :n], start=(k == 0), stop=(k == KC - 1))
            P.dve("scalar_tensor_tensor", reads=[ps, mod_sb, xs], writes=[xs],
                  out=xs[:, ob, :n], in0=ps[:, :n], scalar=mod_sb[:, 2, ob, w:w + 1],
                  in1=xs[:, ob, :n], op0=ALU.mult, op1=ALU.add)
        norm_mod_tile(P, lambda k: xs[:, k, :n], xs, lambda k: hs[:, k, :n], hs, n, w, A, mod_sb, 3,
                      ones_b, aT.ap, aT, t1, tmp)
        for hc in range(64):
            wb, wv = ws.load(w1_d[hc], KD, 128)
            ps = P.psum()
            for k in range(KD):
                P.pe("matmul", reads=[wb, hs], writes=[ps], accum=(k > 0),
                     out=ps[:, :n], lhsT=wv[:, k, :], rhs=hs[:, k, :n], start=(k == 0), stop=(k == KD - 1))
            r = tmp.next()
            P.act("activation", reads=[ps], writes=[r], out=r[:, :n], in_=ps[:, :n], func=AF.Relu)
            P.dve("tensor_tensor", reads=[r], writes=[aT], out=aT[:, hc, :n], in0=r[:, :n], in1=r[:, :n],
                  op=ALU.mult)
        for ob in range(16):
            ps = P.psum()
            for half in range(2):
                wb, wv = ws.load(w2_d[ob * 2 + half], 32, 128)
                for k in range(32):
                    kk = half * 32 + k
                    P.pe("matmul", reads=[wb, aT], writes=[ps], accum=(kk > 0),
                         out=ps[:, :n], lhsT=wv[:, k, :], rhs=aT[:, kk, :n], start=(kk == 0), stop=(kk == 63))
            o = osb.next()
            P.dve("scalar_tensor_tensor", reads=[ps, mod_sb, xs], writes=[o],
                  out=o[:, :n], in0=ps[:, :n], scalar=mod_sb[:, 5, ob, w:w + 1], in1=xs[:, ob, :n],
                  op0=ALU.mult, op1=ALU.add)
            P.dma(out_d[:, ob, st:st + n], o[:, :n], reads=[o])
    return P.finish()


def build_mod():
    P = Prog()
    cv_d = P.dram_in("cv", [128, KD, 2])
    w_d = P.dram_in("w", [12, 128, KD, 512])
    b_d = P.dram_in("b", [128, 48])
    out_d = P.dram_out("mo", [128, 48, 2])
    cv = P.sbuf([128, KD, 2], F32, "cv")
    sg = P.sbuf([128, KD, 2], F32, "sg")
    bs = P.sbuf([128, 48], F32, "bs")
    ob = P.sbuf([128, 48, 2], F32, "ob")
    P.dma(cv[:], cv_d, writes=[cv])
    P.dma(bs[:], b_d, writes=[bs])
    P.act("activation", reads=[cv], writes=[sg], out=sg[:], in_=cv[:], func=AF.Sigmoid)
    P.dve("tensor_tensor", reads=[cv, sg], writes=[sg], out=sg[:], in0=sg[:], in1=cv[:], op=ALU.mult)
    wst = Rot([P.sbuf([128, KD, 512], F32, f"wst{i}") for i in range(3)])
    for blk in range(12):
        wv = wst.next()
        P.dma(wv[:], w_d[blk], writes=[wv])
        for sub in range(4):
            cb = blk * 4 + sub
            ps = P.psum()
            for k in range(KD):
                P.pe("matmul", reads=[wv, sg], writes=[ps], accum=(k > 0),
                     out=ps[:, 0:2], lhsT=wv[:, k, sub * 128:(sub + 1) * 128], rhs=sg[:, k, :],
                     start=(k == 0), stop=(k == KD - 1))
            P.dve("tensor_scalar", reads=[ps, bs], writes=[ob], out=ob[:, cb, :], in0=ps[:, 0:2],
                  scalar1=bs[:, cb:cb + 1], scalar2=None, op0=ALU.add)
    P.dma(out_d, ob[:], reads=[ob])
    return P.finish()


def qk_norm_tile(P, ps, nh, w_ap, wbc, out_ap, outbuf, scr, rope=None):
    n = nh * 128
    sq = scr["sq"].next(); ss = scr["ss"].next(); xn = scr["xn"].next()
    P.act("activation", reads=[ps], writes=[sq], out=sq[:, :n], in_=ps[:, :n], func=AF.Square)
    P.dve("tensor_reduce", reads=[sq], writes=[ss], out=ss[:, :nh],
          in_=sq[:, :n].rearrange("p (h d) -> p h d", h=nh), axis=AX.X, op=ALU.add)
    P.dve("tensor_scalar", reads=[ss], writes=[ss], out=ss[:, :nh], in0=ss[:, :nh], scalar1=1.0 / 128, scalar2=EPS,
          op0=ALU.mult, op1=ALU.add)
    P.act("activation", reads=[ss], writes=[ss], out=ss[:, :nh], in_=ss[:, :nh], func=AF.Sqrt)
    P.dve("reciprocal", reads=[ss], writes=[ss], out=ss[:, :nh], in_=ss[:, :nh])
    P.dve("tensor_tensor", reads=[ps, ss], writes=[xn],
          out=xn[:, :n].rearrange("p (h d) -> p h d", h=nh), in0=ps[:, :n].rearrange("p (h d) -> p h d", h=nh),
          in1=ss[:, :nh].unsqueeze(2).to_broadcast([128, nh, 128]), op=ALU.mult)
    if rope is None:
        P.pool("tensor_tensor", reads=[xn, wbc], writes=[outbuf], out=out_ap, in0=xn[:, :n], in1=w_ap,
               op=ALU.mult)
        return
    cos_ap, sin_ap, rbuf = rope
    P.pool("tensor_tensor", reads=[xn, wbc], writes=[xn], out=xn[:, :n], in0=xn[:, :n], in1=w_ap, op=ALU.mult)
    xv = xn[:, :n].rearrange("p (h i two) -> p h i two", h=nh, two=2)
    ov = out_ap.rearrange("p (h i two) -> p h i two", h=nh, two=2)
    u1 = xv[:, :, :, 0]; u2 = xv[:, :, :, 1]
    cb = cos_ap.unsqueeze(1).to_broadcast([128, nh, 64]); sb = sin_ap.unsqueeze(1).to_broadcast([128, nh, 64])
    ta = scr["ra"].next(); tb = scr["rb"].next()
    tav = ta[:, :nh * 64].rearrange("p (h i) -> p h i", h=nh); tbv = tb[:, :nh * 64].rearrange("p (h i) -> p h i", h=nh)
    P.dve("tensor_tensor", reads=[xn, rbuf], writes=[ta], out=tav, in0=u1, in1=cb, op=ALU.mult)
    P.pool("tensor_tensor", reads=[xn, rbuf], writes=[tb], out=tbv, in0=u2, in1=sb, op=ALU.mult)
    P.dve("tensor_tensor", reads=[ta, tb], writes=[outbuf], out=ov[:, :, :, 0], in0=tav, in1=tbv, op=ALU.subtract)
    ta2 = scr["ra"].next(); tb2 = scr["rb"].next()
    ta2v = ta2[:, :nh * 64].rearrange("p (h i) -> p h i", h=nh); tb2v = tb2[:, :nh * 64].rearrange("p (h i) -> p h i", h=nh)
    P.dve("tensor_tensor", reads=[xn, rbuf], writes=[ta2], out=ta2v, in0=u1, in1=sb, op=ALU.mult)
    P.pool("tensor_tensor", reads=[xn, rbuf], writes=[tb2], out=tb2v, in0=u2, in1=cb, op=ALU.mult)
    P.dve("tensor_tensor", reads=[ta2, tb2], writes=[outbuf], out=ov[:, :, :, 1], in0=ta2v, in1=tb2v, op=ALU.add)


def norm1_all(P, xT_d, mod_d, nw_d, ones_b, hT, ntok_tiles):
    mod_sb, A = mod_vectors(P, mod_d, nw_d, slot_sc=1)
    m = P.mark()
    xs = Rot([P.sbuf([128, KD, 512], F32, f"xs{i}") for i in range(1)])
    sq = P.sbuf([128, KD, 512], BF16, "sqn")
    t1 = Rot([P.sbuf([128, 512], F32, f"t1_{i}") for i in range(2)])
    tmp = Rot([P.sbuf([128, 512], F32, f"tmp{i}") for i in range(3)])
    for (st, n, w) in ntok_tiles:
        x = xs.next()
        P.dma(x[:, :, :n], xT_d[:, :, st:st + n], writes=[x])
        norm_mod_tile(P, lambda k: x[:, k, :n], x, lambda k: hT[:, k, st:st + n], hT, n, w, A, mod_sb, 0,
                      ones_b, sq.ap, sq, t1, tmp)
    P.release(m)
    return mod_sb


def build_c1():
    P = Prog()
    xT_d = P.dram_in("xT", [128, KD, T_ALL])
    mod_d = P.dram_in("mod", [128, 6, KD, 2])
    nw_d = P.dram_in("nw", [128, KD])
    ones_d = P.dram_in("ones", [128, 128])
    w_d = P.dram_in("w", [6, 128, KD, 512])
    qkw_d = P.dram_in("qkw", [128, 2, 512])
    cs_d = P.dram_in("cs", [128, 8, 2, 64])
    q_d = P.dram_out("q", [10, 128, 2048], BF16)
    k_d = P.dram_out("k", [10, 128, 512], BF16)
    v_d = P.dram_out("v", [10, 128, 512], BF16)

    ones_f, ones_b = load_consts(P, ones_d)
    hT = P.sbuf([128, KD, T_ALL], BF16, "hT")
    norm1_all(P, xT_d, mod_d, nw_d, ones_b, hT, TILES)
    qkw = P.sbuf([128, 2, 512], F32, "qkw")
    cs = P.sbuf([128, 8, 2, 64], F32, "cs")
    P.dma(qkw[:], qkw_d, writes=[qkw])
    P.dma(cs[:], cs_d, writes=[cs])
    scr = dict(sq=Rot([P.sbuf([128, 512], F32, f"sq{i}") for i in range(2)]),
               ss=Rot([P.sbuf([128, 8], F32, f"ss{i}") for i in range(2)]),
               xn=Rot([P.sbuf([128, 512], F32, f"xn{i}") for i in range(2)]),
               ra=Rot([P.sbuf([128, 256], F32, f"ra{i}") for i in range(2)]),
               rb=Rot([P.sbuf([128, 256], F32, f"rb{i}") for i in range(2)]))
    ob = Rot([P.sbuf([128, 512], BF16, f"ob{i}") for i in range(3)])
    ws = WStream(P, KD * 512, name="ws", nstage=1, nbf=2)
    for cbk in range(6):
        wb, wv = ws.load(w_d[cbk], KD, 512)
        for tt in range(10):
            ps = P.psum()
            for k in range(KD):
                P.pe("matmul", reads=[hT, wb], writes=[ps], accum=(k > 0),
                     out=ps[:, :], lhsT=hT[:, k, tt * 128:(tt + 1) * 128], rhs=wv[:, k, :],
                     start=(k == 0), stop=(k == KD - 1))
            o = ob.next()
            if cbk < 5:
                rope = (cs[:, tt, 0, :], cs[:, tt, 1, :], cs) if tt < 8 else None
                qk_norm_tile(P, ps, 4, qkw[:, 0 if cbk < 4 else 1, :], qkw, o[:, :], o, scr, rope=rope)
                dst = q_d[tt, :, cbk * 512:(cbk + 1) * 512] if cbk < 4 else k_d[tt]
            else:
                P.act("activation", reads=[ps], writes=[o], out=o[:, :], in_=ps[:, :], func=AF.Copy)
                dst = v_d[tt]
            P.dma(dst, o[:, :], reads=[o])
    return P.finish()


NKB = 66


def build_c2():
    P = Prog()
    qT_d = P.dram_in("qT", [128, 16, T_ALL], BF16)
    kT_d = P.dram_in("kT", [128, 4, NKB * 128], BF16)
    v_d = P.dram_in("v", [128, NKB, 4, 128], BF16)
    ones_d = P.dram_in("ones", [128, 128])
    o_d = P.dram_out("o", [128, 16, T_ALL], BF16)
    ones_f, ones_b = load_consts(P, ones_d)
    qT = P.sbuf([128, 16, T_ALL], BF16, "qT")
    kT = [P.sbuf([128, NKB * 128], BF16, f"kT{g}") for g in range(4)]
    vs = [P.sbuf([128, NKB, 128], BF16, f"v{g}") for g in range(4)]
    P.dma(qT[:], qT_d, writes=[qT])
    for g in range(4):
        P.dma(kT[g][:], kT_d[:, g, :], writes=[kT[g]])
        P.dma(vs[g][:], v_d[:, :, g, :], writes=[vs[g]])
    pT = Rot([P.sbuf([128, 512], BF16, f"pT{i}") for i in range(3)])
    rec = Rot([P.sbuf([128, 512], F32, f"rec{i}") for i in range(2)])
    ob = Rot([P.sbuf([128, 512], BF16, f"ob{i}") for i in range(2)])
    sbanks = Rot([P.bank(i) for i in range(3)])
    obanks = Rot([P.bank(3), P.bank(4)])
    dbanks = Rot([P.bank(5), P.bank(6)])
    scale = 128 ** -0.5
    for qb in range(10):
        nkb = NKB if qb < 8 else 2
        for g in range(4):
            qg = qT[:, 4 * g:4 * g + 4, qb * 128:(qb + 1) * 128]
            O = obanks.next(); Dn = dbanks.next()
            for kb in range(nkb):
                S = sbanks.next()
                P.pe("matmul", reads=[kT[g], qT], writes=[S],
                     out=S[:, :].rearrange("p (h q) -> p h q", h=4), lhsT=kT[g][:, kb * 128:(kb + 1) * 128], rhs=qg,
                     start=True, stop=True)
                p = pT.next()
                P.act("activation", reads=[S], writes=[p], out=p[:, :], in_=S[:, :], func=AF.Exp, scale=scale)
                P.pe("matmul", reads=[vs[g], p], writes=[O], accum=(kb > 0),
                     out=O[:, :], lhsT=vs[g][:, kb, :], rhs=p[:, :], start=(kb == 0), stop=(kb == nkb - 1))
                P.pe("matmul", reads=[ones_b, p], writes=[Dn], accum=(kb > 0),
                     out=Dn[:, :], lhsT=ones_b[:], rhs=p[:, :], start=(kb == 0), stop=(kb == nkb - 1))
            r = rec.next(); o = ob.next()
            P.dve("reciprocal", reads=[Dn], writes=[r], out=r[:, :], in_=Dn[:, :])
            P.dve("tensor_tensor", reads=[O, r], writes=[o], out=o[:, :], in0=O[:, :], in1=r[:, :], op=ALU.mult)
            P.dma(o_d[:, 4 * g:4 * g + 4, qb * 128:(qb + 1) * 128], o[:, :].rearrange("p (h q) -> p h q", h=4),
                  reads=[o])
    return P.finish()


_PROGS = {}


def _prog(name, builder, *args):
    key = (name,) + args
    if key not in _PROGS:
        _PROGS[key] = builder(*args)
    return _PROGS[key]


def _run(nc, in_maps):
    res = run_bass_kernel_spmd(nc, in_maps, core_ids=list(range(NCORES)))
    return res.results


def fm(a):
    Tn, F = a.shape
    return np.ascontiguousarray(a.T.reshape(F // 128, 128, Tn).transpose(1, 0, 2))


def unfm(a):
    p, C, Tn = a.shape
    return np.ascontiguousarray(a.transpose(2, 1, 0).reshape(Tn, C * 128))


def wblocks(w, colblk):
    K, N = w.shape
    return np.ascontiguousarray(w.reshape(K // 128, 128, N // colblk, colblk).transpose(2, 1, 0, 3))


ONES = np.ones((128, 128), np.float32)


def run_mod(c, c_ctx, w_mod, b_mod):
    cv = np.stack([c.reshape(D), c_ctx.reshape(D)], axis=-1)
    cv = np.ascontiguousarray(cv.reshape(KD, 128, 2).transpose(1, 0, 2))
    ins = []
    for i in range(NCORES):
        l, half = i // 2, i % 2
        w = w_mod[l][:, half * 6144:(half + 1) * 6144]
        b = b_mod[l][half * 6144:(half + 1) * 6144]
        ins.append(dict(cv=cv, w=wblocks(w, 512), b=np.ascontiguousarray(b.reshape(48, 128).T)))
    res = _run(_prog("mod", build_mod), ins)
    modall = np.zeros((4, 6 * D, 2), np.float32)
    for i in range(NCORES):
        l, half = i // 2, i % 2
        mo = res[i]["mo"]
        modall[l, half * 6144:(half + 1) * 6144] = mo.transpose(1, 0, 2).reshape(6144, 2)
    return [np.ascontiguousarray(modall[l].reshape(6, KD, 128, 2).transpose(2, 0, 1, 3)) for l in range(4)]


def vec_fm(v):
    return np.ascontiguousarray(v.reshape(KD, 128).T)


def xT_cores(x, ctx):
    return [fm(np.concatenate([x[i * T_LAT:(i + 1) * T_LAT], ctx], axis=0)) for i in range(NCORES)]


def rope_tables():
    t = np.arange(8192)
    half = 64
    inv = (1.0 / (10000.0 ** (np.arange(0, half, 2, dtype=np.float32) / half))).astype(np.float32)
    ang = np.concatenate([(t // 64).astype(np.float32)[:, None] * inv, (t % 64).astype(np.float32)[:, None] * inv], -1)
    return np.cos(ang).astype(np.float32), np.sin(ang).astype(np.float32)


NCP = 4
NLP = 8192 // NCP // 512


def run_post(u_lat, u_ctx, x, ctx, w_out, w1, w2, mod_l, nw2):
    KM = w_out.shape[0]
    wo = wblocks(w_out, 128)
    w1b = wblocks(w1, 128)
    w2b = np.ascontiguousarray(w2.reshape(2, 32, 128, 16, 128).transpose(3, 0, 2, 1, 4).reshape(32, 128, 32, 128))
    nw = vec_fm(nw2)
    per = NCORES // NCP
    tl = 8192 // NCP
    ins = []
    for p in range(NCP):
        uT = np.concatenate([u_lat[p * per + j] for j in range(per)] + [u_ctx], axis=2)
        xT = fm(np.concatenate([x[p * tl:(p + 1) * tl], ctx], axis=0))
        ins.append(dict(uT=np.ascontiguousarray(uT), xT=xT, wo=wo, w1=w1b, w2=w2b, mod=mod_l, nw=nw, ones=ONES))
    res = run_bass_kernel_spmd(_prog("post", build_post, KM, NLP), ins, core_ids=list(range(NCP))).results
    outs = [unfm(res[p]["xo"]) for p in range(NCP)]
    xn = np.concatenate([o[:tl] for o in outs], axis=0)
    cn = outs[0][tl:]
    return xn, cn


def run_c_layer(x, ctx, mod_l, nw1, w_qkv, q_norm, k_norm, w_out, nw2, w1, w2):
    xTs = xT_cores(x, ctx)
    cos, sin = rope_tables()
    qkw = np.stack([np.tile(q_norm, 4), np.tile(k_norm, 4)], 0)
    qkw = np.ascontiguousarray(np.broadcast_to(qkw[None], (128, 2, 512))).astype(np.float32)
    wb = wblocks(w_qkv, 512)
    nw = vec_fm(nw1)
    ins = []
    for i in range(NCORES):
        cs = np.stack([cos[i * T_LAT:(i + 1) * T_LAT], sin[i * T_LAT:(i + 1) * T_LAT]], 1)
        cs = np.ascontiguousarray(cs.reshape(8, 128, 2, 64).transpose(1, 0, 2, 3))
        ins.append(dict(xT=xTs[i], mod=mod_l, nw=nw, ones=ONES, w=wb, qkw=qkw, cs=cs))
    r1 = _run(_prog("c1", build_c1), ins)
    k_ctx = r1[0]["k"][8:10].reshape(256, 4, 128)
    v_ctx = r1[0]["v"][8:10].reshape(256, 4, 128)
    k_all = np.concatenate([k_ctx] + [r1[i]["k"][0:8].reshape(1024, 4, 128) for i in range(NCORES)], 0)
    v_all = np.concatenate([v_ctx] + [r1[i]["v"][0:8].reshape(1024, 4, 128) for i in range(NCORES)], 0)
    kT = np.ascontiguousarray(k_all.transpose(2, 1, 0))
    vv = np.ascontiguousarray(v_all.reshape(NKB, 128, 4, 128).transpose(1, 0, 2, 3))
    ins2 = []
    for i in range(NCORES):
        q = r1[i]["q"].reshape(T_ALL, 16, 128)
        ins2.append(dict(qT=np.ascontiguousarray(q.transpose(2, 1, 0)), kT=kT, v=vv, ones=ONES))
    r2 = _run(_prog("c2", build_c2), ins2)
    u_lat = [r2[i]["o"][:, :, :T_LAT] for i in range(NCORES)]
    u_ctx = r2[0]["o"][:, :, T_LAT:]
    return run_post(u_lat, u_ctx, x, ctx, w_out, w1, w2, mod_l, nw2)


T_AB = T_ALL + 4
TILES_AB = TILES + [(T_ALL, 4, 0)]


def build_ab1():
    P = Prog()
    xT_d = P.dram_in("xT", [128, KD, T_AB])
    mod_d = P.dram_in("mod", [128, 6, KD, 2])
    nw_d = P.dram_in("nw", [128, KD])
    ones_d = P.dram_in("ones", [128, 128])
    wfm_d = P.dram_in("wfm", [32, 128, KD, 128])
    wtm_d = P.dram_in("wtm", [10, 128, KD, 512])
    wdt_d = P.dram_in("wdt", [128, KD, 64])
    cw_d = P.dram_in("cw", [128, 32, 5])
    cb_d = P.dram_in("cb", [128, 32])
    edge_d = P.dram_in("edge", [128, 2])
    qkw_d = P.dram_in("qkw", [128, 2, 512])
    xbc_d = P.dram_out("xbc", [32, 128, T_ALL])
    z_d = P.dram_out("z", [10, 128, 2048])
    q_d = P.dram_out("q", [10, 128, 1024], BF16)
    k_d = P.dram_out("k", [10, 128, 1024], BF16)
    v_d = P.dram_out("v", [10, 128, 1024], BF16)
    dt_d = P.dram_out("dt", [10, 128, 64])

    ones_f, ones_b = load_consts(P, ones_d)
    hT = P.sbuf([128, KD, T_AB], BF16, "hT")
    norm1_all(P, xT_d, mod_d, nw_d, ones_b, hT, TILES_AB)
    cw = P.sbuf([128, 32, 5], F32, "cw"); cbias = P.sbuf([128, 32], F32, "cbias")
    edge = P.sbuf([128, 2], F32, "edge"); qkw = P.sbuf([128, 2, 512], F32, "qkw")
    P.dma(cw[:], cw_d, writes=[cw]); P.dma(cbias[:], cb_d, writes=[cbias])
    P.dma(edge[:], edge_d, writes=[edge]); P.dma(qkw[:], qkw_d, writes=[qkw])
    ws = WStream(P, KD * 512, name="ws", nstage=1, nbf=2)
    Ul = Rot([P.sbuf([128, T_LAT + 4], F32, f"Ul{i}") for i in range(2)])
    Uc = Rot([P.sbuf([128, T_CTX + 4], F32, f"Uc{i}") for i in range(2)])
    for u in Uc.bufs:
        P.dve("memset", writes=[u], ap=u[:], constant=0.0)
    accl = Rot([P.sbuf([128, T_LAT], F32, f"accl{i}") for i in range(2)])
    accc = Rot([P.sbuf([128, T_CTX], F32, f"accc{i}") for i in range(2)])
    for cb in range(32):
        wb, wv = ws.load(wfm_d[cb], KD, 128)
        ul = Ul.next(); uc = Uc.next()
        for (st, n, w) in TILES_AB:
            ps = P.psum()
            for k in range(KD):
                P.pe("matmul", reads=[wb, hT], writes=[ps], accum=(k > 0),
                     out=ps[:, :n], lhsT=wv[:, k, :], rhs=hT[:, k, st:st + n], start=(k == 0), stop=(k == KD - 1))
            if st < T_LAT:
                P.act("activation", reads=[ps], writes=[ul], out=ul[:, 2 + st:2 + st + n], in_=ps[:, :n], func=AF.Copy)
            elif st == T_LAT:
                P.act("activation", reads=[ps], writes=[uc], out=uc[:, 2:2 + n], in_=ps[:, :n], func=AF.Copy)
            else:
                P.dve("tensor_scalar", reads=[ps, edge], writes=[ul], out=ul[:, 0:2], in0=ps[:, 0:2],
                      scalar1=edge[:, 0:1], scalar2=None, op0=ALU.mult)
                P.dve("tensor_scalar", reads=[ps, edge], writes=[ul], out=ul[:, T_LAT + 2:T_LAT + 4], in0=ps[:, 2:4],
                      scalar1=edge[:, 1:2], scalar2=None, op0=ALU.mult)
        for (U, acc, N, off) in ((ul, accl.next(), T_LAT, 0), (uc, accc.next(), T_CTX, T_LAT)):
            P.dve("tensor_scalar", reads=[U, cw, cbias], writes=[acc], out=acc[:, :], in0=U[:, 0:N],
                  scalar1=cw[:, cb, 0:1], scalar2=cbias[:, cb:cb + 1], op0=ALU.mult, op1=ALU.add)
            for kk in range(1, 5):
                P.dve("scalar_tensor_tensor", reads=[U, cw, acc], writes=[acc], out=acc[:, :], in0=U[:, kk:kk + N],
                      scalar=cw[:, cb, kk:kk + 1], in1=acc[:, :], op0=ALU.mult, op1=ALU.add)
            P.act("activation", reads=[acc], writes=[acc], out=acc[:, :], in_=acc[:, :], func=AF.Silu)
            P.dma(xbc_d[cb, :, off:off + N], acc[:, :], reads=[acc])
    scr = dict(sq=Rot([P.sbuf([128, 512], F32, f"sq{i}") for i in range(2)]),
               ss=Rot([P.sbuf([128, 8], F32, f"ss{i}") for i in range(2)]),
               xn=Rot([P.sbuf([128, 512], F32, f"xn{i}") for i in range(2)]))
    of = Rot([P.sbuf([128, 512], F32, f"of{i}") for i in range(2)])
    ob = Rot([P.sbuf([128, 512], BF16, f"ob{i}") for i in range(3)])
    for cbk in range(10):
        wb, wv = ws.load(wtm_d[cbk], KD, 512)
        for tt in range(10):
            ps = P.psum()
            for k in range(KD):
                P.pe("matmul", reads=[hT, wb], writes=[ps], accum=(k > 0),
                     out=ps[:, :], lhsT=hT[:, k, tt * 128:(tt + 1) * 128], rhs=wv[:, k, :],
                     start=(k == 0), stop=(k == KD - 1))
            if cbk < 4:
                o = of.next()
                P.act("activation", reads=[ps], writes=[o], out=o[:, :], in_=ps[:, :], func=AF.Copy)
                P.dma(z_d[tt, :, cbk * 512:(cbk + 1) * 512], o[:, :], reads=[o])
            elif cbk < 8:
                o = ob.next()
                wi = 0 if cbk < 6 else 1
                qk_norm_tile(P, ps, 4, qkw[:, wi, :], qkw, o[:, :], o, scr, rope=None)
                dst = (q_d if cbk < 6 else k_d)[tt, :, (cbk % 2) * 512:(cbk % 2 + 1) * 512]
                P.dma(dst, o[:, :], reads=[o])
            else:
                o = ob.next()
                P.act("activation", reads=[ps], writes=[o], out=o[:, :], in_=ps[:, :], func=AF.Copy)
                P.dma(v_d[tt, :, (cbk % 2) * 512:(cbk % 2 + 1) * 512], o[:, :], reads=[o])
    wb, wv = ws.load(wdt_d, KD, 64)
    for tt in range(10):
        ps = P.psum()
        for k in range(KD):
            P.pe("matmul", reads=[hT, wb], writes=[ps], accum=(k > 0),
                 out=ps[:, :64], lhsT=hT[:, k, tt * 128:(tt + 1) * 128], rhs=wv[:, k, :],
                 start=(k == 0), stop=(k == KD - 1))
        o = of.next()
        P.act("activation", reads=[ps], writes=[o], out=o[:, :64], in_=ps[:, :64], func=AF.Copy)
        P.dma(dt_d[tt], o[:, :64], reads=[o])
    return P.finish()


def run_ab1(x, ctx, mod_l, nw1, w_in, conv_w, conv_b, q_norm, k_norm):
    wfm = wblocks(w_in[:, 2048:6144], 128)
    wtm = wblocks(np.concatenate([w_in[:, 0:2048], w_in[:, 6208:9280]], axis=1), 512)
    wdt = np.ascontiguousarray(w_in[:, 6144:6208].reshape(KD, 128, 64).transpose(1, 0, 2))
    cw = np.ascontiguousarray(conv_w.reshape(5, 32, 128).transpose(2, 1, 0))
    cb = np.ascontiguousarray(conv_b.reshape(32, 128).T)
    qkw = np.stack([np.tile(q_norm, 4), np.tile(k_norm, 4)], 0)
    qkw = np.ascontiguousarray(np.broadcast_to(qkw[None], (128, 2, 512))).astype(np.float32)
    nw = vec_fm(nw1)
    xpad = np.concatenate([np.zeros((2, D), np.float32), x, np.zeros((2, D), np.float32)], 0)
    ins = []
    for i in range(NCORES):
        lo = i * T_LAT
        toks = np.concatenate([x[lo:lo + T_LAT], ctx, xpad[lo:lo + 2], xpad[lo + T_LAT + 2:lo + T_LAT + 4]], 0)
        edge = np.zeros((128, 2), np.float32)
        edge[:, 0] = 1.0 if i > 0 else 0.0
        edge[:, 1] = 1.0 if i < NCORES - 1 else 0.0
        ins.append(dict(xT=fm(toks), mod=mod_l, nw=nw, ones=ONES, wfm=wfm, wtm=wtm, wdt=wdt, cw=cw, cb=cb,
                        edge=edge, qkw=qkw))
    return _run(_prog("ab1", build_ab1), ins)


def build_ssd(mode):
    full = (mode == "B")
    P = Prog()
    xs_d = P.dram_in("xs", [10, 128, 2048])
    bt_d = P.dram_in("btok", [10, 128, 1024])
    BT_d = P.dram_in("BT", [10, 128, 8, 128])
    CT_d = P.dram_in("CT", [10, 128, 8, 128])
    dt_d = P.dram_in("dt", [10, 128, 64])
    par_d = P.dram_in("par", [128, 3, 64])
    tri_d = P.dram_in("tri", [128, 2, 128])
    nm_d = P.dram_in("negmask", [128, 2, 128])
    id_d = P.dram_in("ident", [128, 128])
    ones_d = P.dram_in("ones", [128, 128])
    if full:
        Fl_d = P.dram_in("Flist", [2, 7, 128, 2048])
        Tl_d = P.dram_in("Tlist", [2, 7, 128, 32])
        cF_d = P.dram_in("ctxF", [2, 128, 2048])
        z_d = P.dram_in("z", [10, 128, 2048])
        gw_d = P.dram_in("gw", [128, 2048])
        y_d = P.dram_out("yd", [2, 10, 128, 2048])
        g_d = P.dram_out("g", [10, 128, 2048], BF16)
        ybuf = Buf(y_d, "y_dram")
    else:
        F_d = P.dram_out("F", [2, 128, 2048])
        T_d = P.dram_out("T", [2, 128, 32])
        cFo_d = P.dram_out("ctxF", [2, 128, 2048])

    ones_f, ones_b = load_consts(P, ones_d)
    par = P.sbuf([128, 3, 64], F32, "par"); tri = P.sbuf([128, 2, 128], F32, "tri")
    nm = P.sbuf([128, 2, 128], F32, "nm"); ident = P.sbuf([128, 128], F32, "ident")
    for sb_, d_ in ((par, par_d), (tri, tri_d), (nm, nm_d), (ident, id_d)):
        P.dma(sb_[:], d_, writes=[sb_])
    abc = P.sbuf([128, 64], F32, "abc")
    P.act("activation", reads=[par], writes=[abc], out=abc[:], in_=par[:, 1, :], func=AF.Exp)
    P.dve("tensor_scalar", reads=[abc], writes=[abc], out=abc[:], in0=abc[:], scalar1=-1.0, scalar2=None, op0=ALU.mult)
    dsum = P.sbuf([128, 32], F32, "dsum")
    P.dve("tensor_tensor", reads=[par], writes=[dsum], out=dsum[:], in0=par[:, 2, 0:32], in1=par[:, 2, 32:64], op=ALU.add)

    S = [P.sbuf([128, 2048], F32, f"S{d}") for d in range(2)]
    Sb = [P.sbuf([128, 2048], BF16, f"Sb{d}") for d in range(2)]
    tot_acc = [P.sbuf([128, 32], F32, f"tacc{d}") for d in range(2)]
    xs = P.sbuf([128, 2048], F32, "xs"); xd = P.sbuf([128, 2048], BF16, "xd"); xdd = P.sbuf([128, 2048], BF16, "xdd")
    btf = P.sbuf([128, 1024], F32, "btf"); btb = P.sbuf([128, 1024], BF16, "btb")
    BTf = P.sbuf([128, 8, 128], F32, "BTf"); BTb = P.sbuf([128, 8, 128], BF16, "BTb")
    CTf = P.sbuf([128, 8, 128], F32, "CTf"); CTb = P.sbuf([128, 8, 128], BF16, "CTb")
    sm = {n_: P.sbuf([128, 32], F32, n_) for n_ in
          ("dtr", "x0", "mx", "na", "e", "dt", "dtA", "la", "nla", "ela", "tot", "dend", "cdec", "dtd")}
    dbc = P.sbuf([128, 32, 128], F32, "dbc")
    Lt = Rot([P.sbuf([128, 128], F32, f"Lt{i}") for i in range(2)])
    Mt = Rot([P.sbuf([128, 128], BF16, f"Mt{i}") for i in range(2)])
    ysb = P.sbuf([128, 2048], F32, "ysb")
    tmpg = Rot([P.sbuf([128, 256], F32, f"tg{i}") for i in range(2)])
    tmps = Rot([P.sbuf([128, 256], F32, f"ts{i}") for i in range(2)])
    B_la, B_tot, B_cb, B_yd, B_yo, B_st = P.bank(0), P.bank(1), P.bank(2), P.bank(5), P.bank(6), P.bank(7)
    B_arg = Rot([P.bank(3), P.bank(4)])

    def bc(ap32, g):
        return ap32[:, 4 * g:4 * g + 4].unsqueeze(2).to_broadcast([128, 4, 64])

    def v3(ap, g):
        return ap[:, g * 256:(g + 1) * 256].rearrange("p (k q) -> p k q", k=4)

    def unit(c, d, want_y):
        P.dma(xs[:], xs_d[c], writes=[xs])
        P.dma(btf[:], bt_d[c], writes=[btf])
        P.dma(BTf[:], BT_d[c], writes=[BTf])
        P.dma(CTf[:], CT_d[c], writes=[CTf])
        P.dma(sm["dtr"][:], dt_d[c, :, d * 32:(d + 1) * 32], writes=[sm["dtr"]])
        P.pool("tensor_copy", reads=[btf], writes=[btb], out=btb[:], in_=btf[:])
        P.pool("tensor_copy", reads=[BTf], writes=[BTb], out=BTb[:], in_=BTf[:])
        P.pool("tensor_copy", reads=[CTf], writes=[CTb], out=CTb[:], in_=CTf[:])
        dsl = slice(d * 32, (d + 1) * 32)
        P.dve("tensor_tensor", reads=[sm["dtr"], par], writes=[sm["x0"]], out=sm["x0"][:], in0=sm["dtr"][:],
              in1=par[:, 0, dsl], op=ALU.add)
        P.dve("tensor_scalar", reads=[sm["x0"]], writes=[sm["mx"]], out=sm["mx"][:], in0=sm["x0"][:], scalar1=0.0,
              scalar2=None, op0=ALU.max)
        P.dve("scalar_tensor_tensor", reads=[sm["mx"], sm["x0"]], writes=[sm["na"]], out=sm["na"][:], in0=sm["mx"][:],
              scalar=-2.0, in1=sm["x0"][:], op0=ALU.mult, op1=ALU.add)
        P.act("activation", reads=[sm["na"]], writes=[sm["e"]], out=sm["e"][:], in_=sm["na"][:], func=AF.Exp)
        P.dve("tensor_scalar", reads=[sm["e"]], writes=[sm["e"]], out=sm["e"][:], in0=sm["e"][:], scalar1=1.0,
              scalar2=None, op0=ALU.add)
        P.act("activation", reads=[sm["e"]], writes=[sm["e"]], out=sm["e"][:], in_=sm["e"][:], func=AF.Ln)
        P.dve("tensor_tensor", reads=[sm["mx"], sm["e"]], writes=[sm["dt"]], out=sm["dt"][:], in0=sm["mx"][:],
              in1=sm["e"][:], op=ALU.add)
        P.dve("tensor_tensor", reads=[sm["dt"], abc], writes=[sm["dtA"]], out=sm["dtA"][:], in0=sm["dt"][:],
              in1=abc[:, dsl], op=ALU.mult)
        P.pe("matmul", reads=[tri, sm["dtA"]], writes=[B_la], out=B_la[:, 0:32], lhsT=tri[:, d, :], rhs=sm["dtA"][:],
             start=True, stop=True)
        P.pe("matmul", reads=[ones_f, sm["dtA"]], writes=[B_tot], out=B_tot[:, 0:32], lhsT=ones_f[:], rhs=sm["dtA"][:],
             start=True, stop=True)
        P.dve("tensor_copy", reads=[B_la], writes=[sm["la"]], out=sm["la"][:], in_=B_la[:, 0:32])
        P.dve("tensor_copy", reads=[B_tot], writes=[sm["tot"]], out=sm["tot"][:], in_=B_tot[:, 0:32])
        P.dve("tensor_scalar", reads=[sm["la"]], writes=[sm["nla"]], out=sm["nla"][:], in0=sm["la"][:], scalar1=-1.0,
              scalar2=None, op0=ALU.mult)
        P.act("activation", reads=[sm["la"]], writes=[sm["ela"]], out=sm["ela"][:], in_=sm["la"][:], func=AF.Exp)
        P.dve("tensor_tensor", reads=[sm["tot"], sm["la"]], writes=[sm["dend"]], out=sm["dend"][:], in0=sm["tot"][:],
              in1=sm["la"][:], op=ALU.subtract)
        P.act("activation", reads=[sm["dend"]], writes=[sm["dend"]], out=sm["dend"][:], in_=sm["dend"][:], func=AF.Exp)
        P.act("activation", reads=[sm["tot"]], writes=[sm["cdec"]], out=sm["cdec"][:], in_=sm["tot"][:], func=AF.Exp)
        P.dve("tensor_tensor", reads=[sm["dt"], sm["dend"]], writes=[sm["dtd"]], out=sm["dtd"][:], in0=sm["dt"][:],
              in1=sm["dend"][:], op=ALU.mult)
        P.dve("tensor_tensor", reads=[tot_acc[d], sm["tot"]], writes=[tot_acc[d]], out=tot_acc[d][:], in0=tot_acc[d][:],
              in1=sm["tot"][:], op=ALU.add)
        xs3 = xs[:, :].rearrange("p (h q) -> p h q", h=32)
        P.dve("tensor_tensor", reads=[xs, sm["dtd"]], writes=[xdd], out=xdd[:, :].rearrange("p (h q) -> p h q", h=32),
              in0=xs3, in1=sm["dtd"][:, :].unsqueeze(2).to_broadcast([128, 32, 64]), op=ALU.mult)
        if want_y:
            P.pool("tensor_tensor", reads=[xs, sm["dt"]], writes=[xd], out=xd[:, :].rearrange("p (h q) -> p h q", h=32),
                   in0=xs3, in1=sm["dt"][:, :].unsqueeze(2).to_broadcast([128, 32, 64]), op=ALU.mult)
            P.pool("tensor_copy", reads=[sm["dtA"]], writes=[dbc], out=dbc[:],
                   in_=sm["dtA"][:, :].unsqueeze(2).to_broadcast([128, 32, 128]))
        for g in range(8):
            if want_y:
                P.pe("matmul", reads=[BTb, CTb], writes=[B_cb], out=B_cb[:, 0:128], lhsT=BTb[:, g, :], rhs=CTb[:, g, :],
                     start=True, stop=True)
                for k in range(4):
                    h = 4 * g + k
                    A_ = B_arg.next()
                    P.pe("matmul", reads=[dbc, tri], writes=[A_], out=A_[:, 0:128], lhsT=dbc[:, h, :], rhs=tri[:, d, :],
                         start=True, stop=False)
                    P.pe("matmul", reads=[ident, nm], writes=[A_], accum=True, out=A_[:, 0:128], lhsT=ident[:],
                         rhs=nm[:, d, :], start=False, stop=True)
                    L_ = Lt.next(); M_ = Mt.next()
                    P.act("activation", reads=[A_, sm["nla"]], writes=[L_], out=L_[:], in_=A_[:, 0:128], func=AF.Exp,
                          bias=sm["nla"][:, h:h + 1], scale=1.0)
                    P.dve("tensor_tensor", reads=[L_, B_cb], writes=[M_], out=M_[:], in0=L_[:], in1=B_cb[:, 0:128],
                          op=ALU.mult)
                    P.pe("matmul", reads=[M_, xd], writes=[B_yd], out=B_yd[:, k * 64:(k + 1) * 64], lhsT=M_[:],
                         rhs=xd[:, h * 64:(h + 1) * 64], start=True, stop=True)
                P.pe("matmul", reads=[CTb, Sb[d]], writes=[B_yo], out=B_yo[:, 0:256], lhsT=CTb[:, g, :],
                     rhs=Sb[d][:, g * 256:(g + 1) * 256], start=True, stop=True)
                t_ = tmpg.next()
                t3 = t_[:, :].rearrange("p (k q) -> p k q", k=4)
                P.dve("tensor_tensor", reads=[B_yo, sm["ela"]], writes=[t_], out=t3,
                      in0=B_yo[:, 0:256].rearrange("p (k q) -> p k q", k=4), in1=bc(sm["ela"], g), op=ALU.mult)
                P.dve("tensor_tensor", reads=[t_, B_yd], writes=[ysb], out=ysb[:, g * 256:(g + 1) * 256], in0=t_[:, :],
                      in1=B_yd[:, 0:256], op=ALU.add)
                if d == 0:
                    t2 = tmps.next()
                    P.pool("tensor_tensor", reads=[xs, dsum], writes=[t2], out=t2[:, :].rearrange("p (k q) -> p k q", k=4),
                           in0=v3(xs, g), in1=bc(dsum, g), op=ALU.mult)
                    P.pool("tensor_tensor", reads=[t2, ysb], writes=[ysb], out=ysb[:, g * 256:(g + 1) * 256],
                           in0=ysb[:, g * 256:(g + 1) * 256], in1=t2[:, :], op=ALU.add)
            P.pe("matmul", reads=[btb, xdd], writes=[B_st], out=B_st[:, 0:256], lhsT=btb[:, g * 128:(g + 1) * 128],
                 rhs=xdd[:, g * 256:(g + 1) * 256], start=True, stop=True)
            P.dve("tensor_tensor", reads=[S[d], sm["cdec"]], writes=[S[d]], out=v3(S[d], g), in0=v3(S[d], g),
                  in1=bc(sm["cdec"], g), op=ALU.mult)
            P.dve("tensor_tensor", reads=[S[d], B_st], writes=[S[d]], out=S[d][:, g * 256:(g + 1) * 256],
                  in0=S[d][:, g * 256:(g + 1) * 256], in1=B_st[:, 0:256], op=ALU.add)
            if full:
                P.act("activation", reads=[S[d]], writes=[Sb[d]], out=Sb[d][:, g * 256:(g + 1) * 256],
                      in_=S[d][:, g * 256:(g + 1) * 256], func=AF.Copy)
        if want_y:
            P.dma(y_d[d, c], ysb[:], reads=[ysb], writes=[ybuf])

    def zero_state(d):
        P.dve("memset", writes=[S[d]], ap=S[d][:], constant=0.0)
        P.dve("memset", writes=[Sb[d]], ap=Sb[d][:], constant=0.0)
        P.dve("memset", writes=[tot_acc[d]], ap=tot_acc[d][:], constant=0.0)

    order = {0: (list(range(8)), [8, 9]), 1: (list(range(7, -1, -1)), [9, 8])}
    for d in range(2):
        lat_order, ctx_order = order[d]
        if not full:
            zero_state(d)
            for c in ctx_order:
                unit(c, d, False)
            P.dma(cFo_d[d], S[d][:], reads=[S[d]])
            zero_state(d)
            for c in lat_order:
                unit(c, d, False)
            P.dma(F_d[d], S[d][:], reads=[S[d]])
            P.dma(T_d[d], tot_acc[d][:], reads=[tot_acc[d]])
        else:
            zero_state(d)
            for c in ctx_order:
                unit(c, d, True)
            P.dma(S[d][:], cF_d[d], writes=[S[d]])
            tl = P.sbuf([128, 7, 32], F32, f"tl{d}")
            P.dma(tl[:], Tl_d[d].rearrange("j p h -> p j h"), writes=[tl])
            P.act("activation", reads=[tl], writes=[tl], out=tl[:], in_=tl[:], func=AF.Exp)
            for j in range(7):
                P.dma(xs[:], Fl_d[d, j], writes=[xs])
                P.dve("tensor_tensor", reads=[S[d], tl], writes=[S[d]], out=S[d][:, :].rearrange("p (h q) -> p h q", h=32),
                      in0=S[d][:, :].rearrange("p (h q) -> p h q", h=32),
                      in1=tl[:, j, :].unsqueeze(2).to_broadcast([128, 32, 64]), op=ALU.mult)
                P.dve("tensor_tensor", reads=[S[d], xs], writes=[S[d]], out=S[d][:], in0=S[d][:], in1=xs[:], op=ALU.add)
            P.act("activation", reads=[S[d]], writes=[Sb[d]], out=Sb[d][:], in_=S[d][:], func=AF.Copy)
            for c in lat_order:
                unit(c, d, True)
    if full:
        P.barrier()
        gw = P.sbuf([128, 2048], F32, "gw")
        P.dma(gw[:], gw_d, writes=[gw])
        zt = P.sbuf([128, 2048], F32, "zt"); y2 = P.sbuf([128, 2048], F32, "y2")
        gss = P.sbuf([128, 8], F32, "gss"); go = P.sbuf([128, 2048], BF16, "go")
        for c in range(10):
            P.dma(xs[:], y_d[0, c], reads=[ybuf], writes=[xs])
            P.dma(y2[:], y_d[1, c], reads=[ybuf], writes=[y2])
            P.dma(zt[:], z_d[c], writes=[zt])
            P.act("activation", reads=[zt], writes=[zt], out=zt[:], in_=zt[:], func=AF.Silu)
            P.dve("tensor_tensor", reads=[xs, y2], writes=[xs], out=xs[:], in0=xs[:], in1=y2[:], op=ALU.add)
            P.dve("tensor_tensor", reads=[xs, zt], writes=[xs], out=xs[:], in0=xs[:], in1=zt[:], op=ALU.mult)
            P.act("activation", reads=[xs], writes=[y2], out=y2[:], in_=xs[:], func=AF.Square)
            P.dve("tensor_reduce", reads=[y2], writes=[gss], out=gss[:], in_=y2[:, :].rearrange("p (g q) -> p g q", g=8),
                  axis=AX.X, op=ALU.add)
            P.dve("tensor_scalar", reads=[gss], writes=[gss], out=gss[:], in0=gss[:], scalar1=1.0 / 256, scalar2=EPS,
                  op0=ALU.mult, op1=ALU.add)
            P.act("activation", reads=[gss], writes=[gss], out=gss[:], in_=gss[:], func=AF.Sqrt)
            P.dve("reciprocal", reads=[gss], writes=[gss], out=gss[:], in_=gss[:])
            P.dve("tensor_tensor", reads=[xs, gss], writes=[xs], out=xs[:, :].rearrange("p (g q) -> p g q", g=8),
                  in0=xs[:, :].rearrange("p (g q) -> p g q", g=8), in1=gss[:, :].unsqueeze(2).to_broadcast([128, 8, 256]),
                  op=ALU.mult)
            P.pool("tensor_tensor", reads=[xs, gw], writes=[go], out=go[:], in0=xs[:], in1=gw[:], op=ALU.mult)
            P.dma(g_d[c], go[:], reads=[go])
    return P.finish()


def _ssd_consts():
    t = np.arange(128)
    tri = np.stack([(t[:, None] <= t[None, :]), (t[:, None] >= t[None, :])], 1).astype(np.float32)
    valid = np.stack([(t[None, :] >= t[:, None]), (t[None, :] <= t[:, None])], 1)
    negmask = np.where(valid, 0.0, -30000.0).astype(np.float32)
    return np.ascontiguousarray(tri), np.ascontiguousarray(negmask), np.eye(128, dtype=np.float32)


def run_ssd(r1, dt_bias, a_log, d_skip, norm_w):
    tri, negmask, ident = _ssd_consts()
    par = np.stack([dt_bias.reshape(64), a_log.reshape(64), d_skip.reshape(64)], 0)
    par = np.ascontiguousarray(np.broadcast_to(par[None], (128, 3, 64))).astype(np.float32)
    base = []
    for i in range(NCORES):
        xbc = r1[i]["xbc"]
        xbc_t = np.ascontiguousarray(xbc.reshape(4096, T_ALL).T)
        xs = xbc_t[:, 0:2048].reshape(10, 128, 2048)
        btok = xbc_t[:, 2048:3072].reshape(10, 128, 1024)
        BT = xbc[16:24].reshape(8, 128, 10, 128).transpose(2, 1, 0, 3)
        CT = xbc[24:32].reshape(8, 128, 10, 128).transpose(2, 1, 0, 3)
        base.append(dict(xs=np.ascontiguousarray(xs), btok=np.ascontiguousarray(btok), BT=np.ascontiguousarray(BT),
                         CT=np.ascontiguousarray(CT), dt=r1[i]["dt"], par=par, tri=tri, negmask=negmask, ident=ident,
                         ones=ONES))
    ra = _run(_prog("ssdA", build_ssd, "A"), base)
    ctxF = ra[0]["ctxF"]
    gw = np.ascontiguousarray(np.broadcast_to(norm_w[None], (128, 2048))).astype(np.float32)
    insb = []
    for i in range(NCORES):
        Fl = np.zeros((2, 7, 128, 2048), np.float32)
        Tl = np.zeros((2, 7, 128, 32), np.float32)
        for j, cj in enumerate(range(0, i)):
            Fl[0, j] = ra[cj]["F"][0]; Tl[0, j] = ra[cj]["T"][0]
        for j, cj in enumerate(range(NCORES - 1, i, -1)):
            Fl[1, j] = ra[cj]["F"][1]; Tl[1, j] = ra[cj]["T"][1]
        d = dict(base[i]); d.update(Flist=Fl, Tlist=Tl, ctxF=ctxF, z=r1[i]["z"], gw=gw)
        insb.append(d)
    rb = _run(_prog("ssdB", build_ssd, "B"), insb)
    return ra, rb


def build_na():
    P = Prog()
    qT_d = P.dram_in("qT", [128, 8, T_ALL], BF16)
    kT_d = P.dram_in("kT", [128, 8, 2048], BF16)
    ve_d = P.dram_in("ve", [128, 16, 8, 128], BF16)
    vo_d = P.dram_in("vo", [128, 16, 8, 128], BF16)
    kcT_d = P.dram_in("kcT", [128, 8, 256], BF16)
    vc_d = P.dram_in("vc", [128, 2, 8, 128], BF16)
    tt_d = P.dram_in("tt", [128, 8, 8, 64])
    vm_d = P.dram_in("vm", [128, 16, 8])
    ones_d = P.dram_in("ones", [128, 128])
    o_d = P.dram_out("o", [128, 8, T_ALL], BF16)
    ones_f, ones_b = load_consts(P, ones_d)
    qT = P.sbuf([128, 8, T_ALL], BF16, "qT"); kT = P.sbuf([128, 8, 2048], BF16, "kT")
    ve = P.sbuf([128, 16, 8, 128], BF16, "ve"); vo = P.sbuf([128, 16, 8, 128], BF16, "vo")
    kcT = P.sbuf([128, 8, 256], BF16, "kcT"); vc = P.sbuf([128, 2, 8, 128], BF16, "vc")
    TT = P.sbuf([128, 8, 8, 64], F32, "TT"); vm = P.sbuf([128, 16, 8], F32, "vm")
    oT = P.sbuf([128, 8, T_ALL], BF16, "oT")
    for sb_, d_ in ((qT, qT_d), (kT, kT_d), (ve, ve_d), (vo, vo_d), (kcT, kcT_d), (vc, vc_d), (TT, tt_d), (vm, vm_d)):
        P.dma(sb_[:], d_, writes=[sb_])
    tb = Rot([P.sbuf([128, 512], F32, f"tb{i}") for i in range(2)])
    pw = Rot([P.sbuf([128, 512], BF16, f"pw{i}") for i in range(2)])
    pc = Rot([P.sbuf([128, 128], BF16, f"pc{i}") for i in range(2)])
    rec = Rot([P.sbuf([128, 128], F32, f"rec{i}") for i in range(2)])
    BA = Rot([P.bank(0), P.bank(1)]); BB = Rot([P.bank(2), P.bank(3)])
    BO = Rot([P.bank(4), P.bank(5)]); BD = Rot([P.bank(6), P.bank(7)])
    scale = 128 ** -0.5
    for lr in range(16):
        for h in range(8):
            q = qT[:, h, lr * 64:(lr + 1) * 64]
            A_ = BA.next(); B_ = BB.next(); O = BO.next(); Dn = BD.next()
            for pb in range(8):
                off = (lr + 2 * pb) * 64
                P.pe("matmul", reads=[kT, qT], writes=[A_], out=A_[:, pb * 64:(pb + 1) * 64], lhsT=kT[:, h, off:off + 128],
                     rhs=q, start=True, stop=True)
            for cb in range(2):
                P.pe("matmul", reads=[kcT, qT], writes=[B_], out=B_[:, cb * 64:(cb + 1) * 64],
                     lhsT=kcT[:, h, cb * 128:(cb + 1) * 128], rhs=q, start=True, stop=True)
            t_ = tb.next(); p_ = pw.next(); c_ = pc.next()
            P.dve("scalar_tensor_tensor", reads=[A_, TT], writes=[t_], out=t_[:, :], in0=A_[:, :], scalar=scale,
                  in1=TT[:, h, :, :].rearrange("p a b -> p (a b)"), op0=ALU.mult, op1=ALU.add)
            P.pool("tensor_tensor", reads=[t_, vm], writes=[t_], out=t_[:, :].rearrange("p (a b) -> p a b", a=8),
                   in0=t_[:, :].rearrange("p (a b) -> p a b", a=8),
                   in1=vm[:, lr, :].unsqueeze(2).to_broadcast([128, 8, 64]), op=ALU.add)
            P.act("activation", reads=[t_], writes=[p_], out=p_[:, :], in_=t_[:, :], func=AF.Exp)
            P.act("activation", reads=[B_], writes=[c_], out=c_[:, :], in_=B_[:, 0:128], func=AF.Exp, scale=scale)
            for pb in range(8):
                row = lr + 2 * pb
                vsrc = ve[:, row // 2, h, :] if row % 2 == 0 else vo[:, row // 2, h, :]
                vbuf = ve if row % 2 == 0 else vo
                P.pe("matmul", reads=[vbuf, p_], writes=[O], accum=(pb > 0), out=O[:, 0:64], lhsT=vsrc,
                     rhs=p_[:, pb * 64:(pb + 1) * 64], start=(pb == 0), stop=False)
            for cb in range(2):
                P.pe("matmul", reads=[vc, c_], writes=[O], accum=True, out=O[:, 0:64], lhsT=vc[:, cb, h, :],
                     rhs=c_[:, cb * 64:(cb + 1) * 64], start=False, stop=(cb == 1))
            for pb in range(8):
                P.pe("matmul", reads=[ones_b, p_], writes=[Dn], accum=(pb > 0), out=Dn[:, 0:64], lhsT=ones_b[:],
                     rhs=p_[:, pb * 64:(pb + 1) * 64], start=(pb == 0), stop=False)
            for cb in range(2):
                P.pe("matmul", reads=[ones_b, c_], writes=[Dn], accum=True, out=Dn[:, 0:64], lhsT=ones_b[:],
                     rhs=c_[:, cb * 64:(cb + 1) * 64], start=False, stop=(cb == 1))
            r_ = rec.next()
            P.dve("reciprocal", reads=[Dn], writes=[r_], out=r_[:, 0:64], in_=Dn[:, 0:64])
            P.dve("tensor_tensor", reads=[O, r_], writes=[oT], out=oT[:, h, lr * 64:(lr + 1) * 64], in0=O[:, 0:64],
                  in1=r_[:, 0:64], op=ALU.mult)
    for qb in range(2):
        for h in range(8):
            q = qT[:, h, T_LAT + qb * 128:T_LAT + (qb + 1) * 128]
            B_ = BB.next(); O = BO.next(); Dn = BD.next()
            for cb in range(2):
                P.pe("matmul", reads=[kcT, qT], writes=[B_], out=B_[:, cb * 128:(cb + 1) * 128],
                     lhsT=kcT[:, h, cb * 128:(cb + 1) * 128], rhs=q, start=True, stop=True)
            p_ = pw.next()
            P.act("activation", reads=[B_], writes=[p_], out=p_[:, 0:256], in_=B_[:, 0:256], func=AF.Exp, scale=scale)
            for cb in range(2):
                P.pe("matmul", reads=[vc, p_], writes=[O], accum=(cb > 0), out=O[:, 0:128], lhsT=vc[:, cb, h, :],
                     rhs=p_[:, cb * 128:(cb + 1) * 128], start=(cb == 0), stop=(cb == 1))
            for cb in range(2):
                P.pe("matmul", reads=[ones_b, p_], writes=[Dn], accum=(cb > 0), out=Dn[:, 0:128], lhsT=ones_b[:],
                     rhs=p_[:, cb * 128:(cb + 1) * 128], start=(cb == 0), stop=(cb == 1))
            r_ = rec.next()
            P.dve("reciprocal", reads=[Dn], writes=[r_], out=r_[:, 0:128], in_=Dn[:, 0:128])
            P.dve("tensor_tensor", reads=[O, r_], writes=[oT], out=oT[:, h, T_LAT + qb * 128:T_LAT + (qb + 1) * 128],
                  in0=O[:, 0:128], in1=r_[:, 0:128], op=ALU.mult)
    P.dma(o_d, oT[:], reads=[oT])
    return P.finish()


def run_na(r1, rpb):
    a = np.arange(64)
    c0 = np.clip(a - 8, 0, 48)
    b = np.arange(64)
    colok = (b[:, None] >= c0[None, :]) & (b[:, None] < c0[None, :] + 16)
    dc = np.clip(b[:, None] - a[None, :], -15, 15) + 15
    TT = np.full((2, 64, 8, 8, 64), -30000.0, np.float32)
    for pb in range(8):
        for jj in range(2):
            dr = 2 * pb + jj - 1
            if 0 <= dr < 15:
                vals = rpb[:, dr][:, dc]
                TT[jj, :, :, pb, :] = np.where(colok[None], vals, np.float32(-30000.0)).transpose(1, 0, 2)
    TT = np.ascontiguousarray(TT.reshape(128, 8, 8, 64))
    k_lat = np.concatenate([r1[i]["k"][0:8].reshape(T_LAT, 8, 128) for i in range(NCORES)], 0)
    v_lat = np.concatenate([r1[i]["v"][0:8].reshape(T_LAT, 8, 128) for i in range(NCORES)], 0)
    k_ctx = r1[0]["k"][8:10].reshape(T_CTX, 8, 128)
    v_ctx = r1[0]["v"][8:10].reshape(T_CTX, 8, 128)
    kcT = np.ascontiguousarray(k_ctx.transpose(2, 1, 0))
    vc = np.ascontiguousarray(v_ctx.reshape(2, 128, 8, 128).transpose(1, 0, 2, 3))
    zk = np.zeros((64, 8, 128), k_lat.dtype)
    ins = []
    for i in range(NCORES):
        base = 16 * i - 8
        kw = []; vw = []
        for v in range(33):
            row = base + v
            if 0 <= row < 128:
                kw.append(k_lat[row * 64:(row + 1) * 64]); vw.append(v_lat[row * 64:(row + 1) * 64])
            else:
                kw.append(zk); vw.append(zk)
        kwin = np.concatenate(kw[:32], 0)
        kT = np.ascontiguousarray(kwin.transpose(2, 1, 0))
        ve = np.stack([np.concatenate([vw[2 * j], vw[2 * j + 1]], 0) for j in range(16)], 1)
        vo = np.stack([np.concatenate([vw[2 * j + 1], vw[2 * j + 2]], 0) for j in range(16)], 1)
        vm = np.full((2, 64, 16, 8), -30000.0, np.float32)
        for lr in range(16):
            r = 16 * i + lr
            rs = min(max(r - 4, 0), 120)
            for pb in range(8):
                for jj in range(2):
                    krow = r - 8 + 2 * pb + jj
                    if rs <= krow < rs + 8:
                        vm[jj, :, lr, pb] = 0.0
        q = r1[i]["q"].reshape(T_ALL, 8, 128)
        ins.append(dict(qT=np.ascontiguousarray(q.transpose(2, 1, 0)), kT=kT, ve=np.ascontiguousarray(ve),
                        vo=np.ascontiguousarray(vo), kcT=kcT, vc=vc, tt=TT, vm=np.ascontiguousarray(vm.reshape(128, 16, 8)),
                        ones=ONES))
    return _run(_prog("na", build_na), ins)


def run_ab_layer(x, ctx, mod_l, nw1, w_in, conv_w, conv_b, dt_bias, a_log, d_skip, norm_w, q_norm, k_norm, rpb,
                 w_out, nw2, w1, w2):
    r1 = run_ab1(x, ctx, mod_l, nw1, w_in, conv_w, conv_b, q_norm, k_norm)
    ra, rb = run_ssd(r1, dt_bias, a_log, d_skip, norm_w)
    rn = run_na(r1, rpb)
    u_lat = []
    for i in range(NCORES):
        gT = fm(rb[i]["g"].reshape(T_ALL, 2048))
        u_lat.append(np.concatenate([gT[:, :, :T_LAT], rn[i]["o"][:, :, :T_LAT]], axis=1))
    gT0 = fm(rb[0]["g"].reshape(T_ALL, 2048))
    u_ctx = np.concatenate([gT0[:, :, T_LAT:], rn[0]["o"][:, :, T_LAT:]], axis=1)
    return run_post(u_lat, u_ctx, x, ctx, w_out, w1, w2, mod_l, nw2)


def kernel(x, c, ctx, c_ctx, w_mod, b_mod, norm1_w, norm2_w, w_mlp_in, w_mlp_out,
           ab_w_in, ab_conv_w, ab_conv_b, ab_dt_bias, ab_a_log, ab_d_skip, ab_norm_w,
           ab_q_norm, ab_k_norm, ab_rpb, ab_w_out, c_w_qkv, c_q_norm, c_k_norm, c_w_out):
    f = lambda a: np.asarray(a, dtype=np.float32)
    xs = f(x)[0]
    cs = f(ctx)[0]
    mods = run_mod(f(c), f(c_ctx), f(w_mod), f(b_mod))
    for layer in range(4):
        i = layer // 2
        if layer % 2 == 0:
            xs, cs = run_ab_layer(xs, cs, mods[layer], f(norm1_w)[layer], f(ab_w_in)[i], f(ab_conv_w)[i],
                                  f(ab_conv_b)[i], f(ab_dt_bias)[i], f(ab_a_log)[i], f(ab_d_skip)[i],
                                  f(ab_norm_w)[i], f(ab_q_norm)[i], f(ab_k_norm)[i], f(ab_rpb)[i], f(ab_w_out)[i],
                                  f(norm2_w)[layer], f(w_mlp_in)[layer], f(w_mlp_out)[layer])
        else:
            xs, cs = run_c_layer(xs, cs, mods[layer], f(norm1_w)[layer], f(c_w_qkv)[i], f(c_q_norm)[i],
                                 f(c_k_norm)[i], f(c_w_out)[i], f(norm2_w)[layer], f(w_mlp_in)[layer],
                                 f(w_mlp_out)[layer])
    return np.ascontiguousarray(xs[None].astype(np.float32))
```

```python
import numpy as np
from contextlib import ExitStack

import concourse.bass as bass
import concourse.mybir as mybir
from concourse.bass_utils import run_bass_kernel_spmd

F32 = mybir.dt.float32
BF16 = mybir.dt.bfloat16
AF = mybir.ActivationFunctionType
ALU = mybir.AluOpType
AX = mybir.AxisListType

NCORES = 8


class Buf:
    __slots__ = ("ap", "w", "wd", "r", "rd", "name")

    def __init__(self, ap, name=""):
        self.ap = ap
        self.w = {}
        self.wd = []
        self.r = {}
        self.rd = []
        self.name = name

    def __getitem__(self, idx):
        return self.ap[idx]


class Prog:
    ENGS = ("pe", "act", "dve", "pool", "sp")
    DMA_SLOTS = 8

    def __init__(self):
        self.nc = bass.Bass("TRN2", target_bir_lowering=False)
        self.stack = ExitStack()
        self.ops = []
        self.n_by_eng = {e: 0 for e in self.ENGS}
        self._cnt = 0

    def dram_in(self, name, shape, dtype=F32):
        return self.nc.dram_tensor(name, list(shape), dtype, kind="ExternalInput").ap()

    def dram_out(self, name, shape, dtype=F32):
        return self.nc.dram_tensor(name, list(shape), dtype, kind="ExternalOutput").ap()

    ARENA_WORDS = 52736

    def _ensure_arena(self):
        if getattr(self, "arena", None) is None:
            self.arena = self.stack.enter_context(
                self.nc.sbuf_tensor("arena", [128, self.ARENA_WORDS], F32))
            self.top = 0
            self.banks = [Buf(self.stack.enter_context(self.nc.psum_tensor(f"bank{i}", [128, 512], F32)),
                              f"bank{i}") for i in range(8)]
            self.bank_i = 0
            self.last_by_eng = {}
            self.dma_since_barrier = []
            self.barrier_op = None
            self.after_barrier = set()

    def sbuf(self, shape, dtype=F32, name=None):
        self._ensure_arena()
        shape = list(shape)
        esz = 4 if dtype == F32 else 2
        n = 1
        for d in shape[1:]:
            n *= d
        words = (n * esz + 3) // 4
        words = (words + 7) // 8 * 8
        assert self.top + words <= self.ARENA_WORDS, f"SBUF arena overflow {name} {shape} top={self.top}"
        ap = self.arena[0:shape[0], self.top:self.top + words]
        self.top += words
        if dtype != F32:
            ap = ap.bitcast(dtype)
        ap = ap[:, 0:n]
        if len(shape) == 3:
            ap = ap.rearrange("p (a b) -> p a b", a=shape[1])
        elif len(shape) == 4:
            ap = ap.rearrange("p (a b c) -> p a b c", a=shape[1], b=shape[2])
        return Buf(ap, name or "")

    def psum(self, shape=None, dtype=F32, name=None):
        self._ensure_arena()
        b = self.banks[self.bank_i % 8]
        self.bank_i += 1
        return b

    def bank(self, i):
        self._ensure_arena()
        return self.banks[i]

    def mark(self):
        self._ensure_arena()
        return self.top

    def release(self, m):
        self.barrier()
        self.top = m

    def barrier(self):
        self._ensure_arena()
        if not hasattr(self, "_bar_buf"):
            self._bar_buf = self.sbuf([128, 8], F32, "barbuf")
        bb = self._bar_buf
        idx = self.op("dve", "memset", writes=[bb], ap=bb[:], constant=0.0)
        deps = self.ops[idx]["deps"]
        for e, last in self.last_by_eng.items():
            if last != idx:
                deps.add(last)
        deps.update(self.dma_since_barrier)
        deps.discard(idx)
        self.dma_since_barrier = []
        self.barrier_op = idx
        self.after_barrier = set()

    def op(self, eng, meth, reads=(), writes=(), dma=False, accum=False, **kw):
        fn = (meth, kw)
        idx = len(self.ops)
        deps = set()
        for b in reads:
            deps.update(b.w.values())
            deps.update(b.wd)
        for b in writes:
            has_readers = bool(b.r) or bool(b.rd)
            if has_readers:
                for e2, r in b.r.items():
                    if dma or e2 != eng:
                        deps.add(r)
                deps.update(b.rd)
            for e2, w in b.w.items():
                if dma or e2 != eng:
                    deps.add(w)
            if not dma:
                deps.update(b.wd)
        self._ensure_arena()
        if self.barrier_op is not None and eng not in self.after_barrier:
            deps.add(self.barrier_op)
            self.after_barrier.add(eng)
        deps.discard(idx)
        self.ops.append(dict(eng=eng, fn=fn, deps=deps, dma=dma))
        if dma:
            self.dma_since_barrier.append(idx)
        self.last_by_eng[eng] = idx
        for b in reads:
            if dma:
                b.rd.append(idx)
            else:
                b.r[eng] = idx
        for b in writes:
            had_readers = bool(b.r) or bool(b.rd)
            if dma:
                if had_readers:
                    b.w = {}
                    b.wd = []
                b.wd.append(idx)
            else:
                if had_readers:
                    b.w = {}
                b.wd = []
                b.w[eng] = idx
            b.r = {}
            b.rd = []
        return idx

    def pe(self, meth, reads=(), writes=(), accum=False, **kw):
        return self.op("pe", meth, reads, writes, accum=accum, **kw)

    def act(self, meth, reads=(), writes=(), **kw):
        return self.op("act", meth, reads, writes, **kw)

    def dve(self, meth, reads=(), writes=(), **kw):
        return self.op("dve", meth, reads, writes, **kw)

    def pool(self, meth, reads=(), writes=(), **kw):
        return self.op("pool", meth, reads, writes, **kw)

    def dma(self, out, in_, reads=(), writes=(), eng="sp", **kw):
        return self.op(eng, "dma_start", reads, writes, dma=True, out=out, in_=in_, **kw)

    def finish(self, final_wait_ops=None):
        nc = self.nc
        ops = self.ops
        n = len(ops)
        needed = [False] * n
        for i, o in enumerate(ops):
            for d in o["deps"]:
                od = ops[d]
                needed[d] = True
        if final_wait_ops is None:
            final_wait_ops = [i for i, o in enumerate(ops) if o["dma"]][-64:]
        for d in final_wait_ops:
            needed[d] = True
        sems = {e: self.stack.enter_context(nc.semaphore(f"s_{e}")) for e in self.ENGS}
        dma_sems = {e: [self.stack.enter_context(nc.semaphore(f"d_{e}{k}"))
                        for k in range(self.DMA_SLOTS)] for e in self.ENGS}
        sig = [None] * n
        cnt = {e: 0 for e in self.ENGS}
        dcnt = {e: 0 for e in self.ENGS}
        dslot_val = {e: [0] * self.DMA_SLOTS for e in self.ENGS}
        prev_slot_sig = [None] * n
        for i, o in enumerate(ops):
            e = o["eng"]
            if o["dma"]:
                k = dcnt[e] % self.DMA_SLOTS
                dcnt[e] += 1
                if dslot_val[e][k] > 0:
                    prev_slot_sig[i] = (dma_sems[e][k], dslot_val[e][k])
                dslot_val[e][k] += 16
                sig[i] = (dma_sems[e][k], dslot_val[e][k], 16)
            elif needed[i]:
                cnt[e] += 1
                sig[i] = (sems[e], cnt[e], 1)
        per_eng = {e: [] for e in self.ENGS}
        seen = {e: {} for e in self.ENGS}
        for i, o in enumerate(ops):
            e = o["eng"]
            waits = []
            want = {}
            for d in o["deps"]:
                s = sig[d]
                if s is None:
                    continue
                key = id(s[0])
                if key not in want or want[key][1] < s[1]:
                    want[key] = (s[0], s[1])
            if prev_slot_sig[i] is not None:
                s = prev_slot_sig[i]
                key = id(s[0])
                if key not in want or want[key][1] < s[1]:
                    want[key] = s
            for key, (sm, val) in want.items():
                if seen[e].get(key, 0) >= val:
                    continue
                seen[e][key] = val
                waits.append((sm, val))
            per_eng[e].append((waits, o["fn"], sig[i]))
        finals = [sig[d] for d in final_wait_ops]

        def run(engobj, lst, tail=None):
            for waits, fn, s in lst:
                for sm, val in waits:
                    engobj.wait_ge(sm, val)
                ins = getattr(engobj, fn[0])(**fn[1])
                if s is not None:
                    ins.then_inc(s[0], s[2])
            if tail:
                done = {}
                for sm, val, _ in tail:
                    done[id(sm)] = (sm, max(val, done.get(id(sm), (None, 0))[1]))
                for sm, val in done.values():
                    engobj.wait_ge(sm, val)

        with nc.Block() as block:
            @block.tensor
            def _(t):
                run(t, per_eng["pe"])

            @block.scalar
            def _(a):
                run(a, per_eng["act"])

            @block.vector
            def _(v):
                run(v, per_eng["dve"])

            @block.gpsimd
            def _(g):
                run(g, per_eng["pool"])

            @block.sync
            def _(s):
                run(s, per_eng["sp"], tail=finals)
        self.stack.close()
        return nc


D = 2048
KD = D // 128
T_LAT = 1024
T_CTX = 256
T_ALL = T_LAT + T_CTX
TILES = [(0, 512, 0), (512, 512, 0), (1024, 256, 1)]
EPS = 1e-6
HID = 8192


class Rot:
    def __init__(self, bufs):
        self.bufs = bufs
        self.i = 0

    def next(self):
        b = self.bufs[self.i % len(self.bufs)]
        self.i += 1
        return b


def load_consts(P, ones_dram):
    ones_f = P.sbuf([128, 128], F32, "ones_f")
    ones_b = P.sbuf([128, 128], BF16, "ones_b")
    P.dma(ones_f[:], ones_dram, writes=[ones_f])
    P.dve("tensor_copy", reads=[ones_f], writes=[ones_b], out=ones_b[:], in_=ones_f[:])
    return ones_f, ones_b


def mod_vectors(P, mod_dram, nw_dram, slot_sc):
    mod_sb = P.sbuf([128, 6, KD, 2], F32, "mod_sb")
    nw = P.sbuf([128, KD], F32, "nw")
    P.dma(mod_sb[:], mod_dram, writes=[mod_sb])
    P.dma(nw[:], nw_dram, writes=[nw])
    A = []
    for w in range(2):
        a = P.sbuf([128, KD], F32, f"modA{w}")
        P.dve("scalar_tensor_tensor", reads=[mod_sb, nw], writes=[a],
              out=a[:], in0=mod_sb[:, slot_sc, :, w], scalar=1.0, in1=nw[:], op0=ALU.add, op1=ALU.mult)
        A.append(a)
    return mod_sb, A


def norm_mod_tile(P, x_ap_fn, xbuf, h_ap_fn, hbuf, n, w, A, mod_sb, slot_sh, ones_b, sq_ap, sqbuf, t1, tmp):
    for k in range(KD):
        P.act("activation", reads=[xbuf], writes=[sqbuf], out=sq_ap[:, k, :n], in_=x_ap_fn(k), func=AF.Square)
    ps = P.psum()
    for k in range(KD):
        P.pe("matmul", reads=[ones_b, sqbuf], writes=[ps], accum=(k > 0),
             out=ps[:, :n], lhsT=ones_b[:], rhs=sq_ap[:, k, :n], start=(k == 0), stop=(k == KD - 1))
    r = t1.next()
    P.dve("tensor_scalar", reads=[ps], writes=[r], out=r[:, :n], in0=ps[:, :n], scalar1=1.0 / D, scalar2=EPS,
          op0=ALU.mult, op1=ALU.add)
    P.act("activation", reads=[r], writes=[r], out=r[:, :n], in_=r[:, :n], func=AF.Sqrt)
    P.dve("reciprocal", reads=[r], writes=[r], out=r[:, :n], in_=r[:, :n])
    for k in range(KD):
        t = tmp.next()
        P.dve("scalar_tensor_tensor", reads=[xbuf, A[w], r], writes=[t],
              out=t[:, :n], in0=x_ap_fn(k), scalar=A[w][:, k:k + 1], in1=r[:, :n], op0=ALU.mult, op1=ALU.mult)
        P.act("activation", reads=[t, mod_sb], writes=[hbuf],
              out=h_ap_fn(k), in_=t[:, :n], func=AF.Identity, bias=mod_sb[:, slot_sh, k, w:w + 1], scale=1.0)


class WStream:
    def __init__(self, P, words, cast_eng="pool", name="w", nstage=1, nbf=2):
        self.P = P
        self.words = words
        self.stage = Rot([P.sbuf([128, words], F32, f"{name}_st{i}") for i in range(nstage)])
        self.wb = Rot([P.sbuf([128, words], BF16, f"{name}_bf{i}") for i in range(nbf)])
        self.cast_eng = cast_eng
        self.n = 0

    def load(self, dram_ap, a, b):
        P = self.P
        s = self.stage.next(); wb = self.wb.next()
        n = a * b
        assert n <= self.words
        sv = s[:, 0:n].rearrange("p (a b) -> p a b", a=a)
        wv = wb[:, 0:n].rearrange("p (a b) -> p a b", a=a)
        P.dma(sv, dram_ap, writes=[s])
        eng = self.cast_eng
        if eng == "alt":
            eng = ("pool", "dve")[self.n % 2]
        self.n += 1
        if eng == "act":
            P.act("activation", reads=[s], writes=[wb], out=wv, in_=sv, func=AF.Copy)
        else:
            P.op(eng, "tensor_copy", reads=[s], writes=[wb], out=wv, in_=sv)
        return wb, wv


def build_post(KM, NL=2):
    KC = KM // 128
    P = Prog()
    TP = NL * 512 + T_CTX
    tiles = [(j * 512, 512, 0) for j in range(NL)] + [(NL * 512, T_CTX, 1)]
    uT_d = P.dram_in("uT", [128, KC, TP], BF16)
    xT_d = P.dram_in("xT", [128, KD, TP])
    wo_d = P.dram_in("wo", [16, 128, KC, 128])
    w1_d = P.dram_in("w1", [64, 128, KD, 128])
    w2_d = P.dram_in("w2", [32, 128, 32, 128])
    mod_d = P.dram_in("mod", [128, 6, KD, 2])
    nw_d = P.dram_in("nw", [128, KD])
    ones_d = P.dram_in("ones", [128, 128])
    out_d = P.dram_out("xo", [128, KD, TP])

    ones_f, ones_b = load_consts(P, ones_d)
    mod_sb, A = mod_vectors(P, mod_d, nw_d, slot_sc=4)
    xs = P.sbuf([128, KD, 512], F32, "xs")
    us = P.sbuf([128, KC, 512], BF16, "us")
    hs = P.sbuf([128, KD, 512], BF16, "hs")
    aT = P.sbuf([128, 64, 512], BF16, "aT")
    t1 = Rot([P.sbuf([128, 512], F32, f"t1_{i}") for i in range(2)])
    tmp = Rot([P.sbuf([128, 512], F32, f"tmp{i}") for i in range(3)])
    osb = Rot([P.sbuf([128, 512], F32, f"osb{i}") for i in range(3)])
    ws = WStream(P, 4096, name="ws")

    for (st, n, w) in tiles:
        P.dma(xs[:, :, :n], xT_d[:, :, st:st + n], writes=[xs])
        P.dma(us[:, :, :n], uT_d[:, :, st:st + n], writes=[us])
        for ob in range(16):
            wb, wv = ws.load(wo_d[ob], KC, 128)
            ps = P.psum()
            for k in range(KC):
                P.pe("matmul", reads=[wb, us], writes=[ps], accum=(k > 0),
                     out=ps[:, :n], lhsT=wv[:, k, :], rhs=us[:, k, :n], start=(k == 0), stop=(k == KC - 1))
            P.dve("scalar_tensor_tensor", reads=[ps, mod_sb, xs], writes=[xs],
                  out=xs[:, ob, :n], in0=ps[:, :n], scalar=mod_sb[:, 2, ob, w:w + 1],
                  in1=xs[:, ob, :n], op0=ALU.mult, op1=ALU.add)
        norm_mod_tile(P, lambda k: xs[:, k, :n], xs, lambda k: hs[:, k, :n], hs, n, w, A, mod_sb, 3,
                      ones_b, aT.ap, aT, t1, tmp)
        for hc in range(64):
            wb, wv = ws.load(w1_d[hc], KD, 128)
            ps = P.psum()
            for k in range(KD):
                P.pe("matmul", reads=[wb, hs], writes=[ps], accum=(k > 0),
                     out=ps[:, :n], lhsT=wv[:, k, :], rhs=hs[:, k, :n], start=(k == 0), stop=(k == KD - 1))
            r = tmp.next()
            P.act("activation", reads=[ps], writes=[r], out=r[:, :n], in_=ps[:, :n], func=AF.Relu)
            P.dve("tensor_tensor", reads=[r], writes=[aT], out=aT[:, hc, :n], in0=r[:, :n], in1=r[:, :n],
                  op=ALU.mult)
        for ob in range(16):
            ps = P.psum()
            for half in range(2):
                wb, wv = ws.load(w2_d[ob * 2 + half], 32, 128)
                for k in range(32):
                    kk = half * 32 + k
                    P.pe("matmul", reads=[wb, aT], writes=[ps], accum=(kk > 0),
                         out=ps[:, :n], lhsT=wv[:, k, :], rhs=aT[:, kk, :n], start=(kk == 0), stop=(kk == 63))
            o = osb.next()
            P.dve("scalar_tensor_tensor", reads=[ps, mod_sb, xs], writes=[o],
                  out=o[:, :n], in0=ps[:, :n], scalar=mod_sb[:, 5, ob, w:w + 1], in1=xs[:, ob, :n],
                  op0=ALU.mult, op1=ALU.add)
            P.dma(out_d[:, ob, st:st + n], o[:, :n], reads=[o])
    return P.finish()


def build_mod():
    P = Prog()
    cv_d = P.dram_in("cv", [128, KD, 2])
    w_d = P.dram_in("w", [12, 128, KD, 512])
    b_d = P.dram_in("b", [128, 48])
    out_d = P.dram_out("mo", [128, 48, 2])
    cv = P.sbuf([128, KD, 2], F32, "cv")
    sg = P.sbuf([128, KD, 2], F32, "sg")
    bs = P.sbuf([128, 48], F32, "bs")
    ob = P.sbuf([128, 48, 2], F32, "ob")
    P.dma(cv[:], cv_d, writes=[cv])
    P.dma(bs[:], b_d, writes=[bs])
    P.act("activation", reads=[cv], writes=[sg], out=sg[:], in_=cv[:], func=AF.Sigmoid)
    P.dve("tensor_tensor", reads=[cv, sg], writes=[sg], out=sg[:], in0=sg[:], in1=cv[:], op=ALU.mult)
    wst = Rot([P.sbuf([128, KD, 512], F32, f"wst{i}") for i in range(3)])
    for blk in range(12):
        wv = wst.next()
        P.dma(wv[:], w_d[blk], writes=[wv])
        for sub in range(4):
            cb = blk * 4 + sub
            ps = P.psum()
            for k in range(KD):
                P.pe("matmul", reads=[wv, sg], writes=[ps], accum=(k > 0),
                     out=ps[:, 0:2], lhsT=wv[:, k, sub * 128:(sub + 1) * 128], rhs=sg[:, k, :],
                     start=(k == 0), stop=(k == KD - 1))
            P.dve("tensor_scalar", reads=[ps, bs], writes=[ob], out=ob[:, cb, :], in0=ps[:, 0:2],
                  scalar1=bs[:, cb:cb + 1], scalar2=None, op0=ALU.add)
    P.dma(out_d, ob[:], reads=[ob])
    return P.finish()


def qk_norm_tile(P, ps, nh, w_ap, wbc, out_ap, outbuf, scr, rope=None):
    n = nh * 128
    sq = scr["sq"].next(); ss = scr["ss"].next(); xn = scr["xn"].next()
    P.act("activation", reads=[ps], writes=[sq], out=sq[:, :n], in_=ps[:, :n], func=AF.Square)
    P.dve("tensor_reduce", reads=[sq], writes=[ss], out=ss[:, :nh],
          in_=sq[:, :n].rearrange("p (h d) -> p h d", h=nh), axis=AX.X, op=ALU.add)
    P.dve("tensor_scalar", reads=[ss], writes=[ss], out=ss[:, :nh], in0=ss[:, :nh], scalar1=1.0 / 128, scalar2=EPS,
          op0=ALU.mult, op1=ALU.add)
    P.act("activation", reads=[ss], writes=[ss], out=ss[:, :nh], in_=ss[:, :nh], func=AF.Sqrt)
    P.dve("reciprocal", reads=[ss], writes=[ss], out=ss[:, :nh], in_=ss[:, :nh])
    P.dve("tensor_tensor", reads=[ps, ss], writes=[xn],
          out=xn[:, :n].rearrange("p (h d) -> p h d", h=nh), in0=ps[:, :n].rearrange("p (h d) -> p h d", h=nh),
          in1=ss[:, :nh].unsqueeze(2).to_broadcast([128, nh, 128]), op=ALU.mult)
    if rope is None:
        P.pool("tensor_tensor", reads=[xn, wbc], writes=[outbuf], out=out_ap, in0=xn[:, :n], in1=w_ap,
               op=ALU.mult)
        return
    cos_ap, sin_ap, rbuf = rope
    P.pool("tensor_tensor", reads=[xn, wbc], writes=[xn], out=xn[:, :n], in0=xn[:, :n], in1=w_ap, op=ALU.mult)
    xv = xn[:, :n].rearrange("p (h i two) -> p h i two", h=nh, two=2)
    ov = out_ap.rearrange("p (h i two) -> p h i two", h=nh, two=2)
    u1 = xv[:, :, :, 0]; u2 = xv[:, :, :, 1]
    cb = cos_ap.unsqueeze(1).to_broadcast([128, nh, 64]); sb = sin_ap.unsqueeze(1).to_broadcast([128, nh, 64])
    ta = scr["ra"].next(); tb = scr["rb"].next()
    tav = ta[:, :nh * 64].rearrange("p (h i) -> p h i", h=nh); tbv = tb[:, :nh * 64].rearrange("p (h i) -> p h i", h=nh)
    P.dve("tensor_tensor", reads=[xn, rbuf], writes=[ta], out=tav, in0=u1, in1=cb, op=ALU.mult)
    P.pool("tensor_tensor", reads=[xn, rbuf], writes=[tb], out=tbv, in0=u2, in1=sb, op=ALU.mult)
    P.dve("tensor_tensor", reads=[ta, tb], writes=[outbuf], out=ov[:, :, :, 0], in0=tav, in1=tbv, op=ALU.subtract)
    ta2 = scr["ra"].next(); tb2 = scr["rb"].next()
    ta2v = ta2[:, :nh * 64].rearrange("p (h i) -> p h i", h=nh); tb2v = tb2[:, :nh * 64].rearrange("p (h i) -> p h i", h=nh)
    P.dve("tensor_tensor", reads=[xn, rbuf], writes=[ta2], out=ta2v, in0=u1, in1=sb, op=ALU.mult)
    P.pool("tensor_tensor", reads=[xn, rbuf], writes=[tb2], out=tb2v, in0=u2, in1=cb, op=ALU.mult)
    P.dve("tensor_tensor", reads=[ta2, tb2], writes=[outbuf], out=ov[:, :, :, 1], in0=ta2v, in1=tb2v, op=ALU.add)


def norm1_all(P, xT_d, mod_d, nw_d, ones_b, hT, ntok_tiles):
    mod_sb, A = mod_vectors(P, mod_d, nw_d, slot_sc=1)
    m = P.mark()
    xs = Rot([P.sbuf([128, KD, 512], F32, f"xs{i}") for i in range(1)])
    sq = P.sbuf([128, KD, 512], BF16, "sqn")
    t1 = Rot([P.sbuf([128, 512], F32, f"t1_{i}") for i in range(2)])
    tmp = Rot([P.sbuf([128, 512], F32, f"tmp{i}") for i in range(3)])
    for (st, n, w) in ntok_tiles:
        x = xs.next()
        P.dma(x[:, :, :n], xT_d[:, :, st:st + n], writes=[x])
        norm_mod_tile(P, lambda k: x[:, k, :n], x, lambda k: hT[:, k, st:st + n], hT, n, w, A, mod_sb, 0,
                      ones_b, sq.ap, sq, t1, tmp)
    P.release(m)
    return mod_sb


def build_c1():
    P = Prog()
    xT_d = P.dram_in("xT", [128, KD, T_ALL])
    mod_d = P.dram_in("mod", [128, 6, KD, 2])
    nw_d = P.dram_in("nw", [128, KD])
    ones_d = P.dram_in("ones", [128, 128])
    w_d = P.dram_in("w", [6, 128, KD, 512])
    qkw_d = P.dram_in("qkw", [128, 2, 512])
    cs_d = P.dram_in("cs", [128, 8, 2, 64])
    q_d = P.dram_out("q", [10, 128, 2048], BF16)
    k_d = P.dram_out("k", [10, 128, 512], BF16)
    v_d = P.dram_out("v", [10, 128, 512], BF16)

    ones_f, ones_b = load_consts(P, ones_d)
    hT = P.sbuf([128, KD, T_ALL], BF16, "hT")
    norm1_all(P, xT_d, mod_d, nw_d, ones_b, hT, TILES)
    qkw = P.sbuf([128, 2, 512], F32, "qkw")
    cs = P.sbuf([128, 8, 2, 64], F32, "cs")
    P.dma(qkw[:], qkw_d, writes=[qkw])
    P.dma(cs[:], cs_d, writes=[cs])
    scr = dict(sq=Rot([P.sbuf([128, 512], F32, f"sq{i}") for i in range(2)]),
               ss=Rot([P.sbuf([128, 8], F32, f"ss{i}") for i in range(2)]),
               xn=Rot([P.sbuf([128, 512], F32, f"xn{i}") for i in range(2)]),
               ra=Rot([P.sbuf([128, 256], F32, f"ra{i}") for i in range(2)]),
               rb=Rot([P.sbuf([128, 256], F32, f"rb{i}") for i in range(2)]))
    ob = Rot([P.sbuf([128, 512], BF16, f"ob{i}") for i in range(3)])
    ws = WStream(P, KD * 512, name="ws", nstage=1, nbf=2)
    for cbk in range(6):
        wb, wv = ws.load(w_d[cbk], KD, 512)
        for tt in range(10):
            ps = P.psum()
            for k in range(KD):
                P.pe("matmul", reads=[hT, wb], writes=[ps], accum=(k > 0),
                     out=ps[:, :], lhsT=hT[:, k, tt * 128:(tt + 1) * 128], rhs=wv[:, k, :],
                     start=(k == 0), stop=(k == KD - 1))
            o = ob.next()
            if cbk < 5:
                rope = (cs[:, tt, 0, :], cs[:, tt, 1, :], cs) if tt < 8 else None
                qk_norm_tile(P, ps, 4, qkw[:, 0 if cbk < 4 else 1, :], qkw, o[:, :], o, scr, rope=rope)
                dst = q_d[tt, :, cbk * 512:(cbk + 1) * 512] if cbk < 4 else k_d[tt]
            else:
                P.act("activation", reads=[ps], writes=[o], out=o[:, :], in_=ps[:, :], func=AF.Copy)
                dst = v_d[tt]
            P.dma(dst, o[:, :], reads=[o])
    return P.finish()


NKB = 66


def build_c2():
    P = Prog()
    qT_d = P.dram_in("qT", [128, 16, T_ALL], BF16)
    kT_d = P.dram_in("kT", [128, 4, NKB * 128], BF16)
    v_d = P.dram_in("v", [128, NKB, 4, 128], BF16)
    ones_d = P.dram_in("ones", [128, 128])
    o_d = P.dram_out("o", [128, 16, T_ALL], BF16)
    ones_f, ones_b = load_consts(P, ones_d)
    qT = P.sbuf([128, 16, T_ALL], BF16, "qT")
    kT = [P.sbuf([128, NKB * 128], BF16, f"kT{g}") for g in range(4)]
    vs = [P.sbuf([128, NKB, 128], BF16, f"v{g}") for g in range(4)]
    P.dma(qT[:], qT_d, writes=[qT])
    for g in range(4):
        P.dma(kT[g][:], kT_d[:, g, :], writes=[kT[g]])
        P.dma(vs[g][:], v_d[:, :, g, :], writes=[vs[g]])
    pT = Rot([P.sbuf([128, 512], BF16, f"pT{i}") for i in range(3)])
    rec = Rot([P.sbuf([128, 512], F32, f"rec{i}") for i in range(2)])
    ob = Rot([P.sbuf([128, 512], BF16, f"ob{i}") for i in range(2)])
    sbanks = Rot([P.bank(i) for i in range(3)])
    obanks = Rot([P.bank(3), P.bank(4)])
    dbanks = Rot([P.bank(5), P.bank(6)])
    scale = 128 ** -0.5
    for qb in range(10):
        nkb = NKB if qb < 8 else 2
        for g in range(4):
            qg = qT[:, 4 * g:4 * g + 4, qb * 128:(qb + 1) * 128]
            O = obanks.next(); Dn = dbanks.next()

            def smm(kb_):
                S_ = sbanks.next()
                P.pe("matmul", reads=[kT[g], qT], writes=[S_],
                     out=S_[:, :].rearrange("p (h q) -> p h q", h=4), lhsT=kT[g][:, kb_ * 128:(kb_ + 1) * 128], rhs=qg,
                     start=True, stop=True)
                return S_
            S = smm(0)
            for kb in range(nkb):
                Sn = smm(kb + 1) if kb + 1 < nkb else None
                p = pT.next()
                P.act("activation", reads=[S], writes=[p], out=p[:, :], in_=S[:, :], func=AF.Exp, scale=scale)
                P.pe("matmul", reads=[vs[g], p], writes=[O], accum=(kb > 0),
                     out=O[:, :], lhsT=vs[g][:, kb, :], rhs=p[:, :], start=(kb == 0), stop=(kb == nkb - 1))
                P.pe("matmul", reads=[ones_b, p], writes=[Dn], accum=(kb > 0),
                     out=Dn[:, :], lhsT=ones_b[:], rhs=p[:, :], start=(kb == 0), stop=(kb == nkb - 1))
                S = Sn
            r = rec.next(); o = ob.next()
            P.dve("reciprocal", reads=[Dn], writes=[r], out=r[:, :], in_=Dn[:, :])
            P.dve("tensor_tensor", reads=[O, r], writes=[o], out=o[:, :], in0=O[:, :], in1=r[:, :], op=ALU.mult)
            P.dma(o_d[:, 4 * g:4 * g + 4, qb * 128:(qb + 1) * 128], o[:, :].rearrange("p (h q) -> p h q", h=4),
                  reads=[o])
    return P.finish()


_PROGS = {}


def _prog(name, builder, *args):
    key = (name,) + args
    if key not in _PROGS:
        _PROGS[key] = builder(*args)
    return _PROGS[key]


def _run(nc, in_maps):
    res = run_bass_kernel_spmd(nc, in_maps, core_ids=list(range(NCORES)))
    return res.results


def fm(a):
    Tn, F = a.shape
    return np.ascontiguousarray(a.T.reshape(F // 128, 128, Tn).transpose(1, 0, 2))


def unfm(a):
    p, C, Tn = a.shape
    return np.ascontiguousarray(a.transpose(2, 1, 0).reshape(Tn, C * 128))


def wblocks(w, colblk):
    K, N = w.shape
    return np.ascontiguousarray(w.reshape(K // 128, 128, N // colblk, colblk).transpose(2, 1, 0, 3))


ONES = np.ones((128, 128), np.float32)


def run_mod(c, c_ctx, w_mod, b_mod):
    cv = np.stack([c.reshape(D), c_ctx.reshape(D)], axis=-1)
    cv = np.ascontiguousarray(cv.reshape(KD, 128, 2).transpose(1, 0, 2))
    ins = []
    for i in range(NCORES):
        l, half = i // 2, i % 2
        w = w_mod[l][:, half * 6144:(half + 1) * 6144]
        b = b_mod[l][half * 6144:(half + 1) * 6144]
        ins.append(dict(cv=cv, w=wblocks(w, 512), b=np.ascontiguousarray(b.reshape(48, 128).T)))
    res = _run(_prog("mod", build_mod), ins)
    modall = np.zeros((4, 6 * D, 2), np.float32)
    for i in range(NCORES):
        l, half = i // 2, i % 2
        mo = res[i]["mo"]
        modall[l, half * 6144:(half + 1) * 6144] = mo.transpose(1, 0, 2).reshape(6144, 2)
    return [np.ascontiguousarray(modall[l].reshape(6, KD, 128, 2).transpose(2, 0, 1, 3)) for l in range(4)]


def vec_fm(v):
    return np.ascontiguousarray(v.reshape(KD, 128).T)


def xT_cores(x, ctx):
    return [fm(np.concatenate([x[i * T_LAT:(i + 1) * T_LAT], ctx], axis=0)) for i in range(NCORES)]


def rope_tables():
    t = np.arange(8192)
    half = 64
    inv = (1.0 / (10000.0 ** (np.arange(0, half, 2, dtype=np.float32) / half))).astype(np.float32)
    ang = np.concatenate([(t // 64).astype(np.float32)[:, None] * inv, (t % 64).astype(np.float32)[:, None] * inv], -1)
    return np.cos(ang).astype(np.float32), np.sin(ang).astype(np.float32)


NCP = 4
NLP = 8192 // NCP // 512


def run_post(u_lat, u_ctx, x, ctx, w_out, w1, w2, mod_l, nw2):
    KM = w_out.shape[0]
    wo = wblocks(w_out, 128)
    w1b = wblocks(w1, 128)
    w2b = np.ascontiguousarray(w2.reshape(2, 32, 128, 16, 128).transpose(3, 0, 2, 1, 4).reshape(32, 128, 32, 128))
    nw = vec_fm(nw2)
    per = NCORES // NCP
    tl = 8192 // NCP
    ins = []
    for p in range(NCP):
        uT = np.concatenate([u_lat[p * per + j] for j in range(per)] + [u_ctx], axis=2)
        xT = fm(np.concatenate([x[p * tl:(p + 1) * tl], ctx], axis=0))
        ins.append(dict(uT=np.ascontiguousarray(uT), xT=xT, wo=wo, w1=w1b, w2=w2b, mod=mod_l, nw=nw, ones=ONES))
    res = run_bass_kernel_spmd(_prog("post", build_post, KM, NLP), ins, core_ids=list(range(NCP))).results
    outs = [unfm(res[p]["xo"]) for p in range(NCP)]
    xn = np.concatenate([o[:tl] for o in outs], axis=0)
    cn = outs[0][tl:]
    return xn, cn


def run_c_layer(x, ctx, mod_l, nw1, w_qkv, q_norm, k_norm, w_out, nw2, w1, w2):
    xTs = xT_cores(x, ctx)
    cos, sin = rope_tables()
    qkw = np.stack([np.tile(q_norm, 4), np.tile(k_norm, 4)], 0)
    qkw = np.ascontiguousarray(np.broadcast_to(qkw[None], (128, 2, 512))).astype(np.float32)
    wb = wblocks(w_qkv, 512)
    nw = vec_fm(nw1)
    ins = []
    for i in range(NCORES):
        cs = np.stack([cos[i * T_LAT:(i + 1) * T_LAT], sin[i * T_LAT:(i + 1) * T_LAT]], 1)
        cs = np.ascontiguousarray(cs.reshape(8, 128, 2, 64).transpose(1, 0, 2, 3))
        ins.append(dict(xT=xTs[i], mod=mod_l, nw=nw, ones=ONES, w=wb, qkw=qkw, cs=cs))
    r1 = _run(_prog("c1", build_c1), ins)
    k_ctx = r1[0]["k"][8:10].reshape(256, 4, 128)
    v_ctx = r1[0]["v"][8:10].reshape(256, 4, 128)
    k_all = np.concatenate([k_ctx] + [r1[i]["k"][0:8].reshape(1024, 4, 128) for i in range(NCORES)], 0)
    v_all = np.concatenate([v_ctx] + [r1[i]["v"][0:8].reshape(1024, 4, 128) for i in range(NCORES)], 0)
    kT = np.ascontiguousarray(k_all.transpose(2, 1, 0))
    vv = np.ascontiguousarray(v_all.reshape(NKB, 128, 4, 128).transpose(1, 0, 2, 3))
    ins2 = []
    for i in range(NCORES):
        q = r1[i]["q"].reshape(T_ALL, 16, 128)
        ins2.append(dict(qT=np.ascontiguousarray(q.transpose(2, 1, 0)), kT=kT, v=vv, ones=ONES))
    r2 = _run(_prog("c2", build_c2), ins2)
    u_lat = [r2[i]["o"][:, :, :T_LAT] for i in range(NCORES)]
    u_ctx = r2[0]["o"][:, :, T_LAT:]
    return run_post2(u_lat, u_ctx, x, ctx, w_out, w1, w2, mod_l, nw2)


T_AB = T_ALL + 4
TILES_AB = TILES + [(T_ALL, 4, 0)]


def build_ab1():
    P = Prog()
    xT_d = P.dram_in("xT", [128, KD, T_AB])
    mod_d = P.dram_in("mod", [128, 6, KD, 2])
    nw_d = P.dram_in("nw", [128, KD])
    ones_d = P.dram_in("ones", [128, 128])
    wfm_d = P.dram_in("wfm", [32, 128, KD, 128])
    wtm_d = P.dram_in("wtm", [10, 128, KD, 512])
    wdt_d = P.dram_in("wdt", [128, KD, 64])
    cw_d = P.dram_in("cw", [128, 32, 5])
    cb_d = P.dram_in("cb", [128, 32])
    edge_d = P.dram_in("edge", [128, 2])
    qkw_d = P.dram_in("qkw", [128, 2, 512])
    xbc_d = P.dram_out("xbc", [32, 128, T_ALL])
    z_d = P.dram_out("z", [10, 128, 2048])
    q_d = P.dram_out("q", [10, 128, 1024], BF16)
    k_d = P.dram_out("k", [10, 128, 1024], BF16)
    v_d = P.dram_out("v", [10, 128, 1024], BF16)
    dt_d = P.dram_out("dt", [10, 128, 64])

    ones_f, ones_b = load_consts(P, ones_d)
    hT = P.sbuf([128, KD, T_AB], BF16, "hT")
    norm1_all(P, xT_d, mod_d, nw_d, ones_b, hT, TILES_AB)
    cw = P.sbuf([128, 32, 5], F32, "cw"); cbias = P.sbuf([128, 32], F32, "cbias")
    edge = P.sbuf([128, 2], F32, "edge"); qkw = P.sbuf([128, 2, 512], F32, "qkw")
    P.dma(cw[:], cw_d, writes=[cw]); P.dma(cbias[:], cb_d, writes=[cbias])
    P.dma(edge[:], edge_d, writes=[edge]); P.dma(qkw[:], qkw_d, writes=[qkw])
    ws = WStream(P, KD * 512, name="ws", nstage=1, nbf=2)
    Ul = Rot([P.sbuf([128, T_LAT + 4], F32, f"Ul{i}") for i in range(2)])
    Uc = Rot([P.sbuf([128, T_CTX + 4], F32, f"Uc{i}") for i in range(2)])
    for u in Uc.bufs:
        P.dve("memset", writes=[u], ap=u[:], constant=0.0)
    accl = Rot([P.sbuf([128, T_LAT], F32, f"accl{i}") for i in range(2)])
    accc = Rot([P.sbuf([128, T_CTX], F32, f"accc{i}") for i in range(2)])
    for cb in range(32):
        wb, wv = ws.load(wfm_d[cb], KD, 128)
        ul = Ul.next(); uc = Uc.next()
        for (st, n, w) in TILES_AB:
            ps = P.psum()
            for k in range(KD):
                P.pe("matmul", reads=[wb, hT], writes=[ps], accum=(k > 0),
                     out=ps[:, :n], lhsT=wv[:, k, :], rhs=hT[:, k, st:st + n], start=(k == 0), stop=(k == KD - 1))
            if st < T_LAT:
                P.act("activation", reads=[ps], writes=[ul], out=ul[:, 2 + st:2 + st + n], in_=ps[:, :n], func=AF.Copy)
            elif st == T_LAT:
                P.act("activation", reads=[ps], writes=[uc], out=uc[:, 2:2 + n], in_=ps[:, :n], func=AF.Copy)
            else:
                P.dve("tensor_scalar", reads=[ps, edge], writes=[ul], out=ul[:, 0:2], in0=ps[:, 0:2],
                      scalar1=edge[:, 0:1], scalar2=None, op0=ALU.mult)
                P.dve("tensor_scalar", reads=[ps, edge], writes=[ul], out=ul[:, T_LAT + 2:T_LAT + 4], in0=ps[:, 2:4],
                      scalar1=edge[:, 1:2], scalar2=None, op0=ALU.mult)
        for (U, acc, N, off) in ((ul, accl.next(), T_LAT, 0), (uc, accc.next(), T_CTX, T_LAT)):
            P.dve("tensor_scalar", reads=[U, cw, cbias], writes=[acc], out=acc[:, :], in0=U[:, 0:N],
                  scalar1=cw[:, cb, 0:1], scalar2=cbias[:, cb:cb + 1], op0=ALU.mult, op1=ALU.add)
            for kk in range(1, 5):
                P.dve("scalar_tensor_tensor", reads=[U, cw, acc], writes=[acc], out=acc[:, :], in0=U[:, kk:kk + N],
                      scalar=cw[:, cb, kk:kk + 1], in1=acc[:, :], op0=ALU.mult, op1=ALU.add)
            P.act("activation", reads=[acc], writes=[acc], out=acc[:, :], in_=acc[:, :], func=AF.Silu)
            P.dma(xbc_d[cb, :, off:off + N], acc[:, :], reads=[acc])
    scr = dict(sq=Rot([P.sbuf([128, 512], F32, f"sq{i}") for i in range(2)]),
               ss=Rot([P.sbuf([128, 8], F32, f"ss{i}") for i in range(2)]),
               xn=Rot([P.sbuf([128, 512], F32, f"xn{i}") for i in range(2)]))
    of = Rot([P.sbuf([128, 512], F32, f"of{i}") for i in range(2)])
    ob = Rot([P.sbuf([128, 512], BF16, f"ob{i}") for i in range(3)])
    for cbk in range(10):
        wb, wv = ws.load(wtm_d[cbk], KD, 512)
        for tt in range(10):
            ps = P.psum()
            for k in range(KD):
                P.pe("matmul", reads=[hT, wb], writes=[ps], accum=(k > 0),
                     out=ps[:, :], lhsT=hT[:, k, tt * 128:(tt + 1) * 128], rhs=wv[:, k, :],
                     start=(k == 0), stop=(k == KD - 1))
            if cbk < 4:
                o = of.next()
                P.act("activation", reads=[ps], writes=[o], out=o[:, :], in_=ps[:, :], func=AF.Copy)
                P.dma(z_d[tt, :, cbk * 512:(cbk + 1) * 512], o[:, :], reads=[o])
            elif cbk < 8:
                o = ob.next()
                wi = 0 if cbk < 6 else 1
                qk_norm_tile(P, ps, 4, qkw[:, wi, :], qkw, o[:, :], o, scr, rope=None)
                dst = (q_d if cbk < 6 else k_d)[tt, :, (cbk % 2) * 512:(cbk % 2 + 1) * 512]
                P.dma(dst, o[:, :], reads=[o])
            else:
                o = ob.next()
                P.act("activation", reads=[ps], writes=[o], out=o[:, :], in_=ps[:, :], func=AF.Copy)
                P.dma(v_d[tt, :, (cbk % 2) * 512:(cbk % 2 + 1) * 512], o[:, :], reads=[o])
    wb, wv = ws.load(wdt_d, KD, 64)
    for tt in range(10):
        ps = P.psum()
        for k in range(KD):
            P.pe("matmul", reads=[hT, wb], writes=[ps], accum=(k > 0),
                 out=ps[:, :64], lhsT=hT[:, k, tt * 128:(tt + 1) * 128], rhs=wv[:, k, :],
                 start=(k == 0), stop=(k == KD - 1))
        o = of.next()
        P.act("activation", reads=[ps], writes=[o], out=o[:, :64], in_=ps[:, :64], func=AF.Copy)
        P.dma(dt_d[tt], o[:, :64], reads=[o])
    return P.finish()


def run_ab1(x, ctx, mod_l, nw1, w_in, conv_w, conv_b, q_norm, k_norm):
    wfm = wblocks(w_in[:, 2048:6144], 128)
    wtm = wblocks(np.concatenate([w_in[:, 0:2048], w_in[:, 6208:9280]], axis=1), 512)
    wdt = np.ascontiguousarray(w_in[:, 6144:6208].reshape(KD, 128, 64).transpose(1, 0, 2))
    cw = np.ascontiguousarray(conv_w.reshape(5, 32, 128).transpose(2, 1, 0))
    cb = np.ascontiguousarray(conv_b.reshape(32, 128).T)
    qkw = np.stack([np.tile(q_norm, 4), np.tile(k_norm, 4)], 0)
    qkw = np.ascontiguousarray(np.broadcast_to(qkw[None], (128, 2, 512))).astype(np.float32)
    nw = vec_fm(nw1)
    xpad = np.concatenate([np.zeros((2, D), np.float32), x, np.zeros((2, D), np.float32)], 0)
    ins = []
    for i in range(NCORES):
        lo = i * T_LAT
        toks = np.concatenate([x[lo:lo + T_LAT], ctx, xpad[lo:lo + 2], xpad[lo + T_LAT + 2:lo + T_LAT + 4]], 0)
        edge = np.zeros((128, 2), np.float32)
        edge[:, 0] = 1.0 if i > 0 else 0.0
        edge[:, 1] = 1.0 if i < NCORES - 1 else 0.0
        ins.append(dict(xT=fm(toks), mod=mod_l, nw=nw, ones=ONES, wfm=wfm, wtm=wtm, wdt=wdt, cw=cw, cb=cb,
                        edge=edge, qkw=qkw))
    return _run(_prog("ab1", build_ab1), ins)


def build_ssd(mode):
    full = (mode == "B")
    P = Prog()
    xs_d = P.dram_in("xs", [10, 128, 2048])
    bt_d = P.dram_in("btok", [10, 128, 1024])
    BT_d = P.dram_in("BT", [10, 128, 8, 128])
    CT_d = P.dram_in("CT", [10, 128, 8, 128])
    dt_d = P.dram_in("dt", [10, 128, 64])
    par_d = P.dram_in("par", [128, 3, 64])
    tri_d = P.dram_in("tri", [128, 2, 128])
    nm_d = P.dram_in("negmask", [128, 2, 128])
    id_d = P.dram_in("ident", [128, 128])
    ones_d = P.dram_in("ones", [128, 128])
    if full:
        Fl_d = P.dram_in("Flist", [2, 7, 128, 2048])
        Tl_d = P.dram_in("Tlist", [2, 7, 128, 32])
        cF_d = P.dram_in("ctxF", [2, 128, 2048])
        z_d = P.dram_in("z", [10, 128, 2048])
        gw_d = P.dram_in("gw", [128, 2048])
        y_d = P.dram_out("yd", [2, 10, 128, 2048])
        g_d = P.dram_out("g", [10, 128, 2048], BF16)
        ybuf = Buf(y_d, "y_dram")
    else:
        F_d = P.dram_out("F", [2, 128, 2048])
        T_d = P.dram_out("T", [2, 128, 32])
        cFo_d = P.dram_out("ctxF", [2, 128, 2048])

    ones_f, ones_b = load_consts(P, ones_d)
    par = P.sbuf([128, 3, 64], F32, "par"); tri = P.sbuf([128, 2, 128], F32, "tri")
    nm = P.sbuf([128, 2, 128], F32, "nm"); ident = P.sbuf([128, 128], F32, "ident")
    for sb_, d_ in ((par, par_d), (tri, tri_d), (nm, nm_d), (ident, id_d)):
        P.dma(sb_[:], d_, writes=[sb_])
    abc = P.sbuf([128, 64], F32, "abc")
    P.act("activation", reads=[par], writes=[abc], out=abc[:], in_=par[:, 1, :], func=AF.Exp)
    P.dve("tensor_scalar", reads=[abc], writes=[abc], out=abc[:], in0=abc[:], scalar1=-1.0, scalar2=None, op0=ALU.mult)
    dsum = P.sbuf([128, 32], F32, "dsum")
    P.dve("tensor_tensor", reads=[par], writes=[dsum], out=dsum[:], in0=par[:, 2, 0:32], in1=par[:, 2, 32:64], op=ALU.add)

    S = [P.sbuf([128, 2048], F32, f"S{d}") for d in range(2)]
    Sb = [P.sbuf([128, 2048], BF16, f"Sb{d}") for d in range(2)]
    tot_acc = [P.sbuf([128, 32], F32, f"tacc{d}") for d in range(2)]
    xs = P.sbuf([128, 2048], F32, "xs"); xd = P.sbuf([128, 2048], BF16, "xd"); xdd = P.sbuf([128, 2048], BF16, "xdd")
    btf = P.sbuf([128, 1024], F32, "btf"); btb = P.sbuf([128, 1024], BF16, "btb")
    BTf = P.sbuf([128, 8, 128], F32, "BTf"); BTb = P.sbuf([128, 8, 128], BF16, "BTb")
    CTf = P.sbuf([128, 8, 128], F32, "CTf"); CTb = P.sbuf([128, 8, 128], BF16, "CTb")
    sm = {n_: P.sbuf([128, 32], F32, n_) for n_ in
          ("dtr", "x0", "mx", "na", "e", "dt", "dtA", "la", "nla", "ela", "tot", "dend", "cdec", "dtd")}
    dbc = P.sbuf([128, 32, 128], F32, "dbc")
    Lt = Rot([P.sbuf([128, 128], F32, f"Lt{i}") for i in range(2)])
    Mt = Rot([P.sbuf([128, 128], BF16, f"Mt{i}") for i in range(2)])
    ysb = P.sbuf([128, 2048], F32, "ysb")
    tmpg = Rot([P.sbuf([128, 256], F32, f"tg{i}") for i in range(2)])
    tmps = Rot([P.sbuf([128, 256], F32, f"ts{i}") for i in range(2)])
    B_la, B_tot, B_cb, B_yd, B_yo, B_st = P.bank(0), P.bank(1), P.bank(2), P.bank(5), P.bank(6), P.bank(7)
    B_arg = Rot([P.bank(3), P.bank(4)])

    def bc(ap32, g):
        return ap32[:, 4 * g:4 * g + 4].unsqueeze(2).to_broadcast([128, 4, 64])

    def v3(ap, g):
        return ap[:, g * 256:(g + 1) * 256].rearrange("p (k q) -> p k q", k=4)

    def unit(c, d, want_y):
        P.dma(xs[:], xs_d[c], writes=[xs])
        P.dma(btf[:], bt_d[c], writes=[btf])
        P.dma(BTf[:], BT_d[c], writes=[BTf])
        P.dma(CTf[:], CT_d[c], writes=[CTf])
        P.dma(sm["dtr"][:], dt_d[c, :, d * 32:(d + 1) * 32], writes=[sm["dtr"]])
        P.pool("tensor_copy", reads=[btf], writes=[btb], out=btb[:], in_=btf[:])
        P.pool("tensor_copy", reads=[BTf], writes=[BTb], out=BTb[:], in_=BTf[:])
        P.pool("tensor_copy", reads=[CTf], writes=[CTb], out=CTb[:], in_=CTf[:])
        dsl = slice(d * 32, (d + 1) * 32)
        P.dve("tensor_tensor", reads=[sm["dtr"], par], writes=[sm["x0"]], out=sm["x0"][:], in0=sm["dtr"][:],
              in1=par[:, 0, dsl], op=ALU.add)
        P.dve("tensor_scalar", reads=[sm["x0"]], writes=[sm["mx"]], out=sm["mx"][:], in0=sm["x0"][:], scalar1=0.0,
              scalar2=None, op0=ALU.max)
        P.dve("scalar_tensor_tensor", reads=[sm["mx"], sm["x0"]], writes=[sm["na"]], out=sm["na"][:], in0=sm["mx"][:],
              scalar=-2.0, in1=sm["x0"][:], op0=ALU.mult, op1=ALU.add)
        P.act("activation", reads=[sm["na"]], writes=[sm["e"]], out=sm["e"][:], in_=sm["na"][:], func=AF.Exp)
        P.dve("tensor_scalar", reads=[sm["e"]], writes=[sm["e"]], out=sm["e"][:], in0=sm["e"][:], scalar1=1.0,
              scalar2=None, op0=ALU.add)
        P.act("activation", reads=[sm["e"]], writes=[sm["e"]], out=sm["e"][:], in_=sm["e"][:], func=AF.Ln)
        P.dve("tensor_tensor", reads=[sm["mx"], sm["e"]], writes=[sm["dt"]], out=sm["dt"][:], in0=sm["mx"][:],
              in1=sm["e"][:], op=ALU.add)
        P.dve("tensor_tensor", reads=[sm["dt"], abc], writes=[sm["dtA"]], out=sm["dtA"][:], in0=sm["dt"][:],
              in1=abc[:, dsl], op=ALU.mult)
        P.pe("matmul", reads=[tri, sm["dtA"]], writes=[B_la], out=B_la[:, 0:32], lhsT=tri[:, d, :], rhs=sm["dtA"][:],
             start=True, stop=True)
        P.pe("matmul", reads=[ones_f, sm["dtA"]], writes=[B_tot], out=B_tot[:, 0:32], lhsT=ones_f[:], rhs=sm["dtA"][:],
             start=True, stop=True)
        P.dve("tensor_copy", reads=[B_la], writes=[sm["la"]], out=sm["la"][:], in_=B_la[:, 0:32])
        P.dve("tensor_copy", reads=[B_tot], writes=[sm["tot"]], out=sm["tot"][:], in_=B_tot[:, 0:32])
        P.dve("tensor_scalar", reads=[sm["la"]], writes=[sm["nla"]], out=sm["nla"][:], in0=sm["la"][:], scalar1=-1.0,
              scalar2=None, op0=ALU.mult)
        P.act("activation", reads=[sm["la"]], writes=[sm["ela"]], out=sm["ela"][:], in_=sm["la"][:], func=AF.Exp)
        P.dve("tensor_tensor", reads=[sm["tot"], sm["la"]], writes=[sm["dend"]], out=sm["dend"][:], in0=sm["tot"][:],
              in1=sm["la"][:], op=ALU.subtract)
        P.act("activation", reads=[sm["dend"]], writes=[sm["dend"]], out=sm["dend"][:], in_=sm["dend"][:], func=AF.Exp)
        P.act("activation", reads=[sm["tot"]], writes=[sm["cdec"]], out=sm["cdec"][:], in_=sm["tot"][:], func=AF.Exp)
        P.dve("tensor_tensor", reads=[sm["dt"], sm["dend"]], writes=[sm["dtd"]], out=sm["dtd"][:], in0=sm["dt"][:],
              in1=sm["dend"][:], op=ALU.mult)
        P.dve("tensor_tensor", reads=[tot_acc[d], sm["tot"]], writes=[tot_acc[d]], out=tot_acc[d][:], in0=tot_acc[d][:],
              in1=sm["tot"][:], op=ALU.add)
        xs3 = xs[:, :].rearrange("p (h q) -> p h q", h=32)
        P.dve("tensor_tensor", reads=[xs, sm["dtd"]], writes=[xdd], out=xdd[:, :].rearrange("p (h q) -> p h q", h=32),
              in0=xs3, in1=sm["dtd"][:, :].unsqueeze(2).to_broadcast([128, 32, 64]), op=ALU.mult)
        if want_y:
            P.pool("tensor_tensor", reads=[xs, sm["dt"]], writes=[xd], out=xd[:, :].rearrange("p (h q) -> p h q", h=32),
                   in0=xs3, in1=sm["dt"][:, :].unsqueeze(2).to_broadcast([128, 32, 64]), op=ALU.mult)
            P.pool("tensor_copy", reads=[sm["dtA"]], writes=[dbc], out=dbc[:],
                   in_=sm["dtA"][:, :].unsqueeze(2).to_broadcast([128, 32, 128]))
        for g in range(8):
            if want_y:
                P.pe("matmul", reads=[BTb, CTb], writes=[B_cb], out=B_cb[:, 0:128], lhsT=BTb[:, g, :], rhs=CTb[:, g, :],
                     start=True, stop=True)
                for k in range(4):
                    h = 4 * g + k
                    A_ = B_arg.next()
                    P.pe("matmul", reads=[dbc, tri], writes=[A_], out=A_[:, 0:128], lhsT=dbc[:, h, :], rhs=tri[:, d, :],
                         start=True, stop=False)
                    P.pe("matmul", reads=[ident, nm], writes=[A_], accum=True, out=A_[:, 0:128], lhsT=ident[:],
                         rhs=nm[:, d, :], start=False, stop=True)
                    L_ = Lt.next(); M_ = Mt.next()
                    P.act("activation", reads=[A_, sm["nla"]], writes=[L_], out=L_[:], in_=A_[:, 0:128], func=AF.Exp,
                          bias=sm["nla"][:, h:h + 1], scale=1.0)
                    P.dve("tensor_tensor", reads=[L_, B_cb], writes=[M_], out=M_[:], in0=L_[:], in1=B_cb[:, 0:128],
                          op=ALU.mult)
                    P.pe("matmul", reads=[M_, xd], writes=[B_yd], out=B_yd[:, k * 64:(k + 1) * 64], lhsT=M_[:],
                         rhs=xd[:, h * 64:(h + 1) * 64], start=True, stop=True)
                P.pe("matmul", reads=[CTb, Sb[d]], writes=[B_yo], out=B_yo[:, 0:256], lhsT=CTb[:, g, :],
                     rhs=Sb[d][:, g * 256:(g + 1) * 256], start=True, stop=True)
                t_ = tmpg.next()
                t3 = t_[:, :].rearrange("p (k q) -> p k q", k=4)
                P.dve("tensor_tensor", reads=[B_yo, sm["ela"]], writes=[t_], out=t3,
                      in0=B_yo[:, 0:256].rearrange("p (k q) -> p k q", k=4), in1=bc(sm["ela"], g), op=ALU.mult)
                P.dve("tensor_tensor", reads=[t_, B_yd], writes=[ysb], out=ysb[:, g * 256:(g + 1) * 256], in0=t_[:, :],
                      in1=B_yd[:, 0:256], op=ALU.add)
                if d == 0:
                    t2 = tmps.next()
                    P.pool("tensor_tensor", reads=[xs, dsum], writes=[t2], out=t2[:, :].rearrange("p (k q) -> p k q", k=4),
                           in0=v3(xs, g), in1=bc(dsum, g), op=ALU.mult)
                    P.pool("tensor_tensor", reads=[t2, ysb], writes=[ysb], out=ysb[:, g * 256:(g + 1) * 256],
                           in0=ysb[:, g * 256:(g + 1) * 256], in1=t2[:, :], op=ALU.add)
            P.pe("matmul", reads=[btb, xdd], writes=[B_st], out=B_st[:, 0:256], lhsT=btb[:, g * 128:(g + 1) * 128],
                 rhs=xdd[:, g * 256:(g + 1) * 256], start=True, stop=True)
            P.dve("tensor_tensor", reads=[S[d], sm["cdec"]], writes=[S[d]], out=v3(S[d], g), in0=v3(S[d], g),
                  in1=bc(sm["cdec"], g), op=ALU.mult)
            P.dve("tensor_tensor", reads=[S[d], B_st], writes=[S[d]], out=S[d][:, g * 256:(g + 1) * 256],
                  in0=S[d][:, g * 256:(g + 1) * 256], in1=B_st[:, 0:256], op=ALU.add)
            if full:
                P.act("activation", reads=[S[d]], writes=[Sb[d]], out=Sb[d][:, g * 256:(g + 1) * 256],
                      in_=S[d][:, g * 256:(g + 1) * 256], func=AF.Copy)
        if want_y:
            P.dma(y_d[d, c], ysb[:], reads=[ysb], writes=[ybuf])

    def zero_state(d):
        P.dve("memset", writes=[S[d]], ap=S[d][:], constant=0.0)
        P.dve("memset", writes=[Sb[d]], ap=Sb[d][:], constant=0.0)
        P.dve("memset", writes=[tot_acc[d]], ap=tot_acc[d][:], constant=0.0)

    order = {0: (list(range(8)), [8, 9]), 1: (list(range(7, -1, -1)), [9, 8])}
    for d in range(2):
        lat_order, ctx_order = order[d]
        if not full:
            zero_state(d)
            for c in ctx_order:
                unit(c, d, False)
            P.dma(cFo_d[d], S[d][:], reads=[S[d]])
            zero_state(d)
            for c in lat_order:
                unit(c, d, False)
            P.dma(F_d[d], S[d][:], reads=[S[d]])
            P.dma(T_d[d], tot_acc[d][:], reads=[tot_acc[d]])
        else:
            zero_state(d)
            for c in ctx_order:
                unit(c, d, True)
            P.dma(S[d][:], cF_d[d], writes=[S[d]])
            tl = P.sbuf([128, 7, 32], F32, f"tl{d}")
            P.dma(tl[:], Tl_d[d].rearrange("j p h -> p j h"), writes=[tl])
            P.act("activation", reads=[tl], writes=[tl], out=tl[:], in_=tl[:], func=AF.Exp)
            for j in range(7):
                P.dma(xs[:], Fl_d[d, j], writes=[xs])
                P.dve("tensor_tensor", reads=[S[d], tl], writes=[S[d]], out=S[d][:, :].rearrange("p (h q) -> p h q", h=32),
                      in0=S[d][:, :].rearrange("p (h q) -> p h q", h=32),
                      in1=tl[:, j, :].unsqueeze(2).to_broadcast([128, 32, 64]), op=ALU.mult)
                P.dve("tensor_tensor", reads=[S[d], xs], writes=[S[d]], out=S[d][:], in0=S[d][:], in1=xs[:], op=ALU.add)
            P.act("activation", reads=[S[d]], writes=[Sb[d]], out=Sb[d][:], in_=S[d][:], func=AF.Copy)
            for c in lat_order:
                unit(c, d, True)
    if full:
        P.barrier()
        gw = P.sbuf([128, 2048], F32, "gw")
        P.dma(gw[:], gw_d, writes=[gw])
        zt = P.sbuf([128, 2048], F32, "zt"); y2 = P.sbuf([128, 2048], F32, "y2")
        gss = P.sbuf([128, 8], F32, "gss"); go = P.sbuf([128, 2048], BF16, "go")
        for c in range(10):
            P.dma(xs[:], y_d[0, c], reads=[ybuf], writes=[xs])
            P.dma(y2[:], y_d[1, c], reads=[ybuf], writes=[y2])
            P.dma(zt[:], z_d[c], writes=[zt])
            P.act("activation", reads=[zt], writes=[zt], out=zt[:], in_=zt[:], func=AF.Silu)
            P.dve("tensor_tensor", reads=[xs, y2], writes=[xs], out=xs[:], in0=xs[:], in1=y2[:], op=ALU.add)
            P.dve("tensor_tensor", reads=[xs, zt], writes=[xs], out=xs[:], in0=xs[:], in1=zt[:], op=ALU.mult)
            P.act("activation", reads=[xs], writes=[y2], out=y2[:], in_=xs[:], func=AF.Square)
            P.dve("tensor_reduce", reads=[y2], writes=[gss], out=gss[:], in_=y2[:, :].rearrange("p (g q) -> p g q", g=8),
                  axis=AX.X, op=ALU.add)
            P.dve("tensor_scalar", reads=[gss], writes=[gss], out=gss[:], in0=gss[:], scalar1=1.0 / 256, scalar2=EPS,
                  op0=ALU.mult, op1=ALU.add)
            P.act("activation", reads=[gss], writes=[gss], out=gss[:], in_=gss[:], func=AF.Sqrt)
            P.dve("reciprocal", reads=[gss], writes=[gss], out=gss[:], in_=gss[:])
            P.dve("tensor_tensor", reads=[xs, gss], writes=[xs], out=xs[:, :].rearrange("p (g q) -> p g q", g=8),
                  in0=xs[:, :].rearrange("p (g q) -> p g q", g=8), in1=gss[:, :].unsqueeze(2).to_broadcast([128, 8, 256]),
                  op=ALU.mult)
            P.pool("tensor_tensor", reads=[xs, gw], writes=[go], out=go[:], in0=xs[:], in1=gw[:], op=ALU.mult)
            P.dma(g_d[c], go[:], reads=[go])
    return P.finish()


def _ssd_consts():
    t = np.arange(128)
    tri = np.stack([(t[:, None] <= t[None, :]), (t[:, None] >= t[None, :])], 1).astype(np.float32)
    valid = np.stack([(t[None, :] >= t[:, None]), (t[None, :] <= t[:, None])], 1)
    negmask = np.where(valid, 0.0, -30000.0).astype(np.float32)
    return np.ascontiguousarray(tri), np.ascontiguousarray(negmask), np.eye(128, dtype=np.float32)


def run_ssd(r1, dt_bias, a_log, d_skip, norm_w):
    tri, negmask, ident = _ssd_consts()
    par = np.stack([dt_bias.reshape(64), a_log.reshape(64), d_skip.reshape(64)], 0)
    par = np.ascontiguousarray(np.broadcast_to(par[None], (128, 3, 64))).astype(np.float32)
    base = []
    for i in range(NCORES):
        xbc = r1[i]["xbc"]
        xbc_t = np.ascontiguousarray(xbc.reshape(4096, T_ALL).T)
        xs = xbc_t[:, 0:2048].reshape(10, 128, 2048)
        btok = xbc_t[:, 2048:3072].reshape(10, 128, 1024)
        BT = xbc[16:24].reshape(8, 128, 10, 128).transpose(2, 1, 0, 3)
        CT = xbc[24:32].reshape(8, 128, 10, 128).transpose(2, 1, 0, 3)
        base.append(dict(xs=np.ascontiguousarray(xs), btok=np.ascontiguousarray(btok), BT=np.ascontiguousarray(BT),
                         CT=np.ascontiguousarray(CT), dt=r1[i]["dt"], par=par, tri=tri, negmask=negmask, ident=ident,
                         ones=ONES))
    ra = _run(_prog("ssdA", build_ssd, "A"), base)
    ctxF = ra[0]["ctxF"]
    gw = np.ascontiguousarray(np.broadcast_to(norm_w[None], (128, 2048))).astype(np.float32)
    insb = []
    for i in range(NCORES):
        Fl = np.zeros((2, 7, 128, 2048), np.float32)
        Tl = np.zeros((2, 7, 128, 32), np.float32)
        for j, cj in enumerate(range(0, i)):
            Fl[0, j] = ra[cj]["F"][0]; Tl[0, j] = ra[cj]["T"][0]
        for j, cj in enumerate(range(NCORES - 1, i, -1)):
            Fl[1, j] = ra[cj]["F"][1]; Tl[1, j] = ra[cj]["T"][1]
        d = dict(base[i]); d.update(Flist=Fl, Tlist=Tl, ctxF=ctxF, z=r1[i]["z"], gw=gw)
        insb.append(d)
    rb = _run(_prog("ssdB", build_ssd, "B"), insb)
    return ra, rb


def build_na():
    P = Prog()
    qT_d = P.dram_in("qT", [128, 8, T_ALL], BF16)
    kT_d = P.dram_in("kT", [128, 8, 2048], BF16)
    ve_d = P.dram_in("ve", [128, 16, 8, 128], BF16)
    vo_d = P.dram_in("vo", [128, 16, 8, 128], BF16)
    kcT_d = P.dram_in("kcT", [128, 8, 256], BF16)
    vc_d = P.dram_in("vc", [128, 2, 8, 128], BF16)
    tt_d = P.dram_in("tt", [128, 8, 8, 64])
    vm_d = P.dram_in("vm", [128, 16, 8])
    ones_d = P.dram_in("ones", [128, 128])
    o_d = P.dram_out("o", [128, 8, T_ALL], BF16)
    ones_f, ones_b = load_consts(P, ones_d)
    qT = P.sbuf([128, 8, T_ALL], BF16, "qT"); kT = P.sbuf([128, 8, 2048], BF16, "kT")
    ve = P.sbuf([128, 16, 8, 128], BF16, "ve"); vo = P.sbuf([128, 16, 8, 128], BF16, "vo")
    kcT = P.sbuf([128, 8, 256], BF16, "kcT"); vc = P.sbuf([128, 2, 8, 128], BF16, "vc")
    TT = P.sbuf([128, 8, 8, 64], F32, "TT"); vm = P.sbuf([128, 16, 8], F32, "vm")
    oT = P.sbuf([128, 8, T_ALL], BF16, "oT")
    for sb_, d_ in ((qT, qT_d), (kT, kT_d), (ve, ve_d), (vo, vo_d), (kcT, kcT_d), (vc, vc_d), (TT, tt_d), (vm, vm_d)):
        P.dma(sb_[:], d_, writes=[sb_])
    tb = Rot([P.sbuf([128, 512], F32, f"tb{i}") for i in range(2)])
    pw = Rot([P.sbuf([128, 512], BF16, f"pw{i}") for i in range(2)])
    pc = Rot([P.sbuf([128, 128], BF16, f"pc{i}") for i in range(2)])
    rec = Rot([P.sbuf([128, 128], F32, f"rec{i}") for i in range(2)])
    BA = Rot([P.bank(0), P.bank(1)]); BB = Rot([P.bank(2), P.bank(3)])
    BO = Rot([P.bank(4), P.bank(5)]); BD = Rot([P.bank(6), P.bank(7)])
    scale = 128 ** -0.5
    for lr in range(16):
        for h in range(8):
            q = qT[:, h, lr * 64:(lr + 1) * 64]
            A_ = BA.next(); B_ = BB.next(); O = BO.next(); Dn = BD.next()
            for pb in range(8):
                off = (lr + 2 * pb) * 64
                P.pe("matmul", reads=[kT, qT], writes=[A_], out=A_[:, pb * 64:(pb + 1) * 64], lhsT=kT[:, h, off:off + 128],
                     rhs=q, start=True, stop=True)
            for cb in range(2):
                P.pe("matmul", reads=[kcT, qT], writes=[B_], out=B_[:, cb * 64:(cb + 1) * 64],
                     lhsT=kcT[:, h, cb * 128:(cb + 1) * 128], rhs=q, start=True, stop=True)
            t_ = tb.next(); p_ = pw.next(); c_ = pc.next()
            P.dve("scalar_tensor_tensor", reads=[A_, TT], writes=[t_], out=t_[:, :], in0=A_[:, :], scalar=scale,
                  in1=TT[:, h, :, :].rearrange("p a b -> p (a b)"), op0=ALU.mult, op1=ALU.add)
            P.pool("tensor_tensor", reads=[t_, vm], writes=[t_], out=t_[:, :].rearrange("p (a b) -> p a b", a=8),
                   in0=t_[:, :].rearrange("p (a b) -> p a b", a=8),
                   in1=vm[:, lr, :].unsqueeze(2).to_broadcast([128, 8, 64]), op=ALU.add)
            P.act("activation", reads=[t_], writes=[p_], out=p_[:, :], in_=t_[:, :], func=AF.Exp)
            P.act("activation", reads=[B_], writes=[c_], out=c_[:, :], in_=B_[:, 0:128], func=AF.Exp, scale=scale)
            for pb in range(8):
                row = lr + 2 * pb
                vsrc = ve[:, row // 2, h, :] if row % 2 == 0 else vo[:, row // 2, h, :]
                vbuf = ve if row % 2 == 0 else vo
                P.pe("matmul", reads=[vbuf, p_], writes=[O], accum=(pb > 0), out=O[:, 0:64], lhsT=vsrc,
                     rhs=p_[:, pb * 64:(pb + 1) * 64], start=(pb == 0), stop=False)
            for cb in range(2):
                P.pe("matmul", reads=[vc, c_], writes=[O], accum=True, out=O[:, 0:64], lhsT=vc[:, cb, h, :],
                     rhs=c_[:, cb * 64:(cb + 1) * 64], start=False, stop=(cb == 1))
            for pb in range(8):
                P.pe("matmul", reads=[ones_b, p_], writes=[Dn], accum=(pb > 0), out=Dn[:, 0:64], lhsT=ones_b[:],
                     rhs=p_[:, pb * 64:(pb + 1) * 64], start=(pb == 0), stop=False)
            for cb in range(2):
                P.pe("matmul", reads=[ones_b, c_], writes=[Dn], accum=True, out=Dn[:, 0:64], lhsT=ones_b[:],
                     rhs=c_[:, cb * 64:(cb + 1) * 64], start=False, stop=(cb == 1))
            r_ = rec.next()
            P.dve("reciprocal", reads=[Dn], writes=[r_], out=r_[:, 0:64], in_=Dn[:, 0:64])
            P.dve("tensor_tensor", reads=[O, r_], writes=[oT], out=oT[:, h, lr * 64:(lr + 1) * 64], in0=O[:, 0:64],
                  in1=r_[:, 0:64], op=ALU.mult)
    for qb in range(2):
        for h in range(8):
            q = qT[:, h, T_LAT + qb * 128:T_LAT + (qb + 1) * 128]
            B_ = BB.next(); O = BO.next(); Dn = BD.next()
            for cb in range(2):
                P.pe("matmul", reads=[kcT, qT], writes=[B_], out=B_[:, cb * 128:(cb + 1) * 128],
                     lhsT=kcT[:, h, cb * 128:(cb + 1) * 128], rhs=q, start=True, stop=True)
            p_ = pw.next()
            P.act("activation", reads=[B_], writes=[p_], out=p_[:, 0:256], in_=B_[:, 0:256], func=AF.Exp, scale=scale)
            for cb in range(2):
                P.pe("matmul", reads=[vc, p_], writes=[O], accum=(cb > 0), out=O[:, 0:128], lhsT=vc[:, cb, h, :],
                     rhs=p_[:, cb * 128:(cb + 1) * 128], start=(cb == 0), stop=(cb == 1))
            for cb in range(2):
                P.pe("matmul", reads=[ones_b, p_], writes=[Dn], accum=(cb > 0), out=Dn[:, 0:128], lhsT=ones_b[:],
                     rhs=p_[:, cb * 128:(cb + 1) * 128], start=(cb == 0), stop=(cb == 1))
            r_ = rec.next()
            P.dve("reciprocal", reads=[Dn], writes=[r_], out=r_[:, 0:128], in_=Dn[:, 0:128])
            P.dve("tensor_tensor", reads=[O, r_], writes=[oT], out=oT[:, h, T_LAT + qb * 128:T_LAT + (qb + 1) * 128],
                  in0=O[:, 0:128], in1=r_[:, 0:128], op=ALU.mult)
    P.dma(o_d, oT[:], reads=[oT])
    return P.finish()


def run_na(r1, rpb):
    a = np.arange(64)
    c0 = np.clip(a - 8, 0, 48)
    b = np.arange(64)
    colok = (b[:, None] >= c0[None, :]) & (b[:, None] < c0[None, :] + 16)
    dc = np.clip(b[:, None] - a[None, :], -15, 15) + 15
    TT = np.full((2, 64, 8, 8, 64), -30000.0, np.float32)
    for pb in range(8):
        for jj in range(2):
            dr = 2 * pb + jj - 1
            if 0 <= dr < 15:
                vals = rpb[:, dr][:, dc]
                TT[jj, :, :, pb, :] = np.where(colok[None], vals, np.float32(-30000.0)).transpose(1, 0, 2)
    TT = np.ascontiguousarray(TT.reshape(128, 8, 8, 64))
    k_lat = np.concatenate([r1[i]["k"][0:8].reshape(T_LAT, 8, 128) for i in range(NCORES)], 0)
    v_lat = np.concatenate([r1[i]["v"][0:8].reshape(T_LAT, 8, 128) for i in range(NCORES)], 0)
    k_ctx = r1[0]["k"][8:10].reshape(T_CTX, 8, 128)
    v_ctx = r1[0]["v"][8:10].reshape(T_CTX, 8, 128)
    kcT = np.ascontiguousarray(k_ctx.transpose(2, 1, 0))
    vc = np.ascontiguousarray(v_ctx.reshape(2, 128, 8, 128).transpose(1, 0, 2, 3))
    zk = np.zeros((64, 8, 128), k_lat.dtype)
    ins = []
    for i in range(NCORES):
        base = 16 * i - 8
        kw = []; vw = []
        for v in range(33):
            row = base + v
            if 0 <= row < 128:
                kw.append(k_lat[row * 64:(row + 1) * 64]); vw.append(v_lat[row * 64:(row + 1) * 64])
            else:
                kw.append(zk); vw.append(zk)
        kwin = np.concatenate(kw[:32], 0)
        kT = np.ascontiguousarray(kwin.transpose(2, 1, 0))
        ve = np.stack([np.concatenate([vw[2 * j], vw[2 * j + 1]], 0) for j in range(16)], 1)
        vo = np.stack([np.concatenate([vw[2 * j + 1], vw[2 * j + 2]], 0) for j in range(16)], 1)
        vm = np.full((2, 64, 16, 8), -30000.0, np.float32)
        for lr in range(16):
            r = 16 * i + lr
            rs = min(max(r - 4, 0), 120)
            for pb in range(8):
                for jj in range(2):
                    krow = r - 8 + 2 * pb + jj
                    if rs <= krow < rs + 8:
                        vm[jj, :, lr, pb] = 0.0
        q = r1[i]["q"].reshape(T_ALL, 8, 128)
        ins.append(dict(qT=np.ascontiguousarray(q.transpose(2, 1, 0)), kT=kT, ve=np.ascontiguousarray(ve),
                        vo=np.ascontiguousarray(vo), kcT=kcT, vc=vc, tt=TT, vm=np.ascontiguousarray(vm.reshape(128, 16, 8)),
                        ones=ONES))
    return _run(_prog("na", build_na), ins)


def run_ab_layer(x, ctx, mod_l, nw1, w_in, conv_w, conv_b, dt_bias, a_log, d_skip, norm_w, q_norm, k_norm, rpb,
                 w_out, nw2, w1, w2):
    r1 = run_ab1(x, ctx, mod_l, nw1, w_in, conv_w, conv_b, q_norm, k_norm)
    ra, rb = run_ssd(r1, dt_bias, a_log, d_skip, norm_w)
    rn = run_na(r1, rpb)
    u_lat = []
    for i in range(NCORES):
        gT = fm(rb[i]["g"].reshape(T_ALL, 2048))
        u_lat.append(np.concatenate([gT[:, :, :T_LAT], rn[i]["o"][:, :, :T_LAT]], axis=1))
    gT0 = fm(rb[0]["g"].reshape(T_ALL, 2048))
    u_ctx = np.concatenate([gT0[:, :, T_LAT:], rn[0]["o"][:, :, T_LAT:]], axis=1)
    return run_post2(u_lat, u_ctx, x, ctx, w_out, w1, w2, mod_l, nw2)


def kernel(x, c, ctx, c_ctx, w_mod, b_mod, norm1_w, norm2_w, w_mlp_in, w_mlp_out,
           ab_w_in, ab_conv_w, ab_conv_b, ab_dt_bias, ab_a_log, ab_d_skip, ab_norm_w,
           ab_q_norm, ab_k_norm, ab_rpb, ab_w_out, c_w_qkv, c_q_norm, c_k_norm, c_w_out):
    f = lambda a: np.asarray(a, dtype=np.float32)
    xs = f(x)[0]
    cs = f(ctx)[0]
    mods = run_mod(f(c), f(c_ctx), f(w_mod), f(b_mod))
    for layer in range(4):
        i = layer // 2
        if layer % 2 == 0:
            xs, cs = run_ab_layer(xs, cs, mods[layer], f(norm1_w)[layer], f(ab_w_in)[i], f(ab_conv_w)[i],
                                  f(ab_conv_b)[i], f(ab_dt_bias)[i], f(ab_a_log)[i], f(ab_d_skip)[i],
                                  f(ab_norm_w)[i], f(ab_q_norm)[i], f(ab_k_norm)[i], f(ab_rpb)[i], f(ab_w_out)[i],
                                  f(norm2_w)[layer], f(w_mlp_in)[layer], f(w_mlp_out)[layer])
        else:
            xs, cs = run_c_layer(xs, cs, mods[layer], f(norm1_w)[layer], f(c_w_qkv)[i], f(c_q_norm)[i],
                                 f(c_k_norm)[i], f(c_w_out)[i], f(norm2_w)[layer], f(w_mlp_in)[layer],
                                 f(w_mlp_out)[layer])
    return np.ascontiguousarray(xs[None].astype(np.float32))


def build_post2(KM):
    KC = KM // 128
    HQ = 8
    P = Prog()
    uT_d = P.dram_in("uT", [128, KC, T_ALL], BF16)
    xT_d = P.dram_in("xT", [128, KD, T_ALL])
    wo_d = P.dram_in("wo", [16, 128, KC, 128])
    w1_d = P.dram_in("w1", [64, 128, KD, 128])
    w2_d = P.dram_in("w2", [8, 16, 128, HQ, 128])
    mod_d = P.dram_in("mod", [128, 6, KD, 2])
    nw_d = P.dram_in("nw", [128, KD])
    ones_d = P.dram_in("ones", [128, 128])
    out_d = P.dram_out("xo", [128, KD, T_ALL])

    ones_f, ones_b = load_consts(P, ones_d)
    mod_sb, A = mod_vectors(P, mod_d, nw_d, slot_sc=4)
    xs = [P.sbuf([128, KD, n], F32, f"xs{j}") for j, (st, n, w) in enumerate(TILES)]
    for j, (st, n, w) in enumerate(TILES):
        P.dma(xs[j][:], xT_d[:, :, st:st + n], writes=[xs[j]])
    m1 = P.mark()
    us = P.sbuf([128, KC, T_ALL], BF16, "us")
    P.dma(us[:], uT_d, writes=[us])
    ws = WStream(P, KC * 128, name="wsA", nstage=2, nbf=2)
    for ob in range(16):
        wb, wv = ws.load(wo_d[ob], KC, 128)
        for j, (st, n, w) in enumerate(TILES):
            ps = P.psum()
            for k in range(KC):
                P.pe("matmul", reads=[wb, us], writes=[ps], accum=(k > 0),
                     out=ps[:, :n], lhsT=wv[:, k, :], rhs=us[:, k, st:st + n], start=(k == 0), stop=(k == KC - 1))
            P.dve("scalar_tensor_tensor", reads=[ps, mod_sb, xs[j]], writes=[xs[j]],
                  out=xs[j][:, ob, :], in0=ps[:, :n], scalar=mod_sb[:, 2, ob, w:w + 1],
                  in1=xs[j][:, ob, :], op0=ALU.mult, op1=ALU.add)
    P.release(m1)
    hs = [P.sbuf([128, KD, n], BF16, f"hs{j}") for j, (st, n, w) in enumerate(TILES)]
    t1 = Rot([P.sbuf([128, 512], F32, f"t1_{i}") for i in range(2)])
    tmp = Rot([P.sbuf([128, 512], F32, f"tmp{i}") for i in range(3)])
    m2 = P.mark()
    sq = P.sbuf([128, KD, 512], BF16, "sq")
    for j, (st, n, w) in enumerate(TILES):
        norm_mod_tile(P, lambda k, j=j: xs[j][:, k, :], xs[j], lambda k, j=j: hs[j][:, k, :], hs[j], n, w, A, mod_sb, 3,
                      ones_b, sq.ap, sq, t1, tmp)
    P.release(m2)
    aq = [P.sbuf([128, HQ, n], BF16, f"aq{j}") for j, (st, n, w) in enumerate(TILES)]
    ws = WStream(P, KD * 128, name="wsB", nstage=2, nbf=2)
    for hq in range(64 // HQ):
        for hc in range(HQ):
            wb, wv = ws.load(w1_d[hq * HQ + hc], KD, 128)
            for j, (st, n, w) in enumerate(TILES):
                ps = P.psum()
                for k in range(KD):
                    P.pe("matmul", reads=[wb, hs[j]], writes=[ps], accum=(k > 0),
                         out=ps[:, :n], lhsT=wv[:, k, :], rhs=hs[j][:, k, :], start=(k == 0), stop=(k == KD - 1))
                r = tmp.next()
                P.act("activation", reads=[ps], writes=[r], out=r[:, :n], in_=ps[:, :n], func=AF.Relu)
                P.pool("tensor_tensor", reads=[r], writes=[aq[j]], out=aq[j][:, hc, :], in0=r[:, :n], in1=r[:, :n],
                       op=ALU.mult)
        for ob in range(16):
            wb, wv = ws.load(w2_d[hq, ob], HQ, 128)
            for j, (st, n, w) in enumerate(TILES):
                ps = P.psum()
                for k in range(HQ):
                    P.pe("matmul", reads=[wb, aq[j]], writes=[ps], accum=(k > 0),
                         out=ps[:, :n], lhsT=wv[:, k, :], rhs=aq[j][:, k, :], start=(k == 0), stop=(k == HQ - 1))
                P.dve("scalar_tensor_tensor", reads=[ps, mod_sb, xs[j]], writes=[xs[j]],
                      out=xs[j][:, ob, :], in0=ps[:, :n], scalar=mod_sb[:, 5, ob, w:w + 1], in1=xs[j][:, ob, :],
                      op0=ALU.mult, op1=ALU.add)
    for j, (st, n, w) in enumerate(TILES):
        P.dma(out_d[:, :, st:st + n], xs[j][:], reads=[xs[j]])
    return P.finish()


def run_post2(u_lat, u_ctx, x, ctx, w_out, w1, w2, mod_l, nw2):
    KM = w_out.shape[0]
    wo = wblocks(w_out, 128)
    w1b = wblocks(w1, 128)
    w2b = np.ascontiguousarray(w2.reshape(8, 8, 128, 16, 128).transpose(0, 3, 2, 1, 4))
    nw = vec_fm(nw2)
    xTs = xT_cores(x, ctx)
    ins = []
    for i in range(NCORES):
        uT = np.ascontiguousarray(np.concatenate([u_lat[i], u_ctx], axis=2))
        ins.append(dict(uT=uT, xT=xTs[i], wo=wo, w1=w1b, w2=w2b, mod=mod_l, nw=nw, ones=ONES))
    res = _run(_prog("post2", build_post2, KM), ins)
    outs = [unfm(res[i]["xo"]) for i in range(NCORES)]
    xn = np.concatenate([o[:T_LAT] for o in outs], axis=0)
    cn = outs[0][T_LAT:]
    return xn, cn
```

```python
import numpy as np
from contextlib import ExitStack

import concourse.bass as bass
import concourse.mybir as mybir
from concourse.bass_utils import run_bass_kernel_spmd

F32 = mybir.dt.float32
BF16 = mybir.dt.bfloat16
AF = mybir.ActivationFunctionType
ALU = mybir.AluOpType
AX = mybir.AxisListType

NCORES = 8


class Buf:
    __slots__ = ("ap", "w", "wd", "r", "rd", "name")

    def __init__(self, ap, name=""):
        self.ap = ap
        self.w = {}
        self.wd = []
        self.r = {}
        self.rd = []
        self.name = name

    def __getitem__(self, idx):
        return self.ap[idx]


class Prog:
    ENGS = ("pe", "act", "dve", "pool", "sp")
    DMA_SLOTS = 8

    def __init__(self):
        self.nc = bass.Bass("TRN2", target_bir_lowering=False)
        self.stack = ExitStack()
        self.ops = []
        self.n_by_eng = {e: 0 for e in self.ENGS}
        self._cnt = 0

    def dram_in(self, name, shape, dtype=F32):
        return self.nc.dram_tensor(name, list(shape), dtype, kind="ExternalInput").ap()

    def dram_out(self, name, shape, dtype=F32):
        return self.nc.dram_tensor(name, list(shape), dtype, kind="ExternalOutput").ap()

    ARENA_WORDS = 52736

    def _ensure_arena(self):
        if getattr(self, "arena", None) is None:
            self.arena = self.stack.enter_context(
                self.nc.sbuf_tensor("arena", [128, self.ARENA_WORDS], F32))
            self.top = 0
            self.banks = [Buf(self.stack.enter_context(self.nc.psum_tensor(f"bank{i}", [128, 512], F32)),
                              f"bank{i}") for i in range(8)]
            self.bank_i = 0
            self.last_by_eng = {}
            self.dma_since_barrier = []
            self.barrier_op = None
            self.after_barrier = set()

    def sbuf(self, shape, dtype=F32, name=None):
        self._ensure_arena()
        shape = list(shape)
        esz = 4 if dtype == F32 else 2
        n = 1
        for d in shape[1:]:
            n *= d
        words = (n * esz + 3) // 4
        words = (words + 7) // 8 * 8
        assert self.top + words <= self.ARENA_WORDS, f"SBUF arena overflow {name} {shape} top={self.top}"
        ap = self.arena[0:shape[0], self.top:self.top + words]
        self.top += words
        if dtype != F32:
            ap = ap.bitcast(dtype)
        ap = ap[:, 0:n]
        if len(shape) == 3:
            ap = ap.rearrange("p (a b) -> p a b", a=shape[1])
        elif len(shape) == 4:
            ap = ap.rearrange("p (a b c) -> p a b c", a=shape[1], b=shape[2])
        return Buf(ap, name or "")

    def psum(self, shape=None, dtype=F32, name=None):
        self._ensure_arena()
        b = self.banks[self.bank_i % 8]
        self.bank_i += 1
        return b

    def bank(self, i):
        self._ensure_arena()
        return self.banks[i]

    def mark(self):
        self._ensure_arena()
        return self.top

    def release(self, m):
        self.barrier()
        self.top = m

    def barrier(self):
        self._ensure_arena()
        if not hasattr(self, "_bar_buf"):
            self._bar_buf = self.sbuf([128, 8], F32, "barbuf")
        bb = self._bar_buf
        idx = self.op("dve", "memset", writes=[bb], ap=bb[:], constant=0.0)
        deps = self.ops[idx]["deps"]
        for e, last in self.last_by_eng.items():
            if last != idx:
                deps.add(last)
        deps.update(self.dma_since_barrier)
        deps.discard(idx)
        self.dma_since_barrier = []
        self.barrier_op = idx
        self.after_barrier = set()

    def op(self, eng, meth, reads=(), writes=(), dma=False, accum=False, **kw):
        fn = (meth, kw)
        idx = len(self.ops)
        deps = set()
        for b in reads:
            deps.update(b.w.values())
            deps.update(b.wd)
        for b in writes:
            has_readers = bool(b.r) or bool(b.rd)
            if has_readers:
                for e2, r in b.r.items():
                    if dma or e2 != eng:
                        deps.add(r)
                deps.update(b.rd)
            for e2, w in b.w.items():
                if dma or e2 != eng:
                    deps.add(w)
            if not dma:
                deps.update(b.wd)
        self._ensure_arena()
        if self.barrier_op is not None and eng not in self.after_barrier:
            deps.add(self.barrier_op)
            self.after_barrier.add(eng)
        deps.discard(idx)
        self.ops.append(dict(eng=eng, fn=fn, deps=deps, dma=dma))
        if dma:
            self.dma_since_barrier.append(idx)
        self.last_by_eng[eng] = idx
        for b in reads:
            if dma:
                b.rd.append(idx)
            else:
                b.r[eng] = idx
        for b in writes:
            had_readers = bool(b.r) or bool(b.rd)
            if dma:
                if had_readers:
                    b.w = {}
                    b.wd = []
                b.wd.append(idx)
            else:
                if had_readers:
                    b.w = {}
                b.wd = []
                b.w[eng] = idx
            b.r = {}
            b.rd = []
        return idx

    def pe(self, meth, reads=(), writes=(), accum=False, **kw):
        return self.op("pe", meth, reads, writes, accum=accum, **kw)

    def act(self, meth, reads=(), writes=(), **kw):
        return self.op("act", meth, reads, writes, **kw)

    def dve(self, meth, reads=(), writes=(), **kw):
        return self.op("dve", meth, reads, writes, **kw)

    def pool(self, meth, reads=(), writes=(), **kw):
        return self.op("pool", meth, reads, writes, **kw)

    def dma(self, out, in_, reads=(), writes=(), eng="sp", **kw):
        return self.op(eng, "dma_start", reads, writes, dma=True, out=out, in_=in_, **kw)

    def finish(self, final_wait_ops=None):
        nc = self.nc
        ops = self.ops
        n = len(ops)
        needed = [False] * n
        for i, o in enumerate(ops):
            for d in o["deps"]:
                od = ops[d]
                needed[d] = True
        if final_wait_ops is None:
            final_wait_ops = [i for i, o in enumerate(ops) if o["dma"]][-64:]
        for d in final_wait_ops:
            needed[d] = True
        sems = {e: self.stack.enter_context(nc.semaphore(f"s_{e}")) for e in self.ENGS}
        dma_sems = {e: [self.stack.enter_context(nc.semaphore(f"d_{e}{k}"))
                        for k in range(self.DMA_SLOTS)] for e in self.ENGS}
        sig = [None] * n
        cnt = {e: 0 for e in self.ENGS}
        dcnt = {e: 0 for e in self.ENGS}
        dslot_val = {e: [0] * self.DMA_SLOTS for e in self.ENGS}
        prev_slot_sig = [None] * n
        for i, o in enumerate(ops):
            e = o["eng"]
            if o["dma"]:
                k = dcnt[e] % self.DMA_SLOTS
                dcnt[e] += 1
                if dslot_val[e][k] > 0:
                    prev_slot_sig[i] = (dma_sems[e][k], dslot_val[e][k])
                dslot_val[e][k] += 16
                sig[i] = (dma_sems[e][k], dslot_val[e][k], 16)
            elif needed[i]:
                cnt[e] += 1
                sig[i] = (sems[e], cnt[e], 1)
        per_eng = {e: [] for e in self.ENGS}
        seen = {e: {} for e in self.ENGS}
        for i, o in enumerate(ops):
            e = o["eng"]
            waits = []
            want = {}
            for d in o["deps"]:
                s = sig[d]
                if s is None:
                    continue
                key = id(s[0])
                if key not in want or want[key][1] < s[1]:
                    want[key] = (s[0], s[1])
            if prev_slot_sig[i] is not None:
                s = prev_slot_sig[i]
                key = id(s[0])
                if key not in want or want[key][1] < s[1]:
                    want[key] = s
            for key, (sm, val) in want.items():
                if seen[e].get(key, 0) >= val:
                    continue
                seen[e][key] = val
                waits.append((sm, val))
            per_eng[e].append((waits, o["fn"], sig[i]))
        finals = [sig[d] for d in final_wait_ops]

        def run(engobj, lst, tail=None):
            for waits, fn, s in lst:
                for sm, val in waits:
                    engobj.wait_ge(sm, val)
                ins = getattr(engobj, fn[0])(**fn[1])
                if s is not None:
                    ins.then_inc(s[0], s[2])
            if tail:
                done = {}
                for sm, val, _ in tail:
                    done[id(sm)] = (sm, max(val, done.get(id(sm), (None, 0))[1]))
                for sm, val in done.values():
                    engobj.wait_ge(sm, val)

        with nc.Block() as block:
            @block.tensor
            def _(t):
                run(t, per_eng["pe"])

            @block.scalar
            def _(a):
                run(a, per_eng["act"])

            @block.vector
            def _(v):
                run(v, per_eng["dve"])

            @block.gpsimd
            def _(g):
                run(g, per_eng["pool"])

            @block.sync
            def _(s):
                run(s, per_eng["sp"], tail=finals)
        self.stack.close()
        return nc


D = 2048
KD = D // 128
T_LAT = 1024
T_CTX = 256
T_ALL = T_LAT + T_CTX
TILES = [(0, 512, 0), (512, 512, 0), (1024, 256, 1)]
EPS = 1e-6
HID = 8192


class Rot:
    def __init__(self, bufs):
        self.bufs = bufs
        self.i = 0

    def next(self):
        b = self.bufs[self.i % len(self.bufs)]
        self.i += 1
        return b


def load_consts(P, ones_dram):
    ones_f = P.sbuf([128, 128], F32, "ones_f")
    ones_b = P.sbuf([128, 128], BF16, "ones_b")
    P.dma(ones_f[:], ones_dram, writes=[ones_f])
    P.dve("tensor_copy", reads=[ones_f], writes=[ones_b], out=ones_b[:], in_=ones_f[:])
    return ones_f, ones_b


def mod_vectors(P, mod_dram, nw_dram, slot_sc):
    mod_sb = P.sbuf([128, 6, KD, 2], F32, "mod_sb")
    nw = P.sbuf([128, KD], F32, "nw")
    P.dma(mod_sb[:], mod_dram, writes=[mod_sb])
    P.dma(nw[:], nw_dram, writes=[nw])
    A = []
    for w in range(2):
        a = P.sbuf([128, KD], F32, f"modA{w}")
        P.dve("scalar_tensor_tensor", reads=[mod_sb, nw], writes=[a],
              out=a[:], in0=mod_sb[:, slot_sc, :, w], scalar=1.0, in1=nw[:], op0=ALU.add, op1=ALU.mult)
        A.append(a)
    return mod_sb, A


def norm_mod_tile(P, x_ap_fn, xbuf, h_ap_fn, hbuf, n, w, A, mod_sb, slot_sh, ones_b, sq_ap, sqbuf, t1, tmp):
    for k in range(KD):
        P.act("activation", reads=[xbuf], writes=[sqbuf], out=sq_ap[:, k, :n], in_=x_ap_fn(k), func=AF.Square)
    ps = P.psum()
    for k in range(KD):
        P.pe("matmul", reads=[ones_b, sqbuf], writes=[ps], accum=(k > 0),
             out=ps[:, :n], lhsT=ones_b[:], rhs=sq_ap[:, k, :n], start=(k == 0), stop=(k == KD - 1))
    r = t1.next()
    P.dve("tensor_scalar", reads=[ps], writes=[r], out=r[:, :n], in0=ps[:, :n], scalar1=1.0 / D, scalar2=EPS,
          op0=ALU.mult, op1=ALU.add)
    P.act("activation", reads=[r], writes=[r], out=r[:, :n], in_=r[:, :n], func=AF.Sqrt)
    P.dve("reciprocal", reads=[r], writes=[r], out=r[:, :n], in_=r[:, :n])
    for k in range(KD):
        t = tmp.next()
        P.dve("scalar_tensor_tensor", reads=[xbuf, A[w], r], writes=[t],
              out=t[:, :n], in0=x_ap_fn(k), scalar=A[w][:, k:k + 1], in1=r[:, :n], op0=ALU.mult, op1=ALU.mult)
        P.act("activation", reads=[t, mod_sb], writes=[hbuf],
              out=h_ap_fn(k), in_=t[:, :n], func=AF.Identity, bias=mod_sb[:, slot_sh, k, w:w + 1], scale=1.0)


class WStream:
    def __init__(self, P, words, cast_eng=("pool", "act"), name="w", nstage=2, nbf=2):
        self.P = P
        self.words = words
        self.stage = Rot([P.sbuf([128, words], F32, f"{name}_st{i}") for i in range(nstage)])
        self.wb = Rot([(P.sbuf([128, words], BF16, f"{name}_bf{i}"), Buf(None, "hA"), Buf(None, "hB"))
                       for i in range(nbf)])
        self.cast_eng = cast_eng

    def load(self, dram_ap, a, b):
        P = self.P
        s = self.stage.next(); wb, hA, hB = self.wb.next()
        n = a * b
        assert n <= self.words
        sv = s[:, 0:n].rearrange("p (a b) -> p a b", a=a)
        wv = wb[:, 0:n].rearrange("p (a b) -> p a b", a=a)
        P.dma(sv, dram_ap, writes=[s])
        h = a // 2
        for (lo, hi, eng, hb) in ((0, h, self.cast_eng[0], hA), (h, a, self.cast_eng[1], hB)):
            if eng == "act":
                P.act("activation", reads=[s], writes=[hb], out=wv[:, lo:hi, :], in_=sv[:, lo:hi, :], func=AF.Copy)
            else:
                P.op(eng, "tensor_copy", reads=[s], writes=[hb], out=wv[:, lo:hi, :], in_=sv[:, lo:hi, :])
        return [hA, hB], wv


def build_post(KM, NL=2):
    KC = KM // 128
    P = Prog()
    TP = NL * 512 + T_CTX
    tiles = [(j * 512, 512, 0) for j in range(NL)] + [(NL * 512, T_CTX, 1)]
    uT_d = P.dram_in("uT", [128, KC, TP], BF16)
    xT_d = P.dram_in("xT", [128, KD, TP])
    wo_d = P.dram_in("wo", [16, 128, KC, 128])
    w1_d = P.dram_in("w1", [64, 128, KD, 128])
    w2_d = P.dram_in("w2", [32, 128, 32, 128])
    mod_d = P.dram_in("mod", [128, 6, KD, 2])
    nw_d = P.dram_in("nw", [128, KD])
    ones_d = P.dram_in("ones", [128, 128])
    out_d = P.dram_out("xo", [128, KD, TP])

    ones_f, ones_b = load_consts(P, ones_d)
    mod_sb, A = mod_vectors(P, mod_d, nw_d, slot_sc=4)
    xs = P.sbuf([128, KD, 512], F32, "xs")
    us = P.sbuf([128, KC, 512], BF16, "us")
    hs = P.sbuf([128, KD, 512], BF16, "hs")
    aT = P.sbuf([128, 64, 512], BF16, "aT")
    t1 = Rot([P.sbuf([128, 512], F32, f"t1_{i}") for i in range(2)])
    tmp = Rot([P.sbuf([128, 512], F32, f"tmp{i}") for i in range(3)])
    osb = Rot([P.sbuf([128, 512], F32, f"osb{i}") for i in range(3)])
    ws = WStream(P, 4096, name="ws", nstage=1)

    for (st, n, w) in tiles:
        P.dma(xs[:, :, :n], xT_d[:, :, st:st + n], writes=[xs])
        P.dma(us[:, :, :n], uT_d[:, :, st:st + n], writes=[us])
        for ob in range(16):
            wb, wv = ws.load(wo_d[ob], KC, 128)
            ps = P.psum()
            for k in range(KC):
                P.pe("matmul", reads=[*wb, us], writes=[ps], accum=(k > 0),
                     out=ps[:, :n], lhsT=wv[:, k, :], rhs=us[:, k, :n], start=(k == 0), stop=(k == KC - 1))
            P.dve("scalar_tensor_tensor", reads=[ps, mod_sb, xs], writes=[xs],
                  out=xs[:, ob, :n], in0=ps[:, :n], scalar=mod_sb[:, 2, ob, w:w + 1],
                  in1=xs[:, ob, :n], op0=ALU.mult, op1=ALU.add)
        norm_mod_tile(P, lambda k: xs[:, k, :n], xs, lambda k: hs[:, k, :n], hs, n, w, A, mod_sb, 3,
                      ones_b, aT.ap, aT, t1, tmp)
        for hc in range(64):
            wb, wv = ws.load(w1_d[hc], KD, 128)
            ps = P.psum()
            for k in range(KD):
                P.pe("matmul", reads=[*wb, hs], writes=[ps], accum=(k > 0),
                     out=ps[:, :n], lhsT=wv[:, k, :], rhs=hs[:, k, :n], start=(k == 0), stop=(k == KD - 1))
            r = tmp.next()
            P.act("activation", reads=[ps], writes=[r], out=r[:, :n], in_=ps[:, :n], func=AF.Relu)
            P.dve("tensor_tensor", reads=[r], writes=[aT], out=aT[:, hc, :n], in0=r[:, :n], in1=r[:, :n],
                  op=ALU.mult)
        for ob in range(16):
            ps = P.psum()
            for half in range(2):
                wb, wv = ws.load(w2_d[ob * 2 + half], 32, 128)
                for k in range(32):
                    kk = half * 32 + k
                    P.pe("matmul", reads=[*wb, aT], writes=[ps], accum=(kk > 0),
                         out=ps[:, :n], lhsT=wv[:, k, :], rhs=aT[:, kk, :n], start=(kk == 0), stop=(kk == 63))
            o = osb.next()
            P.dve("scalar_tensor_tensor", reads=[ps, mod_sb, xs], writes=[o],
                  out=o[:, :n], in0=ps[:, :n], scalar=mod_sb[:, 5, ob, w:w + 1], in1=xs[:, ob, :n],
                  op0=ALU.mult, op1=ALU.add)
            P.dma(out_d[:, ob, st:st + n], o[:, :n], reads=[o])
    return P.finish()


def build_mod():
    P = Prog()
    cv_d = P.dram_in("cv", [128, KD, 2])
    w_d = P.dram_in("w", [12, 128, KD, 512])
    b_d = P.dram_in("b", [128, 48])
    out_d = P.dram_out("mo", [128, 48, 2])
    cv = P.sbuf([128, KD, 2], F32, "cv")
    sg = P.sbuf([128, KD, 2], F32, "sg")
    bs = P.sbuf([128, 48], F32, "bs")
    ob = P.sbuf([128, 48, 2], F32, "ob")
    P.dma(cv[:], cv_d, writes=[cv])
    P.dma(bs[:], b_d, writes=[bs])
    P.act("activation", reads=[cv], writes=[sg], out=sg[:], in_=cv[:], func=AF.Sigmoid)
    P.dve("tensor_tensor", reads=[cv, sg], writes=[sg], out=sg[:], in0=sg[:], in1=cv[:], op=ALU.mult)
    wst = Rot([P.sbuf([128, KD, 512], F32, f"wst{i}") for i in range(3)])
    for blk in range(12):
        wv = wst.next()
        P.dma(wv[:], w_d[blk], writes=[wv])
        for sub in range(4):
            cb = blk * 4 + sub
            ps = P.psum()
            for k in range(KD):
                P.pe("matmul", reads=[wv, sg], writes=[ps], accum=(k > 0),
                     out=ps[:, 0:2], lhsT=wv[:, k, sub * 128:(sub + 1) * 128], rhs=sg[:, k, :],
                     start=(k == 0), stop=(k == KD - 1))
            P.dve("tensor_scalar", reads=[ps, bs], writes=[ob], out=ob[:, cb, :], in0=ps[:, 0:2],
                  scalar1=bs[:, cb:cb + 1], scalar2=None, op0=ALU.add)
    P.dma(out_d, ob[:], reads=[ob])
    return P.finish()


def qk_norm_tile(P, ps, nh, w_ap, wbc, out_ap, outbuf, scr, rope=None):
    n = nh * 128
    sq = scr["sq"].next(); ss = scr["ss"].next(); xn = scr["xn"].next()
    P.act("activation", reads=[ps], writes=[sq], out=sq[:, :n], in_=ps[:, :n], func=AF.Square)
    P.dve("tensor_reduce", reads=[sq], writes=[ss], out=ss[:, :nh],
          in_=sq[:, :n].rearrange("p (h d) -> p h d", h=nh), axis=AX.X, op=ALU.add)
    P.dve("tensor_scalar", reads=[ss], writes=[ss], out=ss[:, :nh], in0=ss[:, :nh], scalar1=1.0 / 128, scalar2=EPS,
          op0=ALU.mult, op1=ALU.add)
    P.act("activation", reads=[ss], writes=[ss], out=ss[:, :nh], in_=ss[:, :nh], func=AF.Sqrt)
    P.dve("reciprocal", reads=[ss], writes=[ss], out=ss[:, :nh], in_=ss[:, :nh])
    P.dve("tensor_tensor", reads=[ps, ss], writes=[xn],
          out=xn[:, :n].rearrange("p (h d) -> p h d", h=nh), in0=ps[:, :n].rearrange("p (h d) -> p h d", h=nh),
          in1=ss[:, :nh].unsqueeze(2).to_broadcast([128, nh, 128]), op=ALU.mult)
    if rope is None:
        P.pool("tensor_tensor", reads=[xn, wbc], writes=[outbuf], out=out_ap, in0=xn[:, :n], in1=w_ap,
               op=ALU.mult)
        return
    cos_ap, sin_ap, rbuf = rope
    P.pool("tensor_tensor", reads=[xn, wbc], writes=[xn], out=xn[:, :n], in0=xn[:, :n], in1=w_ap, op=ALU.mult)
    xv = xn[:, :n].rearrange("p (h i two) -> p h i two", h=nh, two=2)
    ov = out_ap.rearrange("p (h i two) -> p h i two", h=nh, two=2)
    u1 = xv[:, :, :, 0]; u2 = xv[:, :, :, 1]
    cb = cos_ap.unsqueeze(1).to_broadcast([128, nh, 64]); sb = sin_ap.unsqueeze(1).to_broadcast([128, nh, 64])
    ta = scr["ra"].next(); tb = scr["rb"].next()
    tav = ta[:, :nh * 64].rearrange("p (h i) -> p h i", h=nh); tbv = tb[:, :nh * 64].rearrange("p (h i) -> p h i", h=nh)
    P.dve("tensor_tensor", reads=[xn, rbuf], writes=[ta], out=tav, in0=u1, in1=cb, op=ALU.mult)
    P.pool("tensor_tensor", reads=[xn, rbuf], writes=[tb], out=tbv, in0=u2, in1=sb, op=ALU.mult)
    P.dve("tensor_tensor", reads=[ta, tb], writes=[outbuf], out=ov[:, :, :, 0], in0=tav, in1=tbv, op=ALU.subtract)
    ta2 = scr["ra"].next(); tb2 = scr["rb"].next()
    ta2v = ta2[:, :nh * 64].rearrange("p (h i) -> p h i", h=nh); tb2v = tb2[:, :nh * 64].rearrange("p (h i) -> p h i", h=nh)
    P.dve("tensor_tensor", reads=[xn, rbuf], writes=[ta2], out=ta2v, in0=u1, in1=sb, op=ALU.mult)
    P.pool("tensor_tensor", reads=[xn, rbuf], writes=[tb2], out=tb2v, in0=u2, in1=cb, op=ALU.mult)
    P.dve("tensor_tensor", reads=[ta2, tb2], writes=[outbuf], out=ov[:, :, :, 1], in0=ta2v, in1=tb2v, op=ALU.add)


def norm1_all(P, xT_d, mod_d, nw_d, ones_b, hT, ntok_tiles):
    mod_sb, A = mod_vectors(P, mod_d, nw_d, slot_sc=1)
    m = P.mark()
    xs = Rot([P.sbuf([128, KD, 512], F32, f"xs{i}") for i in range(1)])
    sq = P.sbuf([128, KD, 512], BF16, "sqn")
    t1 = Rot([P.sbuf([128, 512], F32, f"t1_{i}") for i in range(2)])
    tmp = Rot([P.sbuf([128, 512], F32, f"tmp{i}") for i in range(3)])
    for (st, n, w) in ntok_tiles:
        x = xs.next()
        P.dma(x[:, :, :n], xT_d[:, :, st:st + n], writes=[x])
        norm_mod_tile(P, lambda k: x[:, k, :n], x, lambda k: hT[:, k, st:st + n], hT, n, w, A, mod_sb, 0,
                      ones_b, sq.ap, sq, t1, tmp)
    P.release(m)
    return mod_sb


def build_c1():
    P = Prog()
    xT_d = P.dram_in("xT", [128, KD, T_ALL])
    mod_d = P.dram_in("mod", [128, 6, KD, 2])
    nw_d = P.dram_in("nw", [128, KD])
    ones_d = P.dram_in("ones", [128, 128])
    w_d = P.dram_in("w", [6, 128, KD, 512])
    qkw_d = P.dram_in("qkw", [128, 2, 512])
    cs_d = P.dram_in("cs", [128, 8, 2, 64])
    q_d = P.dram_out("q", [10, 128, 2048], BF16)
    k_d = P.dram_out("k", [10, 128, 512], BF16)
    v_d = P.dram_out("v", [10, 128, 512], BF16)

    ones_f, ones_b = load_consts(P, ones_d)
    hT = P.sbuf([128, KD, T_ALL], BF16, "hT")
    norm1_all(P, xT_d, mod_d, nw_d, ones_b, hT, TILES)
    qkw = P.sbuf([128, 2, 512], F32, "qkw")
    cs = P.sbuf([128, 8, 2, 64], F32, "cs")
    P.dma(qkw[:], qkw_d, writes=[qkw])
    P.dma(cs[:], cs_d, writes=[cs])
    scr = dict(sq=Rot([P.sbuf([128, 512], F32, f"sq{i}") for i in range(2)]),
               ss=Rot([P.sbuf([128, 8], F32, f"ss{i}") for i in range(2)]),
               xn=Rot([P.sbuf([128, 512], F32, f"xn{i}") for i in range(2)]),
               ra=Rot([P.sbuf([128, 256], F32, f"ra{i}") for i in range(2)]),
               rb=Rot([P.sbuf([128, 256], F32, f"rb{i}") for i in range(2)]))
    ob = Rot([P.sbuf([128, 512], BF16, f"ob{i}") for i in range(3)])
    ws = WStream(P, KD * 512, name="ws", nstage=2, nbf=2)
    for cbk in range(6):
        wb, wv = ws.load(w_d[cbk], KD, 512)
        for tt in range(10):
            ps = P.psum()
            for k in range(KD):
                P.pe("matmul", reads=[hT, *wb], writes=[ps], accum=(k > 0),
                     out=ps[:, :], lhsT=hT[:, k, tt * 128:(tt + 1) * 128], rhs=wv[:, k, :],
                     start=(k == 0), stop=(k == KD - 1))
            o = ob.next()
            if cbk < 5:
                rope = (cs[:, tt, 0, :], cs[:, tt, 1, :], cs) if tt < 8 else None
                qk_norm_tile(P, ps, 4, qkw[:, 0 if cbk < 4 else 1, :], qkw, o[:, :], o, scr, rope=rope)
                dst = q_d[tt, :, cbk * 512:(cbk + 1) * 512] if cbk < 4 else k_d[tt]
            else:
                P.act("activation", reads=[ps], writes=[o], out=o[:, :], in_=ps[:, :], func=AF.Copy)
                dst = v_d[tt]
            P.dma(dst, o[:, :], reads=[o])
    return P.finish()


NKB = 66


def build_c2():
    P = Prog()
    qT_d = P.dram_in("qT", [128, 16, T_ALL], BF16)
    kT_d = P.dram_in("kT", [128, 4, NKB * 128], BF16)
    v_d = P.dram_in("v", [128, NKB, 4, 128], BF16)
    ones_d = P.dram_in("ones", [128, 128])
    o_d = P.dram_out("o", [128, 16, T_ALL], BF16)
    ones_f, ones_b = load_consts(P, ones_d)
    qT = P.sbuf([128, 16, T_ALL], BF16, "qT")
    kT = [P.sbuf([128, NKB * 128], BF16, f"kT{g}") for g in range(4)]
    vs = [P.sbuf([128, NKB, 128], BF16, f"v{g}") for g in range(4)]
    P.dma(qT[:], qT_d, writes=[qT])
    for g in range(4):
        P.dma(kT[g][:], kT_d[:, g, :], writes=[kT[g]])
        P.dma(vs[g][:], v_d[:, :, g, :], writes=[vs[g]])
    pT = Rot([P.sbuf([128, 512], BF16, f"pT{i}") for i in range(3)])
    rec = Rot([P.sbuf([128, 512], F32, f"rec{i}") for i in range(2)])
    ob = Rot([P.sbuf([128, 512], BF16, f"ob{i}") for i in range(2)])
    sbanks = Rot([P.bank(i) for i in range(3)])
    obanks = Rot([P.bank(3), P.bank(4)])
    dbanks = Rot([P.bank(5), P.bank(6)])
    scale = 128 ** -0.5
    for qb in range(10):
        nkb = NKB if qb < 8 else 2
        for g in range(4):
            qg = qT[:, 4 * g:4 * g + 4, qb * 128:(qb + 1) * 128]
            O = obanks.next(); Dn = dbanks.next()

            def smm(kb_):
                S_ = sbanks.next()
                P.pe("matmul", reads=[kT[g], qT], writes=[S_],
                     out=S_[:, :].rearrange("p (h q) -> p h q", h=4), lhsT=kT[g][:, kb_ * 128:(kb_ + 1) * 128], rhs=qg,
                     start=True, stop=True)
                return S_
            S = smm(0)
            for kb in range(nkb):
                Sn = smm(kb + 1) if kb + 1 < nkb else None
                p = pT.next()
                P.act("activation", reads=[S], writes=[p], out=p[:, :], in_=S[:, :], func=AF.Exp, scale=scale)
                P.pe("matmul", reads=[vs[g], p], writes=[O], accum=(kb > 0),
                     out=O[:, :], lhsT=vs[g][:, kb, :], rhs=p[:, :], start=(kb == 0), stop=(kb == nkb - 1))
                P.pe("matmul", reads=[ones_b, p], writes=[Dn], accum=(kb > 0),
                     out=Dn[:, :], lhsT=ones_b[:], rhs=p[:, :], start=(kb == 0), stop=(kb == nkb - 1))
                S = Sn
            r = rec.next(); o = ob.next()
            P.dve("reciprocal", reads=[Dn], writes=[r], out=r[:, :], in_=Dn[:, :])
            P.dve("tensor_tensor", reads=[O, r], writes=[o], out=o[:, :], in0=O[:, :], in1=r[:, :], op=ALU.mult)
            P.dma(o_d[:, 4 * g:4 * g + 4, qb * 128:(qb + 1) * 128], o[:, :].rearrange("p (h q) -> p h q", h=4),
                  reads=[o])
    return P.finish()


_PROGS = {}


def _prog(name, builder, *args):
    key = (name,) + args
    if key not in _PROGS:
        _PROGS[key] = builder(*args)
    return _PROGS[key]


def _run(nc, in_maps):
    res = run_bass_kernel_spmd(nc, in_maps, core_ids=list(range(NCORES)))
    return res.results


def fm(a):
    Tn, F = a.shape
    return np.ascontiguousarray(a.T.reshape(F // 128, 128, Tn).transpose(1, 0, 2))


def unfm(a):
    p, C, Tn = a.shape
    return np.ascontiguousarray(a.transpose(2, 1, 0).reshape(Tn, C * 128))


def wblocks(w, colblk):
    K, N = w.shape
    return np.ascontiguousarray(w.reshape(K // 128, 128, N // colblk, colblk).transpose(2, 1, 0, 3))


ONES = np.ones((128, 128), np.float32)


def run_mod(c, c_ctx, w_mod, b_mod):
    cv = np.stack([c.reshape(D), c_ctx.reshape(D)], axis=-1)
    cv = np.ascontiguousarray(cv.reshape(KD, 128, 2).transpose(1, 0, 2))
    ins = []
    for i in range(NCORES):
        l, half = i // 2, i % 2
        w = w_mod[l][:, half * 6144:(half + 1) * 6144]
        b = b_mod[l][half * 6144:(half + 1) * 6144]
        ins.append(dict(cv=cv, w=wblocks(w, 512), b=np.ascontiguousarray(b.reshape(48, 128).T)))
    res = _run(_prog("mod", build_mod), ins)
    modall = np.zeros((4, 6 * D, 2), np.float32)
    for i in range(NCORES):
        l, half = i // 2, i % 2
        mo = res[i]["mo"]
        modall[l, half * 6144:(half + 1) * 6144] = mo.transpose(1, 0, 2).reshape(6144, 2)
    return [np.ascontiguousarray(modall[l].reshape(6, KD, 128, 2).transpose(2, 0, 1, 3)) for l in range(4)]


def vec_fm(v):
    return np.ascontiguousarray(v.reshape(KD, 128).T)


def xT_cores(x, ctx):
    return [fm(np.concatenate([x[i * T_LAT:(i + 1) * T_LAT], ctx], axis=0)) for i in range(NCORES)]


def rope_tables():
    t = np.arange(8192)
    half = 64
    inv = (1.0 / (10000.0 ** (np.arange(0, half, 2, dtype=np.float32) / half))).astype(np.float32)
    ang = np.concatenate([(t // 64).astype(np.float32)[:, None] * inv, (t % 64).astype(np.float32)[:, None] * inv], -1)
    return np.cos(ang).astype(np.float32), np.sin(ang).astype(np.float32)


NCP = 4
NLP = 8192 // NCP // 512


def run_post(u_lat, u_ctx, x, ctx, w_out, w1, w2, mod_l, nw2):
    KM = w_out.shape[0]
    wo = wblocks(w_out, 128)
    w1b = wblocks(w1, 128)
    w2b = np.ascontiguousarray(w2.reshape(2, 32, 128, 16, 128).transpose(3, 0, 2, 1, 4).reshape(32, 128, 32, 128))
    nw = vec_fm(nw2)
    per = NCORES // NCP
    tl = 8192 // NCP
    ins = []
    for p in range(NCP):
        uT = np.concatenate([u_lat[p * per + j] for j in range(per)] + [u_ctx], axis=2)
        xT = fm(np.concatenate([x[p * tl:(p + 1) * tl], ctx], axis=0))
        ins.append(dict(uT=np.ascontiguousarray(uT), xT=xT, wo=wo, w1=w1b, w2=w2b, mod=mod_l, nw=nw, ones=ONES))
    res = run_bass_kernel_spmd(_prog("post", build_post, KM, NLP), ins, core_ids=list(range(NCP))).results
    outs = [unfm(res[p]["xo"]) for p in range(NCP)]
    xn = np.concatenate([o[:tl] for o in outs], axis=0)
    cn = outs[0][tl:]
    return xn, cn


def run_c_layer(x, ctx, mod_l, nw1, w_qkv, q_norm, k_norm, w_out, nw2, w1, w2):
    xTs = xT_cores(x, ctx)
    cos, sin = rope_tables()
    qkw = np.stack([np.tile(q_norm, 4), np.tile(k_norm, 4)], 0)
    qkw = np.ascontiguousarray(np.broadcast_to(qkw[None], (128, 2, 512))).astype(np.float32)
    wb = wblocks(w_qkv, 512)
    nw = vec_fm(nw1)
    ins = []
    for i in range(NCORES):
        cs = np.stack([cos[i * T_LAT:(i + 1) * T_LAT], sin[i * T_LAT:(i + 1) * T_LAT]], 1)
        cs = np.ascontiguousarray(cs.reshape(8, 128, 2, 64).transpose(1, 0, 2, 3))
        ins.append(dict(xT=xTs[i], mod=mod_l, nw=nw, ones=ONES, w=wb, qkw=qkw, cs=cs))
    r1 = _run(_prog("c1", build_c1), ins)
    k_ctx = r1[0]["k"][8:10].reshape(256, 4, 128)
    v_ctx = r1[0]["v"][8:10].reshape(256, 4, 128)
    k_all = np.concatenate([k_ctx] + [r1[i]["k"][0:8].reshape(1024, 4, 128) for i in range(NCORES)], 0)
    v_all = np.concatenate([v_ctx] + [r1[i]["v"][0:8].reshape(1024, 4, 128) for i in range(NCORES)], 0)
    kT = np.ascontiguousarray(k_all.transpose(2, 1, 0))
    vv = np.ascontiguousarray(v_all.reshape(NKB, 128, 4, 128).transpose(1, 0, 2, 3))
    ins2 = []
    for i in range(NCORES):
        q = r1[i]["q"].reshape(T_ALL, 16, 128)
        ins2.append(dict(qT=np.ascontiguousarray(q.transpose(2, 1, 0)), kT=kT, v=vv, ones=ONES))
    r2 = _run(_prog("c2", build_c2), ins2)
    u_lat = [r2[i]["o"][:, :, :T_LAT] for i in range(NCORES)]
    u_ctx = r2[0]["o"][:, :, T_LAT:]
    return run_post2(u_lat, u_ctx, x, ctx, w_out, w1, w2, mod_l, nw2)


T_AB = T_ALL + 4
TILES_AB = TILES + [(T_ALL, 4, 0)]


def build_ab1():
    P = Prog()
    xT_d = P.dram_in("xT", [128, KD, T_AB])
    mod_d = P.dram_in("mod", [128, 6, KD, 2])
    nw_d = P.dram_in("nw", [128, KD])
    ones_d = P.dram_in("ones", [128, 128])
    wfm_d = P.dram_in("wfm", [32, 128, KD, 128])
    wtm_d = P.dram_in("wtm", [10, 128, KD, 512])
    wdt_d = P.dram_in("wdt", [128, KD, 64])
    cw_d = P.dram_in("cw", [128, 32, 5])
    cb_d = P.dram_in("cb", [128, 32])
    edge_d = P.dram_in("edge", [128, 2])
    qkw_d = P.dram_in("qkw", [128, 2, 512])
    xbc_d = P.dram_out("xbc", [32, 128, T_ALL])
    z_d = P.dram_out("z", [10, 128, 2048])
    q_d = P.dram_out("q", [10, 128, 1024], BF16)
    k_d = P.dram_out("k", [10, 128, 1024], BF16)
    v_d = P.dram_out("v", [10, 128, 1024], BF16)
    dt_d = P.dram_out("dt", [10, 128, 64])

    ones_f, ones_b = load_consts(P, ones_d)
    hT = P.sbuf([128, KD, T_AB], BF16, "hT")
    norm1_all(P, xT_d, mod_d, nw_d, ones_b, hT, TILES_AB)
    cw = P.sbuf([128, 32, 5], F32, "cw"); cbias = P.sbuf([128, 32], F32, "cbias")
    edge = P.sbuf([128, 2], F32, "edge"); qkw = P.sbuf([128, 2, 512], F32, "qkw")
    P.dma(cw[:], cw_d, writes=[cw]); P.dma(cbias[:], cb_d, writes=[cbias])
    P.dma(edge[:], edge_d, writes=[edge]); P.dma(qkw[:], qkw_d, writes=[qkw])
    ws = WStream(P, KD * 512, name="ws", nstage=2, nbf=2)
    Ul = Rot([P.sbuf([128, T_LAT + 4], F32, f"Ul{i}") for i in range(2)])
    Uc = Rot([P.sbuf([128, T_CTX + 4], F32, f"Uc{i}") for i in range(2)])
    for u in Uc.bufs:
        P.dve("memset", writes=[u], ap=u[:], constant=0.0)
    accl = Rot([P.sbuf([128, T_LAT], F32, f"accl{i}") for i in range(2)])
    accc = Rot([P.sbuf([128, T_CTX], F32, f"accc{i}") for i in range(2)])
    for cb in range(32):
        wb, wv = ws.load(wfm_d[cb], KD, 128)
        ul = Ul.next(); uc = Uc.next()
        for (st, n, w) in TILES_AB:
            ps = P.psum()
            for k in range(KD):
                P.pe("matmul", reads=[*wb, hT], writes=[ps], accum=(k > 0),
                     out=ps[:, :n], lhsT=wv[:, k, :], rhs=hT[:, k, st:st + n], start=(k == 0), stop=(k == KD - 1))
            if st < T_LAT:
                P.act("activation", reads=[ps], writes=[ul], out=ul[:, 2 + st:2 + st + n], in_=ps[:, :n], func=AF.Copy)
            elif st == T_LAT:
                P.act("activation", reads=[ps], writes=[uc], out=uc[:, 2:2 + n], in_=ps[:, :n], func=AF.Copy)
            else:
                P.dve("tensor_scalar", reads=[ps, edge], writes=[ul], out=ul[:, 0:2], in0=ps[:, 0:2],
                      scalar1=edge[:, 0:1], scalar2=None, op0=ALU.mult)
                P.dve("tensor_scalar", reads=[ps, edge], writes=[ul], out=ul[:, T_LAT + 2:T_LAT + 4], in0=ps[:, 2:4],
                      scalar1=edge[:, 1:2], scalar2=None, op0=ALU.mult)
        for (U, acc, N, off) in ((ul, accl.next(), T_LAT, 0), (uc, accc.next(), T_CTX, T_LAT)):
            P.dve("tensor_scalar", reads=[U, cw, cbias], writes=[acc], out=acc[:, :], in0=U[:, 0:N],
                  scalar1=cw[:, cb, 0:1], scalar2=cbias[:, cb:cb + 1], op0=ALU.mult, op1=ALU.add)
            for kk in range(1, 5):
                P.dve("scalar_tensor_tensor", reads=[U, cw, acc], writes=[acc], out=acc[:, :], in0=U[:, kk:kk + N],
                      scalar=cw[:, cb, kk:kk + 1], in1=acc[:, :], op0=ALU.mult, op1=ALU.add)
            P.act("activation", reads=[acc], writes=[acc], out=acc[:, :], in_=acc[:, :], func=AF.Silu)
            P.dma(xbc_d[cb, :, off:off + N], acc[:, :], reads=[acc])
    scr = dict(sq=Rot([P.sbuf([128, 512], F32, f"sq{i}") for i in range(2)]),
               ss=Rot([P.sbuf([128, 8], F32, f"ss{i}") for i in range(2)]),
               xn=Rot([P.sbuf([128, 512], F32, f"xn{i}") for i in range(2)]))
    of = Rot([P.sbuf([128, 512], F32, f"of{i}") for i in range(2)])
    ob = Rot([P.sbuf([128, 512], BF16, f"ob{i}") for i in range(3)])
    for cbk in range(10):
        wb, wv = ws.load(wtm_d[cbk], KD, 512)
        for tt in range(10):
            ps = P.psum()
            for k in range(KD):
                P.pe("matmul", reads=[hT, *wb], writes=[ps], accum=(k > 0),
                     out=ps[:, :], lhsT=hT[:, k, tt * 128:(tt + 1) * 128], rhs=wv[:, k, :],
                     start=(k == 0), stop=(k == KD - 1))
            if cbk < 4:
                o = of.next()
                P.act("activation", reads=[ps], writes=[o], out=o[:, :], in_=ps[:, :], func=AF.Copy)
                P.dma(z_d[tt, :, cbk * 512:(cbk + 1) * 512], o[:, :], reads=[o])
            elif cbk < 8:
                o = ob.next()
                wi = 0 if cbk < 6 else 1
                qk_norm_tile(P, ps, 4, qkw[:, wi, :], qkw, o[:, :], o, scr, rope=None)
                dst = (q_d if cbk < 6 else k_d)[tt, :, (cbk % 2) * 512:(cbk % 2 + 1) * 512]
                P.dma(dst, o[:, :], reads=[o])
            else:
                o = ob.next()
                P.act("activation", reads=[ps], writes=[o], out=o[:, :], in_=ps[:, :], func=AF.Copy)
                P.dma(v_d[tt, :, (cbk % 2) * 512:(cbk % 2 + 1) * 512], o[:, :], reads=[o])
    wb, wv = ws.load(wdt_d, KD, 64)
    for tt in range(10):
        ps = P.psum()
        for k in range(KD):
            P.pe("matmul", reads=[hT, *wb], writes=[ps], accum=(k > 0),
                 out=ps[:, :64], lhsT=hT[:, k, tt * 128:(tt + 1) * 128], rhs=wv[:, k, :],
                 start=(k == 0), stop=(k == KD - 1))
        o = of.next()
        P.act("activation", reads=[ps], writes=[o], out=o[:, :64], in_=ps[:, :64], func=AF.Copy)
        P.dma(dt_d[tt], o[:, :64], reads=[o])
    return P.finish()


def run_ab1(x, ctx, mod_l, nw1, w_in, conv_w, conv_b, q_norm, k_norm):
    wfm = wblocks(w_in[:, 2048:6144], 128)
    wtm = wblocks(np.concatenate([w_in[:, 0:2048], w_in[:, 6208:9280]], axis=1), 512)
    wdt = np.ascontiguousarray(w_in[:, 6144:6208].reshape(KD, 128, 64).transpose(1, 0, 2))
    cw = np.ascontiguousarray(conv_w.reshape(5, 32, 128).transpose(2, 1, 0))
    cb = np.ascontiguousarray(conv_b.reshape(32, 128).T)
    qkw = np.stack([np.tile(q_norm, 4), np.tile(k_norm, 4)], 0)
    qkw = np.ascontiguousarray(np.broadcast_to(qkw[None], (128, 2, 512))).astype(np.float32)
    nw = vec_fm(nw1)
    xpad = np.concatenate([np.zeros((2, D), np.float32), x, np.zeros((2, D), np.float32)], 0)
    ins = []
    for i in range(NCORES):
        lo = i * T_LAT
        toks = np.concatenate([x[lo:lo + T_LAT], ctx, xpad[lo:lo + 2], xpad[lo + T_LAT + 2:lo + T_LAT + 4]], 0)
        edge = np.zeros((128, 2), np.float32)
        edge[:, 0] = 1.0 if i > 0 else 0.0
        edge[:, 1] = 1.0 if i < NCORES - 1 else 0.0
        ins.append(dict(xT=fm(toks), mod=mod_l, nw=nw, ones=ONES, wfm=wfm, wtm=wtm, wdt=wdt, cw=cw, cb=cb,
                        edge=edge, qkw=qkw))
    return _run(_prog("ab1", build_ab1), ins)


def build_ssd(mode):
    full = (mode == "B")
    P = Prog()
    xs_d = P.dram_in("xs", [10, 128, 2048])
    bt_d = P.dram_in("btok", [10, 128, 1024])
    BT_d = P.dram_in("BT", [10, 128, 8, 128])
    CT_d = P.dram_in("CT", [10, 128, 8, 128])
    dt_d = P.dram_in("dt", [10, 128, 64])
    par_d = P.dram_in("par", [128, 3, 64])
    tri_d = P.dram_in("tri", [128, 2, 128])
    nm_d = P.dram_in("negmask", [128, 2, 128])
    id_d = P.dram_in("ident", [128, 128])
    ones_d = P.dram_in("ones", [128, 128])
    if full:
        Fl_d = P.dram_in("Flist", [2, 7, 128, 2048])
        Tl_d = P.dram_in("Tlist", [2, 7, 128, 32])
        cF_d = P.dram_in("ctxF", [2, 128, 2048])
        z_d = P.dram_in("z", [10, 128, 2048])
        gw_d = P.dram_in("gw", [128, 2048])
        y_d = P.dram_out("yd", [2, 10, 128, 2048])
        g_d = P.dram_out("g", [10, 128, 2048], BF16)
        ybuf = Buf(y_d, "y_dram")
    else:
        F_d = P.dram_out("F", [2, 128, 2048])
        T_d = P.dram_out("T", [2, 128, 32])
        cFo_d = P.dram_out("ctxF", [2, 128, 2048])

    ones_f, ones_b = load_consts(P, ones_d)
    par = P.sbuf([128, 3, 64], F32, "par"); tri = P.sbuf([128, 2, 128], F32, "tri")
    nm = P.sbuf([128, 2, 128], F32, "nm"); ident = P.sbuf([128, 128], F32, "ident")
    for sb_, d_ in ((par, par_d), (tri, tri_d), (nm, nm_d), (ident, id_d)):
        P.dma(sb_[:], d_, writes=[sb_])
    abc = P.sbuf([128, 64], F32, "abc")
    P.act("activation", reads=[par], writes=[abc], out=abc[:], in_=par[:, 1, :], func=AF.Exp)
    P.dve("tensor_scalar", reads=[abc], writes=[abc], out=abc[:], in0=abc[:], scalar1=-1.0, scalar2=None, op0=ALU.mult)
    dsum = P.sbuf([128, 32], F32, "dsum")
    P.dve("tensor_tensor", reads=[par], writes=[dsum], out=dsum[:], in0=par[:, 2, 0:32], in1=par[:, 2, 32:64], op=ALU.add)

    S = [P.sbuf([128, 2048], F32, f"S{d}") for d in range(2)]
    Sb = [P.sbuf([128, 2048], BF16, f"Sb{d}") for d in range(2)]
    tot_acc = [P.sbuf([128, 32], F32, f"tacc{d}") for d in range(2)]
    m_units = P.mark()
    sets = []
    for par_ in range(2):
        b_ = dict(xs=P.sbuf([128, 2048], F32, f"xs{par_}"), xd=P.sbuf([128, 2048], BF16, f"xd{par_}"),
                  xdd=P.sbuf([128, 2048], BF16, f"xdd{par_}"), btf=P.sbuf([128, 1024], F32, f"btf{par_}"),
                  btb=P.sbuf([128, 1024], BF16, f"btb{par_}"), BTf=P.sbuf([128, 8, 128], F32, f"BTf{par_}"),
                  BTb=P.sbuf([128, 8, 128], BF16, f"BTb{par_}"), CTf=P.sbuf([128, 8, 128], F32, f"CTf{par_}"),
                  CTb=P.sbuf([128, 8, 128], BF16, f"CTb{par_}"), dbc=P.sbuf([128, 32, 128], F32, f"dbc{par_}"))
        for n_ in ("dtr", "x0", "mx", "na", "e", "dt", "dtA", "la", "nla", "ela", "tot", "dend", "cdec", "dtd"):
            b_[n_] = P.sbuf([128, 32], F32, f"{n_}{par_}")
        sets.append(b_)
    fb = P.sbuf([128, 2048], F32, "foldbuf")
    Lt = Rot([P.sbuf([128, 128], F32, f"Lt{i}") for i in range(3)])
    Mt = Rot([P.sbuf([128, 128], BF16, f"Mt{i}") for i in range(3)])
    ysb = P.sbuf([128, 2048], F32, "ysb")
    tmpg = Rot([P.sbuf([128, 256], F32, f"tg{i}") for i in range(2)])
    tmps = Rot([P.sbuf([128, 256], F32, f"ts{i}") for i in range(2)])
    B_lt = P.bank(0)
    B_cb = Rot([P.bank(1), P.bank(2)]); B_arg = Rot([P.bank(3), P.bank(4)]); B_yd = Rot([P.bank(5), P.bank(6)])
    B_os = P.bank(7)

    def bc(ap32, g):
        return ap32[:, 4 * g:4 * g + 4].unsqueeze(2).to_broadcast([128, 4, 64])

    def v3(ap, g):
        return ap[:, g * 256:(g + 1) * 256].rearrange("p (k q) -> p k q", k=4)

    def prologue(c, d, want_y, B):
        sm = B
        xs, xd, xdd, btf, btb, BTf, BTb, CTf, CTb, dbc = (B[k_] for k_ in
            ("xs", "xd", "xdd", "btf", "btb", "BTf", "BTb", "CTf", "CTb", "dbc"))
        P.dma(xs[:], xs_d[c], writes=[xs])
        P.dma(btf[:], bt_d[c], writes=[btf])
        P.dma(BTf[:], BT_d[c], writes=[BTf])
        P.dma(CTf[:], CT_d[c], writes=[CTf])
        P.dma(sm["dtr"][:], dt_d[c, :, d * 32:(d + 1) * 32], writes=[sm["dtr"]])
        P.pool("tensor_copy", reads=[btf], writes=[btb], out=btb[:], in_=btf[:])
        P.pool("tensor_copy", reads=[BTf], writes=[BTb], out=BTb[:], in_=BTf[:])
        P.pool("tensor_copy", reads=[CTf], writes=[CTb], out=CTb[:], in_=CTf[:])
        dsl = slice(d * 32, (d + 1) * 32)
        P.dve("tensor_tensor", reads=[sm["dtr"], par], writes=[sm["x0"]], out=sm["x0"][:], in0=sm["dtr"][:],
              in1=par[:, 0, dsl], op=ALU.add)
        P.dve("tensor_scalar", reads=[sm["x0"]], writes=[sm["mx"]], out=sm["mx"][:], in0=sm["x0"][:], scalar1=0.0,
              scalar2=None, op0=ALU.max)
        P.dve("scalar_tensor_tensor", reads=[sm["mx"], sm["x0"]], writes=[sm["na"]], out=sm["na"][:], in0=sm["mx"][:],
              scalar=-2.0, in1=sm["x0"][:], op0=ALU.mult, op1=ALU.add)
        P.act("activation", reads=[sm["na"]], writes=[sm["e"]], out=sm["e"][:], in_=sm["na"][:], func=AF.Exp)
        P.dve("tensor_scalar", reads=[sm["e"]], writes=[sm["e"]], out=sm["e"][:], in0=sm["e"][:], scalar1=1.0,
              scalar2=None, op0=ALU.add)
        P.act("activation", reads=[sm["e"]], writes=[sm["e"]], out=sm["e"][:], in_=sm["e"][:], func=AF.Ln)
        P.dve("tensor_tensor", reads=[sm["mx"], sm["e"]], writes=[sm["dt"]], out=sm["dt"][:], in0=sm["mx"][:],
              in1=sm["e"][:], op=ALU.add)
        P.dve("tensor_tensor", reads=[sm["dt"], abc], writes=[sm["dtA"]], out=sm["dtA"][:], in0=sm["dt"][:],
              in1=abc[:, dsl], op=ALU.mult)
        P.pe("matmul", reads=[tri, sm["dtA"]], writes=[B_lt], out=B_lt[:, 0:32], lhsT=tri[:, d, :], rhs=sm["dtA"][:],
             start=True, stop=True)
        P.pe("matmul", reads=[ones_f, sm["dtA"]], writes=[B_lt], accum=True, out=B_lt[:, 32:64], lhsT=ones_f[:],
             rhs=sm["dtA"][:], start=True, stop=True)
        P.dve("tensor_copy", reads=[B_lt], writes=[sm["la"]], out=sm["la"][:], in_=B_lt[:, 0:32])
        P.dve("tensor_copy", reads=[B_lt], writes=[sm["tot"]], out=sm["tot"][:], in_=B_lt[:, 32:64])
        P.dve("tensor_scalar", reads=[sm["la"]], writes=[sm["nla"]], out=sm["nla"][:], in0=sm["la"][:], scalar1=-1.0,
              scalar2=None, op0=ALU.mult)
        P.act("activation", reads=[sm["la"]], writes=[sm["ela"]], out=sm["ela"][:], in_=sm["la"][:], func=AF.Exp)
        P.dve("tensor_tensor", reads=[sm["tot"], sm["la"]], writes=[sm["dend"]], out=sm["dend"][:], in0=sm["tot"][:],
              in1=sm["la"][:], op=ALU.subtract)
        P.act("activation", reads=[sm["dend"]], writes=[sm["dend"]], out=sm["dend"][:], in_=sm["dend"][:], func=AF.Exp)
        P.act("activation", reads=[sm["tot"]], writes=[sm["cdec"]], out=sm["cdec"][:], in_=sm["tot"][:], func=AF.Exp)
        P.dve("tensor_tensor", reads=[sm["dt"], sm["dend"]], writes=[sm["dtd"]], out=sm["dtd"][:], in0=sm["dt"][:],
              in1=sm["dend"][:], op=ALU.mult)
        xs3 = xs[:, :].rearrange("p (h q) -> p h q", h=32)
        P.dve("tensor_tensor", reads=[xs, sm["dtd"]], writes=[xdd], out=xdd[:, :].rearrange("p (h q) -> p h q", h=32),
              in0=xs3, in1=sm["dtd"][:, :].unsqueeze(2).to_broadcast([128, 32, 64]), op=ALU.mult)
        if want_y:
            P.pool("tensor_tensor", reads=[xs, sm["dt"]], writes=[xd], out=xd[:, :].rearrange("p (h q) -> p h q", h=32),
                   in0=xs3, in1=sm["dt"][:, :].unsqueeze(2).to_broadcast([128, 32, 64]), op=ALU.mult)
            P.pool("tensor_copy", reads=[sm["dtA"]], writes=[dbc], out=dbc[:],
                   in_=sm["dtA"][:, :].unsqueeze(2).to_broadcast([128, 32, 128]))

    def body(c, d, want_y, B, hook):
        sm = B
        xs, xd, xdd, btb, BTb, CTb, dbc = (B[k_] for k_ in ("xs", "xd", "xdd", "btb", "BTb", "CTb", "dbc"))
        P.dve("tensor_tensor", reads=[tot_acc[d], sm["tot"]], writes=[tot_acc[d]], out=tot_acc[d][:], in0=tot_acc[d][:],
              in1=sm["tot"][:], op=ALU.add)

        def arg_mm(h):
            A_ = B_arg.next()
            P.pe("matmul", reads=[dbc, tri], writes=[A_], out=A_[:, 0:128], lhsT=dbc[:, h, :], rhs=tri[:, d, :],
                 start=True, stop=False)
            P.pe("matmul", reads=[ident, nm], writes=[A_], accum=True, out=A_[:, 0:128], lhsT=ident[:],
                 rhs=nm[:, d, :], start=False, stop=True)
            return A_

        def cb_mm(g):
            C_ = B_cb.next()
            P.pe("matmul", reads=[BTb, CTb], writes=[C_], out=C_[:, 0:128], lhsT=BTb[:, g, :], rhs=CTb[:, g, :],
                 start=True, stop=True)
            return C_
        if want_y:
            cbs = {0: cb_mm(0)}
            A_next = arg_mm(0)
        for g in range(8):
            if want_y:
                Yd = B_yd.next()
                for k in range(4):
                    h = 4 * g + k
                    A_ = A_next
                    L_ = Lt.next(); M_ = Mt.next()
                    P.act("activation", reads=[A_, sm["nla"]], writes=[L_], out=L_[:], in_=A_[:, 0:128], func=AF.Exp,
                          bias=sm["nla"][:, h:h + 1], scale=1.0)
                    P.dve("tensor_tensor", reads=[L_, cbs[g]], writes=[M_], out=M_[:], in0=L_[:], in1=cbs[g][:, 0:128],
                          op=ALU.mult)
                    if h + 1 < 32:
                        if k == 3:
                            cbs[g + 1] = cb_mm(g + 1)
                        A_next = arg_mm(h + 1)
                    P.pe("matmul", reads=[M_, xd], writes=[Yd], out=Yd[:, k * 64:(k + 1) * 64], lhsT=M_[:],
                         rhs=xd[:, h * 64:(h + 1) * 64], start=True, stop=True)
                P.pe("matmul", reads=[CTb, Sb[d]], writes=[B_os], out=B_os[:, 0:256], lhsT=CTb[:, g, :],
                     rhs=Sb[d][:, g * 256:(g + 1) * 256], start=True, stop=True)
                t_ = tmpg.next()
                t3 = t_[:, :].rearrange("p (k q) -> p k q", k=4)
                P.dve("tensor_tensor", reads=[B_os, sm["ela"]], writes=[t_], out=t3,
                      in0=B_os[:, 0:256].rearrange("p (k q) -> p k q", k=4), in1=bc(sm["ela"], g), op=ALU.mult)
                P.dve("tensor_tensor", reads=[t_, Yd], writes=[ysb], out=ysb[:, g * 256:(g + 1) * 256], in0=t_[:, :],
                      in1=Yd[:, 0:256], op=ALU.add)
                if d == 0:
                    t2 = tmps.next()
                    P.pool("tensor_tensor", reads=[xs, dsum], writes=[t2], out=t2[:, :].rearrange("p (k q) -> p k q", k=4),
                           in0=v3(xs, g), in1=bc(dsum, g), op=ALU.mult)
                    P.pool("tensor_tensor", reads=[t2, ysb], writes=[ysb], out=ysb[:, g * 256:(g + 1) * 256],
                           in0=ysb[:, g * 256:(g + 1) * 256], in1=t2[:, :], op=ALU.add)
            P.pe("matmul", reads=[btb, xdd], writes=[B_os], accum=True, out=B_os[:, 256:512],
                 lhsT=btb[:, g * 128:(g + 1) * 128], rhs=xdd[:, g * 256:(g + 1) * 256], start=True, stop=True)
            P.dve("tensor_tensor", reads=[S[d], sm["cdec"]], writes=[S[d]], out=v3(S[d], g), in0=v3(S[d], g),
                  in1=bc(sm["cdec"], g), op=ALU.mult)
            P.dve("tensor_tensor", reads=[S[d], B_os], writes=[S[d]], out=S[d][:, g * 256:(g + 1) * 256],
                  in0=S[d][:, g * 256:(g + 1) * 256], in1=B_os[:, 256:512], op=ALU.add)
            if full:
                P.act("activation", reads=[S[d]], writes=[Sb[d]], out=Sb[d][:, g * 256:(g + 1) * 256],
                      in_=S[d][:, g * 256:(g + 1) * 256], func=AF.Copy)
            if g == 3:
                hook()
        if want_y:
            P.dma(y_d[d, c], ysb[:], reads=[ysb], writes=[ybuf])

    def zero_state(d):
        P.dve("memset", writes=[S[d]], ap=S[d][:], constant=0.0)
        P.dve("memset", writes=[Sb[d]], ap=Sb[d][:], constant=0.0)
        P.dve("memset", writes=[tot_acc[d]], ap=tot_acc[d][:], constant=0.0)

    def fold(d):
        P.dma(S[d][:], cF_d[d], writes=[S[d]])
        tl = P.sbuf([128, 7, 32], F32, f"tl{d}")
        P.dma(tl[:], Tl_d[d].rearrange("j p h -> p j h"), writes=[tl])
        P.act("activation", reads=[tl], writes=[tl], out=tl[:], in_=tl[:], func=AF.Exp)
        for j in range(7):
            P.dma(fb[:], Fl_d[d, j], writes=[fb])
            P.dve("tensor_tensor", reads=[S[d], tl], writes=[S[d]], out=S[d][:, :].rearrange("p (h q) -> p h q", h=32),
                  in0=S[d][:, :].rearrange("p (h q) -> p h q", h=32),
                  in1=tl[:, j, :].unsqueeze(2).to_broadcast([128, 32, 64]), op=ALU.mult)
            P.dve("tensor_tensor", reads=[S[d], fb], writes=[S[d]], out=S[d][:], in0=S[d][:], in1=fb[:], op=ALU.add)
        P.act("activation", reads=[S[d]], writes=[Sb[d]], out=Sb[d][:], in_=S[d][:], func=AF.Copy)

    order = {0: (list(range(8)), [8, 9]), 1: (list(range(7, -1, -1)), [9, 8])}
    steps = []
    for d in range(2):
        lat_order, ctx_order = order[d]
        steps.append(("zero", d))
        steps += [("unit", c, d, full) for c in ctx_order]
        if not full:
            steps += [("save_ctx", d), ("zero", d)]
        else:
            steps.append(("fold", d))
        steps += [("unit", c, d, full) for c in lat_order]
        if not full:
            steps.append(("save_F", d))
    unit_pos = [i for i, st_ in enumerate(steps) if st_[0] == "unit"]
    ordinal = {p_: k_ for k_, p_ in enumerate(unit_pos)}
    done_pro = set()

    def ensure_pro(i):
        if i is None or i in done_pro:
            return
        _, c, d, wy = steps[i]
        prologue(c, d, wy, sets[ordinal[i] % 2])
        done_pro.add(i)
    for i, st_ in enumerate(steps):
        if st_[0] == "unit":
            _, c, d, wy = st_
            ensure_pro(i)
            k_ = ordinal[i]
            nxt = unit_pos[k_ + 1] if k_ + 1 < len(unit_pos) else None
            body(c, d, wy, sets[k_ % 2], lambda nxt=nxt: ensure_pro(nxt))
        elif st_[0] == "zero":
            zero_state(st_[1])
        elif st_[0] == "fold":
            fold(st_[1])
        elif st_[0] == "save_ctx":
            P.dma(cFo_d[st_[1]], S[st_[1]][:], reads=[S[st_[1]]])
        elif st_[0] == "save_F":
            P.dma(F_d[st_[1]], S[st_[1]][:], reads=[S[st_[1]]])
            P.dma(T_d[st_[1]], tot_acc[st_[1]][:], reads=[tot_acc[st_[1]]])
    if full:
        P.release(m_units)
        xs = P.sbuf([128, 2048], F32, "gxs")
        gw = P.sbuf([128, 2048], F32, "gw")
        P.dma(gw[:], gw_d, writes=[gw])
        zt = P.sbuf([128, 2048], F32, "zt"); y2 = P.sbuf([128, 2048], F32, "y2")
        gss = P.sbuf([128, 8], F32, "gss"); go = P.sbuf([128, 2048], BF16, "go")
        for c in range(10):
            P.dma(xs[:], y_d[0, c], reads=[ybuf], writes=[xs])
            P.dma(y2[:], y_d[1, c], reads=[ybuf], writes=[y2])
            P.dma(zt[:], z_d[c], writes=[zt])
            P.act("activation", reads=[zt], writes=[zt], out=zt[:], in_=zt[:], func=AF.Silu)
            P.dve("tensor_tensor", reads=[xs, y2], writes=[xs], out=xs[:], in0=xs[:], in1=y2[:], op=ALU.add)
            P.dve("tensor_tensor", reads=[xs, zt], writes=[xs], out=xs[:], in0=xs[:], in1=zt[:], op=ALU.mult)
            P.act("activation", reads=[xs], writes=[y2], out=y2[:], in_=xs[:], func=AF.Square)
            P.dve("tensor_reduce", reads=[y2], writes=[gss], out=gss[:], in_=y2[:, :].rearrange("p (g q) -> p g q", g=8),
                  axis=AX.X, op=ALU.add)
            P.dve("tensor_scalar", reads=[gss], writes=[gss], out=gss[:], in0=gss[:], scalar1=1.0 / 256, scalar2=EPS,
                  op0=ALU.mult, op1=ALU.add)
            P.act("activation", reads=[gss], writes=[gss], out=gss[:], in_=gss[:], func=AF.Sqrt)
            P.dve("reciprocal", reads=[gss], writes=[gss], out=gss[:], in_=gss[:])
            P.dve("tensor_tensor", reads=[xs, gss], writes=[xs], out=xs[:, :].rearrange("p (g q) -> p g q", g=8),
                  in0=xs[:, :].rearrange("p (g q) -> p g q", g=8), in1=gss[:, :].unsqueeze(2).to_broadcast([128, 8, 256]),
                  op=ALU.mult)
            P.pool("tensor_tensor", reads=[xs, gw], writes=[go], out=go[:], in0=xs[:], in1=gw[:], op=ALU.mult)
            P.dma(g_d[c], go[:], reads=[go])
    return P.finish()


def _ssd_consts():
    t = np.arange(128)
    tri = np.stack([(t[:, None] <= t[None, :]), (t[:, None] >= t[None, :])], 1).astype(np.float32)
    valid = np.stack([(t[None, :] >= t[:, None]), (t[None, :] <= t[:, None])], 1)
    negmask = np.where(valid, 0.0, -30000.0).astype(np.float32)
    return np.ascontiguousarray(tri), np.ascontiguousarray(negmask), np.eye(128, dtype=np.float32)


def run_ssd(r1, dt_bias, a_log, d_skip, norm_w):
    tri, negmask, ident = _ssd_consts()
    par = np.stack([dt_bias.reshape(64), a_log.reshape(64), d_skip.reshape(64)], 0)
    par = np.ascontiguousarray(np.broadcast_to(par[None], (128, 3, 64))).astype(np.float32)
    base = []
    for i in range(NCORES):
        xbc = r1[i]["xbc"]
        xbc_t = np.ascontiguousarray(xbc.reshape(4096, T_ALL).T)
        xs = xbc_t[:, 0:2048].reshape(10, 128, 2048)
        btok = xbc_t[:, 2048:3072].reshape(10, 128, 1024)
        BT = xbc[16:24].reshape(8, 128, 10, 128).transpose(2, 1, 0, 3)
        CT = xbc[24:32].reshape(8, 128, 10, 128).transpose(2, 1, 0, 3)
        base.append(dict(xs=np.ascontiguousarray(xs), btok=np.ascontiguousarray(btok), BT=np.ascontiguousarray(BT),
                         CT=np.ascontiguousarray(CT), dt=r1[i]["dt"], par=par, tri=tri, negmask=negmask, ident=ident,
                         ones=ONES))
    ra = _run(_prog("ssdA", build_ssd, "A"), base)
    ctxF = ra[0]["ctxF"]
    gw = np.ascontiguousarray(np.broadcast_to(norm_w[None], (128, 2048))).astype(np.float32)
    insb = []
    for i in range(NCORES):
        Fl = np.zeros((2, 7, 128, 2048), np.float32)
        Tl = np.zeros((2, 7, 128, 32), np.float32)
        for j, cj in enumerate(range(0, i)):
            Fl[0, j] = ra[cj]["F"][0]; Tl[0, j] = ra[cj]["T"][0]
        for j, cj in enumerate(range(NCORES - 1, i, -1)):
            Fl[1, j] = ra[cj]["F"][1]; Tl[1, j] = ra[cj]["T"][1]
        d = dict(base[i]); d.update(Flist=Fl, Tlist=Tl, ctxF=ctxF, z=r1[i]["z"], gw=gw)
        insb.append(d)
    rb = _run(_prog("ssdB", build_ssd, "B"), insb)
    return ra, rb


def build_na():
    P = Prog()
    qT_d = P.dram_in("qT", [128, 8, T_ALL], BF16)
    kT_d = P.dram_in("kT", [128, 8, 2048], BF16)
    ve_d = P.dram_in("ve", [128, 16, 8, 128], BF16)
    vo_d = P.dram_in("vo", [128, 16, 8, 128], BF16)
    kcT_d = P.dram_in("kcT", [128, 8, 256], BF16)
    vc_d = P.dram_in("vc", [128, 2, 8, 128], BF16)
    tt_d = P.dram_in("tt", [128, 8, 8, 64])
    vm_d = P.dram_in("vm", [128, 16, 8])
    ones_d = P.dram_in("ones", [128, 128])
    o_d = P.dram_out("o", [128, 8, T_ALL], BF16)
    ones_f, ones_b = load_consts(P, ones_d)
    qT = P.sbuf([128, 8, T_ALL], BF16, "qT"); kT = P.sbuf([128, 8, 2048], BF16, "kT")
    ve = P.sbuf([128, 16, 8, 128], BF16, "ve"); vo = P.sbuf([128, 16, 8, 128], BF16, "vo")
    kcT = P.sbuf([128, 8, 256], BF16, "kcT"); vc = P.sbuf([128, 2, 8, 128], BF16, "vc")
    TT = P.sbuf([128, 8, 8, 64], F32, "TT"); vm = P.sbuf([128, 16, 8], F32, "vm")
    oT = P.sbuf([128, 8, T_ALL], BF16, "oT")
    for sb_, d_ in ((qT, qT_d), (kT, kT_d), (ve, ve_d), (vo, vo_d), (kcT, kcT_d), (vc, vc_d), (TT, tt_d), (vm, vm_d)):
        P.dma(sb_[:], d_, writes=[sb_])
    tb = Rot([P.sbuf([128, 512], F32, f"tb{i}") for i in range(2)])
    pw = Rot([P.sbuf([128, 512], BF16, f"pw{i}") for i in range(2)])
    pc = Rot([P.sbuf([128, 128], BF16, f"pc{i}") for i in range(2)])
    rec = Rot([P.sbuf([128, 128], F32, f"rec{i}") for i in range(2)])
    BA = Rot([P.bank(0), P.bank(1)]); BB = Rot([P.bank(2), P.bank(3)])
    BO = Rot([P.bank(4), P.bank(5)]); BD = Rot([P.bank(6), P.bank(7)])
    scale = 128 ** -0.5
    for lr in range(16):
        for h in range(8):
            q = qT[:, h, lr * 64:(lr + 1) * 64]
            A_ = BA.next(); B_ = BB.next(); O = BO.next(); Dn = BD.next()
            for pb in range(8):
                off = (lr + 2 * pb) * 64
                P.pe("matmul", reads=[kT, qT], writes=[A_], out=A_[:, pb * 64:(pb + 1) * 64], lhsT=kT[:, h, off:off + 128],
                     rhs=q, start=True, stop=True)
            for cb in range(2):
                P.pe("matmul", reads=[kcT, qT], writes=[B_], out=B_[:, cb * 64:(cb + 1) * 64],
                     lhsT=kcT[:, h, cb * 128:(cb + 1) * 128], rhs=q, start=True, stop=True)
            t_ = tb.next(); p_ = pw.next(); c_ = pc.next()
            P.dve("scalar_tensor_tensor", reads=[A_, TT], writes=[t_], out=t_[:, :], in0=A_[:, :], scalar=scale,
                  in1=TT[:, h, :, :].rearrange("p a b -> p (a b)"), op0=ALU.mult, op1=ALU.add)
            P.pool("tensor_tensor", reads=[t_, vm], writes=[t_], out=t_[:, :].rearrange("p (a b) -> p a b", a=8),
                   in0=t_[:, :].rearrange("p (a b) -> p a b", a=8),
                   in1=vm[:, lr, :].unsqueeze(2).to_broadcast([128, 8, 64]), op=ALU.add)
            P.act("activation", reads=[t_], writes=[p_], out=p_[:, :], in_=t_[:, :], func=AF.Exp)
            P.act("activation", reads=[B_], writes=[c_], out=c_[:, :], in_=B_[:, 0:128], func=AF.Exp, scale=scale)
            for pb in range(8):
                row = lr + 2 * pb
                vsrc = ve[:, row // 2, h, :] if row % 2 == 0 else vo[:, row // 2, h, :]
                vbuf = ve if row % 2 == 0 else vo
                P.pe("matmul", reads=[vbuf, p_], writes=[O], accum=(pb > 0), out=O[:, 0:64], lhsT=vsrc,
                     rhs=p_[:, pb * 64:(pb + 1) * 64], start=(pb == 0), stop=False)
            for cb in range(2):
                P.pe("matmul", reads=[vc, c_], writes=[O], accum=True, out=O[:, 0:64], lhsT=vc[:, cb, h, :],
                     rhs=c_[:, cb * 64:(cb + 1) * 64], start=False, stop=(cb == 1))
            for pb in range(8):
                P.pe("matmul", reads=[ones_b, p_], writes=[Dn], accum=(pb > 0), out=Dn[:, 0:64], lhsT=ones_b[:],
                     rhs=p_[:, pb * 64:(pb + 1) * 64], start=(pb == 0), stop=False)
            for cb in range(2):
                P.pe("matmul", reads=[ones_b, c_], writes=[Dn], accum=True, out=Dn[:, 0:64], lhsT=ones_b[:],
                     rhs=c_[:, cb * 64:(cb + 1) * 64], start=False, stop=(cb == 1))
            r_ = rec.next()
            P.dve("reciprocal", reads=[Dn], writes=[r_], out=r_[:, 0:64], in_=Dn[:, 0:64])
            P.dve("tensor_tensor", reads=[O, r_], writes=[oT], out=oT[:, h, lr * 64:(lr + 1) * 64], in0=O[:, 0:64],
                  in1=r_[:, 0:64], op=ALU.mult)
    for qb in range(2):
        for h in range(8):
            q = qT[:, h, T_LAT + qb * 128:T_LAT + (qb + 1) * 128]
            B_ = BB.next(); O = BO.next(); Dn = BD.next()
            for cb in range(2):
                P.pe("matmul", reads=[kcT, qT], writes=[B_], out=B_[:, cb * 128:(cb + 1) * 128],
                     lhsT=kcT[:, h, cb * 128:(cb + 1) * 128], rhs=q, start=True, stop=True)
            p_ = pw.next()
            P.act("activation", reads=[B_], writes=[p_], out=p_[:, 0:256], in_=B_[:, 0:256], func=AF.Exp, scale=scale)
            for cb in range(2):
                P.pe("matmul", reads=[vc, p_], writes=[O], accum=(cb > 0), out=O[:, 0:128], lhsT=vc[:, cb, h, :],
                     rhs=p_[:, cb * 128:(cb + 1) * 128], start=(cb == 0), stop=(cb == 1))
            for cb in range(2):
                P.pe("matmul", reads=[ones_b, p_], writes=[Dn], accum=(cb > 0), out=Dn[:, 0:128], lhsT=ones_b[:],
                     rhs=p_[:, cb * 128:(cb + 1) * 128], start=(cb == 0), stop=(cb == 1))
            r_ = rec.next()
            P.dve("reciprocal", reads=[Dn], writes=[r_], out=r_[:, 0:128], in_=Dn[:, 0:128])
            P.dve("tensor_tensor", reads=[O, r_], writes=[oT], out=oT[:, h, T_LAT + qb * 128:T_LAT + (qb + 1) * 128],
                  in0=O[:, 0:128], in1=r_[:, 0:128], op=ALU.mult)
    P.dma(o_d, oT[:], reads=[oT])
    return P.finish()


def run_na(r1, rpb):
    a = np.arange(64)
    c0 = np.clip(a - 8, 0, 48)
    b = np.arange(64)
    colok = (b[:, None] >= c0[None, :]) & (b[:, None] < c0[None, :] + 16)
    dc = np.clip(b[:, None] - a[None, :], -15, 15) + 15
    TT = np.full((2, 64, 8, 8, 64), -30000.0, np.float32)
    for pb in range(8):
        for jj in range(2):
            dr = 2 * pb + jj - 1
            if 0 <= dr < 15:
                vals = rpb[:, dr][:, dc]
                TT[jj, :, :, pb, :] = np.where(colok[None], vals, np.float32(-30000.0)).transpose(1, 0, 2)
    TT = np.ascontiguousarray(TT.reshape(128, 8, 8, 64))
    k_lat = np.concatenate([r1[i]["k"][0:8].reshape(T_LAT, 8, 128) for i in range(NCORES)], 0)
    v_lat = np.concatenate([r1[i]["v"][0:8].reshape(T_LAT, 8, 128) for i in range(NCORES)], 0)
    k_ctx = r1[0]["k"][8:10].reshape(T_CTX, 8, 128)
    v_ctx = r1[0]["v"][8:10].reshape(T_CTX, 8, 128)
    kcT = np.ascontiguousarray(k_ctx.transpose(2, 1, 0))
    vc = np.ascontiguousarray(v_ctx.reshape(2, 128, 8, 128).transpose(1, 0, 2, 3))
    zk = np.zeros((64, 8, 128), k_lat.dtype)
    ins = []
    for i in range(NCORES):
        base = 16 * i - 8
        kw = []; vw = []
        for v in range(33):
            row = base + v
            if 0 <= row < 128:
                kw.append(k_lat[row * 64:(row + 1) * 64]); vw.append(v_lat[row * 64:(row + 1) * 64])
            else:
                kw.append(zk); vw.append(zk)
        kwin = np.concatenate(kw[:32], 0)
        kT = np.ascontiguousarray(kwin.transpose(2, 1, 0))
        ve = np.stack([np.concatenate([vw[2 * j], vw[2 * j + 1]], 0) for j in range(16)], 1)
        vo = np.stack([np.concatenate([vw[2 * j + 1], vw[2 * j + 2]], 0) for j in range(16)], 1)
        vm = np.full((2, 64, 16, 8), -30000.0, np.float32)
        for lr in range(16):
            r = 16 * i + lr
            rs = min(max(r - 4, 0), 120)
            for pb in range(8):
                for jj in range(2):
                    krow = r - 8 + 2 * pb + jj
                    if rs <= krow < rs + 8:
                        vm[jj, :, lr, pb] = 0.0
        q = r1[i]["q"].reshape(T_ALL, 8, 128)
        ins.append(dict(qT=np.ascontiguousarray(q.transpose(2, 1, 0)), kT=kT, ve=np.ascontiguousarray(ve),
                        vo=np.ascontiguousarray(vo), kcT=kcT, vc=vc, tt=TT, vm=np.ascontiguousarray(vm.reshape(128, 16, 8)),
                        ones=ONES))
    return _run(_prog("na", build_na), ins)


def run_ab_layer(x, ctx, mod_l, nw1, w_in, conv_w, conv_b, dt_bias, a_log, d_skip, norm_w, q_norm, k_norm, rpb,
                 w_out, nw2, w1, w2):
    r1 = run_ab1(x, ctx, mod_l, nw1, w_in, conv_w, conv_b, q_norm, k_norm)
    ra, rb = run_ssd(r1, dt_bias, a_log, d_skip, norm_w)
    rn = run_na(r1, rpb)
    u_lat = []
    for i in range(NCORES):
        gT = fm(rb[i]["g"].reshape(T_ALL, 2048))
        u_lat.append(np.concatenate([gT[:, :, :T_LAT], rn[i]["o"][:, :, :T_LAT]], axis=1))
    gT0 = fm(rb[0]["g"].reshape(T_ALL, 2048))
    u_ctx = np.concatenate([gT0[:, :, T_LAT:], rn[0]["o"][:, :, T_LAT:]], axis=1)
    return run_post2(u_lat, u_ctx, x, ctx, w_out, w1, w2, mod_l, nw2)


def kernel(x, c, ctx, c_ctx, w_mod, b_mod, norm1_w, norm2_w, w_mlp_in, w_mlp_out,
           ab_w_in, ab_conv_w, ab_conv_b, ab_dt_bias, ab_a_log, ab_d_skip, ab_norm_w,
           ab_q_norm, ab_k_norm, ab_rpb, ab_w_out, c_w_qkv, c_q_norm, c_k_norm, c_w_out):
    f = lambda a: np.asarray(a, dtype=np.float32)
    xs = f(x)[0]
    cs = f(ctx)[0]
    mods = run_mod(f(c), f(c_ctx), f(w_mod), f(b_mod))
    for layer in range(4):
        i = layer // 2
        if layer % 2 == 0:
            xs, cs = run_ab_layer(xs, cs, mods[layer], f(norm1_w)[layer], f(ab_w_in)[i], f(ab_conv_w)[i],
                                  f(ab_conv_b)[i], f(ab_dt_bias)[i], f(ab_a_log)[i], f(ab_d_skip)[i],
                                  f(ab_norm_w)[i], f(ab_q_norm)[i], f(ab_k_norm)[i], f(ab_rpb)[i], f(ab_w_out)[i],
                                  f(norm2_w)[layer], f(w_mlp_in)[layer], f(w_mlp_out)[layer])
        else:
            xs, cs = run_c_layer(xs, cs, mods[layer], f(norm1_w)[layer], f(c_w_qkv)[i], f(c_q_norm)[i],
                                 f(c_k_norm)[i], f(c_w_out)[i], f(norm2_w)[layer], f(w_mlp_in)[layer],
                                 f(w_mlp_out)[layer])
    return np.ascontiguousarray(xs[None].astype(np.float32))


def build_post2(KM):
    KC = KM // 128
    HQ = 8
    P = Prog()
    uT_d = P.dram_in("uT", [128, KC, T_ALL], BF16)
    xT_d = P.dram_in("xT", [128, KD, T_ALL])
    wo_d = P.dram_in("wo", [16, 128, KC, 128])
    w1_d = P.dram_in("w1", [64, 128, KD, 128])
    w2_d = P.dram_in("w2", [8, 16, 128, HQ, 128])
    mod_d = P.dram_in("mod", [128, 6, KD, 2])
    nw_d = P.dram_in("nw", [128, KD])
    ones_d = P.dram_in("ones", [128, 128])
    out_d = P.dram_out("xo", [128, KD, T_ALL])

    ones_f, ones_b = load_consts(P, ones_d)
    mod_sb, A = mod_vectors(P, mod_d, nw_d, slot_sc=4)
    xs = [P.sbuf([128, KD, n], F32, f"xs{j}") for j, (st, n, w) in enumerate(TILES)]
    for j, (st, n, w) in enumerate(TILES):
        P.dma(xs[j][:], xT_d[:, :, st:st + n], writes=[xs[j]])
    m1 = P.mark()
    us = P.sbuf([128, KC, T_ALL], BF16, "us")
    P.dma(us[:], uT_d, writes=[us])
    ws = WStream(P, KC * 128, name="wsA", nstage=2, nbf=2)
    for ob in range(16):
        wb, wv = ws.load(wo_d[ob], KC, 128)
        for j, (st, n, w) in enumerate(TILES):
            ps = P.psum()
            for k in range(KC):
                P.pe("matmul", reads=[*wb, us], writes=[ps], accum=(k > 0),
                     out=ps[:, :n], lhsT=wv[:, k, :], rhs=us[:, k, st:st + n], start=(k == 0), stop=(k == KC - 1))
            P.dve("scalar_tensor_tensor", reads=[ps, mod_sb, xs[j]], writes=[xs[j]],
                  out=xs[j][:, ob, :], in0=ps[:, :n], scalar=mod_sb[:, 2, ob, w:w + 1],
                  in1=xs[j][:, ob, :], op0=ALU.mult, op1=ALU.add)
    P.release(m1)
    hs = [P.sbuf([128, KD, n], BF16, f"hs{j}") for j, (st, n, w) in enumerate(TILES)]
    t1 = Rot([P.sbuf([128, 512], F32, f"t1_{i}") for i in range(2)])
    tmp = Rot([P.sbuf([128, 512], F32, f"tmp{i}") for i in range(3)])
    m2 = P.mark()
    sq = P.sbuf([128, KD, 512], BF16, "sq")
    for j, (st, n, w) in enumerate(TILES):
        norm_mod_tile(P, lambda k, j=j: xs[j][:, k, :], xs[j], lambda k, j=j: hs[j][:, k, :], hs[j], n, w, A, mod_sb, 3,
                      ones_b, sq.ap, sq, t1, tmp)
    P.release(m2)
    aq = [P.sbuf([128, HQ, n], BF16, f"aq{j}") for j, (st, n, w) in enumerate(TILES)]
    ws = WStream(P, KD * 128, name="wsB", nstage=2, nbf=2)
    for hq in range(64 // HQ):
        for hc in range(HQ):
            wb, wv = ws.load(w1_d[hq * HQ + hc], KD, 128)
            for j, (st, n, w) in enumerate(TILES):
                ps = P.psum()
                for k in range(KD):
                    P.pe("matmul", reads=[*wb, hs[j]], writes=[ps], accum=(k > 0),
                         out=ps[:, :n], lhsT=wv[:, k, :], rhs=hs[j][:, k, :], start=(k == 0), stop=(k == KD - 1))
                r = tmp.next()
                P.act("activation", reads=[ps], writes=[r], out=r[:, :n], in_=ps[:, :n], func=AF.Relu)
                P.dve("tensor_tensor", reads=[r], writes=[aq[j]], out=aq[j][:, hc, :], in0=r[:, :n], in1=r[:, :n],
                      op=ALU.mult)
        for ob in range(16):
            wb, wv = ws.load(w2_d[hq, ob], HQ, 128)
            for j, (st, n, w) in enumerate(TILES):
                ps = P.psum()
                for k in range(HQ):
                    P.pe("matmul", reads=[*wb, aq[j]], writes=[ps], accum=(k > 0),
                         out=ps[:, :n], lhsT=wv[:, k, :], rhs=aq[j][:, k, :], start=(k == 0), stop=(k == HQ - 1))
                P.dve("scalar_tensor_tensor", reads=[ps, mod_sb, xs[j]], writes=[xs[j]],
                      out=xs[j][:, ob, :], in0=ps[:, :n], scalar=mod_sb[:, 5, ob, w:w + 1], in1=xs[j][:, ob, :],
                      op0=ALU.mult, op1=ALU.add)
    for j, (st, n, w) in enumerate(TILES):
        P.dma(out_d[:, :, st:st + n], xs[j][:], reads=[xs[j]])
    return P.finish()


def run_post2(u_lat, u_ctx, x, ctx, w_out, w1, w2, mod_l, nw2):
    KM = w_out.shape[0]
    wo = wblocks(w_out, 128)
    w1b = wblocks(w1, 128)
    w2b = np.ascontiguousarray(w2.reshape(8, 8, 128, 16, 128).transpose(0, 3, 2, 1, 4))
    nw = vec_fm(nw2)
    xTs = xT_cores(x, ctx)
    ins = []
    for i in range(NCORES):
        uT = np.ascontiguousarray(np.concatenate([u_lat[i], u_ctx], axis=2))
        ins.append(dict(uT=uT, xT=xTs[i], wo=wo, w1=w1b, w2=w2b, mod=mod_l, nw=nw, ones=ONES))
    res = _run(_prog("post2", build_post2, KM), ins)
    outs = [unfm(res[i]["xo"]) for i in range(NCORES)]
    xn = np.concatenate([o[:T_LAT] for o in outs], axis=0)
    cn = outs[0][T_LAT:]
    return xn, cn
```

```python
import numpy as np
from contextlib import ExitStack

import concourse.bass as bass
import concourse.mybir as mybir
from concourse.bass_utils import run_bass_kernel_spmd

F32 = mybir.dt.float32
BF16 = mybir.dt.bfloat16
AF = mybir.ActivationFunctionType
ALU = mybir.AluOpType
AX = mybir.AxisListType

NCORES = 8


class Buf:
    __slots__ = ("ap", "w", "wd", "r", "rd", "name")

    def __init__(self, ap, name=""):
        self.ap = ap
        self.w = {}
        self.wd = []
        self.r = {}
        self.rd = []
        self.name = name

    def __getitem__(self, idx):
        return self.ap[idx]


class Prog:
    ENGS = ("pe", "act", "dve", "pool", "sp")
    DMA_SLOTS = 8

    def __init__(self):
        self.nc = bass.Bass("TRN2", target_bir_lowering=False)
        self.stack = ExitStack()
        self.ops = []
        self.n_by_eng = {e: 0 for e in self.ENGS}
        self._cnt = 0

    def dram_in(self, name, shape, dtype=F32):
        return self.nc.dram_tensor(name, list(shape), dtype, kind="ExternalInput").ap()

    def dram_out(self, name, shape, dtype=F32):
        return self.nc.dram_tensor(name, list(shape), dtype, kind="ExternalOutput").ap()

    ARENA_WORDS = 52736

    def _ensure_arena(self):
        if getattr(self, "arena", None) is None:
            self.arena = self.stack.enter_context(
                self.nc.sbuf_tensor("arena", [128, self.ARENA_WORDS], F32))
            self.top = 0
            self.banks = [Buf(self.stack.enter_context(self.nc.psum_tensor(f"bank{i}", [128, 512], F32)),
                              f"bank{i}") for i in range(8)]
            self.bank_i = 0
            self.last_by_eng = {}
            self.dma_since_barrier = []
            self.barrier_op = None
            self.after_barrier = set()

    def sbuf(self, shape, dtype=F32, name=None):
        self._ensure_arena()
        shape = list(shape)
        esz = 4 if dtype == F32 else 2
        n = 1
        for d in shape[1:]:
            n *= d
        words = (n * esz + 3) // 4
        words = (words + 7) // 8 * 8
        assert self.top + words <= self.ARENA_WORDS, f"SBUF arena overflow {name} {shape} top={self.top}"
        ap = self.arena[0:shape[0], self.top:self.top + words]
        self.top += words
        if dtype != F32:
            ap = ap.bitcast(dtype)
        ap = ap[:, 0:n]
        if len(shape) == 3:
            ap = ap.rearrange("p (a b) -> p a b", a=shape[1])
        elif len(shape) == 4:
            ap = ap.rearrange("p (a b c) -> p a b c", a=shape[1], b=shape[2])
        return Buf(ap, name or "")

    def psum(self, shape=None, dtype=F32, name=None):
        self._ensure_arena()
        b = self.banks[self.bank_i % 8]
        self.bank_i += 1
        return b

    def bank(self, i):
        self._ensure_arena()
        return self.banks[i]

    def mark(self):
        self._ensure_arena()
        return self.top

    def release(self, m):
        self.barrier()
        self.top = m

    def barrier(self):
        self._ensure_arena()
        if not hasattr(self, "_bar_buf"):
            self._bar_buf = self.sbuf([128, 8], F32, "barbuf")
        bb = self._bar_buf
        idx = self.op("dve", "memset", writes=[bb], ap=bb[:], constant=0.0)
        deps = self.ops[idx]["deps"]
        for e, last in self.last_by_eng.items():
            if last != idx:
                deps.add(last)
        deps.update(self.dma_since_barrier)
        deps.discard(idx)
        self.dma_since_barrier = []
        self.barrier_op = idx
        self.after_barrier = set()

    def op(self, eng, meth, reads=(), writes=(), dma=False, accum=False, **kw):
        fn = (meth, kw)
        idx = len(self.ops)
        deps = set()
        for b in reads:
            deps.update(b.w.values())
            deps.update(b.wd)
        for b in writes:
            has_readers = bool(b.r) or bool(b.rd)
            if has_readers:
                for e2, r in b.r.items():
                    if dma or e2 != eng:
                        deps.add(r)
                deps.update(b.rd)
            for e2, w in b.w.items():
                if dma or e2 != eng:
                    deps.add(w)
            if not dma:
                deps.update(b.wd)
        self._ensure_arena()
        if self.barrier_op is not None and eng not in self.after_barrier:
            deps.add(self.barrier_op)
            self.after_barrier.add(eng)
        deps.discard(idx)
        self.ops.append(dict(eng=eng, fn=fn, deps=deps, dma=dma))
        if dma:
            self.dma_since_barrier.append(idx)
        self.last_by_eng[eng] = idx
        for b in reads:
            if dma:
                b.rd.append(idx)
            else:
                b.r[eng] = idx
        for b in writes:
            had_readers = bool(b.r) or bool(b.rd)
            if dma:
                if had_readers:
                    b.w = {}
                    b.wd = []
                b.wd.append(idx)
            else:
                if had_readers:
                    b.w = {}
                b.wd = []
                b.w[eng] = idx
            b.r = {}
            b.rd = []
        return idx

    def pe(self, meth, reads=(), writes=(), accum=False, **kw):
        return self.op("pe", meth, reads, writes, accum=accum, **kw)

    def act(self, meth, reads=(), writes=(), **kw):
        return self.op("act", meth, reads, writes, **kw)

    def dve(self, meth, reads=(), writes=(), **kw):
        return self.op("dve", meth, reads, writes, **kw)

    def pool(self, meth, reads=(), writes=(), **kw):
        return self.op("pool", meth, reads, writes, **kw)

    def dma(self, out, in_, reads=(), writes=(), eng="sp", **kw):
        return self.op(eng, "dma_start", reads, writes, dma=True, out=out, in_=in_, **kw)

    def finish(self, final_wait_ops=None):
        nc = self.nc
        ops = self.ops
        n = len(ops)
        needed = [False] * n
        for i, o in enumerate(ops):
            for d in o["deps"]:
                od = ops[d]
                needed[d] = True
        if final_wait_ops is None:
            final_wait_ops = [i for i, o in enumerate(ops) if o["dma"]][-64:]
        for d in final_wait_ops:
            needed[d] = True
        sems = {e: self.stack.enter_context(nc.semaphore(f"s_{e}")) for e in self.ENGS}
        dma_sems = {e: [self.stack.enter_context(nc.semaphore(f"d_{e}{k}"))
                        for k in range(self.DMA_SLOTS)] for e in self.ENGS}
        sig = [None] * n
        cnt = {e: 0 for e in self.ENGS}
        dcnt = {e: 0 for e in self.ENGS}
        dslot_val = {e: [0] * self.DMA_SLOTS for e in self.ENGS}
        prev_slot_sig = [None] * n
        for i, o in enumerate(ops):
            e = o["eng"]
            if o["dma"]:
                k = dcnt[e] % self.DMA_SLOTS
                dcnt[e] += 1
                if dslot_val[e][k] > 0:
                    prev_slot_sig[i] = (dma_sems[e][k], dslot_val[e][k])
                dslot_val[e][k] += 16
                sig[i] = (dma_sems[e][k], dslot_val[e][k], 16)
            elif needed[i]:
                cnt[e] += 1
                sig[i] = (sems[e], cnt[e], 1)
        per_eng = {e: [] for e in self.ENGS}
        seen = {e: {} for e in self.ENGS}
        for i, o in enumerate(ops):
            e = o["eng"]
            waits = []
            want = {}
            for d in o["deps"]:
                s = sig[d]
                if s is None:
                    continue
                key = id(s[0])
                if key not in want or want[key][1] < s[1]:
                    want[key] = (s[0], s[1])
            if prev_slot_sig[i] is not None:
                s = prev_slot_sig[i]
                key = id(s[0])
                if key not in want or want[key][1] < s[1]:
                    want[key] = s
            for key, (sm, val) in want.items():
                if seen[e].get(key, 0) >= val:
                    continue
                seen[e][key] = val
                waits.append((sm, val))
            per_eng[e].append((waits, o["fn"], sig[i]))
        finals = [sig[d] for d in final_wait_ops]

        def run(engobj, lst, tail=None):
            for waits, fn, s in lst:
                for sm, val in waits:
                    engobj.wait_ge(sm, val)
                ins = getattr(engobj, fn[0])(**fn[1])
                if s is not None:
                    ins.then_inc(s[0], s[2])
            if tail:
                done = {}
                for sm, val, _ in tail:
                    done[id(sm)] = (sm, max(val, done.get(id(sm), (None, 0))[1]))
                for sm, val in done.values():
                    engobj.wait_ge(sm, val)

        with nc.Block() as block:
            @block.tensor
            def _(t):
                run(t, per_eng["pe"])

            @block.scalar
            def _(a):
                run(a, per_eng["act"])

            @block.vector
            def _(v):
                run(v, per_eng["dve"])

            @block.gpsimd
            def _(g):
                run(g, per_eng["pool"])

            @block.sync
            def _(s):
                run(s, per_eng["sp"], tail=finals)
        self.stack.close()
        return nc


D = 2048
KD = D // 128
T_LAT = 1024
T_CTX = 256
T_ALL = T_LAT + T_CTX
TILES = [(0, 512, 0), (512, 512, 0), (1024, 256, 1)]
EPS = 1e-6
HID = 8192


class Rot:
    def __init__(self, bufs):
        self.bufs = bufs
        self.i = 0

    def next(self):
        b = self.bufs[self.i % len(self.bufs)]
        self.i += 1
        return b


def load_consts(P, ones_dram):
    ones_f = P.sbuf([128, 128], F32, "ones_f")
    ones_b = P.sbuf([128, 128], BF16, "ones_b")
    P.dma(ones_f[:], ones_dram, writes=[ones_f])
    P.dve("tensor_copy", reads=[ones_f], writes=[ones_b], out=ones_b[:], in_=ones_f[:])
    return ones_f, ones_b


def mod_vectors(P, mod_dram, nw_dram, slot_sc):
    mod_sb = P.sbuf([128, 6, KD, 2], F32, "mod_sb")
    nw = P.sbuf([128, KD], F32, "nw")
    P.dma(mod_sb[:], mod_dram, writes=[mod_sb])
    P.dma(nw[:], nw_dram, writes=[nw])
    A = []
    for w in range(2):
        a = P.sbuf([128, KD], F32, f"modA{w}")
        P.dve("scalar_tensor_tensor", reads=[mod_sb, nw], writes=[a],
              out=a[:], in0=mod_sb[:, slot_sc, :, w], scalar=1.0, in1=nw[:], op0=ALU.add, op1=ALU.mult)
        A.append(a)
    return mod_sb, A


def norm_mod_tile(P, x_ap_fn, xbuf, h_ap_fn, hbuf, n, w, A, mod_sb, slot_sh, ones_b, sq_ap, sqbuf, t1, tmp):
    for k in range(KD):
        P.act("activation", reads=[xbuf], writes=[sqbuf], out=sq_ap[:, k, :n], in_=x_ap_fn(k), func=AF.Square)
    ps = P.psum()
    for k in range(KD):
        P.pe("matmul", reads=[ones_b, sqbuf], writes=[ps], accum=(k > 0),
             out=ps[:, :n], lhsT=ones_b[:], rhs=sq_ap[:, k, :n], start=(k == 0), stop=(k == KD - 1))
    r = t1.next()
    P.dve("tensor_scalar", reads=[ps], writes=[r], out=r[:, :n], in0=ps[:, :n], scalar1=1.0 / D, scalar2=EPS,
          op0=ALU.mult, op1=ALU.add)
    P.act("activation", reads=[r], writes=[r], out=r[:, :n], in_=r[:, :n], func=AF.Sqrt)
    P.dve("reciprocal", reads=[r], writes=[r], out=r[:, :n], in_=r[:, :n])
    for k in range(KD):
        t = tmp.next()
        P.dve("scalar_tensor_tensor", reads=[xbuf, A[w], r], writes=[t],
              out=t[:, :n], in0=x_ap_fn(k), scalar=A[w][:, k:k + 1], in1=r[:, :n], op0=ALU.mult, op1=ALU.mult)
        P.act("activation", reads=[t, mod_sb], writes=[hbuf],
              out=h_ap_fn(k), in_=t[:, :n], func=AF.Identity, bias=mod_sb[:, slot_sh, k, w:w + 1], scale=1.0)


class WStream:
    def __init__(self, P, words, cast_eng=("pool", "act"), name="w", nstage=2, nbf=2):
        self.P = P
        self.words = words
        self.stage = Rot([P.sbuf([128, words], F32, f"{name}_st{i}") for i in range(nstage)])
        self.wb = Rot([(P.sbuf([128, words], BF16, f"{name}_bf{i}"), Buf(None, "hA"), Buf(None, "hB"))
                       for i in range(nbf)])
        self.cast_eng = cast_eng

    def load(self, dram_ap, a, b):
        P = self.P
        s = self.stage.next(); wb, hA, hB = self.wb.next()
        n = a * b
        assert n <= self.words
        sv = s[:, 0:n].rearrange("p (a b) -> p a b", a=a)
        wv = wb[:, 0:n].rearrange("p (a b) -> p a b", a=a)
        P.dma(sv, dram_ap, writes=[s])
        h = a // 2
        for (lo, hi, eng, hb) in ((0, h, self.cast_eng[0], hA), (h, a, self.cast_eng[1], hB)):
            if eng == "act":
                P.act("activation", reads=[s], writes=[hb], out=wv[:, lo:hi, :], in_=sv[:, lo:hi, :], func=AF.Copy)
            else:
                P.op(eng, "tensor_copy", reads=[s], writes=[hb], out=wv[:, lo:hi, :], in_=sv[:, lo:hi, :])
        return [hA, hB], wv


def build_post(KM, NL=2):
    KC = KM // 128
    P = Prog()
    TP = NL * 512 + T_CTX
    tiles = [(j * 512, 512, 0) for j in range(NL)] + [(NL * 512, T_CTX, 1)]
    uT_d = P.dram_in("uT", [128, KC, TP], BF16)
    xT_d = P.dram_in("xT", [128, KD, TP])
    wo_d = P.dram_in("wo", [16, 128, KC, 128])
    w1_d = P.dram_in("w1", [64, 128, KD, 128])
    w2_d = P.dram_in("w2", [32, 128, 32, 128])
    mod_d = P.dram_in("mod", [128, 6, KD, 2])
    nw_d = P.dram_in("nw", [128, KD])
    ones_d = P.dram_in("ones", [128, 128])
    out_d = P.dram_out("xo", [128, KD, TP])

    ones_f, ones_b = load_consts(P, ones_d)
    mod_sb, A = mod_vectors(P, mod_d, nw_d, slot_sc=4)
    xs = P.sbuf([128, KD, 512], F32, "xs")
    us = P.sbuf([128, KC, 512], BF16, "us")
    hs = P.sbuf([128, KD, 512], BF16, "hs")
    aT = P.sbuf([128, 64, 512], BF16, "aT")
    t1 = Rot([P.sbuf([128, 512], F32, f"t1_{i}") for i in range(2)])
    tmp = Rot([P.sbuf([128, 512], F32, f"tmp{i}") for i in range(3)])
    osb = Rot([P.sbuf([128, 512], F32, f"osb{i}") for i in range(3)])
    ws = WStream(P, 4096, name="ws", nstage=1)

    for (st, n, w) in tiles:
        P.dma(xs[:, :, :n], xT_d[:, :, st:st + n], writes=[xs])
        P.dma(us[:, :, :n], uT_d[:, :, st:st + n], writes=[us])
        for ob in range(16):
            wb, wv = ws.load(wo_d[ob], KC, 128)
            ps = P.psum()
            for k in range(KC):
                P.pe("matmul", reads=[*wb, us], writes=[ps], accum=(k > 0),
                     out=ps[:, :n], lhsT=wv[:, k, :], rhs=us[:, k, :n], start=(k == 0), stop=(k == KC - 1))
            P.dve("scalar_tensor_tensor", reads=[ps, mod_sb, xs], writes=[xs],
                  out=xs[:, ob, :n], in0=ps[:, :n], scalar=mod_sb[:, 2, ob, w:w + 1],
                  in1=xs[:, ob, :n], op0=ALU.mult, op1=ALU.add)
        norm_mod_tile(P, lambda k: xs[:, k, :n], xs, lambda k: hs[:, k, :n], hs, n, w, A, mod_sb, 3,
                      ones_b, aT.ap, aT, t1, tmp)
        for hc in range(64):
            wb, wv = ws.load(w1_d[hc], KD, 128)
            ps = P.psum()
            for k in range(KD):
                P.pe("matmul", reads=[*wb, hs], writes=[ps], accum=(k > 0),
                     out=ps[:, :n], lhsT=wv[:, k, :], rhs=hs[:, k, :n], start=(k == 0), stop=(k == KD - 1))
            r = tmp.next()
            P.act("activation", reads=[ps], writes=[r], out=r[:, :n], in_=ps[:, :n], func=AF.Relu)
            P.dve("tensor_tensor", reads=[r], writes=[aT], out=aT[:, hc, :n], in0=r[:, :n], in1=r[:, :n],
                  op=ALU.mult)
        for ob in range(16):
            ps = P.psum()
            for half in range(2):
                wb, wv = ws.load(w2_d[ob * 2 + half], 32, 128)
                for k in range(32):
                    kk = half * 32 + k
                    P.pe("matmul", reads=[*wb, aT], writes=[ps], accum=(kk > 0),
                         out=ps[:, :n], lhsT=wv[:, k, :], rhs=aT[:, kk, :n], start=(kk == 0), stop=(kk == 63))
            o = osb.next()
            P.dve("scalar_tensor_tensor", reads=[ps, mod_sb, xs], writes=[o],
                  out=o[:, :n], in0=ps[:, :n], scalar=mod_sb[:, 5, ob, w:w + 1], in1=xs[:, ob, :n],
                  op0=ALU.mult, op1=ALU.add)
            P.dma(out_d[:, ob, st:st + n], o[:, :n], reads=[o])
    return P.finish()


def build_mod():
    P = Prog()
    cv_d = P.dram_in("cv", [128, KD, 2])
    w_d = P.dram_in("w", [12, 128, KD, 512])
    b_d = P.dram_in("b", [128, 48])
    out_d = P.dram_out("mo", [128, 48, 2])
    cv = P.sbuf([128, KD, 2], F32, "cv")
    sg = P.sbuf([128, KD, 2], F32, "sg")
    bs = P.sbuf([128, 48], F32, "bs")
    ob = P.sbuf([128, 48, 2], F32, "ob")
    P.dma(cv[:], cv_d, writes=[cv])
    P.dma(bs[:], b_d, writes=[bs])
    P.act("activation", reads=[cv], writes=[sg], out=sg[:], in_=cv[:], func=AF.Sigmoid)
    P.dve("tensor_tensor", reads=[cv, sg], writes=[sg], out=sg[:], in0=sg[:], in1=cv[:], op=ALU.mult)
    wst = Rot([P.sbuf([128, KD, 512], F32, f"wst{i}") for i in range(3)])
    for blk in range(12):
        wv = wst.next()
        P.dma(wv[:], w_d[blk], writes=[wv])
        for sub in range(4):
            cb = blk * 4 + sub
            ps = P.psum()
            for k in range(KD):
                P.pe("matmul", reads=[wv, sg], writes=[ps], accum=(k > 0),
                     out=ps[:, 0:2], lhsT=wv[:, k, sub * 128:(sub + 1) * 128], rhs=sg[:, k, :],
                     start=(k == 0), stop=(k == KD - 1))
            P.dve("tensor_scalar", reads=[ps, bs], writes=[ob], out=ob[:, cb, :], in0=ps[:, 0:2],
                  scalar1=bs[:, cb:cb + 1], scalar2=None, op0=ALU.add)
    P.dma(out_d, ob[:], reads=[ob])
    return P.finish()


def qk_norm_tile(P, ps, nh, w_ap, wbc, out_ap, outbuf, scr, rope=None):
    n = nh * 128
    sq = scr["sq"].next(); ss = scr["ss"].next(); xn = scr["xn"].next()
    P.act("activation", reads=[ps], writes=[sq], out=sq[:, :n], in_=ps[:, :n], func=AF.Square)
    P.dve("tensor_reduce", reads=[sq], writes=[ss], out=ss[:, :nh],
          in_=sq[:, :n].rearrange("p (h d) -> p h d", h=nh), axis=AX.X, op=ALU.add)
    P.dve("tensor_scalar", reads=[ss], writes=[ss], out=ss[:, :nh], in0=ss[:, :nh], scalar1=1.0 / 128, scalar2=EPS,
          op0=ALU.mult, op1=ALU.add)
    P.act("activation", reads=[ss], writes=[ss], out=ss[:, :nh], in_=ss[:, :nh], func=AF.Sqrt)
    P.dve("reciprocal", reads=[ss], writes=[ss], out=ss[:, :nh], in_=ss[:, :nh])
    P.dve("tensor_tensor", reads=[ps, ss], writes=[xn],
          out=xn[:, :n].rearrange("p (h d) -> p h d", h=nh), in0=ps[:, :n].rearrange("p (h d) -> p h d", h=nh),
          in1=ss[:, :nh].unsqueeze(2).to_broadcast([128, nh, 128]), op=ALU.mult)
    if rope is None:
        P.pool("tensor_tensor", reads=[xn, wbc], writes=[outbuf], out=out_ap, in0=xn[:, :n], in1=w_ap,
               op=ALU.mult)
        return
    cos_ap, sin_ap, rbuf = rope
    P.pool("tensor_tensor", reads=[xn, wbc], writes=[xn], out=xn[:, :n], in0=xn[:, :n], in1=w_ap, op=ALU.mult)
    xv = xn[:, :n].rearrange("p (h i two) -> p h i two", h=nh, two=2)
    ov = out_ap.rearrange("p (h i two) -> p h i two", h=nh, two=2)
    u1 = xv[:, :, :, 0]; u2 = xv[:, :, :, 1]
    cb = cos_ap.unsqueeze(1).to_broadcast([128, nh, 64]); sb = sin_ap.unsqueeze(1).to_broadcast([128, nh, 64])
    ta = scr["ra"].next(); tb = scr["rb"].next()
    tav = ta[:, :nh * 64].rearrange("p (h i) -> p h i", h=nh); tbv = tb[:, :nh * 64].rearrange("p (h i) -> p h i", h=nh)
    P.dve("tensor_tensor", reads=[xn, rbuf], writes=[ta], out=tav, in0=u1, in1=cb, op=ALU.mult)
    P.pool("tensor_tensor", reads=[xn, rbuf], writes=[tb], out=tbv, in0=u2, in1=sb, op=ALU.mult)
    P.dve("tensor_tensor", reads=[ta, tb], writes=[outbuf], out=ov[:, :, :, 0], in0=tav, in1=tbv, op=ALU.subtract)
    ta2 = scr["ra"].next(); tb2 = scr["rb"].next()
    ta2v = ta2[:, :nh * 64].rearrange("p (h i) -> p h i", h=nh); tb2v = tb2[:, :nh * 64].rearrange("p (h i) -> p h i", h=nh)
    P.dve("tensor_tensor", reads=[xn, rbuf], writes=[ta2], out=ta2v, in0=u1, in1=sb, op=ALU.mult)
    P.pool("tensor_tensor", reads=[xn, rbuf], writes=[tb2], out=tb2v, in0=u2, in1=cb, op=ALU.mult)
    P.dve("tensor_tensor", reads=[ta2, tb2], writes=[outbuf], out=ov[:, :, :, 1], in0=ta2v, in1=tb2v, op=ALU.add)


def norm1_all(P, xT_d, mod_d, nw_d, ones_b, hT, ntok_tiles):
    mod_sb, A = mod_vectors(P, mod_d, nw_d, slot_sc=1)
    m = P.mark()
    xs = Rot([P.sbuf([128, KD, 512], F32, f"xs{i}") for i in range(1)])
    sq = P.sbuf([128, KD, 512], BF16, "sqn")
    t1 = Rot([P.sbuf([128, 512], F32, f"t1_{i}") for i in range(2)])
    tmp = Rot([P.sbuf([128, 512], F32, f"tmp{i}") for i in range(3)])
    for (st, n, w) in ntok_tiles:
        x = xs.next()
        P.dma(x[:, :, :n], xT_d[:, :, st:st + n], writes=[x])
        norm_mod_tile(P, lambda k: x[:, k, :n], x, lambda k: hT[:, k, st:st + n], hT, n, w, A, mod_sb, 0,
                      ones_b, sq.ap, sq, t1, tmp)
    P.release(m)
    return mod_sb


def build_c1():
    P = Prog()
    xT_d = P.dram_in("xT", [128, KD, T_ALL])
    mod_d = P.dram_in("mod", [128, 6, KD, 2])
    nw_d = P.dram_in("nw", [128, KD])
    ones_d = P.dram_in("ones", [128, 128])
    w_d = P.dram_in("w", [6, 128, KD, 512])
    qkw_d = P.dram_in("qkw", [128, 2, 512])
    cs_d = P.dram_in("cs", [128, 8, 2, 64])
    q_d = P.dram_out("q", [10, 128, 2048], BF16)
    k_d = P.dram_out("k", [10, 128, 512], BF16)
    v_d = P.dram_out("v", [10, 128, 512], BF16)

    ones_f, ones_b = load_consts(P, ones_d)
    hT = P.sbuf([128, KD, T_ALL], BF16, "hT")
    norm1_all(P, xT_d, mod_d, nw_d, ones_b, hT, TILES)
    qkw = P.sbuf([128, 2, 512], F32, "qkw")
    cs = P.sbuf([128, 8, 2, 64], F32, "cs")
    P.dma(qkw[:], qkw_d, writes=[qkw])
    P.dma(cs[:], cs_d, writes=[cs])
    scr = dict(sq=Rot([P.sbuf([128, 512], F32, f"sq{i}") for i in range(2)]),
               ss=Rot([P.sbuf([128, 8], F32, f"ss{i}") for i in range(2)]),
               xn=Rot([P.sbuf([128, 512], F32, f"xn{i}") for i in range(2)]),
               ra=Rot([P.sbuf([128, 256], F32, f"ra{i}") for i in range(2)]),
               rb=Rot([P.sbuf([128, 256], F32, f"rb{i}") for i in range(2)]))
    ob = Rot([P.sbuf([128, 512], BF16, f"ob{i}") for i in range(3)])
    ws = WStream(P, KD * 512, name="ws", nstage=2, nbf=2)
    for cbk in range(6):
        wb, wv = ws.load(w_d[cbk], KD, 512)
        for tt in range(10):
            ps = P.psum()
            for k in range(KD):
                P.pe("matmul", reads=[hT, *wb], writes=[ps], accum=(k > 0),
                     out=ps[:, :], lhsT=hT[:, k, tt * 128:(tt + 1) * 128], rhs=wv[:, k, :],
                     start=(k == 0), stop=(k == KD - 1))
            o = ob.next()
            if cbk < 5:
                rope = (cs[:, tt, 0, :], cs[:, tt, 1, :], cs) if tt < 8 else None
                qk_norm_tile(P, ps, 4, qkw[:, 0 if cbk < 4 else 1, :], qkw, o[:, :], o, scr, rope=rope)
                dst = q_d[tt, :, cbk * 512:(cbk + 1) * 512] if cbk < 4 else k_d[tt]
            else:
                P.act("activation", reads=[ps], writes=[o], out=o[:, :], in_=ps[:, :], func=AF.Copy)
                dst = v_d[tt]
            P.dma(dst, o[:, :], reads=[o], eng="act")
    return P.finish()


NKB = 66


def build_c2():
    P = Prog()
    qT_d = P.dram_in("qT", [128, 16, T_ALL], BF16)
    kT_d = P.dram_in("kT", [128, 4, NKB * 128], BF16)
    v_d = P.dram_in("v", [128, NKB, 4, 128], BF16)
    ones_d = P.dram_in("ones", [128, 128])
    o_d = P.dram_out("o", [128, 16, T_ALL], BF16)
    ones_f, ones_b = load_consts(P, ones_d)
    qT = P.sbuf([128, 16, T_ALL], BF16, "qT")
    kT = [P.sbuf([128, NKB * 128], BF16, f"kT{g}") for g in range(4)]
    vs = [P.sbuf([128, NKB, 128], BF16, f"v{g}") for g in range(4)]
    P.dma(qT[:], qT_d, writes=[qT])
    for g in range(4):
        P.dma(kT[g][:], kT_d[:, g, :], writes=[kT[g]])
        P.dma(vs[g][:], v_d[:, :, g, :], writes=[vs[g]])
    pT = Rot([P.sbuf([128, 512], BF16, f"pT{i}") for i in range(3)])
    rec = Rot([P.sbuf([128, 512], F32, f"rec{i}") for i in range(2)])
    ob = Rot([P.sbuf([128, 512], BF16, f"ob{i}") for i in range(2)])
    sbanks = Rot([P.bank(i) for i in range(3)])
    obanks = Rot([P.bank(3), P.bank(4)])
    dbanks = Rot([P.bank(5), P.bank(6)])
    scale = 128 ** -0.5
    for qb in range(10):
        nkb = NKB if qb < 8 else 2
        for g in range(4):
            qg = qT[:, 4 * g:4 * g + 4, qb * 128:(qb + 1) * 128]
            O = obanks.next(); Dn = dbanks.next()

            def smm(kb_):
                S_ = sbanks.next()
                P.pe("matmul", reads=[kT[g], qT], writes=[S_],
                     out=S_[:, :].rearrange("p (h q) -> p h q", h=4), lhsT=kT[g][:, kb_ * 128:(kb_ + 1) * 128], rhs=qg,
                     start=True, stop=True)
                return S_
            S = smm(0)
            for kb in range(nkb):
                Sn = smm(kb + 1) if kb + 1 < nkb else None
                p = pT.next()
                P.act("activation", reads=[S], writes=[p], out=p[:, :], in_=S[:, :], func=AF.Exp, scale=scale)
                P.pe("matmul", reads=[vs[g], p], writes=[O], accum=(kb > 0),
                     out=O[:, :], lhsT=vs[g][:, kb, :], rhs=p[:, :], start=(kb == 0), stop=(kb == nkb - 1))
                P.pe("matmul", reads=[ones_b, p], writes=[Dn], accum=(kb > 0),
                     out=Dn[:, :], lhsT=ones_b[:], rhs=p[:, :], start=(kb == 0), stop=(kb == nkb - 1))
                S = Sn
            r = rec.next(); o = ob.next()
            P.dve("reciprocal", reads=[Dn], writes=[r], out=r[:, :], in_=Dn[:, :])
            P.dve("tensor_tensor", reads=[O, r], writes=[o], out=o[:, :], in0=O[:, :], in1=r[:, :], op=ALU.mult)
            P.dma(o_d[:, 4 * g:4 * g + 4, qb * 128:(qb + 1) * 128], o[:, :].rearrange("p (h q) -> p h q", h=4),
                  reads=[o])
    return P.finish()


_PROGS = {}


def _prog(name, builder, *args):
    key = (name,) + args
    if key not in _PROGS:
        _PROGS[key] = builder(*args)
    return _PROGS[key]


def _run(nc, in_maps):
    res = run_bass_kernel_spmd(nc, in_maps, core_ids=list(range(NCORES)))
    return res.results


def fm(a):
    Tn, F = a.shape
    return np.ascontiguousarray(a.T.reshape(F // 128, 128, Tn).transpose(1, 0, 2))


def unfm(a):
    p, C, Tn = a.shape
    return np.ascontiguousarray(a.transpose(2, 1, 0).reshape(Tn, C * 128))


def wblocks(w, colblk):
    K, N = w.shape
    return np.ascontiguousarray(w.reshape(K // 128, 128, N // colblk, colblk).transpose(2, 1, 0, 3))


ONES = np.ones((128, 128), np.float32)


def run_mod(c, c_ctx, w_mod, b_mod):
    cv = np.stack([c.reshape(D), c_ctx.reshape(D)], axis=-1)
    cv = np.ascontiguousarray(cv.reshape(KD, 128, 2).transpose(1, 0, 2))
    ins = []
    for i in range(NCORES):
        l, half = i // 2, i % 2
        w = w_mod[l][:, half * 6144:(half + 1) * 6144]
        b = b_mod[l][half * 6144:(half + 1) * 6144]
        ins.append(dict(cv=cv, w=wblocks(w, 512), b=np.ascontiguousarray(b.reshape(48, 128).T)))
    res = _run(_prog("mod", build_mod), ins)
    modall = np.zeros((4, 6 * D, 2), np.float32)
    for i in range(NCORES):
        l, half = i // 2, i % 2
        mo = res[i]["mo"]
        modall[l, half * 6144:(half + 1) * 6144] = mo.transpose(1, 0, 2).reshape(6144, 2)
    return [np.ascontiguousarray(modall[l].reshape(6, KD, 128, 2).transpose(2, 0, 1, 3)) for l in range(4)]


def vec_fm(v):
    return np.ascontiguousarray(v.reshape(KD, 128).T)


def xT_cores(x, ctx):
    return [fm(np.concatenate([x[i * T_LAT:(i + 1) * T_LAT], ctx], axis=0)) for i in range(NCORES)]


def rope_tables():
    t = np.arange(8192)
    half = 64
    inv = (1.0 / (10000.0 ** (np.arange(0, half, 2, dtype=np.float32) / half))).astype(np.float32)
    ang = np.concatenate([(t // 64).astype(np.float32)[:, None] * inv, (t % 64).astype(np.float32)[:, None] * inv], -1)
    return np.cos(ang).astype(np.float32), np.sin(ang).astype(np.float32)


NCP = 4
NLP = 8192 // NCP // 512


def run_post(u_lat, u_ctx, x, ctx, w_out, w1, w2, mod_l, nw2):
    KM = w_out.shape[0]
    wo = wblocks(w_out, 128)
    w1b = wblocks(w1, 128)
    w2b = np.ascontiguousarray(w2.reshape(2, 32, 128, 16, 128).transpose(3, 0, 2, 1, 4).reshape(32, 128, 32, 128))
    nw = vec_fm(nw2)
    per = NCORES // NCP
    tl = 8192 // NCP
    ins = []
    for p in range(NCP):
        uT = np.concatenate([u_lat[p * per + j] for j in range(per)] + [u_ctx], axis=2)
        xT = fm(np.concatenate([x[p * tl:(p + 1) * tl], ctx], axis=0))
        ins.append(dict(uT=np.ascontiguousarray(uT), xT=xT, wo=wo, w1=w1b, w2=w2b, mod=mod_l, nw=nw, ones=ONES))
    res = run_bass_kernel_spmd(_prog("post", build_post, KM, NLP), ins, core_ids=list(range(NCP))).results
    outs = [unfm(res[p]["xo"]) for p in range(NCP)]
    xn = np.concatenate([o[:tl] for o in outs], axis=0)
    cn = outs[0][tl:]
    return xn, cn


def run_c_layer(x, ctx, mod_l, nw1, w_qkv, q_norm, k_norm, w_out, nw2, w1, w2):
    xTs = xT_cores(x, ctx)
    cos, sin = rope_tables()
    qkw = np.stack([np.tile(q_norm, 4), np.tile(k_norm, 4)], 0)
    qkw = np.ascontiguousarray(np.broadcast_to(qkw[None], (128, 2, 512))).astype(np.float32)
    wb = wblocks(w_qkv, 512)
    nw = vec_fm(nw1)
    ins = []
    for i in range(NCORES):
        cs = np.stack([cos[i * T_LAT:(i + 1) * T_LAT], sin[i * T_LAT:(i + 1) * T_LAT]], 1)
        cs = np.ascontiguousarray(cs.reshape(8, 128, 2, 64).transpose(1, 0, 2, 3))
        ins.append(dict(xT=xTs[i], mod=mod_l, nw=nw, ones=ONES, w=wb, qkw=qkw, cs=cs))
    r1 = _run(_prog("c1", build_c1), ins)
    k_ctx = r1[0]["k"][8:10].reshape(256, 4, 128)
    v_ctx = r1[0]["v"][8:10].reshape(256, 4, 128)
    k_all = np.concatenate([k_ctx] + [r1[i]["k"][0:8].reshape(1024, 4, 128) for i in range(NCORES)], 0)
    v_all = np.concatenate([v_ctx] + [r1[i]["v"][0:8].reshape(1024, 4, 128) for i in range(NCORES)], 0)
    kT = np.ascontiguousarray(k_all.transpose(2, 1, 0))
    vv = np.ascontiguousarray(v_all.reshape(NKB, 128, 4, 128).transpose(1, 0, 2, 3))
    ins2 = []
    for i in range(NCORES):
        q = r1[i]["q"].reshape(T_ALL, 16, 128)
        ins2.append(dict(qT=np.ascontiguousarray(q.transpose(2, 1, 0)), kT=kT, v=vv, ones=ONES))
    r2 = _run(_prog("c2", build_c2), ins2)
    u_lat = [r2[i]["o"][:, :, :T_LAT] for i in range(NCORES)]
    u_ctx = r2[0]["o"][:, :, T_LAT:]
    return run_post2(u_lat, u_ctx, x, ctx, w_out, w1, w2, mod_l, nw2)


T_AB = T_ALL + 4
TILES_AB = TILES + [(T_ALL, 4, 0)]


def build_ab1():
    P = Prog()
    xT_d = P.dram_in("xT", [128, KD, T_AB])
    mod_d = P.dram_in("mod", [128, 6, KD, 2])
    nw_d = P.dram_in("nw", [128, KD])
    ones_d = P.dram_in("ones", [128, 128])
    wfm_d = P.dram_in("wfm", [32, 128, KD, 128])
    wtm_d = P.dram_in("wtm", [10, 128, KD, 512])
    wdt_d = P.dram_in("wdt", [128, KD, 64])
    cw_d = P.dram_in("cw", [128, 32, 5])
    cb_d = P.dram_in("cb", [128, 32])
    edge_d = P.dram_in("edge", [128, 2])
    qkw_d = P.dram_in("qkw", [128, 2, 512])
    xbc_d = P.dram_out("xbc", [32, 128, T_ALL])
    z_d = P.dram_out("z", [10, 128, 2048])
    q_d = P.dram_out("q", [10, 128, 1024], BF16)
    k_d = P.dram_out("k", [10, 128, 1024], BF16)
    v_d = P.dram_out("v", [10, 128, 1024], BF16)
    dt_d = P.dram_out("dt", [10, 128, 64])

    ones_f, ones_b = load_consts(P, ones_d)
    hT = P.sbuf([128, KD, T_AB], BF16, "hT")
    norm1_all(P, xT_d, mod_d, nw_d, ones_b, hT, TILES_AB)
    cw = P.sbuf([128, 32, 5], F32, "cw"); cbias = P.sbuf([128, 32], F32, "cbias")
    edge = P.sbuf([128, 2], F32, "edge"); qkw = P.sbuf([128, 2, 512], F32, "qkw")
    P.dma(cw[:], cw_d, writes=[cw]); P.dma(cbias[:], cb_d, writes=[cbias])
    P.dma(edge[:], edge_d, writes=[edge]); P.dma(qkw[:], qkw_d, writes=[qkw])
    ws = WStream(P, KD * 512, name="ws", nstage=2, nbf=2)
    Ul = Rot([P.sbuf([128, T_LAT + 4], F32, f"Ul{i}") for i in range(2)])
    Uc = Rot([P.sbuf([128, T_CTX + 4], F32, f"Uc{i}") for i in range(2)])
    for u in Uc.bufs:
        P.dve("memset", writes=[u], ap=u[:], constant=0.0)
    accl = Rot([P.sbuf([128, T_LAT], F32, f"accl{i}") for i in range(2)])
    accc = Rot([P.sbuf([128, T_CTX], F32, f"accc{i}") for i in range(2)])
    for cb in range(32):
        wb, wv = ws.load(wfm_d[cb], KD, 128)
        ul = Ul.next(); uc = Uc.next()
        for (st, n, w) in TILES_AB:
            ps = P.psum()
            for k in range(KD):
                P.pe("matmul", reads=[*wb, hT], writes=[ps], accum=(k > 0),
                     out=ps[:, :n], lhsT=wv[:, k, :], rhs=hT[:, k, st:st + n], start=(k == 0), stop=(k == KD - 1))
            if st < T_LAT:
                P.act("activation", reads=[ps], writes=[ul], out=ul[:, 2 + st:2 + st + n], in_=ps[:, :n], func=AF.Copy)
            elif st == T_LAT:
                P.act("activation", reads=[ps], writes=[uc], out=uc[:, 2:2 + n], in_=ps[:, :n], func=AF.Copy)
            else:
                P.dve("tensor_scalar", reads=[ps, edge], writes=[ul], out=ul[:, 0:2], in0=ps[:, 0:2],
                      scalar1=edge[:, 0:1], scalar2=None, op0=ALU.mult)
                P.dve("tensor_scalar", reads=[ps, edge], writes=[ul], out=ul[:, T_LAT + 2:T_LAT + 4], in0=ps[:, 2:4],
                      scalar1=edge[:, 1:2], scalar2=None, op0=ALU.mult)
        for (U, acc, N, off) in ((ul, accl.next(), T_LAT, 0), (uc, accc.next(), T_CTX, T_LAT)):
            P.dve("tensor_scalar", reads=[U, cw, cbias], writes=[acc], out=acc[:, :], in0=U[:, 0:N],
                  scalar1=cw[:, cb, 0:1], scalar2=cbias[:, cb:cb + 1], op0=ALU.mult, op1=ALU.add)
            for kk in range(1, 5):
                P.dve("scalar_tensor_tensor", reads=[U, cw, acc], writes=[acc], out=acc[:, :], in0=U[:, kk:kk + N],
                      scalar=cw[:, cb, kk:kk + 1], in1=acc[:, :], op0=ALU.mult, op1=ALU.add)
            P.act("activation", reads=[acc], writes=[acc], out=acc[:, :], in_=acc[:, :], func=AF.Silu)
            P.dma(xbc_d[cb, :, off:off + N], acc[:, :], reads=[acc], eng="act")
    scr = dict(sq=Rot([P.sbuf([128, 512], F32, f"sq{i}") for i in range(2)]),
               ss=Rot([P.sbuf([128, 8], F32, f"ss{i}") for i in range(2)]),
               xn=Rot([P.sbuf([128, 512], F32, f"xn{i}") for i in range(2)]))
    of = Rot([P.sbuf([128, 512], F32, f"of{i}") for i in range(2)])
    ob = Rot([P.sbuf([128, 512], BF16, f"ob{i}") for i in range(3)])
    for cbk in range(10):
        wb, wv = ws.load(wtm_d[cbk], KD, 512)
        for tt in range(10):
            ps = P.psum()
            for k in range(KD):
                P.pe("matmul", reads=[hT, *wb], writes=[ps], accum=(k > 0),
                     out=ps[:, :], lhsT=hT[:, k, tt * 128:(tt + 1) * 128], rhs=wv[:, k, :],
                     start=(k == 0), stop=(k == KD - 1))
            if cbk < 4:
                o = of.next()
                P.act("activation", reads=[ps], writes=[o], out=o[:, :], in_=ps[:, :], func=AF.Copy)
                P.dma(z_d[tt, :, cbk * 512:(cbk + 1) * 512], o[:, :], reads=[o], eng="act")
            elif cbk < 8:
                o = ob.next()
                wi = 0 if cbk < 6 else 1
                qk_norm_tile(P, ps, 4, qkw[:, wi, :], qkw, o[:, :], o, scr, rope=None)
                dst = (q_d if cbk < 6 else k_d)[tt, :, (cbk % 2) * 512:(cbk % 2 + 1) * 512]
                P.dma(dst, o[:, :], reads=[o], eng="act")
            else:
                o = ob.next()
                P.act("activation", reads=[ps], writes=[o], out=o[:, :], in_=ps[:, :], func=AF.Copy)
                P.dma(v_d[tt, :, (cbk % 2) * 512:(cbk % 2 + 1) * 512], o[:, :], reads=[o], eng="act")
    wb, wv = ws.load(wdt_d, KD, 64)
    for tt in range(10):
        ps = P.psum()
        for k in range(KD):
            P.pe("matmul", reads=[hT, *wb], writes=[ps], accum=(k > 0),
                 out=ps[:, :64], lhsT=hT[:, k, tt * 128:(tt + 1) * 128], rhs=wv[:, k, :],
                 start=(k == 0), stop=(k == KD - 1))
        o = of.next()
        P.act("activation", reads=[ps], writes=[o], out=o[:, :64], in_=ps[:, :64], func=AF.Copy)
        P.dma(dt_d[tt], o[:, :64], reads=[o], eng="act")
    return P.finish()


def run_ab1(x, ctx, mod_l, nw1, w_in, conv_w, conv_b, q_norm, k_norm):
    wfm = wblocks(w_in[:, 2048:6144], 128)
    wtm = wblocks(np.concatenate([w_in[:, 0:2048], w_in[:, 6208:9280]], axis=1), 512)
    wdt = np.ascontiguousarray(w_in[:, 6144:6208].reshape(KD, 128, 64).transpose(1, 0, 2))
    cw = np.ascontiguousarray(conv_w.reshape(5, 32, 128).transpose(2, 1, 0))
    cb = np.ascontiguousarray(conv_b.reshape(32, 128).T)
    qkw = np.stack([np.tile(q_norm, 4), np.tile(k_norm, 4)], 0)
    qkw = np.ascontiguousarray(np.broadcast_to(qkw[None], (128, 2, 512))).astype(np.float32)
    nw = vec_fm(nw1)
    xpad = np.concatenate([np.zeros((2, D), np.float32), x, np.zeros((2, D), np.float32)], 0)
    ins = []
    for i in range(NCORES):
        lo = i * T_LAT
        toks = np.concatenate([x[lo:lo + T_LAT], ctx, xpad[lo:lo + 2], xpad[lo + T_LAT + 2:lo + T_LAT + 4]], 0)
        edge = np.zeros((128, 2), np.float32)
        edge[:, 0] = 1.0 if i > 0 else 0.0
        edge[:, 1] = 1.0 if i < NCORES - 1 else 0.0
        ins.append(dict(xT=fm(toks), mod=mod_l, nw=nw, ones=ONES, wfm=wfm, wtm=wtm, wdt=wdt, cw=cw, cb=cb,
                        edge=edge, qkw=qkw))
    return _run(_prog("ab1", build_ab1), ins)


def build_ssd(mode):
    full = (mode == "B")
    P = Prog()
    xs_d = P.dram_in("xs", [10, 128, 2048])
    bt_d = P.dram_in("btok", [10, 128, 1024])
    BT_d = P.dram_in("BT", [10, 128, 8, 128])
    CT_d = P.dram_in("CT", [10, 128, 8, 128])
    dt_d = P.dram_in("dt", [10, 128, 64])
    par_d = P.dram_in("par", [128, 3, 64])
    tri_d = P.dram_in("tri", [128, 2, 128])
    nm_d = P.dram_in("negmask", [128, 2, 128])
    id_d = P.dram_in("ident", [128, 128])
    ones_d = P.dram_in("ones", [128, 128])
    if full:
        Fl_d = P.dram_in("Flist", [2, 7, 128, 2048])
        Tl_d = P.dram_in("Tlist", [2, 7, 128, 32])
        cF_d = P.dram_in("ctxF", [2, 128, 2048])
        z_d = P.dram_in("z", [10, 128, 2048])
        gw_d = P.dram_in("gw", [128, 2048])
        y_d = P.dram_out("yd", [2, 10, 128, 2048])
        g_d = P.dram_out("g", [10, 128, 2048], BF16)
        ybuf = Buf(y_d, "y_dram")
    else:
        F_d = P.dram_out("F", [2, 128, 2048])
        T_d = P.dram_out("T", [2, 128, 32])
        cFo_d = P.dram_out("ctxF", [2, 128, 2048])

    ones_f, ones_b = load_consts(P, ones_d)
    par = P.sbuf([128, 3, 64], F32, "par"); tri = P.sbuf([128, 2, 128], F32, "tri")
    nm = P.sbuf([128, 2, 128], F32, "nm"); ident = P.sbuf([128, 128], F32, "ident")
    for sb_, d_ in ((par, par_d), (tri, tri_d), (nm, nm_d), (ident, id_d)):
        P.dma(sb_[:], d_, writes=[sb_])
    abc = P.sbuf([128, 64], F32, "abc")
    P.act("activation", reads=[par], writes=[abc], out=abc[:], in_=par[:, 1, :], func=AF.Exp)
    P.dve("tensor_scalar", reads=[abc], writes=[abc], out=abc[:], in0=abc[:], scalar1=-1.0, scalar2=None, op0=ALU.mult)
    dsum = P.sbuf([128, 32], F32, "dsum")
    P.dve("tensor_tensor", reads=[par], writes=[dsum], out=dsum[:], in0=par[:, 2, 0:32], in1=par[:, 2, 32:64], op=ALU.add)

    S = [P.sbuf([128, 2048], F32, f"S{d}") for d in range(2)]
    Sb = [P.sbuf([128, 2048], BF16, f"Sb{d}") for d in range(2)]
    tot_acc = [P.sbuf([128, 32], F32, f"tacc{d}") for d in range(2)]
    m_units = P.mark()
    sets = []
    for par_ in range(2):
        b_ = dict(xs=P.sbuf([128, 2048], F32, f"xs{par_}"), xd=P.sbuf([128, 2048], BF16, f"xd{par_}"),
                  xdd=P.sbuf([128, 2048], BF16, f"xdd{par_}"), btf=P.sbuf([128, 1024], F32, f"btf{par_}"),
                  btb=P.sbuf([128, 1024], BF16, f"btb{par_}"), BTf=P.sbuf([128, 8, 128], F32, f"BTf{par_}"),
                  BTb=P.sbuf([128, 8, 128], BF16, f"BTb{par_}"), CTf=P.sbuf([128, 8, 128], F32, f"CTf{par_}"),
                  CTb=P.sbuf([128, 8, 128], BF16, f"CTb{par_}"), dbc=P.sbuf([128, 32, 128], F32, f"dbc{par_}"))
        for n_ in ("dtr", "x0", "mx", "na", "e", "dt", "dtA", "la", "nla", "ela", "tot", "dend", "cdec", "dtd"):
            b_[n_] = P.sbuf([128, 32], F32, f"{n_}{par_}")
        sets.append(b_)
    fb = P.sbuf([128, 2048], F32, "foldbuf")
    Lt = Rot([P.sbuf([128, 512], F32, f"Lt{i}") for i in range(2)])
    Mt = Rot([P.sbuf([128, 512], BF16, f"Mt{i}") for i in range(2)])
    ysb = P.sbuf([128, 2048], F32, "ysb")
    tmpg = Rot([P.sbuf([128, 256], F32, f"tg{i}") for i in range(2)])
    tmps = Rot([P.sbuf([128, 256], F32, f"ts{i}") for i in range(2)])
    B_lt = P.bank(0)
    B_cb = Rot([P.bank(1), P.bank(2)]); B_arg = Rot([P.bank(3), P.bank(4)]); B_yd = Rot([P.bank(5), P.bank(6)])
    B_os = P.bank(7)

    def bc(ap32, g):
        return ap32[:, 4 * g:4 * g + 4].unsqueeze(2).to_broadcast([128, 4, 64])

    def v3(ap, g):
        return ap[:, g * 256:(g + 1) * 256].rearrange("p (k q) -> p k q", k=4)

    def prologue(c, d, want_y, B):
        sm = B
        xs, xd, xdd, btf, btb, BTf, BTb, CTf, CTb, dbc = (B[k_] for k_ in
            ("xs", "xd", "xdd", "btf", "btb", "BTf", "BTb", "CTf", "CTb", "dbc"))
        P.dma(xs[:], xs_d[c], writes=[xs])
        P.dma(btf[:], bt_d[c], writes=[btf])
        P.dma(BTf[:], BT_d[c], writes=[BTf])
        P.dma(CTf[:], CT_d[c], writes=[CTf])
        P.dma(sm["dtr"][:], dt_d[c, :, d * 32:(d + 1) * 32], writes=[sm["dtr"]])
        P.pool("tensor_copy", reads=[btf], writes=[btb], out=btb[:], in_=btf[:])
        P.pool("tensor_copy", reads=[BTf], writes=[BTb], out=BTb[:], in_=BTf[:])
        P.pool("tensor_copy", reads=[CTf], writes=[CTb], out=CTb[:], in_=CTf[:])
        dsl = slice(d * 32, (d + 1) * 32)
        P.dve("tensor_tensor", reads=[sm["dtr"], par], writes=[sm["x0"]], out=sm["x0"][:], in0=sm["dtr"][:],
              in1=par[:, 0, dsl], op=ALU.add)
        P.dve("tensor_scalar", reads=[sm["x0"]], writes=[sm["mx"]], out=sm["mx"][:], in0=sm["x0"][:], scalar1=0.0,
              scalar2=None, op0=ALU.max)
        P.dve("scalar_tensor_tensor", reads=[sm["mx"], sm["x0"]], writes=[sm["na"]], out=sm["na"][:], in0=sm["mx"][:],
              scalar=-2.0, in1=sm["x0"][:], op0=ALU.mult, op1=ALU.add)
        P.act("activation", reads=[sm["na"]], writes=[sm["e"]], out=sm["e"][:], in_=sm["na"][:], func=AF.Exp)
        P.dve("tensor_scalar", reads=[sm["e"]], writes=[sm["e"]], out=sm["e"][:], in0=sm["e"][:], scalar1=1.0,
              scalar2=None, op0=ALU.add)
        P.act("activation", reads=[sm["e"]], writes=[sm["e"]], out=sm["e"][:], in_=sm["e"][:], func=AF.Ln)
        P.dve("tensor_tensor", reads=[sm["mx"], sm["e"]], writes=[sm["dt"]], out=sm["dt"][:], in0=sm["mx"][:],
              in1=sm["e"][:], op=ALU.add)
        P.dve("tensor_tensor", reads=[sm["dt"], abc], writes=[sm["dtA"]], out=sm["dtA"][:], in0=sm["dt"][:],
              in1=abc[:, dsl], op=ALU.mult)
        P.pe("matmul", reads=[tri, sm["dtA"]], writes=[B_lt], out=B_lt[:, 0:32], lhsT=tri[:, d, :], rhs=sm["dtA"][:],
             start=True, stop=True)
        P.pe("matmul", reads=[ones_f, sm["dtA"]], writes=[B_lt], accum=True, out=B_lt[:, 32:64], lhsT=ones_f[:],
             rhs=sm["dtA"][:], start=True, stop=True)
        P.dve("tensor_copy", reads=[B_lt], writes=[sm["la"]], out=sm["la"][:], in_=B_lt[:, 0:32])
        P.dve("tensor_copy", reads=[B_lt], writes=[sm["tot"]], out=sm["tot"][:], in_=B_lt[:, 32:64])
        P.dve("tensor_scalar", reads=[sm["la"]], writes=[sm["nla"]], out=sm["nla"][:], in0=sm["la"][:], scalar1=-1.0,
              scalar2=None, op0=ALU.mult)
        P.act("activation", reads=[sm["la"]], writes=[sm["ela"]], out=sm["ela"][:], in_=sm["la"][:], func=AF.Exp)
        P.dve("tensor_tensor", reads=[sm["tot"], sm["la"]], writes=[sm["dend"]], out=sm["dend"][:], in0=sm["tot"][:],
              in1=sm["la"][:], op=ALU.subtract)
        P.act("activation", reads=[sm["dend"]], writes=[sm["dend"]], out=sm["dend"][:], in_=sm["dend"][:], func=AF.Exp)
        P.act("activation", reads=[sm["tot"]], writes=[sm["cdec"]], out=sm["cdec"][:], in_=sm["tot"][:], func=AF.Exp)
        P.dve("tensor_tensor", reads=[sm["dt"], sm["dend"]], writes=[sm["dtd"]], out=sm["dtd"][:], in0=sm["dt"][:],
              in1=sm["dend"][:], op=ALU.mult)
        xs3 = xs[:, :].rearrange("p (h q) -> p h q", h=32)
        P.dve("tensor_tensor", reads=[xs, sm["dtd"]], writes=[xdd], out=xdd[:, :].rearrange("p (h q) -> p h q", h=32),
              in0=xs3, in1=sm["dtd"][:, :].unsqueeze(2).to_broadcast([128, 32, 64]), op=ALU.mult)
        if want_y:
            P.pool("tensor_tensor", reads=[xs, sm["dt"]], writes=[xd], out=xd[:, :].rearrange("p (h q) -> p h q", h=32),
                   in0=xs3, in1=sm["dt"][:, :].unsqueeze(2).to_broadcast([128, 32, 64]), op=ALU.mult)
            P.pool("tensor_copy", reads=[sm["dtA"]], writes=[dbc], out=dbc[:],
                   in_=sm["dtA"][:, :].unsqueeze(2).to_broadcast([128, 32, 128]))

    def body(c, d, want_y, B, hook):
        sm = B
        xs, xd, xdd, btb, BTb, CTb, dbc = (B[k_] for k_ in ("xs", "xd", "xdd", "btb", "BTb", "CTb", "dbc"))
        P.dve("tensor_tensor", reads=[tot_acc[d], sm["tot"]], writes=[tot_acc[d]], out=tot_acc[d][:], in0=tot_acc[d][:],
              in1=sm["tot"][:], op=ALU.add)

        def arg_group(g):
            A_ = B_arg.next()
            for k in range(4):
                h = 4 * g + k
                P.pe("matmul", reads=[dbc, tri], writes=[A_], out=A_[:, k * 128:(k + 1) * 128], lhsT=dbc[:, h, :],
                     rhs=tri[:, d, :], start=True, stop=False)
                P.pe("matmul", reads=[ident, nm], writes=[A_], accum=True, out=A_[:, k * 128:(k + 1) * 128],
                     lhsT=ident[:], rhs=nm[:, d, :], start=False, stop=True)
            return A_

        def cb_mm(g):
            C_ = B_cb.next()
            P.pe("matmul", reads=[BTb, CTb], writes=[C_], out=C_[:, 0:128], lhsT=BTb[:, g, :], rhs=CTb[:, g, :],
                 start=True, stop=True)
            return C_
        if want_y:
            cbs = {0: cb_mm(0)}
            A_next = arg_group(0)
        for g in range(8):
            if want_y:
                A_ = A_next
                L_ = Lt.next(); M_ = Mt.next()
                for k in range(4):
                    h = 4 * g + k
                    P.act("activation", reads=[A_, sm["nla"]], writes=[L_], out=L_[:, k * 128:(k + 1) * 128],
                          in_=A_[:, k * 128:(k + 1) * 128], func=AF.Exp, bias=sm["nla"][:, h:h + 1], scale=1.0)
                P.dve("tensor_tensor", reads=[L_, cbs[g]], writes=[M_], out=M_[:, :].rearrange("p (k l) -> p k l", k=4),
                      in0=L_[:, :].rearrange("p (k l) -> p k l", k=4),
                      in1=cbs[g][:, 0:128].unsqueeze(1).to_broadcast([128, 4, 128]), op=ALU.mult)
                if g + 1 < 8:
                    cbs[g + 1] = cb_mm(g + 1)
                    A_next = arg_group(g + 1)
                Yd = B_yd.next()
                for k in range(4):
                    h = 4 * g + k
                    P.pe("matmul", reads=[M_, xd], writes=[Yd], out=Yd[:, k * 64:(k + 1) * 64],
                         lhsT=M_[:, k * 128:(k + 1) * 128], rhs=xd[:, h * 64:(h + 1) * 64], start=True, stop=True)
                P.pe("matmul", reads=[CTb, Sb[d]], writes=[B_os], out=B_os[:, 0:256], lhsT=CTb[:, g, :],
                     rhs=Sb[d][:, g * 256:(g + 1) * 256], start=True, stop=True)
                t_ = tmpg.next()
                t3 = t_[:, :].rearrange("p (k q) -> p k q", k=4)
                P.dve("tensor_tensor", reads=[B_os, sm["ela"]], writes=[t_], out=t3,
                      in0=B_os[:, 0:256].rearrange("p (k q) -> p k q", k=4), in1=bc(sm["ela"], g), op=ALU.mult)
                P.dve("tensor_tensor", reads=[t_, Yd], writes=[ysb], out=ysb[:, g * 256:(g + 1) * 256], in0=t_[:, :],
                      in1=Yd[:, 0:256], op=ALU.add)
                if d == 0:
                    t2 = tmps.next()
                    P.pool("tensor_tensor", reads=[xs, dsum], writes=[t2], out=t2[:, :].rearrange("p (k q) -> p k q", k=4),
                           in0=v3(xs, g), in1=bc(dsum, g), op=ALU.mult)
                    P.pool("tensor_tensor", reads=[t2, ysb], writes=[ysb], out=ysb[:, g * 256:(g + 1) * 256],
                           in0=ysb[:, g * 256:(g + 1) * 256], in1=t2[:, :], op=ALU.add)
            P.pe("matmul", reads=[btb, xdd], writes=[B_os], accum=True, out=B_os[:, 256:512],
                 lhsT=btb[:, g * 128:(g + 1) * 128], rhs=xdd[:, g * 256:(g + 1) * 256], start=True, stop=True)
            P.dve("tensor_tensor", reads=[S[d], sm["cdec"]], writes=[S[d]], out=v3(S[d], g), in0=v3(S[d], g),
                  in1=bc(sm["cdec"], g), op=ALU.mult)
            P.dve("tensor_tensor", reads=[S[d], B_os], writes=[S[d]], out=S[d][:, g * 256:(g + 1) * 256],
                  in0=S[d][:, g * 256:(g + 1) * 256], in1=B_os[:, 256:512], op=ALU.add)
            if full:
                P.act("activation", reads=[S[d]], writes=[Sb[d]], out=Sb[d][:, g * 256:(g + 1) * 256],
                      in_=S[d][:, g * 256:(g + 1) * 256], func=AF.Copy)
            if g == 3:
                hook()
        if want_y:
            P.dma(y_d[d, c], ysb[:], reads=[ysb], writes=[ybuf])

    def zero_state(d):
        P.dve("memset", writes=[S[d]], ap=S[d][:], constant=0.0)
        P.dve("memset", writes=[Sb[d]], ap=Sb[d][:], constant=0.0)
        P.dve("memset", writes=[tot_acc[d]], ap=tot_acc[d][:], constant=0.0)

    def fold(d):
        P.dma(S[d][:], cF_d[d], writes=[S[d]])
        tl = P.sbuf([128, 7, 32], F32, f"tl{d}")
        P.dma(tl[:], Tl_d[d].rearrange("j p h -> p j h"), writes=[tl])
        P.act("activation", reads=[tl], writes=[tl], out=tl[:], in_=tl[:], func=AF.Exp)
        for j in range(7):
            P.dma(fb[:], Fl_d[d, j], writes=[fb])
            P.dve("tensor_tensor", reads=[S[d], tl], writes=[S[d]], out=S[d][:, :].rearrange("p (h q) -> p h q", h=32),
                  in0=S[d][:, :].rearrange("p (h q) -> p h q", h=32),
                  in1=tl[:, j, :].unsqueeze(2).to_broadcast([128, 32, 64]), op=ALU.mult)
            P.dve("tensor_tensor", reads=[S[d], fb], writes=[S[d]], out=S[d][:], in0=S[d][:], in1=fb[:], op=ALU.add)
        P.act("activation", reads=[S[d]], writes=[Sb[d]], out=Sb[d][:], in_=S[d][:], func=AF.Copy)

    order = {0: (list(range(8)), [8, 9]), 1: (list(range(7, -1, -1)), [9, 8])}
    steps = []
    for d in range(2):
        lat_order, ctx_order = order[d]
        steps.append(("zero", d))
        steps += [("unit", c, d, full) for c in ctx_order]
        if not full:
            steps += [("save_ctx", d), ("zero", d)]
        else:
            steps.append(("fold", d))
        steps += [("unit", c, d, full) for c in lat_order]
        if not full:
            steps.append(("save_F", d))
    unit_pos = [i for i, st_ in enumerate(steps) if st_[0] == "unit"]
    ordinal = {p_: k_ for k_, p_ in enumerate(unit_pos)}
    done_pro = set()

    def ensure_pro(i):
        if i is None or i in done_pro:
            return
        _, c, d, wy = steps[i]
        prologue(c, d, wy, sets[ordinal[i] % 2])
        done_pro.add(i)
    for i, st_ in enumerate(steps):
        if st_[0] == "unit":
            _, c, d, wy = st_
            ensure_pro(i)
            k_ = ordinal[i]
            nxt = unit_pos[k_ + 1] if k_ + 1 < len(unit_pos) else None
            body(c, d, wy, sets[k_ % 2], lambda nxt=nxt: ensure_pro(nxt))
        elif st_[0] == "zero":
            zero_state(st_[1])
        elif st_[0] == "fold":
            fold(st_[1])
        elif st_[0] == "save_ctx":
            P.dma(cFo_d[st_[1]], S[st_[1]][:], reads=[S[st_[1]]])
        elif st_[0] == "save_F":
            P.dma(F_d[st_[1]], S[st_[1]][:], reads=[S[st_[1]]])
            P.dma(T_d[st_[1]], tot_acc[st_[1]][:], reads=[tot_acc[st_[1]]])
    if full:
        P.release(m_units)
        xs = P.sbuf([128, 2048], F32, "gxs")
        gw = P.sbuf([128, 2048], F32, "gw")
        P.dma(gw[:], gw_d, writes=[gw])
        zt = P.sbuf([128, 2048], F32, "zt"); y2 = P.sbuf([128, 2048], F32, "y2")
        gss = P.sbuf([128, 8], F32, "gss"); go = P.sbuf([128, 2048], BF16, "go")
        for c in range(10):
            P.dma(xs[:], y_d[0, c], reads=[ybuf], writes=[xs])
            P.dma(y2[:], y_d[1, c], reads=[ybuf], writes=[y2])
            P.dma(zt[:], z_d[c], writes=[zt])
            P.act("activation", reads=[zt], writes=[zt], out=zt[:], in_=zt[:], func=AF.Silu)
            P.dve("tensor_tensor", reads=[xs, y2], writes=[xs], out=xs[:], in0=xs[:], in1=y2[:], op=ALU.add)
            P.dve("tensor_tensor", reads=[xs, zt], writes=[xs], out=xs[:], in0=xs[:], in1=zt[:], op=ALU.mult)
            P.act("activation", reads=[xs], writes=[y2], out=y2[:], in_=xs[:], func=AF.Square)
            P.dve("tensor_reduce", reads=[y2], writes=[gss], out=gss[:], in_=y2[:, :].rearrange("p (g q) -> p g q", g=8),
                  axis=AX.X, op=ALU.add)
            P.dve("tensor_scalar", reads=[gss], writes=[gss], out=gss[:], in0=gss[:], scalar1=1.0 / 256, scalar2=EPS,
                  op0=ALU.mult, op1=ALU.add)
            P.act("activation", reads=[gss], writes=[gss], out=gss[:], in_=gss[:], func=AF.Sqrt)
            P.dve("reciprocal", reads=[gss], writes=[gss], out=gss[:], in_=gss[:])
            P.dve("tensor_tensor", reads=[xs, gss], writes=[xs], out=xs[:, :].rearrange("p (g q) -> p g q", g=8),
                  in0=xs[:, :].rearrange("p (g q) -> p g q", g=8), in1=gss[:, :].unsqueeze(2).to_broadcast([128, 8, 256]),
                  op=ALU.mult)
            P.pool("tensor_tensor", reads=[xs, gw], writes=[go], out=go[:], in0=xs[:], in1=gw[:], op=ALU.mult)
            P.dma(g_d[c], go[:], reads=[go])
    return P.finish()


def _ssd_consts():
    t = np.arange(128)
    tri = np.stack([(t[:, None] <= t[None, :]), (t[:, None] >= t[None, :])], 1).astype(np.float32)
    valid = np.stack([(t[None, :] >= t[:, None]), (t[None, :] <= t[:, None])], 1)
    negmask = np.where(valid, 0.0, -30000.0).astype(np.float32)
    return np.ascontiguousarray(tri), np.ascontiguousarray(negmask), np.eye(128, dtype=np.float32)


def run_ssd(r1, dt_bias, a_log, d_skip, norm_w):
    tri, negmask, ident = _ssd_consts()
    par = np.stack([dt_bias.reshape(64), a_log.reshape(64), d_skip.reshape(64)], 0)
    par = np.ascontiguousarray(np.broadcast_to(par[None], (128, 3, 64))).astype(np.float32)
    base = []
    for i in range(NCORES):
        xbc = r1[i]["xbc"]
        xbc_t = np.ascontiguousarray(xbc.reshape(4096, T_ALL).T)
        xs = xbc_t[:, 0:2048].reshape(10, 128, 2048)
        btok = xbc_t[:, 2048:3072].reshape(10, 128, 1024)
        BT = xbc[16:24].reshape(8, 128, 10, 128).transpose(2, 1, 0, 3)
        CT = xbc[24:32].reshape(8, 128, 10, 128).transpose(2, 1, 0, 3)
        base.append(dict(xs=np.ascontiguousarray(xs), btok=np.ascontiguousarray(btok), BT=np.ascontiguousarray(BT),
                         CT=np.ascontiguousarray(CT), dt=r1[i]["dt"], par=par, tri=tri, negmask=negmask, ident=ident,
                         ones=ONES))
    ra = _run(_prog("ssdA", build_ssd, "A"), base)
    ctxF = ra[0]["ctxF"]
    gw = np.ascontiguousarray(np.broadcast_to(norm_w[None], (128, 2048))).astype(np.float32)
    insb = []
    for i in range(NCORES):
        Fl = np.zeros((2, 7, 128, 2048), np.float32)
        Tl = np.zeros((2, 7, 128, 32), np.float32)
        for j, cj in enumerate(range(0, i)):
            Fl[0, j] = ra[cj]["F"][0]; Tl[0, j] = ra[cj]["T"][0]
        for j, cj in enumerate(range(NCORES - 1, i, -1)):
            Fl[1, j] = ra[cj]["F"][1]; Tl[1, j] = ra[cj]["T"][1]
        d = dict(base[i]); d.update(Flist=Fl, Tlist=Tl, ctxF=ctxF, z=r1[i]["z"], gw=gw)
        insb.append(d)
    rb = _run(_prog("ssdB", build_ssd, "B"), insb)
    return ra, rb


def build_na():
    P = Prog()
    qT_d = P.dram_in("qT", [128, 8, T_ALL], BF16)
    kT_d = P.dram_in("kT", [128, 8, 2048], BF16)
    ve_d = P.dram_in("ve", [128, 16, 8, 128], BF16)
    vo_d = P.dram_in("vo", [128, 16, 8, 128], BF16)
    kcT_d = P.dram_in("kcT", [128, 8, 256], BF16)
    vc_d = P.dram_in("vc", [128, 2, 8, 128], BF16)
    tt_d = P.dram_in("tt", [128, 8, 8, 64])
    vm_d = P.dram_in("vm", [128, 16, 8])
    ones_d = P.dram_in("ones", [128, 128])
    o_d = P.dram_out("o", [128, 8, T_ALL], BF16)
    ones_f, ones_b = load_consts(P, ones_d)
    qT = P.sbuf([128, 8, T_ALL], BF16, "qT"); kT = P.sbuf([128, 8, 2048], BF16, "kT")
    ve = P.sbuf([128, 16, 8, 128], BF16, "ve"); vo = P.sbuf([128, 16, 8, 128], BF16, "vo")
    kcT = P.sbuf([128, 8, 256], BF16, "kcT"); vc = P.sbuf([128, 2, 8, 128], BF16, "vc")
    TT = P.sbuf([128, 8, 8, 64], F32, "TT"); vm = P.sbuf([128, 16, 8], F32, "vm")
    oT = P.sbuf([128, 8, T_ALL], BF16, "oT")
    for sb_, d_ in ((qT, qT_d), (kT, kT_d), (ve, ve_d), (vo, vo_d), (kcT, kcT_d), (vc, vc_d), (TT, tt_d), (vm, vm_d)):
        P.dma(sb_[:], d_, writes=[sb_])
    tb = Rot([P.sbuf([128, 512], F32, f"tb{i}") for i in range(2)])
    pw = Rot([P.sbuf([128, 512], BF16, f"pw{i}") for i in range(2)])
    pc = Rot([P.sbuf([128, 128], BF16, f"pc{i}") for i in range(2)])
    rec = Rot([P.sbuf([128, 128], F32, f"rec{i}") for i in range(2)])
    BA = Rot([P.bank(0), P.bank(1)]); BB = Rot([P.bank(2), P.bank(3)])
    BO = Rot([P.bank(4), P.bank(5)]); BD = Rot([P.bank(6), P.bank(7)])
    scale = 128 ** -0.5
    for lr in range(16):
        for h in range(8):
            q = qT[:, h, lr * 64:(lr + 1) * 64]
            A_ = BA.next(); B_ = BB.next(); O = BO.next(); Dn = BD.next()
            for pb in range(8):
                off = (lr + 2 * pb) * 64
                P.pe("matmul", reads=[kT, qT], writes=[A_], out=A_[:, pb * 64:(pb + 1) * 64], lhsT=kT[:, h, off:off + 128],
                     rhs=q, start=True, stop=True)
            for cb in range(2):
                P.pe("matmul", reads=[kcT, qT], writes=[B_], out=B_[:, cb * 64:(cb + 1) * 64],
                     lhsT=kcT[:, h, cb * 128:(cb + 1) * 128], rhs=q, start=True, stop=True)
            t_ = tb.next(); p_ = pw.next(); c_ = pc.next()
            P.dve("scalar_tensor_tensor", reads=[A_, TT], writes=[t_], out=t_[:, :], in0=A_[:, :], scalar=scale,
                  in1=TT[:, h, :, :].rearrange("p a b -> p (a b)"), op0=ALU.mult, op1=ALU.add)
            P.pool("tensor_tensor", reads=[t_, vm], writes=[t_], out=t_[:, :].rearrange("p (a b) -> p a b", a=8),
                   in0=t_[:, :].rearrange("p (a b) -> p a b", a=8),
                   in1=vm[:, lr, :].unsqueeze(2).to_broadcast([128, 8, 64]), op=ALU.add)
            P.act("activation", reads=[t_], writes=[p_], out=p_[:, :], in_=t_[:, :], func=AF.Exp)
            P.act("activation", reads=[B_], writes=[c_], out=c_[:, :], in_=B_[:, 0:128], func=AF.Exp, scale=scale)
            for pb in range(8):
                row = lr + 2 * pb
                vsrc = ve[:, row // 2, h, :] if row % 2 == 0 else vo[:, row // 2, h, :]
                vbuf = ve if row % 2 == 0 else vo
                P.pe("matmul", reads=[vbuf, p_], writes=[O], accum=(pb > 0), out=O[:, 0:64], lhsT=vsrc,
                     rhs=p_[:, pb * 64:(pb + 1) * 64], start=(pb == 0), stop=False)
            for cb in range(2):
                P.pe("matmul", reads=[vc, c_], writes=[O], accum=True, out=O[:, 0:64], lhsT=vc[:, cb, h, :],
                     rhs=c_[:, cb * 64:(cb + 1) * 64], start=False, stop=(cb == 1))
            for pb in range(8):
                P.pe("matmul", reads=[ones_b, p_], writes=[Dn], accum=(pb > 0), out=Dn[:, 0:64], lhsT=ones_b[:],
                     rhs=p_[:, pb * 64:(pb + 1) * 64], start=(pb == 0), stop=False)
            for cb in range(2):
                P.pe("matmul", reads=[ones_b, c_], writes=[Dn], accum=True, out=Dn[:, 0:64], lhsT=ones_b[:],
                     rhs=c_[:, cb * 64:(cb + 1) * 64], start=False, stop=(cb == 1))
            r_ = rec.next()
            P.dve("reciprocal", reads=[Dn], writes=[r_], out=r_[:, 0:64], in_=Dn[:, 0:64])
            P.dve("tensor_tensor", reads=[O, r_], writes=[oT], out=oT[:, h, lr * 64:(lr + 1) * 64], in0=O[:, 0:64],
                  in1=r_[:, 0:64], op=ALU.mult)
    for qb in range(2):
        for h in range(8):
            q = qT[:, h, T_LAT + qb * 128:T_LAT + (qb + 1) * 128]
            B_ = BB.next(); O = BO.next(); Dn = BD.next()
            for cb in range(2):
                P.pe("matmul", reads=[kcT, qT], writes=[B_], out=B_[:, cb * 128:(cb + 1) * 128],
                     lhsT=kcT[:, h, cb * 128:(cb + 1) * 128], rhs=q, start=True, stop=True)
            p_ = pw.next()
            P.act("activation", reads=[B_], writes=[p_], out=p_[:, 0:256], in_=B_[:, 0:256], func=AF.Exp, scale=scale)
            for cb in range(2):
                P.pe("matmul", reads=[vc, p_], writes=[O], accum=(cb > 0), out=O[:, 0:128], lhsT=vc[:, cb, h, :],
                     rhs=p_[:, cb * 128:(cb + 1) * 128], start=(cb == 0), stop=(cb == 1))
            for cb in range(2):
                P.pe("matmul", reads=[ones_b, p_], writes=[Dn], accum=(cb > 0), out=Dn[:, 0:128], lhsT=ones_b[:],
                     rhs=p_[:, cb * 128:(cb + 1) * 128], start=(cb == 0), stop=(cb == 1))
            r_ = rec.next()
            P.dve("reciprocal", reads=[Dn], writes=[r_], out=r_[:, 0:128], in_=Dn[:, 0:128])
            P.dve("tensor_tensor", reads=[O, r_], writes=[oT], out=oT[:, h, T_LAT + qb * 128:T_LAT + (qb + 1) * 128],
                  in0=O[:, 0:128], in1=r_[:, 0:128], op=ALU.mult)
    P.dma(o_d, oT[:], reads=[oT])
    return P.finish()


def run_na(r1, rpb):
    a = np.arange(64)
    c0 = np.clip(a - 8, 0, 48)
    b = np.arange(64)
    colok = (b[:, None] >= c0[None, :]) & (b[:, None] < c0[None, :] + 16)
    dc = np.clip(b[:, None] - a[None, :], -15, 15) + 15
    TT = np.full((2, 64, 8, 8, 64), -30000.0, np.float32)
    for pb in range(8):
        for jj in range(2):
            dr = 2 * pb + jj - 1
            if 0 <= dr < 15:
                vals = rpb[:, dr][:, dc]
                TT[jj, :, :, pb, :] = np.where(colok[None], vals, np.float32(-30000.0)).transpose(1, 0, 2)
    TT = np.ascontiguousarray(TT.reshape(128, 8, 8, 64))
    k_lat = np.concatenate([r1[i]["k"][0:8].reshape(T_LAT, 8, 128) for i in range(NCORES)], 0)
    v_lat = np.concatenate([r1[i]["v"][0:8].reshape(T_LAT, 8, 128) for i in range(NCORES)], 0)
    k_ctx = r1[0]["k"][8:10].reshape(T_CTX, 8, 128)
    v_ctx = r1[0]["v"][8:10].reshape(T_CTX, 8, 128)
    kcT = np.ascontiguousarray(k_ctx.transpose(2, 1, 0))
    vc = np.ascontiguousarray(v_ctx.reshape(2, 128, 8, 128).transpose(1, 0, 2, 3))
    zk = np.zeros((64, 8, 128), k_lat.dtype)
    ins = []
    for i in range(NCORES):
        base = 16 * i - 8
        kw = []; vw = []
        for v in range(33):
            row = base + v
            if 0 <= row < 128:
                kw.append(k_lat[row * 64:(row + 1) * 64]); vw.append(v_lat[row * 64:(row + 1) * 64])
            else:
                kw.append(zk); vw.append(zk)
        kwin = np.concatenate(kw[:32], 0)
        kT = np.ascontiguousarray(kwin.transpose(2, 1, 0))
        ve = np.stack([np.concatenate([vw[2 * j], vw[2 * j + 1]], 0) for j in range(16)], 1)
        vo = np.stack([np.concatenate([vw[2 * j + 1], vw[2 * j + 2]], 0) for j in range(16)], 1)
        vm = np.full((2, 64, 16, 8), -30000.0, np.float32)
        for lr in range(16):
            r = 16 * i + lr
            rs = min(max(r - 4, 0), 120)
            for pb in range(8):
                for jj in range(2):
                    krow = r - 8 + 2 * pb + jj
                    if rs <= krow < rs + 8:
                        vm[jj, :, lr, pb] = 0.0
        q = r1[i]["q"].reshape(T_ALL, 8, 128)
        ins.append(dict(qT=np.ascontiguousarray(q.transpose(2, 1, 0)), kT=kT, ve=np.ascontiguousarray(ve),
                        vo=np.ascontiguousarray(vo), kcT=kcT, vc=vc, tt=TT, vm=np.ascontiguousarray(vm.reshape(128, 16, 8)),
                        ones=ONES))
    return _run(_prog("na", build_na), ins)


def run_ab_layer(x, ctx, mod_l, nw1, w_in, conv_w, conv_b, dt_bias, a_log, d_skip, norm_w, q_norm, k_norm, rpb,
                 w_out, nw2, w1, w2):
    r1 = run_ab1(x, ctx, mod_l, nw1, w_in, conv_w, conv_b, q_norm, k_norm)
    ra, rb = run_ssd(r1, dt_bias, a_log, d_skip, norm_w)
    rn = run_na(r1, rpb)
    u_lat = []
    for i in range(NCORES):
        gT = fm(rb[i]["g"].reshape(T_ALL, 2048))
        u_lat.append(np.concatenate([gT[:, :, :T_LAT], rn[i]["o"][:, :, :T_LAT]], axis=1))
    gT0 = fm(rb[0]["g"].reshape(T_ALL, 2048))
    u_ctx = np.concatenate([gT0[:, :, T_LAT:], rn[0]["o"][:, :, T_LAT:]], axis=1)
    return run_post2(u_lat, u_ctx, x, ctx, w_out, w1, w2, mod_l, nw2)


def kernel(x, c, ctx, c_ctx, w_mod, b_mod, norm1_w, norm2_w, w_mlp_in, w_mlp_out,
           ab_w_in, ab_conv_w, ab_conv_b, ab_dt_bias, ab_a_log, ab_d_skip, ab_norm_w,
           ab_q_norm, ab_k_norm, ab_rpb, ab_w_out, c_w_qkv, c_q_norm, c_k_norm, c_w_out):
    f = lambda a: np.asarray(a, dtype=np.float32)
    xs = f(x)[0]
    cs = f(ctx)[0]
    mods = run_mod(f(c), f(c_ctx), f(w_mod), f(b_mod))
    for layer in range(4):
        i = layer // 2
        if layer % 2 == 0:
            xs, cs = run_ab_layer(xs, cs, mods[layer], f(norm1_w)[layer], f(ab_w_in)[i], f(ab_conv_w)[i],
                                  f(ab_conv_b)[i], f(ab_dt_bias)[i], f(ab_a_log)[i], f(ab_d_skip)[i],
                                  f(ab_norm_w)[i], f(ab_q_norm)[i], f(ab_k_norm)[i], f(ab_rpb)[i], f(ab_w_out)[i],
                                  f(norm2_w)[layer], f(w_mlp_in)[layer], f(w_mlp_out)[layer])
        else:
            xs, cs = run_c_layer(xs, cs, mods[layer], f(norm1_w)[layer], f(c_w_qkv)[i], f(c_q_norm)[i],
                                 f(c_k_norm)[i], f(c_w_out)[i], f(norm2_w)[layer], f(w_mlp_in)[layer],
                                 f(w_mlp_out)[layer])
    return np.ascontiguousarray(xs[None].astype(np.float32))


def build_post2(KM):
    KC = KM // 128
    HQ = 8
    P = Prog()
    uT_d = P.dram_in("uT", [128, KC, T_ALL], BF16)
    xT_d = P.dram_in("xT", [128, KD, T_ALL])
    wo_d = P.dram_in("wo", [16, 128, KC, 128])
    w1_d = P.dram_in("w1", [64, 128, KD, 128])
    w2_d = P.dram_in("w2", [8, 16, 128, HQ, 128])
    mod_d = P.dram_in("mod", [128, 6, KD, 2])
    nw_d = P.dram_in("nw", [128, KD])
    ones_d = P.dram_in("ones", [128, 128])
    out_d = P.dram_out("xo", [128, KD, T_ALL])

    ones_f, ones_b = load_consts(P, ones_d)
    mod_sb, A = mod_vectors(P, mod_d, nw_d, slot_sc=4)
    xs = [P.sbuf([128, KD, n], F32, f"xs{j}") for j, (st, n, w) in enumerate(TILES)]
    for j, (st, n, w) in enumerate(TILES):
        P.dma(xs[j][:], xT_d[:, :, st:st + n], writes=[xs[j]])
    m1 = P.mark()
    us = P.sbuf([128, KC, T_ALL], BF16, "us")
    P.dma(us[:], uT_d, writes=[us])
    ws = WStream(P, KC * 128, name="wsA", nstage=2, nbf=2)
    for ob in range(16):
        wb, wv = ws.load(wo_d[ob], KC, 128)
        for j, (st, n, w) in enumerate(TILES):
            ps = P.psum()
            for k in range(KC):
                P.pe("matmul", reads=[*wb, us], writes=[ps], accum=(k > 0),
                     out=ps[:, :n], lhsT=wv[:, k, :], rhs=us[:, k, st:st + n], start=(k == 0), stop=(k == KC - 1))
            P.dve("scalar_tensor_tensor", reads=[ps, mod_sb, xs[j]], writes=[xs[j]],
                  out=xs[j][:, ob, :], in0=ps[:, :n], scalar=mod_sb[:, 2, ob, w:w + 1],
                  in1=xs[j][:, ob, :], op0=ALU.mult, op1=ALU.add)
    P.release(m1)
    hs = [P.sbuf([128, KD, n], BF16, f"hs{j}") for j, (st, n, w) in enumerate(TILES)]
    t1 = Rot([P.sbuf([128, 512], F32, f"t1_{i}") for i in range(2)])
    tmp = Rot([P.sbuf([128, 512], F32, f"tmp{i}") for i in range(3)])
    m2 = P.mark()
    sq = P.sbuf([128, KD, 512], BF16, "sq")
    for j, (st, n, w) in enumerate(TILES):
        norm_mod_tile(P, lambda k, j=j: xs[j][:, k, :], xs[j], lambda k, j=j: hs[j][:, k, :], hs[j], n, w, A, mod_sb, 3,
                      ones_b, sq.ap, sq, t1, tmp)
    P.release(m2)
    aq = [P.sbuf([128, HQ, n], BF16, f"aq{j}") for j, (st, n, w) in enumerate(TILES)]
    ws = WStream(P, KD * 128, name="wsB", nstage=2, nbf=2)
    for hq in range(64 // HQ):
        for hc in range(HQ):
            wb, wv = ws.load(w1_d[hq * HQ + hc], KD, 128)
            for j, (st, n, w) in enumerate(TILES):
                ps = P.psum()
                for k in range(KD):
                    P.pe("matmul", reads=[*wb, hs[j]], writes=[ps], accum=(k > 0),
                         out=ps[:, :n], lhsT=wv[:, k, :], rhs=hs[j][:, k, :], start=(k == 0), stop=(k == KD - 1))
                r = tmp.next()
                P.act("activation", reads=[ps], writes=[r], out=r[:, :n], in_=ps[:, :n], func=AF.Relu)
                P.dve("tensor_tensor", reads=[r], writes=[aq[j]], out=aq[j][:, hc, :], in0=r[:, :n], in1=r[:, :n],
                      op=ALU.mult)
        for ob in range(16):
            wb, wv = ws.load(w2_d[hq, ob], HQ, 128)
            for j, (st, n, w) in enumerate(TILES):
                ps = P.psum()
                for k in range(HQ):
                    P.pe("matmul", reads=[*wb, aq[j]], writes=[ps], accum=(k > 0),
                         out=ps[:, :n], lhsT=wv[:, k, :], rhs=aq[j][:, k, :], start=(k == 0), stop=(k == HQ - 1))
                P.dve("scalar_tensor_tensor", reads=[ps, mod_sb, xs[j]], writes=[xs[j]],
                      out=xs[j][:, ob, :], in0=ps[:, :n], scalar=mod_sb[:, 5, ob, w:w + 1], in1=xs[j][:, ob, :],
                      op0=ALU.mult, op1=ALU.add)
    for j, (st, n, w) in enumerate(TILES):
        P.dma(out_d[:, :, st:st + n], xs[j][:], reads=[xs[j]])
    return P.finish()


def run_post2(u_lat, u_ctx, x, ctx, w_out, w1, w2, mod_l, nw2):
    KM = w_out.shape[0]
    wo = wblocks(w_out, 128)
    w1b = wblocks(w1, 128)
    w2b = np.ascontiguousarray(w2.reshape(8, 8, 128, 16, 128).transpose(0, 3, 2, 1, 4))
    nw = vec_fm(nw2)
    xTs = xT_cores(x, ctx)
    ins = []
    for i in range(NCORES):
        uT = np.ascontiguousarray(np.concatenate([u_lat[i], u_ctx], axis=2))
        ins.append(dict(uT=uT, xT=xTs[i], wo=wo, w1=w1b, w2=w2b, mod=mod_l, nw=nw, ones=ONES))
    res = _run(_prog("post2", build_post2, KM), ins)
    outs = [unfm(res[i]["xo"]) for i in range(NCORES)]
    xn = np.concatenate([o[:T_LAT] for o in outs], axis=0)
    cn = outs[0][T_LAT:]
    return xn, cn
```

```python
import numpy as np
from contextlib import ExitStack

import concourse.bass as bass
import concourse.mybir as mybir
from concourse.bass_utils import run_bass_kernel_spmd

F32 = mybir.dt.float32
BF16 = mybir.dt.bfloat16
AF = mybir.ActivationFunctionType
ALU = mybir.AluOpType
AX = mybir.AxisListType

NCORES = 8


class Buf:
    __slots__ = ("ap", "w", "wd", "r", "rd", "name")

    def __init__(self, ap, name=""):
        self.ap = ap
        self.w = {}
        self.wd = []
        self.r = {}
        self.rd = []
        self.name = name

    def __getitem__(self, idx):
        return self.ap[idx]


class Prog:
    ENGS = ("pe", "act", "dve", "pool", "sp")
    DMA_SLOTS = 8

    def __init__(self):
        self.nc = bass.Bass("TRN2", target_bir_lowering=False)
        self.stack = ExitStack()
        self.ops = []
        self.n_by_eng = {e: 0 for e in self.ENGS}
        self._cnt = 0

    def dram_in(self, name, shape, dtype=F32):
        return self.nc.dram_tensor(name, list(shape), dtype, kind="ExternalInput").ap()

    def dram_out(self, name, shape, dtype=F32):
        return self.nc.dram_tensor(name, list(shape), dtype, kind="ExternalOutput").ap()

    ARENA_WORDS = 52736

    def _ensure_arena(self):
        if getattr(self, "arena", None) is None:
            self.arena = self.stack.enter_context(
                self.nc.sbuf_tensor("arena", [128, self.ARENA_WORDS], F32))
            self.top = 0
            self.banks = [Buf(self.stack.enter_context(self.nc.psum_tensor(f"bank{i}", [128, 512], F32)),
                              f"bank{i}") for i in range(8)]
            self.bank_i = 0
            self.last_by_eng = {}
            self.dma_since_barrier = []
            self.barrier_op = None
            self.after_barrier = set()

    def sbuf(self, shape, dtype=F32, name=None):
        self._ensure_arena()
        shape = list(shape)
        esz = 4 if dtype == F32 else 2
        n = 1
        for d in shape[1:]:
            n *= d
        words = (n * esz + 3) // 4
        words = (words + 7) // 8 * 8
        assert self.top + words <= self.ARENA_WORDS, f"SBUF arena overflow {name} {shape} top={self.top}"
        ap = self.arena[0:shape[0], self.top:self.top + words]
        self.top += words
        if dtype != F32:
            ap = ap.bitcast(dtype)
        ap = ap[:, 0:n]
        if len(shape) == 3:
            ap = ap.rearrange("p (a b) -> p a b", a=shape[1])
        elif len(shape) == 4:
            ap = ap.rearrange("p (a b c) -> p a b c", a=shape[1], b=shape[2])
        return Buf(ap, name or "")

    def psum(self, shape=None, dtype=F32, name=None):
        self._ensure_arena()
        b = self.banks[self.bank_i % 8]
        self.bank_i += 1
        return b

    def bank(self, i):
        self._ensure_arena()
        return self.banks[i]

    def mark(self):
        self._ensure_arena()
        return self.top

    def release(self, m):
        self.barrier()
        self.top = m

    def barrier(self):
        self._ensure_arena()
        if not hasattr(self, "_bar_buf"):
            self._bar_buf = self.sbuf([128, 8], F32, "barbuf")
        bb = self._bar_buf
        idx = self.op("dve", "memset", writes=[bb], ap=bb[:], constant=0.0)
        deps = self.ops[idx]["deps"]
        for e, last in self.last_by_eng.items():
            if last != idx:
                deps.add(last)
        deps.update(self.dma_since_barrier)
        deps.discard(idx)
        self.dma_since_barrier = []
        self.barrier_op = idx
        self.after_barrier = set()

    def op(self, eng, meth, reads=(), writes=(), dma=False, accum=False, **kw):
        fn = (meth, kw)
        idx = len(self.ops)
        deps = set()
        for b in reads:
            deps.update(b.w.values())
            deps.update(b.wd)
        for b in writes:
            has_readers = bool(b.r) or bool(b.rd)
            if has_readers:
                for e2, r in b.r.items():
                    if dma or e2 != eng:
                        deps.add(r)
                deps.update(b.rd)
            for e2, w in b.w.items():
                if dma or e2 != eng:
                    deps.add(w)
            if not dma:
                deps.update(b.wd)
        self._ensure_arena()
        if self.barrier_op is not None and eng not in self.after_barrier:
            deps.add(self.barrier_op)
            self.after_barrier.add(eng)
        deps.discard(idx)
        self.ops.append(dict(eng=eng, fn=fn, deps=deps, dma=dma))
        if dma:
            self.dma_since_barrier.append(idx)
        self.last_by_eng[eng] = idx
        for b in reads:
            if dma:
                b.rd.append(idx)
            else:
                b.r[eng] = idx
        for b in writes:
            had_readers = bool(b.r) or bool(b.rd)
            if dma:
                if had_readers:
                    b.w = {}
                    b.wd = []
                b.wd.append(idx)
            else:
                if had_readers:
                    b.w = {}
                b.wd = []
                b.w[eng] = idx
            b.r = {}
            b.rd = []
        return idx

    def pe(self, meth, reads=(), writes=(), accum=False, **kw):
        return self.op("pe", meth, reads, writes, accum=accum, **kw)

    def act(self, meth, reads=(), writes=(), **kw):
        return self.op("act", meth, reads, writes, **kw)

    def dve(self, meth, reads=(), writes=(), **kw):
        return self.op("dve", meth, reads, writes, **kw)

    def pool(self, meth, reads=(), writes=(), **kw):
        return self.op("pool", meth, reads, writes, **kw)

    def dma(self, out, in_, reads=(), writes=(), eng="sp", **kw):
        return self.op(eng, "dma_start", reads, writes, dma=True, out=out, in_=in_, **kw)

    def finish(self, final_wait_ops=None):
        nc = self.nc
        ops = self.ops
        n = len(ops)
        needed = [False] * n
        for i, o in enumerate(ops):
            for d in o["deps"]:
                od = ops[d]
                needed[d] = True
        if final_wait_ops is None:
            final_wait_ops = [i for i, o in enumerate(ops) if o["dma"]][-64:]
        for d in final_wait_ops:
            needed[d] = True
        sems = {e: self.stack.enter_context(nc.semaphore(f"s_{e}")) for e in self.ENGS}
        dma_sems = {e: [self.stack.enter_context(nc.semaphore(f"d_{e}{k}"))
                        for k in range(self.DMA_SLOTS)] for e in self.ENGS}
        sig = [None] * n
        cnt = {e: 0 for e in self.ENGS}
        dcnt = {e: 0 for e in self.ENGS}
        dslot_val = {e: [0] * self.DMA_SLOTS for e in self.ENGS}
        prev_slot_sig = [None] * n
        for i, o in enumerate(ops):
            e = o["eng"]
            if o["dma"]:
                k = dcnt[e] % self.DMA_SLOTS
                dcnt[e] += 1
                if dslot_val[e][k] > 0:
                    prev_slot_sig[i] = (dma_sems[e][k], dslot_val[e][k])
                dslot_val[e][k] += 16
                sig[i] = (dma_sems[e][k], dslot_val[e][k], 16)
            elif needed[i]:
                cnt[e] += 1
                sig[i] = (sems[e], cnt[e], 1)
        per_eng = {e: [] for e in self.ENGS}
        seen = {e: {} for e in self.ENGS}
        for i, o in enumerate(ops):
            e = o["eng"]
            waits = []
            want = {}
            for d in o["deps"]:
                s = sig[d]
                if s is None:
                    continue
                key = id(s[0])
                if key not in want or want[key][1] < s[1]:
                    want[key] = (s[0], s[1])
            if prev_slot_sig[i] is not None:
                s = prev_slot_sig[i]
                key = id(s[0])
                if key not in want or want[key][1] < s[1]:
                    want[key] = s
            for key, (sm, val) in want.items():
                if seen[e].get(key, 0) >= val:
                    continue
                seen[e][key] = val
                waits.append((sm, val))
            per_eng[e].append((waits, o["fn"], sig[i]))
        finals = [sig[d] for d in final_wait_ops]

        def run(engobj, lst, tail=None):
            for waits, fn, s in lst:
                for sm, val in waits:
                    engobj.wait_ge(sm, val)
                ins = getattr(engobj, fn[0])(**fn[1])
                if s is not None:
                    ins.then_inc(s[0], s[2])
            if tail:
                done = {}
                for sm, val, _ in tail:
                    done[id(sm)] = (sm, max(val, done.get(id(sm), (None, 0))[1]))
                for sm, val in done.values():
                    engobj.wait_ge(sm, val)

        with nc.Block() as block:
            @block.tensor
            def _(t):
                run(t, per_eng["pe"])

            @block.scalar
            def _(a):
                run(a, per_eng["act"])

            @block.vector
            def _(v):
                run(v, per_eng["dve"])

            @block.gpsimd
            def _(g):
                run(g, per_eng["pool"])

            @block.sync
            def _(s):
                run(s, per_eng["sp"], tail=finals)
        self.stack.close()
        return nc


D = 2048
KD = D // 128
T_LAT = 1024
T_CTX = 256
T_ALL = T_LAT + T_CTX
TILES = [(0, 512, 0), (512, 512, 0), (1024, 256, 1)]
EPS = 1e-6
HID = 8192


class Rot:
    def __init__(self, bufs):
        self.bufs = bufs
        self.i = 0

    def next(self):
        b = self.bufs[self.i % len(self.bufs)]
        self.i += 1
        return b


def load_consts(P, ones_dram):
    ones_f = P.sbuf([128, 128], F32, "ones_f")
    ones_b = P.sbuf([128, 128], BF16, "ones_b")
    P.dma(ones_f[:], ones_dram, writes=[ones_f])
    P.dve("tensor_copy", reads=[ones_f], writes=[ones_b], out=ones_b[:], in_=ones_f[:])
    return ones_f, ones_b


def mod_vectors(P, mod_dram, nw_dram, slot_sc):
    mod_sb = P.sbuf([128, 6, KD, 2], F32, "mod_sb")
    nw = P.sbuf([128, KD], F32, "nw")
    P.dma(mod_sb[:], mod_dram, writes=[mod_sb])
    P.dma(nw[:], nw_dram, writes=[nw])
    A = []
    for w in range(2):
        a = P.sbuf([128, KD], F32, f"modA{w}")
        P.dve("scalar_tensor_tensor", reads=[mod_sb, nw], writes=[a],
              out=a[:], in0=mod_sb[:, slot_sc, :, w], scalar=1.0, in1=nw[:], op0=ALU.add, op1=ALU.mult)
        A.append(a)
    return mod_sb, A


def norm_mod_tile(P, x_ap_fn, xbuf, h_ap_fn, hbuf, n, w, A, mod_sb, slot_sh, ones_b, sq_ap, sqbuf, t1, tmp):
    for k in range(KD):
        P.act("activation", reads=[xbuf], writes=[sqbuf], out=sq_ap[:, k, :n], in_=x_ap_fn(k), func=AF.Square)
    ps = P.psum()
    for k in range(KD):
        P.pe("matmul", reads=[ones_b, sqbuf], writes=[ps], accum=(k > 0),
             out=ps[:, :n], lhsT=ones_b[:], rhs=sq_ap[:, k, :n], start=(k == 0), stop=(k == KD - 1))
    r = t1.next()
    P.dve("tensor_scalar", reads=[ps], writes=[r], out=r[:, :n], in0=ps[:, :n], scalar1=1.0 / D, scalar2=EPS,
          op0=ALU.mult, op1=ALU.add)
    P.act("activation", reads=[r], writes=[r], out=r[:, :n], in_=r[:, :n], func=AF.Sqrt)
    P.dve("reciprocal", reads=[r], writes=[r], out=r[:, :n], in_=r[:, :n])
    for k in range(KD):
        t = tmp.next()
        P.dve("scalar_tensor_tensor", reads=[xbuf, A[w], r], writes=[t],
              out=t[:, :n], in0=x_ap_fn(k), scalar=A[w][:, k:k + 1], in1=r[:, :n], op0=ALU.mult, op1=ALU.mult)
        P.act("activation", reads=[t, mod_sb], writes=[hbuf],
              out=h_ap_fn(k), in_=t[:, :n], func=AF.Identity, bias=mod_sb[:, slot_sh, k, w:w + 1], scale=1.0)


class WStream:
    def __init__(self, P, words, cast_eng=("pool", "act"), name="w", nstage=2, nbf=2):
        self.P = P
        self.words = words
        self.stage = Rot([P.sbuf([128, words], F32, f"{name}_st{i}") for i in range(nstage)])
        self.wb = Rot([(P.sbuf([128, words], BF16, f"{name}_bf{i}"), Buf(None, "hA"), Buf(None, "hB"))
                       for i in range(nbf)])
        self.cast_eng = cast_eng

    def load(self, dram_ap, a, b):
        P = self.P
        s = self.stage.next(); wb, hA, hB = self.wb.next()
        n = a * b
        assert n <= self.words
        sv = s[:, 0:n].rearrange("p (a b) -> p a b", a=a)
        wv = wb[:, 0:n].rearrange("p (a b) -> p a b", a=a)
        P.dma(sv, dram_ap, writes=[s])
        h = a // 2
        for (lo, hi, eng, hb) in ((0, h, self.cast_eng[0], hA), (h, a, self.cast_eng[1], hB)):
            if eng == "act":
                P.act("activation", reads=[s], writes=[hb], out=wv[:, lo:hi, :], in_=sv[:, lo:hi, :], func=AF.Copy)
            else:
                P.op(eng, "tensor_copy", reads=[s], writes=[hb], out=wv[:, lo:hi, :], in_=sv[:, lo:hi, :])
        return [hA, hB], wv


def build_post(KM, NL=2):
    KC = KM // 128
    P = Prog()
    TP = NL * 512 + T_CTX
    tiles = [(j * 512, 512, 0) for j in range(NL)] + [(NL * 512, T_CTX, 1)]
    uT_d = P.dram_in("uT", [128, KC, TP], BF16)
    xT_d = P.dram_in("xT", [128, KD, TP])
    wo_d = P.dram_in("wo", [16, 128, KC, 128])
    w1_d = P.dram_in("w1", [64, 128, KD, 128])
    w2_d = P.dram_in("w2", [32, 128, 32, 128])
    mod_d = P.dram_in("mod", [128, 6, KD, 2])
    nw_d = P.dram_in("nw", [128, KD])
    ones_d = P.dram_in("ones", [128, 128])
    out_d = P.dram_out("xo", [128, KD, TP])

    ones_f, ones_b = load_consts(P, ones_d)
    mod_sb, A = mod_vectors(P, mod_d, nw_d, slot_sc=4)
    xs = P.sbuf([128, KD, 512], F32, "xs")
    us = P.sbuf([128, KC, 512], BF16, "us")
    hs = P.sbuf([128, KD, 512], BF16, "hs")
    aT = P.sbuf([128, 64, 512], BF16, "aT")
    t1 = Rot([P.sbuf([128, 512], F32, f"t1_{i}") for i in range(2)])
    tmp = Rot([P.sbuf([128, 512], F32, f"tmp{i}") for i in range(3)])
    osb = Rot([P.sbuf([128, 512], F32, f"osb{i}") for i in range(3)])
    ws = WStream(P, 4096, name="ws", nstage=1)

    for (st, n, w) in tiles:
        P.dma(xs[:, :, :n], xT_d[:, :, st:st + n], writes=[xs])
        P.dma(us[:, :, :n], uT_d[:, :, st:st + n], writes=[us])
        for ob in range(16):
            wb, wv = ws.load(wo_d[ob], KC, 128)
            ps = P.psum()
            for k in range(KC):
                P.pe("matmul", reads=[*wb, us], writes=[ps], accum=(k > 0),
                     out=ps[:, :n], lhsT=wv[:, k, :], rhs=us[:, k, :n], start=(k == 0), stop=(k == KC - 1))
            P.dve("scalar_tensor_tensor", reads=[ps, mod_sb, xs], writes=[xs],
                  out=xs[:, ob, :n], in0=ps[:, :n], scalar=mod_sb[:, 2, ob, w:w + 1],
                  in1=xs[:, ob, :n], op0=ALU.mult, op1=ALU.add)
        norm_mod_tile(P, lambda k: xs[:, k, :n], xs, lambda k: hs[:, k, :n], hs, n, w, A, mod_sb, 3,
                      ones_b, aT.ap, aT, t1, tmp)
        for hc in range(64):
            wb, wv = ws.load(w1_d[hc], KD, 128)
            ps = P.psum()
            for k in range(KD):
                P.pe("matmul", reads=[*wb, hs], writes=[ps], accum=(k > 0),
                     out=ps[:, :n], lhsT=wv[:, k, :], rhs=hs[:, k, :n], start=(k == 0), stop=(k == KD - 1))
            r = tmp.next()
            P.act("activation", reads=[ps], writes=[r], out=r[:, :n], in_=ps[:, :n], func=AF.Relu)
            P.dve("tensor_tensor", reads=[r], writes=[aT], out=aT[:, hc, :n], in0=r[:, :n], in1=r[:, :n],
                  op=ALU.mult)
        for ob in range(16):
            ps = P.psum()
            for half in range(2):
                wb, wv = ws.load(w2_d[ob * 2 + half], 32, 128)
                for k in range(32):
                    kk = half * 32 + k
                    P.pe("matmul", reads=[*wb, aT], writes=[ps], accum=(kk > 0),
                         out=ps[:, :n], lhsT=wv[:, k, :], rhs=aT[:, kk, :n], start=(kk == 0), stop=(kk == 63))
            o = osb.next()
            P.dve("scalar_tensor_tensor", reads=[ps, mod_sb, xs], writes=[o],
                  out=o[:, :n], in0=ps[:, :n], scalar=mod_sb[:, 5, ob, w:w + 1], in1=xs[:, ob, :n],
                  op0=ALU.mult, op1=ALU.add)
            P.dma(out_d[:, ob, st:st + n], o[:, :n], reads=[o])
    return P.finish()


def build_mod():
    P = Prog()
    cv_d = P.dram_in("cv", [128, KD, 2])
    w_d = P.dram_in("w", [12, 128, KD, 512])
    b_d = P.dram_in("b", [128, 48])
    out_d = P.dram_out("mo", [128, 48, 2])
    cv = P.sbuf([128, KD, 2], F32, "cv")
    sg = P.sbuf([128, KD, 2], F32, "sg")
    bs = P.sbuf([128, 48], F32, "bs")
    ob = P.sbuf([128, 48, 2], F32, "ob")
    P.dma(cv[:], cv_d, writes=[cv])
    P.dma(bs[:], b_d, writes=[bs])
    P.act("activation", reads=[cv], writes=[sg], out=sg[:], in_=cv[:], func=AF.Sigmoid)
    P.dve("tensor_tensor", reads=[cv, sg], writes=[sg], out=sg[:], in0=sg[:], in1=cv[:], op=ALU.mult)
    wst = Rot([P.sbuf([128, KD, 512], F32, f"wst{i}") for i in range(3)])
    for blk in range(12):
        wv = wst.next()
        P.dma(wv[:], w_d[blk], writes=[wv])
        for sub in range(4):
            cb = blk * 4 + sub
            ps = P.psum()
            for k in range(KD):
                P.pe("matmul", reads=[wv, sg], writes=[ps], accum=(k > 0),
                     out=ps[:, 0:2], lhsT=wv[:, k, sub * 128:(sub + 1) * 128], rhs=sg[:, k, :],
                     start=(k == 0), stop=(k == KD - 1))
            P.dve("tensor_scalar", reads=[ps, bs], writes=[ob], out=ob[:, cb, :], in0=ps[:, 0:2],
                  scalar1=bs[:, cb:cb + 1], scalar2=None, op0=ALU.add)
    P.dma(out_d, ob[:], reads=[ob])
    return P.finish()


def qk_norm_tile(P, ps, nh, w_ap, wbc, out_ap, outbuf, scr, rope=None):
    n = nh * 128
    sq = scr["sq"].next(); ss = scr["ss"].next(); xn = scr["xn"].next()
    P.act("activation", reads=[ps], writes=[sq], out=sq[:, :n], in_=ps[:, :n], func=AF.Square)
    P.dve("tensor_reduce", reads=[sq], writes=[ss], out=ss[:, :nh],
          in_=sq[:, :n].rearrange("p (h d) -> p h d", h=nh), axis=AX.X, op=ALU.add)
    P.dve("tensor_scalar", reads=[ss], writes=[ss], out=ss[:, :nh], in0=ss[:, :nh], scalar1=1.0 / 128, scalar2=EPS,
          op0=ALU.mult, op1=ALU.add)
    P.act("activation", reads=[ss], writes=[ss], out=ss[:, :nh], in_=ss[:, :nh], func=AF.Sqrt)
    P.dve("reciprocal", reads=[ss], writes=[ss], out=ss[:, :nh], in_=ss[:, :nh])
    P.dve("tensor_tensor", reads=[ps, ss], writes=[xn],
          out=xn[:, :n].rearrange("p (h d) -> p h d", h=nh), in0=ps[:, :n].rearrange("p (h d) -> p h d", h=nh),
          in1=ss[:, :nh].unsqueeze(2).to_broadcast([128, nh, 128]), op=ALU.mult)
    if rope is None:
        P.pool("tensor_tensor", reads=[xn, wbc], writes=[outbuf], out=out_ap, in0=xn[:, :n], in1=w_ap,
               op=ALU.mult)
        return
    cos_ap, sin_ap, rbuf = rope
    P.pool("tensor_tensor", reads=[xn, wbc], writes=[xn], out=xn[:, :n], in0=xn[:, :n], in1=w_ap, op=ALU.mult)
    xv = xn[:, :n].rearrange("p (h i two) -> p h i two", h=nh, two=2)
    ov = out_ap.rearrange("p (h i two) -> p h i two", h=nh, two=2)
    u1 = xv[:, :, :, 0]; u2 = xv[:, :, :, 1]
    cb = cos_ap.unsqueeze(1).to_broadcast([128, nh, 64]); sb = sin_ap.unsqueeze(1).to_broadcast([128, nh, 64])
    ta = scr["ra"].next(); tb = scr["rb"].next()
    tav = ta[:, :nh * 64].rearrange("p (h i) -> p h i", h=nh); tbv = tb[:, :nh * 64].rearrange("p (h i) -> p h i", h=nh)
    P.dve("tensor_tensor", reads=[xn, rbuf], writes=[ta], out=tav, in0=u1, in1=cb, op=ALU.mult)
    P.pool("tensor_tensor", reads=[xn, rbuf], writes=[tb], out=tbv, in0=u2, in1=sb, op=ALU.mult)
    P.dve("tensor_tensor", reads=[ta, tb], writes=[outbuf], out=ov[:, :, :, 0], in0=tav, in1=tbv, op=ALU.subtract)
    ta2 = scr["ra"].next(); tb2 = scr["rb"].next()
    ta2v = ta2[:, :nh * 64].rearrange("p (h i) -> p h i", h=nh); tb2v = tb2[:, :nh * 64].rearrange("p (h i) -> p h i", h=nh)
    P.dve("tensor_tensor", reads=[xn, rbuf], writes=[ta2], out=ta2v, in0=u1, in1=sb, op=ALU.mult)
    P.pool("tensor_tensor", reads=[xn, rbuf], writes=[tb2], out=tb2v, in0=u2, in1=cb, op=ALU.mult)
    P.dve("tensor_tensor", reads=[ta2, tb2], writes=[outbuf], out=ov[:, :, :, 1], in0=ta2v, in1=tb2v, op=ALU.add)


def norm1_all(P, xT_d, mod_d, nw_d, ones_b, hT, ntok_tiles):
    mod_sb, A = mod_vectors(P, mod_d, nw_d, slot_sc=1)
    m = P.mark()
    xs = Rot([P.sbuf([128, KD, 512], F32, f"xs{i}") for i in range(1)])
    sq = P.sbuf([128, KD, 512], BF16, "sqn")
    t1 = Rot([P.sbuf([128, 512], F32, f"t1_{i}") for i in range(2)])
    tmp = Rot([P.sbuf([128, 512], F32, f"tmp{i}") for i in range(3)])
    for (st, n, w) in ntok_tiles:
        x = xs.next()
        P.dma(x[:, :, :n], xT_d[:, :, st:st + n], writes=[x])
        norm_mod_tile(P, lambda k: x[:, k, :n], x, lambda k: hT[:, k, st:st + n], hT, n, w, A, mod_sb, 0,
                      ones_b, sq.ap, sq, t1, tmp)
    P.release(m)
    return mod_sb


def build_c1():
    P = Prog()
    xT_d = P.dram_in("xT", [128, KD, T_ALL])
    mod_d = P.dram_in("mod", [128, 6, KD, 2])
    nw_d = P.dram_in("nw", [128, KD])
    ones_d = P.dram_in("ones", [128, 128])
    w_d = P.dram_in("w", [6, 128, KD, 512])
    qkw_d = P.dram_in("qkw", [128, 2, 512])
    cs_d = P.dram_in("cs", [128, 8, 2, 64])
    q_d = P.dram_out("q", [10, 128, 2048], BF16)
    k_d = P.dram_out("k", [10, 128, 512], BF16)
    v_d = P.dram_out("v", [10, 128, 512], BF16)

    ones_f, ones_b = load_consts(P, ones_d)
    hT = P.sbuf([128, KD, T_ALL], BF16, "hT")
    norm1_all(P, xT_d, mod_d, nw_d, ones_b, hT, TILES)
    qkw = P.sbuf([128, 2, 512], F32, "qkw")
    cs = P.sbuf([128, 8, 2, 64], F32, "cs")
    P.dma(qkw[:], qkw_d, writes=[qkw])
    P.dma(cs[:], cs_d, writes=[cs])
    scr = dict(sq=Rot([P.sbuf([128, 512], F32, f"sq{i}") for i in range(2)]),
               ss=Rot([P.sbuf([128, 8], F32, f"ss{i}") for i in range(2)]),
               xn=Rot([P.sbuf([128, 512], F32, f"xn{i}") for i in range(2)]),
               ra=Rot([P.sbuf([128, 256], F32, f"ra{i}") for i in range(2)]),
               rb=Rot([P.sbuf([128, 256], F32, f"rb{i}") for i in range(2)]))
    ob = Rot([P.sbuf([128, 512], BF16, f"ob{i}") for i in range(3)])
    ws = WStream(P, KD * 512, name="ws", nstage=2, nbf=2)
    for cbk in range(6):
        wb, wv = ws.load(w_d[cbk], KD, 512)
        for tt in range(10):
            ps = P.psum()
            for k in range(KD):
                P.pe("matmul", reads=[hT, *wb], writes=[ps], accum=(k > 0),
                     out=ps[:, :], lhsT=hT[:, k, tt * 128:(tt + 1) * 128], rhs=wv[:, k, :],
                     start=(k == 0), stop=(k == KD - 1))
            o = ob.next()
            if cbk < 5:
                rope = (cs[:, tt, 0, :], cs[:, tt, 1, :], cs) if tt < 8 else None
                qk_norm_tile(P, ps, 4, qkw[:, 0 if cbk < 4 else 1, :], qkw, o[:, :], o, scr, rope=rope)
                dst = q_d[tt, :, cbk * 512:(cbk + 1) * 512] if cbk < 4 else k_d[tt]
            else:
                P.act("activation", reads=[ps], writes=[o], out=o[:, :], in_=ps[:, :], func=AF.Copy)
                dst = v_d[tt]
            P.dma(dst, o[:, :], reads=[o], eng="act")
    return P.finish()


NKB = 66


def build_c2():
    P = Prog()
    qT_d = P.dram_in("qT", [128, 16, T_ALL], BF16)
    kT_d = P.dram_in("kT", [128, 4, NKB * 128], BF16)
    v_d = P.dram_in("v", [128, NKB, 4, 128], BF16)
    ones_d = P.dram_in("ones", [128, 128])
    o_d = P.dram_out("o", [128, 16, T_ALL], BF16)
    ones_f, ones_b = load_consts(P, ones_d)
    qT = P.sbuf([128, 16, T_ALL], BF16, "qT")
    kT = [P.sbuf([128, NKB * 128], BF16, f"kT{g}") for g in range(4)]
    vs = [P.sbuf([128, NKB, 128], BF16, f"v{g}") for g in range(4)]
    P.dma(qT[:], qT_d, writes=[qT])
    for g in range(4):
        P.dma(kT[g][:], kT_d[:, g, :], writes=[kT[g]])
        P.dma(vs[g][:], v_d[:, :, g, :], writes=[vs[g]])
    pT = Rot([P.sbuf([128, 512], BF16, f"pT{i}") for i in range(4)])
    rec = Rot([P.sbuf([128, 512], F32, f"rec{i}") for i in range(2)])
    ob = Rot([P.sbuf([128, 512], BF16, f"ob{i}") for i in range(2)])
    sbanks = Rot([P.bank(0), P.bank(1), P.bank(2), P.bank(7)])
    obanks = Rot([P.bank(3), P.bank(4)])
    dbanks = Rot([P.bank(5), P.bank(6)])
    scale = 128 ** -0.5
    for qb in range(10):
        nkb = NKB if qb < 8 else 2
        for g in range(4):
            qg = qT[:, 4 * g:4 * g + 4, qb * 128:(qb + 1) * 128]
            O = obanks.next(); Dn = dbanks.next()

            def smm(kb_):
                S_ = sbanks.next()
                P.pe("matmul", reads=[kT[g], qT], writes=[S_],
                     out=S_[:, :].rearrange("p (h q) -> p h q", h=4), lhsT=kT[g][:, kb_ * 128:(kb_ + 1) * 128], rhs=qg,
                     start=True, stop=True)
                return S_
            Sq = [smm(kb_) for kb_ in range(min(2, nkb))]
            for kb in range(nkb):
                if kb + 2 < nkb:
                    Sq.append(smm(kb + 2))
                S = Sq[kb]
                p = pT.next()
                P.act("activation", reads=[S], writes=[p], out=p[:, :], in_=S[:, :], func=AF.Exp, scale=scale)
                P.pe("matmul", reads=[vs[g], p], writes=[O], accum=(kb > 0),
                     out=O[:, :], lhsT=vs[g][:, kb, :], rhs=p[:, :], start=(kb == 0), stop=(kb == nkb - 1))
                P.pe("matmul", reads=[ones_b, p], writes=[Dn], accum=(kb > 0),
                     out=Dn[:, :], lhsT=ones_b[:], rhs=p[:, :], start=(kb == 0), stop=(kb == nkb - 1))
            r = rec.next(); o = ob.next()
            P.dve("reciprocal", reads=[Dn], writes=[r], out=r[:, :], in_=Dn[:, :])
            P.dve("tensor_tensor", reads=[O, r], writes=[o], out=o[:, :], in0=O[:, :], in1=r[:, :], op=ALU.mult)
            P.dma(o_d[:, 4 * g:4 * g + 4, qb * 128:(qb + 1) * 128], o[:, :].rearrange("p (h q) -> p h q", h=4),
                  reads=[o])
    return P.finish()


_PROGS = {}


def _prog(name, builder, *args):
    key = (name,) + args
    if key not in _PROGS:
        _PROGS[key] = builder(*args)
    return _PROGS[key]


def _run(nc, in_maps):
    res = run_bass_kernel_spmd(nc, in_maps, core_ids=list(range(NCORES)))
    return res.results


def fm(a):
    Tn, F = a.shape
    return np.ascontiguousarray(a.T.reshape(F // 128, 128, Tn).transpose(1, 0, 2))


def unfm(a):
    p, C, Tn = a.shape
    return np.ascontiguousarray(a.transpose(2, 1, 0).reshape(Tn, C * 128))


def wblocks(w, colblk):
    K, N = w.shape
    return np.ascontiguousarray(w.reshape(K // 128, 128, N // colblk, colblk).transpose(2, 1, 0, 3))


ONES = np.ones((128, 128), np.float32)


def run_mod(c, c_ctx, w_mod, b_mod):
    cv = np.stack([c.reshape(D), c_ctx.reshape(D)], axis=-1)
    cv = np.ascontiguousarray(cv.reshape(KD, 128, 2).transpose(1, 0, 2))
    ins = []
    for i in range(NCORES):
        l, half = i // 2, i % 2
        w = w_mod[l][:, half * 6144:(half + 1) * 6144]
        b = b_mod[l][half * 6144:(half + 1) * 6144]
        ins.append(dict(cv=cv, w=wblocks(w, 512), b=np.ascontiguousarray(b.reshape(48, 128).T)))
    res = _run(_prog("mod", build_mod), ins)
    modall = np.zeros((4, 6 * D, 2), np.float32)
    for i in range(NCORES):
        l, half = i // 2, i % 2
        mo = res[i]["mo"]
        modall[l, half * 6144:(half + 1) * 6144] = mo.transpose(1, 0, 2).reshape(6144, 2)
    return [np.ascontiguousarray(modall[l].reshape(6, KD, 128, 2).transpose(2, 0, 1, 3)) for l in range(4)]


def vec_fm(v):
    return np.ascontiguousarray(v.reshape(KD, 128).T)


def xT_cores(x, ctx):
    return [fm(np.concatenate([x[i * T_LAT:(i + 1) * T_LAT], ctx], axis=0)) for i in range(NCORES)]


def rope_tables():
    t = np.arange(8192)
    half = 64
    inv = (1.0 / (10000.0 ** (np.arange(0, half, 2, dtype=np.float32) / half))).astype(np.float32)
    ang = np.concatenate([(t // 64).astype(np.float32)[:, None] * inv, (t % 64).astype(np.float32)[:, None] * inv], -1)
    return np.cos(ang).astype(np.float32), np.sin(ang).astype(np.float32)


NCP = 4
NLP = 8192 // NCP // 512


def run_post(u_lat, u_ctx, x, ctx, w_out, w1, w2, mod_l, nw2):
    KM = w_out.shape[0]
    wo = wblocks(w_out, 128)
    w1b = wblocks(w1, 128)
    w2b = np.ascontiguousarray(w2.reshape(2, 32, 128, 16, 128).transpose(3, 0, 2, 1, 4).reshape(32, 128, 32, 128))
    nw = vec_fm(nw2)
    per = NCORES // NCP
    tl = 8192 // NCP
    ins = []
    for p in range(NCP):
        uT = np.concatenate([u_lat[p * per + j] for j in range(per)] + [u_ctx], axis=2)
        xT = fm(np.concatenate([x[p * tl:(p + 1) * tl], ctx], axis=0))
        ins.append(dict(uT=np.ascontiguousarray(uT), xT=xT, wo=wo, w1=w1b, w2=w2b, mod=mod_l, nw=nw, ones=ONES))
    res = run_bass_kernel_spmd(_prog("post", build_post, KM, NLP), ins, core_ids=list(range(NCP))).results
    outs = [unfm(res[p]["xo"]) for p in range(NCP)]
    xn = np.concatenate([o[:tl] for o in outs], axis=0)
    cn = outs[0][tl:]
    return xn, cn


def run_c_layer(x, ctx, mod_l, nw1, w_qkv, q_norm, k_norm, w_out, nw2, w1, w2):
    xTs = xT_cores(x, ctx)
    cos, sin = rope_tables()
    qkw = np.stack([np.tile(q_norm, 4), np.tile(k_norm, 4)], 0)
    qkw = np.ascontiguousarray(np.broadcast_to(qkw[None], (128, 2, 512))).astype(np.float32)
    wb = wblocks(w_qkv, 512)
    nw = vec_fm(nw1)
    ins = []
    for i in range(NCORES):
        cs = np.stack([cos[i * T_LAT:(i + 1) * T_LAT], sin[i * T_LAT:(i + 1) * T_LAT]], 1)
        cs = np.ascontiguousarray(cs.reshape(8, 128, 2, 64).transpose(1, 0, 2, 3))
        ins.append(dict(xT=xTs[i], mod=mod_l, nw=nw, ones=ONES, w=wb, qkw=qkw, cs=cs))
    r1 = _run(_prog("c1", build_c1), ins)
    k_ctx = r1[0]["k"][8:10].reshape(256, 4, 128)
    v_ctx = r1[0]["v"][8:10].reshape(256, 4, 128)
    k_all = np.concatenate([k_ctx] + [r1[i]["k"][0:8].reshape(1024, 4, 128) for i in range(NCORES)], 0)
    v_all = np.concatenate([v_ctx] + [r1[i]["v"][0:8].reshape(1024, 4, 128) for i in range(NCORES)], 0)
    kT = np.ascontiguousarray(k_all.transpose(2, 1, 0))
    vv = np.ascontiguousarray(v_all.reshape(NKB, 128, 4, 128).transpose(1, 0, 2, 3))
    ins2 = []
    for i in range(NCORES):
        q = r1[i]["q"].reshape(T_ALL, 16, 128)
        ins2.append(dict(qT=np.ascontiguousarray(q.transpose(2, 1, 0)), kT=kT, v=vv, ones=ONES))
    r2 = _run(_prog("c2", build_c2), ins2)
    u_lat = [r2[i]["o"][:, :, :T_LAT] for i in range(NCORES)]
    u_ctx = r2[0]["o"][:, :, T_LAT:]
    return run_post2(u_lat, u_ctx, x, ctx, w_out, w1, w2, mod_l, nw2)


T_AB = T_ALL + 4
TILES_AB = TILES + [(T_ALL, 4, 0)]


def build_ab1():
    P = Prog()
    xT_d = P.dram_in("xT", [128, KD, T_AB])
    mod_d = P.dram_in("mod", [128, 6, KD, 2])
    nw_d = P.dram_in("nw", [128, KD])
    ones_d = P.dram_in("ones", [128, 128])
    wfm_d = P.dram_in("wfm", [32, 128, KD, 128])
    wtm_d = P.dram_in("wtm", [10, 128, KD, 512])
    wdt_d = P.dram_in("wdt", [128, KD, 64])
    cw_d = P.dram_in("cw", [128, 32, 5])
    cb_d = P.dram_in("cb", [128, 32])
    edge_d = P.dram_in("edge", [128, 2])
    qkw_d = P.dram_in("qkw", [128, 2, 512])
    xbc_d = P.dram_out("xbc", [32, 128, T_ALL])
    z_d = P.dram_out("z", [10, 128, 2048])
    q_d = P.dram_out("q", [10, 128, 1024], BF16)
    k_d = P.dram_out("k", [10, 128, 1024], BF16)
    v_d = P.dram_out("v", [10, 128, 1024], BF16)
    dt_d = P.dram_out("dt", [10, 128, 64])

    ones_f, ones_b = load_consts(P, ones_d)
    hT = P.sbuf([128, KD, T_AB], BF16, "hT")
    norm1_all(P, xT_d, mod_d, nw_d, ones_b, hT, TILES_AB)
    cw = P.sbuf([128, 32, 5], F32, "cw"); cbias = P.sbuf([128, 32], F32, "cbias")
    edge = P.sbuf([128, 2], F32, "edge"); qkw = P.sbuf([128, 2, 512], F32, "qkw")
    P.dma(cw[:], cw_d, writes=[cw]); P.dma(cbias[:], cb_d, writes=[cbias])
    P.dma(edge[:], edge_d, writes=[edge]); P.dma(qkw[:], qkw_d, writes=[qkw])
    ws = WStream(P, KD * 512, name="ws", nstage=2, nbf=2)
    Ul = Rot([P.sbuf([128, T_LAT + 4], F32, f"Ul{i}") for i in range(2)])
    Uc = Rot([P.sbuf([128, T_CTX + 4], F32, f"Uc{i}") for i in range(2)])
    for u in Uc.bufs:
        P.dve("memset", writes=[u], ap=u[:], constant=0.0)
    accl = Rot([P.sbuf([128, T_LAT], F32, f"accl{i}") for i in range(2)])
    accc = Rot([P.sbuf([128, T_CTX], F32, f"accc{i}") for i in range(2)])
    for cb in range(32):
        wb, wv = ws.load(wfm_d[cb], KD, 128)
        ul = Ul.next(); uc = Uc.next()
        for (st, n, w) in TILES_AB:
            ps = P.psum()
            for k in range(KD):
                P.pe("matmul", reads=[*wb, hT], writes=[ps], accum=(k > 0),
                     out=ps[:, :n], lhsT=wv[:, k, :], rhs=hT[:, k, st:st + n], start=(k == 0), stop=(k == KD - 1))
            if st < T_LAT:
                P.act("activation", reads=[ps], writes=[ul], out=ul[:, 2 + st:2 + st + n], in_=ps[:, :n], func=AF.Copy)
            elif st == T_LAT:
                P.act("activation", reads=[ps], writes=[uc], out=uc[:, 2:2 + n], in_=ps[:, :n], func=AF.Copy)
            else:
                P.dve("tensor_scalar", reads=[ps, edge], writes=[ul], out=ul[:, 0:2], in0=ps[:, 0:2],
                      scalar1=edge[:, 0:1], scalar2=None, op0=ALU.mult)
                P.dve("tensor_scalar", reads=[ps, edge], writes=[ul], out=ul[:, T_LAT + 2:T_LAT + 4], in0=ps[:, 2:4],
                      scalar1=edge[:, 1:2], scalar2=None, op0=ALU.mult)
        for (U, acc, N, off) in ((ul, accl.next(), T_LAT, 0), (uc, accc.next(), T_CTX, T_LAT)):
            P.dve("tensor_scalar", reads=[U, cw, cbias], writes=[acc], out=acc[:, :], in0=U[:, 0:N],
                  scalar1=cw[:, cb, 0:1], scalar2=cbias[:, cb:cb + 1], op0=ALU.mult, op1=ALU.add)
            for kk in range(1, 5):
                P.dve("scalar_tensor_tensor", reads=[U, cw, acc], writes=[acc], out=acc[:, :], in0=U[:, kk:kk + N],
                      scalar=cw[:, cb, kk:kk + 1], in1=acc[:, :], op0=ALU.mult, op1=ALU.add)
            P.act("activation", reads=[acc], writes=[acc], out=acc[:, :], in_=acc[:, :], func=AF.Silu)
            P.dma(xbc_d[cb, :, off:off + N], acc[:, :], reads=[acc], eng="act")
    scr = dict(sq=Rot([P.sbuf([128, 512], F32, f"sq{i}") for i in range(2)]),
               ss=Rot([P.sbuf([128, 8], F32, f"ss{i}") for i in range(2)]),
               xn=Rot([P.sbuf([128, 512], F32, f"xn{i}") for i in range(2)]))
    of = Rot([P.sbuf([128, 512], F32, f"of{i}") for i in range(2)])
    ob = Rot([P.sbuf([128, 512], BF16, f"ob{i}") for i in range(3)])
    for cbk in range(10):
        wb, wv = ws.load(wtm_d[cbk], KD, 512)
        for tt in range(10):
            ps = P.psum()
            for k in range(KD):
                P.pe("matmul", reads=[hT, *wb], writes=[ps], accum=(k > 0),
                     out=ps[:, :], lhsT=hT[:, k, tt * 128:(tt + 1) * 128], rhs=wv[:, k, :],
                     start=(k == 0), stop=(k == KD - 1))
            if cbk < 4:
                o = of.next()
                P.act("activation", reads=[ps], writes=[o], out=o[:, :], in_=ps[:, :], func=AF.Copy)
                P.dma(z_d[tt, :, cbk * 512:(cbk + 1) * 512], o[:, :], reads=[o], eng="act")
            elif cbk < 8:
                o = ob.next()
                wi = 0 if cbk < 6 else 1
                qk_norm_tile(P, ps, 4, qkw[:, wi, :], qkw, o[:, :], o, scr, rope=None)
                dst = (q_d if cbk < 6 else k_d)[tt, :, (cbk % 2) * 512:(cbk % 2 + 1) * 512]
                P.dma(dst, o[:, :], reads=[o], eng="act")
            else:
                o = ob.next()
                P.act("activation", reads=[ps], writes=[o], out=o[:, :], in_=ps[:, :], func=AF.Copy)
                P.dma(v_d[tt, :, (cbk % 2) * 512:(cbk % 2 + 1) * 512], o[:, :], reads=[o], eng="act")
    wb, wv = ws.load(wdt_d, KD, 64)
    for tt in range(10):
        ps = P.psum()
        for k in range(KD):
            P.pe("matmul", reads=[hT, *wb], writes=[ps], accum=(k > 0),
                 out=ps[:, :64], lhsT=hT[:, k, tt * 128:(tt + 1) * 128], rhs=wv[:, k, :],
                 start=(k == 0), stop=(k == KD - 1))
        o = of.next()
        P.act("activation", reads=[ps], writes=[o], out=o[:, :64], in_=ps[:, :64], func=AF.Copy)
        P.dma(dt_d[tt], o[:, :64], reads=[o], eng="act")
    return P.finish()


def run_ab1(x, ctx, mod_l, nw1, w_in, conv_w, conv_b, q_norm, k_norm):
    wfm = wblocks(w_in[:, 2048:6144], 128)
    wtm = wblocks(np.concatenate([w_in[:, 0:2048], w_in[:, 6208:9280]], axis=1), 512)
    wdt = np.ascontiguousarray(w_in[:, 6144:6208].reshape(KD, 128, 64).transpose(1, 0, 2))
    cw = np.ascontiguousarray(conv_w.reshape(5, 32, 128).transpose(2, 1, 0))
    cb = np.ascontiguousarray(conv_b.reshape(32, 128).T)
    qkw = np.stack([np.tile(q_norm, 4), np.tile(k_norm, 4)], 0)
    qkw = np.ascontiguousarray(np.broadcast_to(qkw[None], (128, 2, 512))).astype(np.float32)
    nw = vec_fm(nw1)
    xpad = np.concatenate([np.zeros((2, D), np.float32), x, np.zeros((2, D), np.float32)], 0)
    ins = []
    for i in range(NCORES):
        lo = i * T_LAT
        toks = np.concatenate([x[lo:lo + T_LAT], ctx, xpad[lo:lo + 2], xpad[lo + T_LAT + 2:lo + T_LAT + 4]], 0)
        edge = np.zeros((128, 2), np.float32)
        edge[:, 0] = 1.0 if i > 0 else 0.0
        edge[:, 1] = 1.0 if i < NCORES - 1 else 0.0
        ins.append(dict(xT=fm(toks), mod=mod_l, nw=nw, ones=ONES, wfm=wfm, wtm=wtm, wdt=wdt, cw=cw, cb=cb,
                        edge=edge, qkw=qkw))
    return _run(_prog("ab1", build_ab1), ins)


def build_ssd(mode):
    full = (mode == "B")
    P = Prog()
    xs_d = P.dram_in("xs", [10, 128, 2048])
    bt_d = P.dram_in("btok", [10, 128, 1024])
    BT_d = P.dram_in("BT", [10, 128, 8, 128])
    CT_d = P.dram_in("CT", [10, 128, 8, 128])
    dt_d = P.dram_in("dt", [10, 128, 64])
    par_d = P.dram_in("par", [128, 3, 64])
    tri_d = P.dram_in("tri", [128, 2, 128])
    nm_d = P.dram_in("negmask", [128, 2, 128])
    id_d = P.dram_in("ident", [128, 128])
    ones_d = P.dram_in("ones", [128, 128])
    if full:
        Fl_d = P.dram_in("Flist", [2, 7, 128, 2048])
        Tl_d = P.dram_in("Tlist", [2, 7, 128, 32])
        cF_d = P.dram_in("ctxF", [2, 128, 2048])
        z_d = P.dram_in("z", [10, 128, 2048])
        gw_d = P.dram_in("gw", [128, 2048])
        y_d = P.dram_out("yd", [2, 10, 128, 2048])
        g_d = P.dram_out("g", [10, 128, 2048], BF16)
        ybuf = Buf(y_d, "y_dram")
    else:
        F_d = P.dram_out("F", [2, 128, 2048])
        T_d = P.dram_out("T", [2, 128, 32])
        cFo_d = P.dram_out("ctxF", [2, 128, 2048])

    ones_f, ones_b = load_consts(P, ones_d)
    par = P.sbuf([128, 3, 64], F32, "par"); tri = P.sbuf([128, 2, 128], F32, "tri")
    nm = P.sbuf([128, 2, 128], F32, "nm"); ident = P.sbuf([128, 128], F32, "ident")
    for sb_, d_ in ((par, par_d), (tri, tri_d), (nm, nm_d), (ident, id_d)):
        P.dma(sb_[:], d_, writes=[sb_])
    abc = P.sbuf([128, 64], F32, "abc")
    P.act("activation", reads=[par], writes=[abc], out=abc[:], in_=par[:, 1, :], func=AF.Exp)
    P.dve("tensor_scalar", reads=[abc], writes=[abc], out=abc[:], in0=abc[:], scalar1=-1.0, scalar2=None, op0=ALU.mult)
    dsum = P.sbuf([128, 32], F32, "dsum")
    P.dve("tensor_tensor", reads=[par], writes=[dsum], out=dsum[:], in0=par[:, 2, 0:32], in1=par[:, 2, 32:64], op=ALU.add)

    S = [P.sbuf([128, 2048], F32, f"S{d}") for d in range(2)]
    Sb = [P.sbuf([128, 2048], BF16, f"Sb{d}") for d in range(2)]
    tot_acc = [P.sbuf([128, 32], F32, f"tacc{d}") for d in range(2)]
    m_units = P.mark()
    sets = []
    for par_ in range(2):
        b_ = dict(xs=P.sbuf([128, 2048], F32, f"xs{par_}"), xd=P.sbuf([128, 2048], BF16, f"xd{par_}"),
                  xdd=P.sbuf([128, 2048], BF16, f"xdd{par_}"), btf=P.sbuf([128, 1024], F32, f"btf{par_}"),
                  btb=P.sbuf([128, 1024], BF16, f"btb{par_}"), BTf=P.sbuf([128, 8, 128], F32, f"BTf{par_}"),
                  BTb=P.sbuf([128, 8, 128], BF16, f"BTb{par_}"), CTf=P.sbuf([128, 8, 128], F32, f"CTf{par_}"),
                  CTb=P.sbuf([128, 8, 128], BF16, f"CTb{par_}"), dbc=P.sbuf([128, 32, 128], F32, f"dbc{par_}"))
        for n_ in ("dtr", "x0", "mx", "na", "e", "dt", "dtA", "la", "nla", "ela", "tot", "dend", "cdec", "dtd"):
            b_[n_] = P.sbuf([128, 32], F32, f"{n_}{par_}")
        sets.append(b_)
    fb = P.sbuf([128, 2048], F32, "foldbuf")
    Lt = Rot([P.sbuf([128, 512], F32, f"Lt{i}") for i in range(2)])
    Mt = Rot([P.sbuf([128, 512], BF16, f"Mt{i}") for i in range(2)])
    ysb = P.sbuf([128, 2048], F32, "ysb")
    tmpg = Rot([P.sbuf([128, 256], F32, f"tg{i}") for i in range(2)])
    tmps = Rot([P.sbuf([128, 256], F32, f"ts{i}") for i in range(2)])
    B_lt = P.bank(0)
    B_cb = Rot([P.bank(1), P.bank(2)]); B_arg = Rot([P.bank(3), P.bank(4)]); B_yd = Rot([P.bank(5), P.bank(6)])
    B_os = P.bank(7)

    def bc(ap32, g):
        return ap32[:, 4 * g:4 * g + 4].unsqueeze(2).to_broadcast([128, 4, 64])

    def v3(ap, g):
        return ap[:, g * 256:(g + 1) * 256].rearrange("p (k q) -> p k q", k=4)

    def pro_loads(c, d, want_y, B):
        sm = B
        xs, btf, BTf, CTf = (B[k_] for k_ in ("xs", "btf", "BTf", "CTf"))
        P.dma(sm["dtr"][:], dt_d[c, :, d * 32:(d + 1) * 32], writes=[sm["dtr"]])
        P.dma(xs[:], xs_d[c], writes=[xs])
        P.dma(btf[:], bt_d[c], writes=[btf])
        P.dma(BTf[:], BT_d[c], writes=[BTf])
        P.dma(CTf[:], CT_d[c], writes=[CTf])

    def prologue(c, d, want_y, B):
        sm = B
        xs, xd, xdd, btf, btb, BTf, BTb, CTf, CTb, dbc = (B[k_] for k_ in
            ("xs", "xd", "xdd", "btf", "btb", "BTf", "BTb", "CTf", "CTb", "dbc"))
        P.pool("tensor_copy", reads=[btf], writes=[btb], out=btb[:], in_=btf[:])
        P.pool("tensor_copy", reads=[BTf], writes=[BTb], out=BTb[:], in_=BTf[:])
        P.pool("tensor_copy", reads=[CTf], writes=[CTb], out=CTb[:], in_=CTf[:])
        dsl = slice(d * 32, (d + 1) * 32)
        P.dve("tensor_tensor", reads=[sm["dtr"], par], writes=[sm["x0"]], out=sm["x0"][:], in0=sm["dtr"][:],
              in1=par[:, 0, dsl], op=ALU.add)
        P.dve("tensor_scalar", reads=[sm["x0"]], writes=[sm["mx"]], out=sm["mx"][:], in0=sm["x0"][:], scalar1=0.0,
              scalar2=None, op0=ALU.max)
        P.dve("scalar_tensor_tensor", reads=[sm["mx"], sm["x0"]], writes=[sm["na"]], out=sm["na"][:], in0=sm["mx"][:],
              scalar=-2.0, in1=sm["x0"][:], op0=ALU.mult, op1=ALU.add)
        P.act("activation", reads=[sm["na"]], writes=[sm["e"]], out=sm["e"][:], in_=sm["na"][:], func=AF.Exp)
        P.dve("tensor_scalar", reads=[sm["e"]], writes=[sm["e"]], out=sm["e"][:], in0=sm["e"][:], scalar1=1.0,
              scalar2=None, op0=ALU.add)
        P.act("activation", reads=[sm["e"]], writes=[sm["e"]], out=sm["e"][:], in_=sm["e"][:], func=AF.Ln)
        P.dve("tensor_tensor", reads=[sm["mx"], sm["e"]], writes=[sm["dt"]], out=sm["dt"][:], in0=sm["mx"][:],
              in1=sm["e"][:], op=ALU.add)
        P.dve("tensor_tensor", reads=[sm["dt"], abc], writes=[sm["dtA"]], out=sm["dtA"][:], in0=sm["dt"][:],
              in1=abc[:, dsl], op=ALU.mult)
        P.pe("matmul", reads=[tri, sm["dtA"]], writes=[B_lt], out=B_lt[:, 0:32], lhsT=tri[:, d, :], rhs=sm["dtA"][:],
             start=True, stop=True)
        P.pe("matmul", reads=[ones_f, sm["dtA"]], writes=[B_lt], accum=True, out=B_lt[:, 32:64], lhsT=ones_f[:],
             rhs=sm["dtA"][:], start=True, stop=True)
        P.dve("tensor_copy", reads=[B_lt], writes=[sm["la"]], out=sm["la"][:], in_=B_lt[:, 0:32])
        P.dve("tensor_copy", reads=[B_lt], writes=[sm["tot"]], out=sm["tot"][:], in_=B_lt[:, 32:64])
        P.dve("tensor_scalar", reads=[sm["la"]], writes=[sm["nla"]], out=sm["nla"][:], in0=sm["la"][:], scalar1=-1.0,
              scalar2=None, op0=ALU.mult)
        P.act("activation", reads=[sm["la"]], writes=[sm["ela"]], out=sm["ela"][:], in_=sm["la"][:], func=AF.Exp)
        P.dve("tensor_tensor", reads=[sm["tot"], sm["la"]], writes=[sm["dend"]], out=sm["dend"][:], in0=sm["tot"][:],
              in1=sm["la"][:], op=ALU.subtract)
        P.act("activation", reads=[sm["dend"]], writes=[sm["dend"]], out=sm["dend"][:], in_=sm["dend"][:], func=AF.Exp)
        P.act("activation", reads=[sm["tot"]], writes=[sm["cdec"]], out=sm["cdec"][:], in_=sm["tot"][:], func=AF.Exp)
        P.dve("tensor_tensor", reads=[sm["dt"], sm["dend"]], writes=[sm["dtd"]], out=sm["dtd"][:], in0=sm["dt"][:],
              in1=sm["dend"][:], op=ALU.mult)
        xs3 = xs[:, :].rearrange("p (h q) -> p h q", h=32)
        P.dve("tensor_tensor", reads=[xs, sm["dtd"]], writes=[xdd], out=xdd[:, :].rearrange("p (h q) -> p h q", h=32),
              in0=xs3, in1=sm["dtd"][:, :].unsqueeze(2).to_broadcast([128, 32, 64]), op=ALU.mult)
        if want_y:
            P.pool("tensor_tensor", reads=[xs, sm["dt"]], writes=[xd], out=xd[:, :].rearrange("p (h q) -> p h q", h=32),
                   in0=xs3, in1=sm["dt"][:, :].unsqueeze(2).to_broadcast([128, 32, 64]), op=ALU.mult)
            P.pool("tensor_copy", reads=[sm["dtA"]], writes=[dbc], out=dbc[:],
                   in_=sm["dtA"][:, :].unsqueeze(2).to_broadcast([128, 32, 128]))

    def body(c, d, want_y, B, hook, hook0):
        sm = B
        hook0()
        xs, xd, xdd, btb, BTb, CTb, dbc = (B[k_] for k_ in ("xs", "xd", "xdd", "btb", "BTb", "CTb", "dbc"))
        P.dve("tensor_tensor", reads=[tot_acc[d], sm["tot"]], writes=[tot_acc[d]], out=tot_acc[d][:], in0=tot_acc[d][:],
              in1=sm["tot"][:], op=ALU.add)

        def arg_group(g):
            A_ = B_arg.next()
            for k in range(4):
                h = 4 * g + k
                P.pe("matmul", reads=[dbc, tri], writes=[A_], out=A_[:, k * 128:(k + 1) * 128], lhsT=dbc[:, h, :],
                     rhs=tri[:, d, :], start=True, stop=False)
                P.pe("matmul", reads=[ident, nm], writes=[A_], accum=True, out=A_[:, k * 128:(k + 1) * 128],
                     lhsT=ident[:], rhs=nm[:, d, :], start=False, stop=True)
            return A_

        def cb_mm(g):
            C_ = B_cb.next()
            P.pe("matmul", reads=[BTb, CTb], writes=[C_], out=C_[:, 0:128], lhsT=BTb[:, g, :], rhs=CTb[:, g, :],
                 start=True, stop=True)
            return C_
        if want_y:
            cbs = {0: cb_mm(0)}
            A_next = arg_group(0)
        for g in range(8):
            if want_y:
                A_ = A_next
                L_ = Lt.next(); M_ = Mt.next()
                for k in range(4):
                    h = 4 * g + k
                    P.act("activation", reads=[A_, sm["nla"]], writes=[L_], out=L_[:, k * 128:(k + 1) * 128],
                          in_=A_[:, k * 128:(k + 1) * 128], func=AF.Exp, bias=sm["nla"][:, h:h + 1], scale=1.0)
                P.dve("tensor_tensor", reads=[L_, cbs[g]], writes=[M_], out=M_[:, :].rearrange("p (k l) -> p k l", k=4),
                      in0=L_[:, :].rearrange("p (k l) -> p k l", k=4),
                      in1=cbs[g][:, 0:128].unsqueeze(1).to_broadcast([128, 4, 128]), op=ALU.mult)
                if g + 1 < 8:
                    cbs[g + 1] = cb_mm(g + 1)
                    A_next = arg_group(g + 1)
                Yd = B_yd.next()
                for k in range(4):
                    h = 4 * g + k
                    P.pe("matmul", reads=[M_, xd], writes=[Yd], out=Yd[:, k * 64:(k + 1) * 64],
                         lhsT=M_[:, k * 128:(k + 1) * 128], rhs=xd[:, h * 64:(h + 1) * 64], start=True, stop=True)
                P.pe("matmul", reads=[CTb, Sb[d]], writes=[B_os], out=B_os[:, 0:256], lhsT=CTb[:, g, :],
                     rhs=Sb[d][:, g * 256:(g + 1) * 256], start=True, stop=True)
                t_ = tmpg.next()
                t3 = t_[:, :].rearrange("p (k q) -> p k q", k=4)
                P.dve("tensor_tensor", reads=[B_os, sm["ela"]], writes=[t_], out=t3,
                      in0=B_os[:, 0:256].rearrange("p (k q) -> p k q", k=4), in1=bc(sm["ela"], g), op=ALU.mult)
                P.dve("tensor_tensor", reads=[t_, Yd], writes=[ysb], out=ysb[:, g * 256:(g + 1) * 256], in0=t_[:, :],
                      in1=Yd[:, 0:256], op=ALU.add)
                if d == 0:
                    t2 = tmps.next()
                    P.pool("tensor_tensor", reads=[xs, dsum], writes=[t2], out=t2[:, :].rearrange("p (k q) -> p k q", k=4),
                           in0=v3(xs, g), in1=bc(dsum, g), op=ALU.mult)
                    P.pool("tensor_tensor", reads=[t2, ysb], writes=[ysb], out=ysb[:, g * 256:(g + 1) * 256],
                           in0=ysb[:, g * 256:(g + 1) * 256], in1=t2[:, :], op=ALU.add)
            P.pe("matmul", reads=[btb, xdd], writes=[B_os], accum=True, out=B_os[:, 256:512],
                 lhsT=btb[:, g * 128:(g + 1) * 128], rhs=xdd[:, g * 256:(g + 1) * 256], start=True, stop=True)
            P.dve("tensor_tensor", reads=[S[d], sm["cdec"]], writes=[S[d]], out=v3(S[d], g), in0=v3(S[d], g),
                  in1=bc(sm["cdec"], g), op=ALU.mult)
            P.dve("tensor_tensor", reads=[S[d], B_os], writes=[S[d]], out=S[d][:, g * 256:(g + 1) * 256],
                  in0=S[d][:, g * 256:(g + 1) * 256], in1=B_os[:, 256:512], op=ALU.add)
            if full:
                P.act("activation", reads=[S[d]], writes=[Sb[d]], out=Sb[d][:, g * 256:(g + 1) * 256],
                      in_=S[d][:, g * 256:(g + 1) * 256], func=AF.Copy)
            if g == 3:
                hook()
        if want_y:
            P.dma(y_d[d, c], ysb[:], reads=[ysb], writes=[ybuf])

    def zero_state(d):
        P.dve("memset", writes=[S[d]], ap=S[d][:], constant=0.0)
        P.dve("memset", writes=[Sb[d]], ap=Sb[d][:], constant=0.0)
        P.dve("memset", writes=[tot_acc[d]], ap=tot_acc[d][:], constant=0.0)

    def fold(d):
        P.dma(S[d][:], cF_d[d], writes=[S[d]])
        tl = P.sbuf([128, 7, 32], F32, f"tl{d}")
        P.dma(tl[:], Tl_d[d].rearrange("j p h -> p j h"), writes=[tl])
        P.act("activation", reads=[tl], writes=[tl], out=tl[:], in_=tl[:], func=AF.Exp)
        for j in range(7):
            P.dma(fb[:], Fl_d[d, j], writes=[fb])
            P.dve("tensor_tensor", reads=[S[d], tl], writes=[S[d]], out=S[d][:, :].rearrange("p (h q) -> p h q", h=32),
                  in0=S[d][:, :].rearrange("p (h q) -> p h q", h=32),
                  in1=tl[:, j, :].unsqueeze(2).to_broadcast([128, 32, 64]), op=ALU.mult)
            P.dve("tensor_tensor", reads=[S[d], fb], writes=[S[d]], out=S[d][:], in0=S[d][:], in1=fb[:], op=ALU.add)
        P.act("activation", reads=[S[d]], writes=[Sb[d]], out=Sb[d][:], in_=S[d][:], func=AF.Copy)

    order = {0: (list(range(8)), [8, 9]), 1: (list(range(7, -1, -1)), [9, 8])}
    steps = []
    for d in range(2):
        lat_order, ctx_order = order[d]
        steps.append(("zero", d))
        steps += [("unit", c, d, full) for c in ctx_order]
        if not full:
            steps += [("save_ctx", d), ("zero", d)]
        else:
            steps.append(("fold", d))
        steps += [("unit", c, d, full) for c in lat_order]
        if not full:
            steps.append(("save_F", d))
    unit_pos = [i for i, st_ in enumerate(steps) if st_[0] == "unit"]
    ordinal = {p_: k_ for k_, p_ in enumerate(unit_pos)}
    done_pro = set()
    done_ld = set()

    def ensure_ld(i):
        if i is None or i in done_ld:
            return
        _, c, d, wy = steps[i]
        pro_loads(c, d, wy, sets[ordinal[i] % 2])
        done_ld.add(i)

    def ensure_pro(i):
        if i is None or i in done_pro:
            return
        ensure_ld(i)
        _, c, d, wy = steps[i]
        prologue(c, d, wy, sets[ordinal[i] % 2])
        done_pro.add(i)
    for i, st_ in enumerate(steps):
        if st_[0] == "unit":
            _, c, d, wy = st_
            ensure_pro(i)
            k_ = ordinal[i]
            nxt = unit_pos[k_ + 1] if k_ + 1 < len(unit_pos) else None
            body(c, d, wy, sets[k_ % 2], lambda nxt=nxt: ensure_pro(nxt), lambda nxt=nxt: ensure_ld(nxt))
        elif st_[0] == "zero":
            zero_state(st_[1])
        elif st_[0] == "fold":
            fold(st_[1])
        elif st_[0] == "save_ctx":
            P.dma(cFo_d[st_[1]], S[st_[1]][:], reads=[S[st_[1]]])
        elif st_[0] == "save_F":
            P.dma(F_d[st_[1]], S[st_[1]][:], reads=[S[st_[1]]])
            P.dma(T_d[st_[1]], tot_acc[st_[1]][:], reads=[tot_acc[st_[1]]])
    if full:
        P.release(m_units)
        xs = P.sbuf([128, 2048], F32, "gxs")
        gw = P.sbuf([128, 2048], F32, "gw")
        P.dma(gw[:], gw_d, writes=[gw])
        zt = P.sbuf([128, 2048], F32, "zt"); y2 = P.sbuf([128, 2048], F32, "y2")
        gss = P.sbuf([128, 8], F32, "gss"); go = P.sbuf([128, 2048], BF16, "go")
        for c in range(10):
            P.dma(xs[:], y_d[0, c], reads=[ybuf], writes=[xs])
            P.dma(y2[:], y_d[1, c], reads=[ybuf], writes=[y2])
            P.dma(zt[:], z_d[c], writes=[zt])
            P.act("activation", reads=[zt], writes=[zt], out=zt[:], in_=zt[:], func=AF.Silu)
            P.dve("tensor_tensor", reads=[xs, y2], writes=[xs], out=xs[:], in0=xs[:], in1=y2[:], op=ALU.add)
            P.dve("tensor_tensor", reads=[xs, zt], writes=[xs], out=xs[:], in0=xs[:], in1=zt[:], op=ALU.mult)
            P.act("activation", reads=[xs], writes=[y2], out=y2[:], in_=xs[:], func=AF.Square)
            P.dve("tensor_reduce", reads=[y2], writes=[gss], out=gss[:], in_=y2[:, :].rearrange("p (g q) -> p g q", g=8),
                  axis=AX.X, op=ALU.add)
            P.dve("tensor_scalar", reads=[gss], writes=[gss], out=gss[:], in0=gss[:], scalar1=1.0 / 256, scalar2=EPS,
                  op0=ALU.mult, op1=ALU.add)
            P.act("activation", reads=[gss], writes=[gss], out=gss[:], in_=gss[:], func=AF.Sqrt)
            P.dve("reciprocal", reads=[gss], writes=[gss], out=gss[:], in_=gss[:])
            P.dve("tensor_tensor", reads=[xs, gss], writes=[xs], out=xs[:, :].rearrange("p (g q) -> p g q", g=8),
                  in0=xs[:, :].rearrange("p (g q) -> p g q", g=8), in1=gss[:, :].unsqueeze(2).to_broadcast([128, 8, 256]),
                  op=ALU.mult)
            P.pool("tensor_tensor", reads=[xs, gw], writes=[go], out=go[:], in0=xs[:], in1=gw[:], op=ALU.mult)
            P.dma(g_d[c], go[:], reads=[go])
    return P.finish()


def _ssd_consts():
    t = np.arange(128)
    tri = np.stack([(t[:, None] <= t[None, :]), (t[:, None] >= t[None, :])], 1).astype(np.float32)
    valid = np.stack([(t[None, :] >= t[:, None]), (t[None, :] <= t[:, None])], 1)
    negmask = np.where(valid, 0.0, -30000.0).astype(np.float32)
    return np.ascontiguousarray(tri), np.ascontiguousarray(negmask), np.eye(128, dtype=np.float32)


def run_ssd(r1, dt_bias, a_log, d_skip, norm_w):
    tri, negmask, ident = _ssd_consts()
    par = np.stack([dt_bias.reshape(64), a_log.reshape(64), d_skip.reshape(64)], 0)
    par = np.ascontiguousarray(np.broadcast_to(par[None], (128, 3, 64))).astype(np.float32)
    base = []
    for i in range(NCORES):
        xbc = r1[i]["xbc"]
        xbc_t = np.ascontiguousarray(xbc.reshape(4096, T_ALL).T)
        xs = xbc_t[:, 0:2048].reshape(10, 128, 2048)
        btok = xbc_t[:, 2048:3072].reshape(10, 128, 1024)
        BT = xbc[16:24].reshape(8, 128, 10, 128).transpose(2, 1, 0, 3)
        CT = xbc[24:32].reshape(8, 128, 10, 128).transpose(2, 1, 0, 3)
        base.append(dict(xs=np.ascontiguousarray(xs), btok=np.ascontiguousarray(btok), BT=np.ascontiguousarray(BT),
                         CT=np.ascontiguousarray(CT), dt=r1[i]["dt"], par=par, tri=tri, negmask=negmask, ident=ident,
                         ones=ONES))
    ra = _run(_prog("ssdA", build_ssd, "A"), base)
    ctxF = ra[0]["ctxF"]
    gw = np.ascontiguousarray(np.broadcast_to(norm_w[None], (128, 2048))).astype(np.float32)
    insb = []
    for i in range(NCORES):
        Fl = np.zeros((2, 7, 128, 2048), np.float32)
        Tl = np.zeros((2, 7, 128, 32), np.float32)
        for j, cj in enumerate(range(0, i)):
            Fl[0, j] = ra[cj]["F"][0]; Tl[0, j] = ra[cj]["T"][0]
        for j, cj in enumerate(range(NCORES - 1, i, -1)):
            Fl[1, j] = ra[cj]["F"][1]; Tl[1, j] = ra[cj]["T"][1]
        d = dict(base[i]); d.update(Flist=Fl, Tlist=Tl, ctxF=ctxF, z=r1[i]["z"], gw=gw)
        insb.append(d)
    rb = _run(_prog("ssdB", build_ssd, "B"), insb)
    return ra, rb


def build_na():
    P = Prog()
    qT_d = P.dram_in("qT", [128, 8, T_ALL], BF16)
    kT_d = P.dram_in("kT", [128, 8, 2048], BF16)
    ve_d = P.dram_in("ve", [128, 16, 8, 128], BF16)
    vo_d = P.dram_in("vo", [128, 16, 8, 128], BF16)
    kcT_d = P.dram_in("kcT", [128, 8, 256], BF16)
    vc_d = P.dram_in("vc", [128, 2, 8, 128], BF16)
    tt_d = P.dram_in("tt", [128, 8, 8, 64])
    vm_d = P.dram_in("vm", [128, 16, 8])
    ones_d = P.dram_in("ones", [128, 128])
    o_d = P.dram_out("o", [128, 8, T_ALL], BF16)
    ones_f, ones_b = load_consts(P, ones_d)
    qT = P.sbuf([128, 8, T_ALL], BF16, "qT"); kT = P.sbuf([128, 8, 2048], BF16, "kT")
    ve = P.sbuf([128, 16, 8, 128], BF16, "ve"); vo = P.sbuf([128, 16, 8, 128], BF16, "vo")
    kcT = P.sbuf([128, 8, 256], BF16, "kcT"); vc = P.sbuf([128, 2, 8, 128], BF16, "vc")
    TT = P.sbuf([128, 8, 8, 64], F32, "TT"); vm = P.sbuf([128, 16, 8], F32, "vm")
    oT = P.sbuf([128, 8, T_ALL], BF16, "oT")
    for sb_, d_ in ((qT, qT_d), (kT, kT_d), (ve, ve_d), (vo, vo_d), (kcT, kcT_d), (vc, vc_d), (TT, tt_d), (vm, vm_d)):
        P.dma(sb_[:], d_, writes=[sb_])
    tb = Rot([P.sbuf([128, 512], F32, f"tb{i}") for i in range(2)])
    pw = Rot([P.sbuf([128, 512], BF16, f"pw{i}") for i in range(2)])
    pc = Rot([P.sbuf([128, 128], BF16, f"pc{i}") for i in range(2)])
    rec = Rot([P.sbuf([128, 128], F32, f"rec{i}") for i in range(2)])
    BA = Rot([P.bank(0), P.bank(1)]); BB = Rot([P.bank(2), P.bank(3)])
    BO = Rot([P.bank(4), P.bank(5)]); BD = Rot([P.bank(6), P.bank(7)])
    scale = 128 ** -0.5
    its = [(lr, h) for lr in range(16) for h in range(8)]

    def s_stage(lr, h):
        q = qT[:, h, lr * 64:(lr + 1) * 64]
        A_ = BA.next(); B_ = BB.next()
        for pb in range(8):
            off = (lr + 2 * pb) * 64
            P.pe("matmul", reads=[kT, qT], writes=[A_], out=A_[:, pb * 64:(pb + 1) * 64], lhsT=kT[:, h, off:off + 128],
                 rhs=q, start=True, stop=True)
        for cb in range(2):
            P.pe("matmul", reads=[kcT, qT], writes=[B_], out=B_[:, cb * 64:(cb + 1) * 64],
                 lhsT=kcT[:, h, cb * 128:(cb + 1) * 128], rhs=q, start=True, stop=True)
        return A_, B_
    nxt = s_stage(*its[0])
    for i, (lr, h) in enumerate(its):
        if True:
            A_, B_ = nxt
            O = BO.next(); Dn = BD.next()
            t_ = tb.next(); p_ = pw.next(); c_ = pc.next()
            P.dve("scalar_tensor_tensor", reads=[A_, TT], writes=[t_], out=t_[:, :], in0=A_[:, :], scalar=scale,
                  in1=TT[:, h, :, :].rearrange("p a b -> p (a b)"), op0=ALU.mult, op1=ALU.add)
            P.pool("tensor_tensor", reads=[t_, vm], writes=[t_], out=t_[:, :].rearrange("p (a b) -> p a b", a=8),
                   in0=t_[:, :].rearrange("p (a b) -> p a b", a=8),
                   in1=vm[:, lr, :].unsqueeze(2).to_broadcast([128, 8, 64]), op=ALU.add)
            P.act("activation", reads=[t_], writes=[p_], out=p_[:, :], in_=t_[:, :], func=AF.Exp)
            P.act("activation", reads=[B_], writes=[c_], out=c_[:, :], in_=B_[:, 0:128], func=AF.Exp, scale=scale)
            if i + 1 < len(its):
                nxt = s_stage(*its[i + 1])
            for pb in range(8):
                row = lr + 2 * pb
                vsrc = ve[:, row // 2, h, :] if row % 2 == 0 else vo[:, row // 2, h, :]
                vbuf = ve if row % 2 == 0 else vo
                P.pe("matmul", reads=[vbuf, p_], writes=[O], accum=(pb > 0), out=O[:, 0:64], lhsT=vsrc,
                     rhs=p_[:, pb * 64:(pb + 1) * 64], start=(pb == 0), stop=False)
            for cb in range(2):
                P.pe("matmul", reads=[vc, c_], writes=[O], accum=True, out=O[:, 0:64], lhsT=vc[:, cb, h, :],
                     rhs=c_[:, cb * 64:(cb + 1) * 64], start=False, stop=(cb == 1))
            for pb in range(8):
                P.pe("matmul", reads=[ones_b, p_], writes=[Dn], accum=(pb > 0), out=Dn[:, 0:64], lhsT=ones_b[:],
                     rhs=p_[:, pb * 64:(pb + 1) * 64], start=(pb == 0), stop=False)
            for cb in range(2):
                P.pe("matmul", reads=[ones_b, c_], writes=[Dn], accum=True, out=Dn[:, 0:64], lhsT=ones_b[:],
                     rhs=c_[:, cb * 64:(cb + 1) * 64], start=False, stop=(cb == 1))
            r_ = rec.next()
            P.dve("reciprocal", reads=[Dn], writes=[r_], out=r_[:, 0:64], in_=Dn[:, 0:64])
            P.dve("tensor_tensor", reads=[O, r_], writes=[oT], out=oT[:, h, lr * 64:(lr + 1) * 64], in0=O[:, 0:64],
                  in1=r_[:, 0:64], op=ALU.mult)
    for qb in range(2):
        for h in range(8):
            q = qT[:, h, T_LAT + qb * 128:T_LAT + (qb + 1) * 128]
            B_ = BB.next(); O = BO.next(); Dn = BD.next()
            for cb in range(2):
                P.pe("matmul", reads=[kcT, qT], writes=[B_], out=B_[:, cb * 128:(cb + 1) * 128],
                     lhsT=kcT[:, h, cb * 128:(cb + 1) * 128], rhs=q, start=True, stop=True)
            p_ = pw.next()
            P.act("activation", reads=[B_], writes=[p_], out=p_[:, 0:256], in_=B_[:, 0:256], func=AF.Exp, scale=scale)
            for cb in range(2):
                P.pe("matmul", reads=[vc, p_], writes=[O], accum=(cb > 0), out=O[:, 0:128], lhsT=vc[:, cb, h, :],
                     rhs=p_[:, cb * 128:(cb + 1) * 128], start=(cb == 0), stop=(cb == 1))
            for cb in range(2):
                P.pe("matmul", reads=[ones_b, p_], writes=[Dn], accum=(cb > 0), out=Dn[:, 0:128], lhsT=ones_b[:],
                     rhs=p_[:, cb * 128:(cb + 1) * 128], start=(cb == 0), stop=(cb == 1))
            r_ = rec.next()
            P.dve("reciprocal", reads=[Dn], writes=[r_], out=r_[:, 0:128], in_=Dn[:, 0:128])
            P.dve("tensor_tensor", reads=[O, r_], writes=[oT], out=oT[:, h, T_LAT + qb * 128:T_LAT + (qb + 1) * 128],
                  in0=O[:, 0:128], in1=r_[:, 0:128], op=ALU.mult)
    P.dma(o_d, oT[:], reads=[oT])
    return P.finish()


def run_na(r1, rpb):
    a = np.arange(64)
    c0 = np.clip(a - 8, 0, 48)
    b = np.arange(64)
    colok = (b[:, None] >= c0[None, :]) & (b[:, None] < c0[None, :] + 16)
    dc = np.clip(b[:, None] - a[None, :], -15, 15) + 15
    TT = np.full((2, 64, 8, 8, 64), -30000.0, np.float32)
    for pb in range(8):
        for jj in range(2):
            dr = 2 * pb + jj - 1
            if 0 <= dr < 15:
                vals = rpb[:, dr][:, dc]
                TT[jj, :, :, pb, :] = np.where(colok[None], vals, np.float32(-30000.0)).transpose(1, 0, 2)
    TT = np.ascontiguousarray(TT.reshape(128, 8, 8, 64))
    k_lat = np.concatenate([r1[i]["k"][0:8].reshape(T_LAT, 8, 128) for i in range(NCORES)], 0)
    v_lat = np.concatenate([r1[i]["v"][0:8].reshape(T_LAT, 8, 128) for i in range(NCORES)], 0)
    k_ctx = r1[0]["k"][8:10].reshape(T_CTX, 8, 128)
    v_ctx = r1[0]["v"][8:10].reshape(T_CTX, 8, 128)
    kcT = np.ascontiguousarray(k_ctx.transpose(2, 1, 0))
    vc = np.ascontiguousarray(v_ctx.reshape(2, 128, 8, 128).transpose(1, 0, 2, 3))
    zk = np.zeros((64, 8, 128), k_lat.dtype)
    ins = []
    for i in range(NCORES):
        base = 16 * i - 8
        kw = []; vw = []
        for v in range(33):
            row = base + v
            if 0 <= row < 128:
                kw.append(k_lat[row * 64:(row + 1) * 64]); vw.append(v_lat[row * 64:(row + 1) * 64])
            else:
                kw.append(zk); vw.append(zk)
        kwin = np.concatenate(kw[:32], 0)
        kT = np.ascontiguousarray(kwin.transpose(2, 1, 0))
        ve = np.stack([np.concatenate([vw[2 * j], vw[2 * j + 1]], 0) for j in range(16)], 1)
        vo = np.stack([np.concatenate([vw[2 * j + 1], vw[2 * j + 2]], 0) for j in range(16)], 1)
        vm = np.full((2, 64, 16, 8), -30000.0, np.float32)
        for lr in range(16):
            r = 16 * i + lr
            rs = min(max(r - 4, 0), 120)
            for pb in range(8):
                for jj in range(2):
                    krow = r - 8 + 2 * pb + jj
                    if rs <= krow < rs + 8:
                        vm[jj, :, lr, pb] = 0.0
        q = r1[i]["q"].reshape(T_ALL, 8, 128)
        ins.append(dict(qT=np.ascontiguousarray(q.transpose(2, 1, 0)), kT=kT, ve=np.ascontiguousarray(ve),
                        vo=np.ascontiguousarray(vo), kcT=kcT, vc=vc, tt=TT, vm=np.ascontiguousarray(vm.reshape(128, 16, 8)),
                        ones=ONES))
    return _run(_prog("na", build_na), ins)


def run_ab_layer(x, ctx, mod_l, nw1, w_in, conv_w, conv_b, dt_bias, a_log, d_skip, norm_w, q_norm, k_norm, rpb,
                 w_out, nw2, w1, w2):
    r1 = run_ab1(x, ctx, mod_l, nw1, w_in, conv_w, conv_b, q_norm, k_norm)
    ra, rb = run_ssd(r1, dt_bias, a_log, d_skip, norm_w)
    rn = run_na(r1, rpb)
    u_lat = []
    for i in range(NCORES):
        gT = fm(rb[i]["g"].reshape(T_ALL, 2048))
        u_lat.append(np.concatenate([gT[:, :, :T_LAT], rn[i]["o"][:, :, :T_LAT]], axis=1))
    gT0 = fm(rb[0]["g"].reshape(T_ALL, 2048))
    u_ctx = np.concatenate([gT0[:, :, T_LAT:], rn[0]["o"][:, :, T_LAT:]], axis=1)
    return run_post2(u_lat, u_ctx, x, ctx, w_out, w1, w2, mod_l, nw2)


def kernel(x, c, ctx, c_ctx, w_mod, b_mod, norm1_w, norm2_w, w_mlp_in, w_mlp_out,
           ab_w_in, ab_conv_w, ab_conv_b, ab_dt_bias, ab_a_log, ab_d_skip, ab_norm_w,
           ab_q_norm, ab_k_norm, ab_rpb, ab_w_out, c_w_qkv, c_q_norm, c_k_norm, c_w_out):
    f = lambda a: np.asarray(a, dtype=np.float32)
    xs = f(x)[0]
    cs = f(ctx)[0]
    mods = run_mod(f(c), f(c_ctx), f(w_mod), f(b_mod))
    for layer in range(4):
        i = layer // 2
        if layer % 2 == 0:
            xs, cs = run_ab_layer(xs, cs, mods[layer], f(norm1_w)[layer], f(ab_w_in)[i], f(ab_conv_w)[i],
                                  f(ab_conv_b)[i], f(ab_dt_bias)[i], f(ab_a_log)[i], f(ab_d_skip)[i],
                                  f(ab_norm_w)[i], f(ab_q_norm)[i], f(ab_k_norm)[i], f(ab_rpb)[i], f(ab_w_out)[i],
                                  f(norm2_w)[layer], f(w_mlp_in)[layer], f(w_mlp_out)[layer])
        else:
            xs, cs = run_c_layer(xs, cs, mods[layer], f(norm1_w)[layer], f(c_w_qkv)[i], f(c_q_norm)[i],
                                 f(c_k_norm)[i], f(c_w_out)[i], f(norm2_w)[layer], f(w_mlp_in)[layer],
                                 f(w_mlp_out)[layer])
    return np.ascontiguousarray(xs[None].astype(np.float32))


def build_post2(KM):
    KC = KM // 128
    HQ = 8
    P = Prog()
    uT_d = P.dram_in("uT", [128, KC, T_ALL], BF16)
    xT_d = P.dram_in("xT", [128, KD, T_ALL])
    wo_d = P.dram_in("wo", [16, 128, KC, 128])
    w1_d = P.dram_in("w1", [64, 128, KD, 128])
    w2_d = P.dram_in("w2", [8, 16, 128, HQ, 128])
    mod_d = P.dram_in("mod", [128, 6, KD, 2])
    nw_d = P.dram_in("nw", [128, KD])
    ones_d = P.dram_in("ones", [128, 128])
    out_d = P.dram_out("xo", [128, KD, T_ALL])

    ones_f, ones_b = load_consts(P, ones_d)
    mod_sb, A = mod_vectors(P, mod_d, nw_d, slot_sc=4)
    xs = [P.sbuf([128, KD, n], F32, f"xs{j}") for j, (st, n, w) in enumerate(TILES)]
    for j, (st, n, w) in enumerate(TILES):
        P.dma(xs[j][:], xT_d[:, :, st:st + n], writes=[xs[j]])
    m1 = P.mark()
    us = P.sbuf([128, KC, T_ALL], BF16, "us")
    P.dma(us[:], uT_d, writes=[us])
    ws = WStream(P, KC * 128, name="wsA", nstage=2, nbf=2)
    for ob in range(16):
        wb, wv = ws.load(wo_d[ob], KC, 128)
        for j, (st, n, w) in enumerate(TILES):
            ps = P.psum()
            for k in range(KC):
                P.pe("matmul", reads=[*wb, us], writes=[ps], accum=(k > 0),
                     out=ps[:, :n], lhsT=wv[:, k, :], rhs=us[:, k, st:st + n], start=(k == 0), stop=(k == KC - 1))
            P.dve("scalar_tensor_tensor", reads=[ps, mod_sb, xs[j]], writes=[xs[j]],
                  out=xs[j][:, ob, :], in0=ps[:, :n], scalar=mod_sb[:, 2, ob, w:w + 1],
                  in1=xs[j][:, ob, :], op0=ALU.mult, op1=ALU.add)
    P.release(m1)
    hs = [P.sbuf([128, KD, n], BF16, f"hs{j}") for j, (st, n, w) in enumerate(TILES)]
    t1 = Rot([P.sbuf([128, 512], F32, f"t1_{i}") for i in range(2)])
    tmp = Rot([P.sbuf([128, 512], F32, f"tmp{i}") for i in range(3)])
    m2 = P.mark()
    sq = P.sbuf([128, KD, 512], BF16, "sq")
    for j, (st, n, w) in enumerate(TILES):
        norm_mod_tile(P, lambda k, j=j: xs[j][:, k, :], xs[j], lambda k, j=j: hs[j][:, k, :], hs[j], n, w, A, mod_sb, 3,
                      ones_b, sq.ap, sq, t1, tmp)
    P.release(m2)
    aq = [P.sbuf([128, HQ, n], BF16, f"aq{j}") for j, (st, n, w) in enumerate(TILES)]
    ws = WStream(P, KD * 128, name="wsB", nstage=2, nbf=2)
    for hq in range(64 // HQ):
        for hc in range(HQ):
            wb, wv = ws.load(w1_d[hq * HQ + hc], KD, 128)
            for j, (st, n, w) in enumerate(TILES):
                ps = P.psum()
                for k in range(KD):
                    P.pe("matmul", reads=[*wb, hs[j]], writes=[ps], accum=(k > 0),
                         out=ps[:, :n], lhsT=wv[:, k, :], rhs=hs[j][:, k, :], start=(k == 0), stop=(k == KD - 1))
                r = tmp.next()
                P.act("activation", reads=[ps], writes=[r], out=r[:, :n], in_=ps[:, :n], func=AF.Relu)
                P.dve("tensor_tensor", reads=[r], writes=[aq[j]], out=aq[j][:, hc, :], in0=r[:, :n], in1=r[:, :n],
                      op=ALU.mult)
        for ob in range(16):
            wb, wv = ws.load(w2_d[hq, ob], HQ, 128)
            for j, (st, n, w) in enumerate(TILES):
                ps = P.psum()
                for k in range(HQ):
                    P.pe("matmul", reads=[*wb, aq[j]], writes=[ps], accum=(k > 0),
                         out=ps[:, :n], lhsT=wv[:, k, :], rhs=aq[j][:, k, :], start=(k == 0), stop=(k == HQ - 1))
                P.dve("scalar_tensor_tensor", reads=[ps, mod_sb, xs[j]], writes=[xs[j]],
                      out=xs[j][:, ob, :], in0=ps[:, :n], scalar=mod_sb[:, 5, ob, w:w + 1], in1=xs[j][:, ob, :],
                      op0=ALU.mult, op1=ALU.add)
    for j, (st, n, w) in enumerate(TILES):
        P.dma(out_d[:, :, st:st + n], xs[j][:], reads=[xs[j]])
    return P.finish()


def run_post2(u_lat, u_ctx, x, ctx, w_out, w1, w2, mod_l, nw2):
    KM = w_out.shape[0]
    wo = wblocks(w_out, 128)
    w1b = wblocks(w1, 128)
    w2b = np.ascontiguousarray(w2.reshape(8, 8, 128, 16, 128).transpose(0, 3, 2, 1, 4))
    nw = vec_fm(nw2)
    xTs = xT_cores(x, ctx)
    ins = []
    for i in range(NCORES):
        uT = np.ascontiguousarray(np.concatenate([u_lat[i], u_ctx], axis=2))
        ins.append(dict(uT=uT, xT=xTs[i], wo=wo, w1=w1b, w2=w2b, mod=mod_l, nw=nw, ones=ONES))
    res = _run(_prog("post2", build_post2, KM), ins)
    outs = [unfm(res[i]["xo"]) for i in range(NCORES)]
    xn = np.concatenate([o[:T_LAT] for o in outs], axis=0)
    cn = outs[0][T_LAT:]
    return xn, cn
```
